# Optimizing a Trainium2 kernel written in Bass

```python
import jax, jax.numpy as jnp
from jax import lax
import numpy as np

D_MODEL = 1024
BATCH = 8
SEQ = 2048
DEPTH = 1

LRU_WIDTH = 1024
LRU_BLOCKS = 16
LRU_BLOCK_W = LRU_WIDTH // LRU_BLOCKS
CONV_WIDTH = 4
LRU_C = 8.0
HEAD_DIM = 64
N_Q_HEADS = 16
N_KV_HEADS = 4
Q_PER_KV = N_Q_HEADS // N_KV_HEADS
ATTN_WIDTH = N_Q_HEADS * HEAD_DIM
KV_WIDTH = N_KV_HEADS * HEAD_DIM
WINDOW = 128
BLOCK = 128
ROPE_THETA = 10000.0
D_FF = 2816
MACARON_SCALE = 0.5
NORM_EPS = 1e-6
MASK_VALUE = -1e30
IN_WIDTH = 2 * LRU_WIDTH + ATTN_WIDTH + 2 * KV_WIDTH + 2 * D_MODEL
IN_SPLITS = (
    LRU_WIDTH,
    2 * LRU_WIDTH,
    2 * LRU_WIDTH + ATTN_WIDTH,
    2 * LRU_WIDTH + ATTN_WIDTH + KV_WIDTH,
    2 * LRU_WIDTH + ATTN_WIDTH + 2 * KV_WIDTH,
    2 * LRU_WIDTH + ATTN_WIDTH + 2 * KV_WIDTH + D_MODEL,
)

kernel_name = "hybrid_rglru_swa_sink_macaron_block"


def rms_norm(x, g):
    xf = x.astype(jnp.float32)
    y = xf * lax.rsqrt(jnp.mean(xf * xf, axis=-1, keepdims=True) + NORM_EPS)
    return (y * g.astype(jnp.float32)).astype(x.dtype)


def swiglu(x, w_gu, w_down):
    g, u = jnp.split(x @ w_gu, 2, axis=-1)
    return (jax.nn.silu(g) * u) @ w_down


def rope_tables(seq_len):
    half = HEAD_DIM // 2
    inv_freq = ROPE_THETA ** (-jnp.arange(half, dtype=jnp.float32) / half)
    ang = jnp.arange(seq_len, dtype=jnp.float32)[:, None] * inv_freq[None, :]
    return jnp.cos(ang), jnp.sin(ang)


def apply_rope(x, cos, sin):
    half = HEAD_DIM // 2
    xf = x.astype(jnp.float32)
    x1, x2 = xf[..., :half], xf[..., half:]
    c = cos[None, :, None, :]
    s = sin[None, :, None, :]
    return jnp.concatenate([x1 * c - x2 * s, x2 * c + x1 * s], axis=-1).astype(x.dtype)


def causal_depthwise_conv(x, w, b):
    s = x.shape[1]
    xp = jnp.pad(x, ((0, 0), (CONV_WIDTH - 1, 0), (0, 0)))
    y = xp[:, 0:s] * w[0]
    for k in range(1, CONV_WIDTH):
        y = y + xp[:, k:k + s] * w[k]
    return y + b


def block_diag_linear(x, w, b):
    bsz, s, _ = x.shape
    xb = x.reshape(bsz, s, LRU_BLOCKS, LRU_BLOCK_W)
    y = jnp.einsum('bsnc,ncd->bsnd', xb, w).reshape(bsz, s, LRU_WIDTH)
    return y + b


def rg_lru(x, w_a, b_a, w_x, b_x, lam):
    xf = x.astype(jnp.float32)
    r = jax.nn.sigmoid(block_diag_linear(x, w_a, b_a).astype(jnp.float32))
    i = jax.nn.sigmoid(block_diag_linear(x, w_x, b_x).astype(jnp.float32))
    log_a = -LRU_C * r * jax.nn.softplus(-lam.astype(jnp.float32))
    a = jnp.exp(log_a)
    mult = jnp.sqrt(jnp.maximum(-jnp.expm1(2.0 * log_a), 0.0))
    bterm = mult * (i * xf)

    def combine(left, right):
        a1, b1 = left
        a2, b2 = right
        return a1 * a2, a2 * b1 + b2

    _, h = lax.associative_scan(combine, (a, bterm), axis=1)
    return h.astype(x.dtype)


def band_blocks(t):
    bsz, s, h, d = t.shape
    tb = t.reshape(bsz, s // BLOCK, BLOCK, h, d)
    prev = jnp.pad(tb, ((0, 0), (1, 0), (0, 0), (0, 0), (0, 0)))[:, :-1]
    return jnp.concatenate([prev, tb], axis=2)


def sliding_window_attention_with_sinks(q, k, v, sinks):
    bsz, s = q.shape[0], q.shape[1]
    nb = s // BLOCK
    qb = q.reshape(bsz, nb, BLOCK, N_KV_HEADS, Q_PER_KV, HEAD_DIM)
    kb = band_blocks(k)
    vb = band_blocks(v)
    scores = jnp.einsum('bnqhgd,bnshd->bhgnqs', qb, kb).astype(jnp.float32) * (HEAD_DIM ** -0.5)
    qi = jnp.arange(BLOCK)[:, None]
    si = jnp.arange(2 * BLOCK)[None, :]
    diff = BLOCK + qi - si
    key_pos = (jnp.arange(nb)[:, None, None] - 1) * BLOCK + si[None]
    mask = ((diff >= 0) & (diff < WINDOW))[None] & (key_pos >= 0)
    scores = jnp.where(mask, scores, MASK_VALUE)
    sink = sinks.astype(jnp.float32).reshape(1, N_KV_HEADS, Q_PER_KV, 1, 1, 1)
    sink = jnp.broadcast_to(sink, scores.shape[:-1] + (1,))
    probs = jax.nn.softmax(jnp.concatenate([scores, sink], axis=-1), axis=-1)[..., :-1]
    out = jnp.einsum('bhgnqs,bnshd->bnqhgd', probs.astype(v.dtype), vb)
    return out.reshape(bsz, s, ATTN_WIDTH)


def hybrid_mixer(u, cos, sin, w_in, conv_w, conv_b, lru_w_a, lru_b_a, lru_w_x, lru_b_x,
                 lru_lambda, attn_sinks, w_proj_lru, w_proj_attn, w_out):
    bsz, s, _ = u.shape
    z = u @ w_in
    gate_br, x_br, q, k, v, g_lru, g_attn = jnp.split(z, IN_SPLITS, axis=-1)
    xc = causal_depthwise_conv(x_br, conv_w, conv_b)
    y_lru = rg_lru(xc, lru_w_a, lru_b_a, lru_w_x, lru_b_x, lru_lambda) * jax.nn.gelu(gate_br)
    q = apply_rope(q.reshape(bsz, s, N_Q_HEADS, HEAD_DIM), cos, sin)
    k = apply_rope(k.reshape(bsz, s, N_KV_HEADS, HEAD_DIM), cos, sin)
    v = v.reshape(bsz, s, N_KV_HEADS, HEAD_DIM)
    y_attn = sliding_window_attention_with_sinks(q, k, v, attn_sinks)
    merged = jax.nn.sigmoid(g_lru) * (y_lru @ w_proj_lru) + jax.nn.sigmoid(g_attn) * (y_attn @ w_proj_attn)
    return merged @ w_out


def setup_inputs(seed: int = 0) -> dict:
    key = jax.random.key(seed)
    ks = jax.random.split(key, 24)

    def nrm(k, shape, scale):
        return jax.random.normal(k, shape, jnp.float32) * scale

    def gain(k, shape):
        return 1.0 + 0.05 * jax.random.normal(k, shape, jnp.float32)

    u = jax.random.uniform(ks[13], (DEPTH, LRU_WIDTH), jnp.float32, minval=0.9, maxval=0.999)
    a0 = u ** (1.0 / LRU_C)
    lam = jnp.log(a0) - jnp.log1p(-a0)
    return {
        "x": nrm(ks[0], (BATCH, SEQ, D_MODEL), 1.0),
        "ffn1_pre_g": gain(ks[1], (DEPTH, D_MODEL)),
        "ffn1_w_gu": nrm(ks[2], (DEPTH, D_MODEL, 2 * D_FF), D_MODEL ** -0.5),
        "ffn1_w_down": nrm(ks[3], (DEPTH, D_FF, D_MODEL), D_FF ** -0.5),
        "ffn1_post_g": gain(ks[4], (DEPTH, D_MODEL)),
        "mix_pre_g": gain(ks[5], (DEPTH, D_MODEL)),
        "w_in": nrm(ks[6], (DEPTH, D_MODEL, IN_WIDTH), D_MODEL ** -0.5),
        "conv_w": nrm(ks[7], (DEPTH, CONV_WIDTH, LRU_WIDTH), CONV_WIDTH ** -0.5),
        "conv_b": nrm(ks[8], (DEPTH, LRU_WIDTH), 0.01),
        "lru_w_a": nrm(ks[9], (DEPTH, LRU_BLOCKS, LRU_BLOCK_W, LRU_BLOCK_W), LRU_BLOCK_W ** -0.5),
        "lru_b_a": nrm(ks[10], (DEPTH, LRU_WIDTH), 0.01),
        "lru_w_x": nrm(ks[11], (DEPTH, LRU_BLOCKS, LRU_BLOCK_W, LRU_BLOCK_W), LRU_BLOCK_W ** -0.5),
        "lru_b_x": nrm(ks[12], (DEPTH, LRU_WIDTH), 0.01),
        "lru_lambda": lam,
        "attn_sinks": nrm(ks[14], (DEPTH, N_Q_HEADS), 0.5),
        "w_proj_lru": nrm(ks[15], (DEPTH, LRU_WIDTH, D_MODEL), LRU_WIDTH ** -0.5),
        "w_proj_attn": nrm(ks[16], (DEPTH, ATTN_WIDTH, D_MODEL), ATTN_WIDTH ** -0.5),
        "w_out": nrm(ks[17], (DEPTH, D_MODEL, D_MODEL), D_MODEL ** -0.5),
        "mix_post_g": gain(ks[18], (DEPTH, D_MODEL)),
        "ffn2_pre_g": gain(ks[19], (DEPTH, D_MODEL)),
        "ffn2_w_gu": nrm(ks[20], (DEPTH, D_MODEL, 2 * D_FF), D_MODEL ** -0.5),
        "ffn2_w_down": nrm(ks[21], (DEPTH, D_FF, D_MODEL), D_FF ** -0.5),
        "ffn2_post_g": gain(ks[22], (DEPTH, D_MODEL)),
    }


def reference(x, ffn1_pre_g, ffn1_w_gu, ffn1_w_down, ffn1_post_g,
              mix_pre_g, w_in, conv_w, conv_b, lru_w_a, lru_b_a, lru_w_x, lru_b_x,
              lru_lambda, attn_sinks, w_proj_lru, w_proj_attn, w_out, mix_post_g,
              ffn2_pre_g, ffn2_w_gu, ffn2_w_down, ffn2_post_g):
    cos, sin = rope_tables(x.shape[1])
    h = x
    for l in range(DEPTH):
        f = swiglu(rms_norm(h, ffn1_pre_g[l]), ffn1_w_gu[l], ffn1_w_down[l])
        h = h + MACARON_SCALE * rms_norm(f, ffn1_post_g[l])
        m = hybrid_mixer(rms_norm(h, mix_pre_g[l]), cos, sin, w_in[l], conv_w[l], conv_b[l],
                         lru_w_a[l], lru_b_a[l], lru_w_x[l], lru_b_x[l], lru_lambda[l],
                         attn_sinks[l], w_proj_lru[l], w_proj_attn[l], w_out[l])
        h = h + rms_norm(m, mix_post_g[l])
        f = swiglu(rms_norm(h, ffn2_pre_g[l]), ffn2_w_gu[l], ffn2_w_down[l])
        h = h + MACARON_SCALE * rms_norm(f, ffn2_post_g[l])
    return h
```

```python
import os
from contextlib import ExitStack

import numpy as np
import concourse.bass as bass
import concourse.mybir as mybir
from concourse.bass_utils import run_bass_kernel_spmd

F32 = mybir.dt.float32
BF16 = mybir.dt.bfloat16
AF = mybir.ActivationFunctionType
ALU = mybir.AluOpType

D = 1024
S = 2048
B = 8
DFF = 2816
NFF = DFF // 128
EPS = 1e-6
NCORES = 8

ENGS = ("pe", "act", "dve", "pool", "sp")


class V:
    __slots__ = ("ap", "regs")

    def __init__(self, ap, regs):
        self.ap = ap
        self.regs = regs


class Buf:
    def __init__(self, ap, space, byte_off, esize, free_shape):
        self.ap = ap
        self.space = space
        self.off = byte_off
        self.esize = esize
        self.fs = tuple(free_shape)
        st = []
        acc = 1
        for d in reversed(self.fs):
            st.append(acc)
            acc *= d
        self.strides = tuple(reversed(st))
        self.nbytes = acc * esize

    def __getitem__(self, key):
        if not isinstance(key, tuple):
            key = (key,)
        ap = self.ap[key]
        fk = key[1:]
        rng = []
        for i, d in enumerate(self.fs):
            if i < len(fk):
                k = fk[i]
                if isinstance(k, slice):
                    a = 0 if k.start is None else k.start
                    b = d if k.stop is None else k.stop
                else:
                    a, b = k, k + 1
            else:
                a, b = 0, d
            assert 0 <= a < b <= d, (key, self.fs)
            rng.append((a, b))
        j = -1
        for i, (a, b) in enumerate(rng):
            if (a, b) != (0, self.fs[i]):
                j = i
        if j < 0:
            return self.all()
        combos = [0]
        for i in range(j):
            a, b = rng[i]
            combos = [c + x * self.strides[i] for c in combos for x in range(a, b)]
            if len(combos) > 64:
                combos = None
                break
        if combos is None:
            lo = sum(a * s for (a, b), s in zip(rng, self.strides))
            hi = sum((b - 1) * s for (a, b), s in zip(rng, self.strides))
            return V(ap, [(self.space, self.off + lo * self.esize, self.off + (hi + 1) * self.esize)])
        a, b = rng[j]
        st = self.strides[j]
        regs = [(self.space, self.off + (c + a * st) * self.esize, self.off + (c + b * st) * self.esize)
                for c in combos]
        regs.sort()
        out = [regs[0]]
        for r in regs[1:]:
            if r[1] == out[-1][2]:
                out[-1] = (out[-1][0], out[-1][1], r[2])
            else:
                out.append(r)
        return V(ap, out)

    def all(self):
        return V(self.ap, [(self.space, self.off, self.off + self.nbytes)])


class Op:
    __slots__ = ("eng", "fn", "idx", "deps", "ddeps", "is_dma", "dma_key", "dma_cnt", "signal", "cnt", "waits")


class Rec:
    __slots__ = ("lo", "hi", "op", "w")

    def __init__(self, lo, hi, op, w):
        self.lo = lo
        self.hi = hi
        self.op = op
        self.w = w


class Prog:
    def __init__(self):
        self.ops = {e: [] for e in ENGS}
        self.recs = {"S": [], "P": []}
        self.dma_cnt = {}
        self.out_dmas = []

    def add(self, eng, fn, reads=(), writes=(), dma_key=None):
        op = Op()
        op.eng = eng
        op.fn = fn
        op.idx = len(self.ops[eng])
        op.is_dma = dma_key is not None
        op.dma_key = dma_key
        op.signal = False
        op.cnt = 0
        op.waits = None
        if op.is_dma:
            self.dma_cnt[dma_key] = self.dma_cnt.get(dma_key, 0) + 16
            op.dma_cnt = self.dma_cnt[dma_key]
        deps = {}
        ddeps = {}

        def add_dep(d, raw):
            if d is op:
                return
            if d.is_dma:
                k = ("d", d.dma_key)
                v = self.dma_cnt[d.dma_key] - (16 if (op.is_dma and op.dma_key == d.dma_key) else 0)
                if v > ddeps.get(k, 0):
                    ddeps[k] = v
                return
            if d.eng == eng and not op.is_dma:
                if eng == "pe":
                    return
            deps[id(d)] = d

        for v in reads:
            for (sp, lo, hi) in v.regs:
                for r in self.recs[sp]:
                    if r.w and r.lo < hi and lo < r.hi:
                        add_dep(r.op, True)
                    elif (sp == "P" and not r.w and r.op.eng != eng
                          and r.lo // 2048 <= (hi - 1) // 2048 and lo // 2048 <= (r.hi - 1) // 2048):
                        add_dep(r.op, True)
        for v in writes:
            for (sp, lo, hi) in v.regs:
                for r in self.recs[sp]:
                    if r.lo < hi and lo < r.hi:
                        add_dep(r.op, False)
        op.deps = list(deps.values())
        op.ddeps = ddeps
        inorder = not op.is_dma
        for v in reads:
            for (sp, lo, hi) in v.regs:
                lst = self.recs[sp]
                if inorder:
                    lst[:] = [r for r in lst if not ((not r.w) and (not r.op.is_dma) and r.op.eng == eng
                                                     and lo <= r.lo and r.hi <= hi)]
                lst.append(Rec(lo, hi, op, False))
        for v in writes:
            for (sp, lo, hi) in v.regs:
                lst = self.recs[sp]
                lst[:] = [r for r in lst if not (lo <= r.lo and r.hi <= hi)]
                lst.append(Rec(lo, hi, op, True))
        self.ops[eng].append(op)
        return op

    def mm(self, out, lhsT, rhs, start=True, stop=True, **kw):
        return self.add("pe", lambda e: e.matmul(out.ap, lhsT.ap, rhs.ap, start=start, stop=stop, **kw),
                        reads=[lhsT, rhs], writes=[out])

    def act(self, out, in_, func, bias=None, scale=None, eng="act"):
        reads = [in_]
        kw = {}
        if bias is not None:
            if isinstance(bias, V):
                reads.append(bias)
                kw["bias"] = bias.ap
            else:
                kw["bias"] = float(bias)
        if scale is not None:
            if isinstance(scale, V):
                reads.append(scale)
                kw["scale"] = scale.ap
            else:
                kw["scale"] = float(scale)
        return self.add(eng, lambda e: e.activation(out.ap, in_.ap, func, **kw), reads=reads, writes=[out])

    def tt(self, out, in0, in1, op, eng="dve"):
        return self.add(eng, lambda e: e.tensor_tensor(out.ap, in0.ap, in1.ap, op), reads=[in0, in1], writes=[out])

    def ts(self, out, in0, s1, s2, op0, op1=None, eng="dve"):
        reads = [in0]
        a1 = s1
        a2 = s2
        if isinstance(s1, V):
            reads.append(s1)
            a1 = s1.ap
        if isinstance(s2, V):
            reads.append(s2)
            a2 = s2.ap
        if op1 is None:
            return self.add(eng, lambda e: e.tensor_scalar(out.ap, in0.ap, a1, None, op0), reads=reads, writes=[out])
        return self.add(eng, lambda e: e.tensor_scalar(out.ap, in0.ap, a1, a2, op0, op1), reads=reads, writes=[out])

    def stt(self, out, in0, scalar, in1, op0, op1):
        reads = [in0, in1]
        a = scalar
        if isinstance(scalar, V):
            reads.append(scalar)
            a = scalar.ap
        return self.add("dve", lambda e: e.scalar_tensor_tensor(out.ap, in0.ap, a, in1.ap, op0, op1),
                        reads=reads, writes=[out])

    def recip(self, out, in_):
        return self.add("dve", lambda e: e.reciprocal(out.ap, in_.ap), reads=[in_], writes=[out])

    def copy(self, out, in_, eng="dve"):
        if eng == "act":
            return self.add("act", lambda e: e.copy(out.ap, in_.ap), reads=[in_], writes=[out])
        return self.add(eng, lambda e: e.tensor_copy(out.ap, in_.ap), reads=[in_], writes=[out])

    def scan(self, out, d0, d1, init, op0, op1):
        return self.add("dve", lambda e: e.tensor_tensor_scan(out.ap, d0.ap, d1.ap, init.ap, op0, op1),
                        reads=[d0, d1, init], writes=[out])

    def memset(self, out, val, eng="pool"):
        return self.add(eng, lambda e: e.memset(out.ap, val), writes=[out])

    def dma(self, eng, out, in_, key, **kw):
        reads = [in_] if isinstance(in_, V) else []
        writes = [out] if isinstance(out, V) else []
        oa = out.ap if isinstance(out, V) else out
        ia = in_.ap if isinstance(in_, V) else in_
        op = self.add(eng, lambda e: e.dma_start(out=oa, in_=ia, **kw), reads=reads, writes=writes, dma_key=key)
        if not isinstance(out, V):
            self.out_dmas.append(op)
        return op

    def emit(self, nc, final_eng="sp"):
        fin = Op()
        fin.eng = final_eng
        fin.fn = None
        fin.idx = len(self.ops[final_eng])
        fin.is_dma = False
        fin.dma_key = None
        fin.signal = False
        fin.cnt = 0
        fin.deps = []
        fin.ddeps = {("d", o.dma_key): self.dma_cnt[o.dma_key] for o in self.out_dmas}
        self.ops[final_eng].append(fin)

        for eng in ENGS:
            seen = {}
            for op in self.ops[eng]:
                w = {}
                for k, val in op.ddeps.items():
                    if seen.get(k, 0) >= val:
                        continue
                    w[k] = (val, None)
                for d in op.deps:
                    k = ("e", d.eng)
                    if seen.get(k, -1) >= d.idx:
                        continue
                    if k not in w or w[k][0] < d.idx:
                        w[k] = (d.idx, d)
                for k, (val, d) in w.items():
                    seen[k] = val
                    if d is not None:
                        d.signal = True
                op.waits = list(w.items())
        for eng in ENGS:
            c = 0
            for op in self.ops[eng]:
                if op.signal:
                    c += 1
                    op.cnt = c

        with ExitStack() as st:
            esem = {e: st.enter_context(nc.semaphore("s_" + e)) for e in ENGS}
            dsem = {k: st.enter_context(nc.semaphore("d_" + str(k))) for k in self.dma_cnt}
            block = st.enter_context(nc.Block())

            def run(eng, e):
                for op in self.ops[eng]:
                    for k, (val, d) in op.waits:
                        if k[0] == "d":
                            e.wait_ge(dsem[k[1]], val)
                        else:
                            e.wait_ge(esem[k[1]], d.cnt)
                    if op.fn is None:
                        continue
                    ins = op.fn(e)
                    if op.is_dma:
                        ins.then_inc(dsem[op.dma_key], 16)
                    elif op.signal:
                        ins.then_inc(esem[eng], 1)

            @block.tensor
            def _(e):
                run("pe", e)

            @block.scalar
            def _(e):
                run("act", e)

            @block.vector
            def _(e):
                run("dve", e)

            @block.gpsimd
            def _(e):
                run("pool", e)

            @block.sync
            def _(e):
                run("sp", e)


class Arena:
    def __init__(self, arena_ap, nbytes):
        self.ap = arena_ap
        self.n = nbytes
        self.top = 0
        self.peak = 0

    def alloc(self, free_shape, dtype):
        es = 2 if dtype == BF16 else 4
        n = es
        for d in free_shape:
            n *= d
        off = (self.top + 63) // 64 * 64
        assert off + n <= self.n, ("arena overflow", off, n, self.n)
        self.top = off + n
        self.peak = max(self.peak, self.top)
        ap = self.ap[:, off // 4:(off + n + 3) // 4]
        if dtype == BF16:
            ap = ap.bitcast(BF16)
        if len(free_shape) == 2:
            ap = ap.rearrange("p (a b) -> p a b", a=free_shape[0])
        elif len(free_shape) == 3:
            ap = ap.rearrange("p (a b c) -> p a b c", a=free_shape[0], b=free_shape[1])
        elif len(free_shape) == 4:
            ap = ap.rearrange("p (a b c d) -> p a b c d", a=free_shape[0], b=free_shape[1], c=free_shape[2])
        return Buf(ap, "S", off, es, free_shape)

    def view_at(self, off, free_shape, dtype):
        es = 2 if dtype == BF16 else 4
        n = es
        for d in free_shape:
            n *= d
        ap = self.ap[:, off // 4:(off + n + 3) // 4]
        if dtype == BF16:
            ap = ap.bitcast(BF16)
        if len(free_shape) == 2:
            ap = ap.rearrange("p (a b) -> p a b", a=free_shape[0])
        return Buf(ap, "S", off, es, free_shape)

    def mark(self):
        return self.top

    def release(self, m):
        self.top = m


ARENA_BYTES = 212480
SLOT_BYTES = 8192
NSLOTS = 4

PP_GAIN = 0
PP_CONVW = 48
PP_CONVB = 80
PP_BA = 88
PP_BX = 96
PP_LAM = 104
PP_SINK = 112
NPP = 128


def build_program(stage):
    SUB = int(os.environ.get("MK_SUB", "9"))
    M2L = int(os.environ.get("MK_M2", "9"))
    M3L = int(os.environ.get("MK_M3", "9"))
    nc = bass.Bass("TRN2", target_bir_lowering=False)
    dram = {}

    def din(name, shape, dt=F32):
        dram[name] = nc.dram_tensor(name, list(shape), dt, kind="ExternalInput").ap()
        return dram[name]

    xT = din("xT", (D, S))
    pp_d = din("pp", (128, NPP))
    w_gu1 = din("w_gu1", (D, 2 * DFF))
    w_dn1 = din("w_dn1", (DFF, D))
    w_gu2 = din("w_gu2", (D, 2 * DFF))
    w_dn2 = din("w_dn2", (DFF, D))
    w_in = din("w_in", (D, 5632))
    w_kv = din("w_kv", (D, 1024))
    w_pl = din("w_pl", (D, D))
    w_pa = din("w_pa", (D, D))
    w_o = din("w_o", (D, D))
    lw_a = din("lw_a", (16, 64, 64))
    lw_x = din("lw_x", (16, 64, 64))
    rope_d = din("rope", (2, 128, S))
    cst_d = din("cst", (128, 128 + 1024))
    outT = nc.dram_tensor("outT", [D, S], F32, kind="ExternalOutput").ap()

    P = Prog()
    with ExitStack() as st:
        arena_t = st.enter_context(nc.sbuf_tensor("arena", [128, ARENA_BYTES // 4], F32))
        psum_t = st.enter_context(nc.psum_tensor("psum", [128, 4096], F32))
        A = Arena(arena_t, ARENA_BYTES)

        def bank(i, n=1):
            return Buf(psum_t[:, i * 512:(i + n) * 512], "P", i * 2048, 4, (n * 512,))

        banks = [bank(i) for i in range(8)]

        hT = A.alloc((8, S), F32)
        pp = A.alloc((NPP,), F32)
        hp = A.alloc((48,), F32)
        cf = A.alloc((8,), F32)
        lc = A.alloc((64,), F32)
        ones = A.alloc((128,), BF16)
        BDa = A.alloc((8, 128), BF16)
        BDx = A.alloc((8, 128), BF16)
        slots = [A.alloc((SLOT_BYTES // 2,), BF16) for _ in range(NSLOTS)]
        slot_i = [0]

        def next_slot(shape):
            i = slot_i[0] % NSLOTS
            slot_i[0] += 1
            sb = slots[i]
            n = 1
            for d in shape:
                n *= d
            assert n * 2 <= SLOT_BYTES
            ap = sb.ap[:, 0:n]
            if len(shape) == 2:
                ap = ap.rearrange("p (a b) -> p a b", a=shape[0])
            return Buf(ap, "S", sb.off, 2, shape), "w%d" % i

        def wload(dst, src, key):
            P.dma("pool", dst, src, key)

        def kp(ap):
            return ap.rearrange("(k p) n -> p k n", p=128)

        P.dma("sp", pp.all(), pp_d, "pp")
        for c in range(8):
            P.dma("sp", hT[:, c, :], xT[c * 128:(c + 1) * 128, :], "x%d" % c)
        P.memset(ones.all(), 1.0)
        P.memset(cf[:, 0:1], EPS)
        P.memset(cf[:, 1:2], 1.0)
        P.ts(hp.all(), pp[:, 0:48], 0.5, None, ALU.mult)
        epsv = cf[:, 0:1]
        onev = cf[:, 1:2]

        def gain(i, c):
            return pp[:, PP_GAIN + i * 8 + c:PP_GAIN + i * 8 + c + 1]

        def hgain(i, c):
            return hp[:, i * 8 + c:i * 8 + c + 1]

        if stage >= 2:
            P.memset(BDa.all(), 0.0)
            P.memset(BDx.all(), 0.0)
            for (bd, lw, key) in ((BDa, lw_a, "bda"), (BDx, lw_x, "bdx")):
                src = lw.rearrange("(c two) i o -> two i c o", two=2)
                for two in range(2):
                    P.dma("pool", bd[two * 64:(two + 1) * 64, :, two * 64:(two + 1) * 64], src[two], key)
            X = lc[:, 0:8]
            T_ = lc[:, 8:16]
            CL = lc[:, 16:24]
            HCL = lc[:, 24:32]
            P.act(X, pp[:, PP_LAM:PP_LAM + 8], AF.Exp, scale=-1.0)
            P.ts(T_, X, 1.0 / 3.0, -0.5, ALU.mult, ALU.add)
            P.tt(T_, T_, X, ALU.mult)
            P.ts(T_, T_, 1.0, None, ALU.add)
            P.tt(T_, T_, X, ALU.mult)
            P.ts(CL, T_, -8.0, None, ALU.mult)
            P.ts(HCL, T_, -4.0, None, ALU.mult)
            P.ts(lc[:, 32:48], pp[:, PP_BA:PP_BA + 16], 0.5, None, ALU.mult)
            P.act(lc[:, 48:64], pp[:, PP_SINK:PP_SINK + 16], AF.Exp)

        def prenorm(gi, uT, t0, ntok, sq_bufs, rstd, ssb):
            k = 0
            for sub in range(ntok // 512):
                ts_ = t0 + sub * 512
                ss = banks[ssb[sub % len(ssb)]]
                rs = rstd[:, sub * 512:(sub + 1) * 512]
                for c in range(8):
                    sq = sq_bufs[k % len(sq_bufs)]
                    k += 1
                    P.act(sq.all(), hT[:, c, ts_:ts_ + 512], AF.Square)
                    P.mm(ss.all(), ones.all(), sq.all(), start=(c == 0), stop=(c == 7))
                P.act(rs, ss.all(), AF.Sqrt, bias=epsv, scale=1.0 / D)
                P.recip(rs, rs)
                for c in range(8):
                    P.stt(uT[:, c, sub * 512:(sub + 1) * 512], hT[:, c, ts_:ts_ + 512], gain(gi, c),
                          rs, ALU.mult, ALU.mult)

        def ffn(gi_pre, gi_post, w_gu, w_dn):
            m = A.mark()
            TB = 1024
            uT = A.alloc((8, TB), BF16)
            actT = A.alloc((NFF, TB), BF16)
            fsb = A.alloc((8, TB), F32)
            sqb = [A.alloc((512,), BF16) for _ in range(2)]
            rstd = A.alloc((TB,), F32)
            sgb = [A.alloc((512,), F32) for _ in range(2)]
            tmpb = [A.alloc((512,), F32) for _ in range(2)]
            for tb in range(S // TB):
                T0 = tb * TB
                prenorm(gi_pre, uT, T0, TB, sqb, rstd, (6, 7))
                k = 0
                for grp in range(NFF // 2):
                    W, key = next_slot((8, 512))
                    wload(W[:, :, 0:256], kp(w_gu[:, grp * 256:(grp + 1) * 256]), key)
                    wload(W[:, :, 256:512], kp(w_gu[:, DFF + grp * 256:DFF + (grp + 1) * 256]), key)
                    for f2 in range(2):
                        ffc = grp * 2 + f2
                        for sub in range(2):
                            pg = banks[(k % 2) * 2]
                            pu = banks[(k % 2) * 2 + 1]
                            sg = sgb[k % 2]
                            k += 1
                            for dc in range(8):
                                P.mm(pg.all(), W[:, dc, f2 * 128:(f2 + 1) * 128], uT[:, dc, sub * 512:(sub + 1) * 512],
                                     start=(dc == 0), stop=(dc == 7))
                            for dc in range(8):
                                P.mm(pu.all(), W[:, dc, 256 + f2 * 128:256 + (f2 + 1) * 128],
                                     uT[:, dc, sub * 512:(sub + 1) * 512], start=(dc == 0), stop=(dc == 7))
                            P.act(sg.all(), pg.all(), AF.Silu)
                            P.tt(actT[:, ffc, sub * 512:(sub + 1) * 512], sg.all(), pu.all(), ALU.mult)
                k = 0
                for dc in range(8):
                    W, key = next_slot((NFF, 128))
                    wload(W.all(), kp(w_dn[:, dc * 128:(dc + 1) * 128]), key)
                    for sub in range(2):
                        pf = banks[4 + (k % 2)]
                        sq = sqb[k % 2]
                        k += 1
                        for ffc in range(NFF):
                            P.mm(pf.all(), W[:, ffc, :], actT[:, ffc, sub * 512:(sub + 1) * 512],
                                 start=(ffc == 0), stop=(ffc == NFF - 1))
                        P.copy(fsb[:, dc, sub * 512:(sub + 1) * 512], pf.all(), eng="act")
                        P.act(sq.all(), pf.all(), AF.Square)
                        P.mm(banks[6 + sub].all(), ones.all(), sq.all(), start=(dc == 0), stop=(dc == 7))
                for sub in range(2):
                    rs = rstd[:, sub * 512:(sub + 1) * 512]
                    P.act(rs, banks[6 + sub].all(), AF.Sqrt, bias=epsv, scale=1.0 / D)
                    P.recip(rs, rs)
                k = 0
                for sub in range(2):
                    ts_ = T0 + sub * 512
                    for c in range(8):
                        tmp = tmpb[k % 2]
                        k += 1
                        P.stt(tmp.all(), fsb[:, c, sub * 512:(sub + 1) * 512], hgain(gi_post, c),
                              rstd[:, sub * 512:(sub + 1) * 512], ALU.mult, ALU.mult)
                        P.tt(hT[:, c, ts_:ts_ + 512], hT[:, c, ts_:ts_ + 512], tmp.all(), ALU.add, eng="pool")
            A.release(m)

        def mixer():
            m = A.mark()
            TB = 512
            uT = A.alloc((8, TB), BF16)
            y_lru = A.alloc((8, TB), BF16)
            qy = A.alloc((16, TB), BF16)
            msb = A.view_at(qy.off, (8, TB), F32)
            kT = A.alloc((4, 640), BF16)
            vS = A.alloc((5, 512), BF16)
            merged = A.alloc((8, TB), BF16)
            ropeC = A.alloc((TB,), F32)
            ropeS = A.alloc((TB,), F32)
            NT = 14
            tmp = [A.alloc((516,), F32) for _ in range(NT)]
            Eb = [A.alloc((1024,), BF16) for _ in range(2)]
            qbb = [A.alloc((512,), BF16) for _ in range(2)]
            sqb = [A.alloc((512,), BF16) for _ in range(2)]
            rstd = A.alloc((TB,), F32)
            Psw = A.alloc((128,), BF16)
            mask = A.alloc((1024,), BF16)
            xcar = A.alloc((8, 4), F32)
            hcar = A.alloc((8,), F32)
            ti = [0]
            bi = [0]
            qi = [0]

            def T():
                t = tmp[ti[0] % NT]
                ti[0] += 1
                return t

            def nb():
                b = banks[bi[0] % 6]
                bi[0] += 1
                return b

            def QB():
                b = qbb[qi[0] % 2]
                qi[0] += 1
                return b

            P.dma("pool", Psw.all(), cst_d[:, 0:128], "psw")
            P.dma("pool", mask.all(), cst_d[:, 128:1152], "msk")
            P.memset(xcar.all(), 0.0)
            P.memset(hcar.all(), 0.0)
            C1 = 0.7978845608028654
            C2 = C1 * 0.044715
            cvw = lambda k, c: pp[:, PP_CONVW + k * 8 + c:PP_CONVW + k * 8 + c + 1]
            cvb = lambda c: pp[:, PP_CONVB + c:PP_CONVB + c + 1]
            lcv = lambda base, c: lc[:, base + c:base + c + 1]
            F = slice(0, 512)
            pstride = ARENA_BYTES // 4

            for tb in range(S // TB):
                t0 = tb * TB
                prenorm(2, uT, t0, TB, sqb, rstd, (6, 7))
                P.dma("sp", ropeC.all(), rope_d[0][:, t0:t0 + TB], "rc")
                P.dma("sp", ropeS.all(), rope_d[1][:, t0:t0 + TB], "rs")

                for cp in range(4 if SUB >= 1 else 0):
                    W, key = next_slot((8, 512))
                    wload(W[:, :, 0:256], kp(w_in[:, cp * 256:(cp + 1) * 256]), key)
                    wload(W[:, :, 256:512], kp(w_in[:, 1024 + cp * 256:1024 + (cp + 1) * 256]), key)
                    for c2 in range(2):
                        c = cp * 2 + c2
                        pg = nb()
                        px = nb()
                        for dc in range(8):
                            P.mm(px.all(), W[:, dc, 256 + c2 * 128:256 + (c2 + 1) * 128], uT[:, dc, :],
                                 start=(dc == 0), stop=(dc == 7))
                        for dc in range(8):
                            P.mm(pg.all(), W[:, dc, c2 * 128:(c2 + 1) * 128], uT[:, dc, :],
                                 start=(dc == 0), stop=(dc == 7))
                        xf = T()
                        P.copy(xf[:, 0:3], xcar[:, c, 0:3], eng="pool")
                        P.copy(xf[:, 3:515], px.all(), eng="act")
                        P.copy(xcar[:, c, 0:3], xf[:, 512:515], eng="pool")
                        if M2L < 2:
                            continue
                        xc = T()
                        P.ts(xc[:, F], xf[:, 0:512], cvw(0, c), cvb(c), ALU.mult, ALU.add)
                        for k in range(1, 4):
                            P.stt(xc[:, F], xf[:, k:k + 512], cvw(k, c), xc[:, F], ALU.mult, ALU.add)
                        xcb = QB()
                        P.copy(xcb.all(), xc[:, F], eng="act")
                        pr = nb()
                        pi = nb()
                        P.mm(pr.all(), BDa[:, c, :], xcb.all())
                        P.mm(pi.all(), BDx[:, c, :], xcb.all())
                        if M2L < 3:
                            continue
                        thr = T()
                        thi = T()
                        P.act(thr[:, F], pr.all(), AF.Tanh, bias=lcv(32, c), scale=0.5)
                        P.act(thi[:, F], pi.all(), AF.Tanh, bias=lcv(40, c), scale=0.5)
                        a = T()
                        mu = T()
                        P.act(a[:, F], thr[:, F], AF.Exp, bias=lcv(24, c), scale=lcv(24, c))
                        P.act(mu[:, F], thr[:, F], AF.Exp, bias=lcv(16, c), scale=lcv(16, c))
                        sq = T()
                        P.act(sq[:, F], pg.all(), AF.Square)
                        P.ts(sq[:, F], sq[:, F], C2, C1, ALU.mult, ALU.add)
                        P.tt(sq[:, F], sq[:, F], pg.all(), ALU.mult)
                        P.act(sq[:, F], sq[:, F], AF.Tanh)
                        P.act(mu[:, F], mu[:, F], AF.Sqrt, bias=onev, scale=-1.0)
                        if M2L < 4:
                            continue
                        t1 = T()
                        P.stt(t1[:, F], thi[:, F], 1.0, xc[:, F], ALU.add, ALU.mult)
                        P.stt(t1[:, F], t1[:, F], 0.5, mu[:, F], ALU.mult, ALU.mult)
                        if M2L < 5:
                            continue
                        hs = T()
                        P.scan(hs[:, F], a[:, F], t1[:, F], hcar[:, c:c + 1], ALU.mult, ALU.add)
                        if M2L < 6:
                            continue
                        P.copy(hcar[:, c:c + 1], hs[:, 511:512], eng="dve")
                        P.stt(sq[:, F], sq[:, F], 1.0, pg.all(), ALU.add, ALU.mult)
                        P.stt(y_lru[:, c, :], sq[:, F], 0.5, hs[:, F], ALU.mult, ALU.mult)

                if SUB < 2:
                    continue
                RCL = int(os.environ.get("MK_RC", "9"))

                def rope_chunk(pq, outv):
                    if RCL < 1:
                        return
                    qb = QB()
                    P.copy(qb.all(), pq.all(), eng="act")
                    ps = nb()
                    P.mm(ps.all(), (ones if os.environ.get("MK_X") == "1" else Psw).all(), qb.all())
                    if RCL < 2:
                        return
                    r1 = T()
                    r2 = T()
                    P.tt(r1[:, F], ropeC.all(), pq.all(), ALU.mult)
                    P.tt(r2[:, F], ropeS.all(), ps.all(), ALU.mult)
                    if RCL >= 3:
                        P.tt(outv, r1[:, F], r2[:, F], ALU.add, eng=("pool" if os.environ.get("MK_Y") == "1" else "dve"))

                for qp in range(2):
                    W, key = next_slot((8, 512))
                    wload(W.all(), kp(w_in[:, 2048 + qp * 512:2048 + (qp + 1) * 512]), key)
                    for c4 in range(4):
                        pq = nb()
                        for dc in range(8):
                            P.mm(pq.all(), W[:, dc, c4 * 128:(c4 + 1) * 128], uT[:, dc, :], start=(dc == 0), stop=(dc == 7))
                        rope_chunk(pq, qy[:, qp * 4 + c4, :])
                W, key = next_slot((8, 512))
                wload(W.all(), kp(w_kv[:, 0:512]), key)
                for j in range(4 if M3L >= 2 else 0):
                    pk = nb()
                    for dc in range(8):
                        P.mm(pk.all(), W[:, dc, j * 128:(j + 1) * 128], uT[:, dc, :], start=(dc == 0), stop=(dc == 7))
                    rope_chunk(pk, kT[:, j, 128:640])
                W, key = next_slot((8, 512))
                wload(W.all(), kp(w_kv[:, 512:1024]), key)
                for i in range(4 if M3L >= 3 else 0):
                    pv = nb()
                    for dc in range(8):
                        P.mm(pv.all(), uT[:, dc, i * 128:(i + 1) * 128], W[:, dc, :], start=(dc == 0), stop=(dc == 7))
                    P.copy(vS[:, 1 + i, :], pv.all(), eng="act")

                k_ = 0
                for n in range(4 if SUB >= 3 else 0):
                    nglob = tb * 4 + n
                    kbs = (0, 1) if nglob > 0 else (1,)
                    for j in range(4):
                        pb = (k_ % 2) * 2
                        S2 = Buf(psum_t[:, pb * 512:(pb + 2) * 512].rearrange("p (h k g q) -> p h k g q", h=2, k=2, g=2),
                                 "P", pb * 2048, 4, (2, 2, 2, 128))
                        po = banks[4 + (k_ % 2)]
                        den = banks[6 + (k_ % 2)]
                        E = Eb[k_ % 2]
                        k_ += 1
                        for kb in kbs:
                            kc = slice((n + kb) * 128, (n + kb + 1) * 128)
                            for hh2 in range(2):
                                for half in range(2):
                                    rows = slice(half * 64, half * 64 + 64)
                                    P.mm(S2[:, half, kb, hh2, :], kT[rows, j, kc],
                                         qy[rows, 2 * j + hh2, n * 128:(n + 1) * 128])
                        P.act(E.all(), S2.all(), AF.Exp, scale=0.125)
                        P.tt(E.all(), E.all(), mask.all(), ALU.mult)
                        E4 = Buf(E.ap.rearrange("p (h k g q) -> p h k g q", h=2, k=2, g=2), "S", E.off, 2, (2, 2, 2, 128))
                        for idx, kb in enumerate(kbs):
                            P.mm(po.all(), vS[:, n + kb, j * 128:(j + 1) * 128], E4[:, :, kb, :, :],
                                 start=(idx == 0), stop=(idx == len(kbs) - 1))
                        for idx, kb in enumerate(kbs):
                            P.mm(den.all(), ones.all(), E4[:, :, kb, :, :],
                                 start=(idx == 0), stop=(idx == len(kbs) - 1))
                        rec = T()
                        sink_ap = bass.AP(arena_t, lc.off // 4 + 48 + 4 * j, [[pstride, 128], [1, 2], [2, 2], [0, 128]])
                        sinkv = V(sink_ap, lc[:, 48:64].regs)
                        rec4 = Buf(rec.ap[:, 0:512].rearrange("p (h g q) -> p h g q", h=2, g=2), "S", rec.off, 4, (2, 2, 128))
                        den4 = Buf(den.ap.rearrange("p (h g q) -> p h g q", h=2, g=2), "P", den.off, 4, (2, 2, 128))
                        po4 = Buf(po.ap.rearrange("p (h g q) -> p h g q", h=2, g=2), "P", po.off, 4, (2, 2, 128))
                        P.tt(rec4.all(), den4.all(), sinkv, ALU.add)
                        P.recip(rec[:, F], rec[:, F])
                        for half in range(2):
                            rows = slice(half * 64, half * 64 + 64)
                            P.tt(qy[rows, 8 + 2 * j:8 + 2 * j + 2, n * 128:(n + 1) * 128], po4[rows, half, :, :],
                                 rec4[rows, half, :, :], ALU.mult)
                if tb < S // TB - 1:
                    P.copy(kT[:, :, 0:128], kT[:, :, 512:640], eng="pool")
                    P.copy(vS[:, 0, :], vS[:, 4, :], eng="pool")

                if SUB < 4:
                    continue
                for g in range(2):
                    sg = [T() for _ in range(4)]
                    sa = [T() for _ in range(4)]
                    W, key = next_slot((8, 512))
                    wload(W.all(), kp(w_in[:, 3584 + g * 512:3584 + (g + 1) * 512]), key)
                    for d4 in range(4):
                        p_ = nb()
                        for dc in range(8):
                            P.mm(p_.all(), W[:, dc, d4 * 128:(d4 + 1) * 128], uT[:, dc, :], start=(dc == 0), stop=(dc == 7))
                        P.act(sg[d4][:, F], p_.all(), AF.Sigmoid)
                    W, key = next_slot((8, 512))
                    wload(W.all(), kp(w_pl[:, g * 512:(g + 1) * 512]), key)
                    for d4 in range(4):
                        p_ = nb()
                        for dc in range(8):
                            P.mm(p_.all(), W[:, dc, d4 * 128:(d4 + 1) * 128], y_lru[:, dc, :], start=(dc == 0), stop=(dc == 7))
                        P.tt(sg[d4][:, F], sg[d4][:, F], p_.all(), ALU.mult)
                    W, key = next_slot((8, 512))
                    wload(W.all(), kp(w_in[:, 4608 + g * 512:4608 + (g + 1) * 512]), key)
                    for d4 in range(4):
                        p_ = nb()
                        for dc in range(8):
                            P.mm(p_.all(), W[:, dc, d4 * 128:(d4 + 1) * 128], uT[:, dc, :], start=(dc == 0), stop=(dc == 7))
                        P.act(sa[d4][:, F], p_.all(), AF.Sigmoid)
                    W, key = next_slot((8, 512))
                    wload(W.all(), kp(w_pa[:, g * 512:(g + 1) * 512]), key)
                    for d4 in range(4):
                        p_ = nb()
                        for dc in range(8):
                            P.mm(p_.all(), W[:, dc, d4 * 128:(d4 + 1) * 128], qy[:, 8 + dc, :], start=(dc == 0), stop=(dc == 7))
                        P.tt(sa[d4][:, F], sa[d4][:, F], p_.all(), ALU.mult)
                        P.tt(merged[:, g * 4 + d4, :], sa[d4][:, F], sg[d4][:, F], ALU.add)

                k_ = 0
                for g in range(2):
                    W, key = next_slot((8, 512))
                    wload(W.all(), kp(w_o[:, g * 512:(g + 1) * 512]), key)
                    for d4 in range(4):
                        dcp = g * 4 + d4
                        p_ = nb()
                        sq = sqb[k_ % 2]
                        k_ += 1
                        for dc in range(8):
                            P.mm(p_.all(), W[:, dc, d4 * 128:(d4 + 1) * 128], merged[:, dc, :], start=(dc == 0), stop=(dc == 7))
                        P.copy(msb[:, dcp, :], p_.all(), eng="act")
                        P.act(sq.all(), p_.all(), AF.Square)
                        P.mm(banks[6].all(), ones.all(), sq.all(), start=(dcp == 0), stop=(dcp == 7))
                P.act(rstd.all(), banks[6].all(), AF.Sqrt, bias=epsv, scale=1.0 / D)
                P.recip(rstd.all(), rstd.all())
                for c in range(8):
                    tm = T()
                    P.stt(tm[:, F], msb[:, c, :], gain(3, c), rstd.all(), ALU.mult, ALU.mult)
                    P.tt(hT[:, c, t0:t0 + TB], hT[:, c, t0:t0 + TB], tm[:, F], ALU.add, eng="pool")
            A.release(m)

        if stage >= 1:
            ffn(0, 1, w_gu1, w_dn1)
        if stage >= 2:
            mixer()
        if stage >= 3:
            ffn(4, 5, w_gu2, w_dn2)

        for c in range(8):
            P.dma("sp", outT[c * 128:(c + 1) * 128, :], hT[:, c, :], "o%d" % c)

        P.emit(nc)
        print("arena peak bytes", A.peak, "ops", {e: len(P.ops[e]) for e in ENGS})
    return nc


_CACHE = {}


def _rope_tables():
    half = 32
    inv_freq = (np.float32(10000.0) ** (-np.arange(half, dtype=np.float32) / np.float32(half))).astype(np.float32)
    ang = (np.arange(S, dtype=np.float32)[:, None] * inv_freq[None, :]).astype(np.float32)
    cos = np.cos(ang).astype(np.float32).T
    sin = np.sin(ang).astype(np.float32).T
    C = np.zeros((128, S), np.float32)
    Sg = np.zeros((128, S), np.float32)
    for p in range(128):
        i = p % 32
        C[p] = cos[i]
        Sg[p] = -sin[i] if (p % 64) < 32 else sin[i]
    return np.stack([C, Sg], 0)


def _consts():
    c = np.zeros((128, 128 + 1024), np.float32)
    for m in range(128):
        k = m + 32 if (m % 64) < 32 else m - 32
        c[k, m] = 1.0
    s = np.arange(128)[:, None]
    q = np.arange(128)[None, :]
    prev = (q < s).astype(np.float32)
    cur = (q >= s).astype(np.float32)
    mk = np.concatenate([prev, prev, cur, cur, prev, prev, cur, cur], axis=1)
    c[:, 128:] = mk
    return c


def kernel(**inp):
    stage = int(os.environ.get("MK_STAGE", "3"))
    if stage not in _CACHE:
        _CACHE[stage] = build_program(stage)
    nc = _CACHE[stage]
    f = lambda a: np.ascontiguousarray(np.asarray(a, dtype=np.float32))
    x = f(inp["x"])

    def col(v):
        return f(v).reshape(8, 128).T

    pp = np.zeros((128, NPP), np.float32)
    for i, nm in enumerate(["ffn1_pre_g", "ffn1_post_g", "mix_pre_g", "mix_post_g", "ffn2_pre_g", "ffn2_post_g"]):
        pp[:, PP_GAIN + i * 8:PP_GAIN + (i + 1) * 8] = col(inp[nm][0])
    cw = f(inp["conv_w"][0])
    for k in range(4):
        pp[:, PP_CONVW + k * 8:PP_CONVW + (k + 1) * 8] = col(cw[k])
    pp[:, PP_CONVB:PP_CONVB + 8] = col(inp["conv_b"][0])
    pp[:, PP_BA:PP_BA + 8] = col(inp["lru_b_a"][0])
    pp[:, PP_BX:PP_BX + 8] = col(inp["lru_b_x"][0])
    pp[:, PP_LAM:PP_LAM + 8] = col(inp["lru_lambda"][0])
    pp[:, PP_SINK:PP_SINK + 16] = np.broadcast_to(f(inp["attn_sinks"][0])[None, :], (128, 16))

    w_in = f(inp["w_in"][0])
    kcols = w_in[:, 3072:3328].reshape(D, 4, 64)
    vcols = w_in[:, 3328:3584].reshape(D, 4, 64)
    w_kv = np.concatenate([np.repeat(kcols[:, :, None, :], 2, axis=2).reshape(D, 512),
                           np.repeat(vcols[:, :, None, :], 2, axis=2).reshape(D, 512)], axis=1)
    shared = {
        "pp": pp,
        "w_gu1": f(inp["ffn1_w_gu"][0]), "w_dn1": f(inp["ffn1_w_down"][0]),
        "w_gu2": f(inp["ffn2_w_gu"][0]), "w_dn2": f(inp["ffn2_w_down"][0]),
        "w_in": w_in, "w_kv": f(w_kv),
        "w_pl": f(inp["w_proj_lru"][0]), "w_pa": f(inp["w_proj_attn"][0]), "w_o": f(inp["w_out"][0]),
        "lw_a": f(inp["lru_w_a"][0]), "lw_x": f(inp["lru_w_x"][0]),
        "rope": _rope_tables(), "cst": _consts(),
    }
    in_maps = []
    for b in range(NCORES):
        m = dict(shared)
        m["xT"] = np.ascontiguousarray(x[b].T)
        in_maps.append(m)
    res = run_bass_kernel_spmd(nc, in_maps, core_ids=list(range(NCORES)))
    out = np.stack([np.asarray(res.results[b]["outT"]).T for b in range(NCORES)], axis=0)
    return np.ascontiguousarray(out.astype(np.float32))
```

```python
import os
from contextlib import ExitStack

import numpy as np
import concourse.bass as bass
import concourse.mybir as mybir
from concourse.bass_utils import run_bass_kernel_spmd

F32 = mybir.dt.float32
BF16 = mybir.dt.bfloat16
AF = mybir.ActivationFunctionType
ALU = mybir.AluOpType

D = 1024
S = 2048
B = 8
DFF = 2816
NFF = DFF // 128
EPS = 1e-6
NCORES = 8

ENGS = ("pe", "act", "dve", "pool", "sp")


class V:
    __slots__ = ("ap", "regs")

    def __init__(self, ap, regs):
        self.ap = ap
        self.regs = regs


class Buf:
    def __init__(self, ap, space, byte_off, esize, free_shape):
        self.ap = ap
        self.space = space
        self.off = byte_off
        self.esize = esize
        self.fs = tuple(free_shape)
        st = []
        acc = 1
        for d in reversed(self.fs):
            st.append(acc)
            acc *= d
        self.strides = tuple(reversed(st))
        self.nbytes = acc * esize

    def __getitem__(self, key):
        if not isinstance(key, tuple):
            key = (key,)
        ap = self.ap[key]
        fk = key[1:]
        rng = []
        for i, d in enumerate(self.fs):
            if i < len(fk):
                k = fk[i]
                if isinstance(k, slice):
                    a = 0 if k.start is None else k.start
                    b = d if k.stop is None else k.stop
                else:
                    a, b = k, k + 1
            else:
                a, b = 0, d
            assert 0 <= a < b <= d, (key, self.fs)
            rng.append((a, b))
        j = -1
        for i, (a, b) in enumerate(rng):
            if (a, b) != (0, self.fs[i]):
                j = i
        if j < 0:
            return self.all()
        combos = [0]
        for i in range(j):
            a, b = rng[i]
            combos = [c + x * self.strides[i] for c in combos for x in range(a, b)]
            if len(combos) > 64:
                combos = None
                break
        if combos is None:
            lo = sum(a * s for (a, b), s in zip(rng, self.strides))
            hi = sum((b - 1) * s for (a, b), s in zip(rng, self.strides))
            return V(ap, [(self.space, self.off + lo * self.esize, self.off + (hi + 1) * self.esize)])
        a, b = rng[j]
        st = self.strides[j]
        regs = [(self.space, self.off + (c + a * st) * self.esize, self.off + (c + b * st) * self.esize)
                for c in combos]
        regs.sort()
        out = [regs[0]]
        for r in regs[1:]:
            if r[1] == out[-1][2]:
                out[-1] = (out[-1][0], out[-1][1], r[2])
            else:
                out.append(r)
        return V(ap, out)

    def all(self):
        return V(self.ap, [(self.space, self.off, self.off + self.nbytes)])


class Op:
    __slots__ = ("eng", "fn", "idx", "deps", "ddeps", "is_dma", "dma_key", "dma_cnt", "signal", "cnt", "waits",
                 "gid", "odeps", "dmadeps", "cost", "nbytes", "fin", "succ", "npred", "deps_all")


class Rec:
    __slots__ = ("lo", "hi", "op", "w")

    def __init__(self, lo, hi, op, w):
        self.lo = lo
        self.hi = hi
        self.op = op
        self.w = w


class Prog:
    def __init__(self):
        self.ops = {e: [] for e in ENGS}
        self.recs = {"S": [], "P": []}
        self.dma_cnt = {}
        self.out_dmas = []
        self.ngid = 0
        self.last_dma = {}

    def add(self, eng, fn, reads=(), writes=(), dma_key=None, cost=0.3, nbytes=0):
        op = Op()
        op.eng = eng
        op.fn = fn
        op.gid = self.ngid
        self.ngid += 1
        op.cost = cost
        op.nbytes = nbytes
        op.odeps = []
        op.dmadeps = []
        op.idx = len(self.ops[eng])
        op.is_dma = dma_key is not None
        op.dma_key = dma_key
        op.signal = False
        op.cnt = 0
        op.waits = None
        if op.is_dma:
            self.dma_cnt[dma_key] = self.dma_cnt.get(dma_key, 0) + 16
            op.dma_cnt = self.dma_cnt[dma_key]
        deps = {}
        ddeps = {}

        def add_dep(d, raw):
            if d is op:
                return
            if d.is_dma:
                k = ("d", d.dma_key)
                v = self.dma_cnt[d.dma_key] - (16 if (op.is_dma and op.dma_key == d.dma_key) else 0)
                if v > ddeps.get(k, 0):
                    ddeps[k] = v
                op.dmadeps.append(d)
                return
            if d.eng == eng and not op.is_dma:
                if eng == "pe":
                    op.odeps.append(d)
                    return
            deps[id(d)] = d

        for v in reads:
            for (sp, lo, hi) in v.regs:
                for r in self.recs[sp]:
                    if r.w and r.lo < hi and lo < r.hi:
                        add_dep(r.op, True)
                    elif (sp == "P" and not r.w and r.op.eng != eng
                          and r.lo // 2048 <= (hi - 1) // 2048 and lo // 2048 <= (r.hi - 1) // 2048):
                        add_dep(r.op, True)
        for v in writes:
            for (sp, lo, hi) in v.regs:
                for r in self.recs[sp]:
                    if r.lo < hi and lo < r.hi:
                        add_dep(r.op, False)
        op.deps = list(deps.values())
        op.ddeps = ddeps
        inorder = not op.is_dma
        for v in reads:
            for (sp, lo, hi) in v.regs:
                lst = self.recs[sp]
                if inorder:
                    keep = []
                    for r in lst:
                        if (not r.w) and (not r.op.is_dma) and r.op.eng == eng and lo <= r.lo and r.hi <= hi:
                            if r.op is not op:
                                op.odeps.append(r.op)
                        else:
                            keep.append(r)
                    lst[:] = keep
                lst.append(Rec(lo, hi, op, False))
        for v in writes:
            for (sp, lo, hi) in v.regs:
                lst = self.recs[sp]
                lst[:] = [r for r in lst if not (lo <= r.lo and r.hi <= hi)]
                lst.append(Rec(lo, hi, op, True))
        if op.is_dma:
            prev = self.last_dma.get(eng)
            if prev is not None:
                op.odeps.append(prev)
            self.last_dma[eng] = op
        self.ops[eng].append(op)
        return op

    def schedule(self, window=600):
        import heapq
        allops = [op for e in ENGS for op in self.ops[e]]
        for op in allops:
            op.succ = []
            op.fin = None
        for op in allops:
            preds = {}
            for d in op.deps:
                preds[id(d)] = d
            for d in op.odeps:
                preds[id(d)] = d
            for d in op.dmadeps:
                preds[id(d)] = d
            op.npred = len(preds)
            for d in preds.values():
                d.succ.append(op)
            op.deps_all = None
        ready = {e: [] for e in ENGS}
        for op in allops:
            if op.npred == 0:
                heapq.heappush(ready[op.eng], (op.gid, id(op), op))
        efree = {e: 0.0 for e in ENGS}
        dma_free = [0.0]
        order = {e: [] for e in ENGS}
        LAT = 0.25
        nleft = len(allops)
        mingid = {e: 0 for e in ENGS}

        def est_start(op):
            t = efree[op.eng]
            for d in op.deps:
                x = d.fin + (LAT if d.eng != op.eng else 0.05)
                if x > t:
                    t = x
            for d in op.dmadeps:
                x = d.fin + LAT
                if x > t:
                    t = x
            for d in op.odeps:
                x = d.fin if not d.is_dma else d.cnt
                if x > t:
                    t = x
            return t

        while nleft:
            best = None
            for e in ENGS:
                h = ready[e]
                if not h:
                    continue
                g0 = h[0][0]
                cand = heapq.nsmallest(12, h)
                for (g, _, op) in cand:
                    if g - g0 > window:
                        break
                    s = est_start(op)
                    if best is None or (s, g) < (best[0], best[1]):
                        best = (s, g, op)
            s, g, op = best
            h = ready[op.eng]
            h.remove((op.gid, id(op), op))
            heapq.heapify(h)
            if op.is_dma:
                issue = 1.0 if op.eng == "pool" else 0.15
                efree[op.eng] = s + issue
                op.cnt = s + issue
                t0 = max(s + issue, dma_free[0])
                dur = op.nbytes / 300e3
                dma_free[0] = t0 + dur
                op.fin = t0 + dur + 2.0
            else:
                op.fin = s + op.cost
                efree[op.eng] = op.fin
            order[op.eng].append(op)
            nleft -= 1
            for sc in op.succ:
                sc.npred -= 1
                if sc.npred == 0:
                    heapq.heappush(ready[sc.eng], (sc.gid, id(sc), sc))
        for e in ENGS:
            assert len(order[e]) == len(self.ops[e])
            self.ops[e] = order[e]
            for i, op in enumerate(order[e]):
                op.idx = i
                op.cnt = 0
        self.est_time = max(op.fin for op in allops)

    def mm(self, out, lhsT, rhs, start=True, stop=True, **kw):
        n = rhs.ap.free_size()
        return self.add("pe", lambda e: e.matmul(out.ap, lhsT.ap, rhs.ap, start=start, stop=stop, **kw),
                        reads=[lhsT, rhs], writes=[out], cost=max(n, 64) / 1950.0 + 0.012)

    def act(self, out, in_, func, bias=None, scale=None, eng="act"):
        reads = [in_]
        kw = {}
        if bias is not None:
            if isinstance(bias, V):
                reads.append(bias)
                kw["bias"] = bias.ap
            else:
                kw["bias"] = float(bias)
        if scale is not None:
            if isinstance(scale, V):
                reads.append(scale)
                kw["scale"] = scale.ap
            else:
                kw["scale"] = float(scale)
        return self.add(eng, lambda e: e.activation(out.ap, in_.ap, func, **kw), reads=reads, writes=[out],
                        cost=(in_.ap.free_size() + 260) / 1400.0)

    def tt(self, out, in0, in1, op, eng="dve"):
        return self.add(eng, lambda e: e.tensor_tensor(out.ap, in0.ap, in1.ap, op), reads=[in0, in1], writes=[out],
                        cost=self.vcost(eng, in0.ap.free_size()))

    def ts(self, out, in0, s1, s2, op0, op1=None, eng="dve"):
        reads = [in0]
        a1 = s1
        a2 = s2
        if isinstance(s1, V):
            reads.append(s1)
            a1 = s1.ap
        if isinstance(s2, V):
            reads.append(s2)
            a2 = s2.ap
        c = self.vcost(eng, in0.ap.free_size())
        if op1 is None:
            return self.add(eng, lambda e: e.tensor_scalar(out.ap, in0.ap, a1, None, op0), reads=reads, writes=[out], cost=c)
        return self.add(eng, lambda e: e.tensor_scalar(out.ap, in0.ap, a1, a2, op0, op1), reads=reads, writes=[out], cost=c)

    def stt(self, out, in0, scalar, in1, op0, op1):
        reads = [in0, in1]
        a = scalar
        if isinstance(scalar, V):
            reads.append(scalar)
            a = scalar.ap
        return self.add("dve", lambda e: e.scalar_tensor_tensor(out.ap, in0.ap, a, in1.ap, op0, op1),
                        reads=reads, writes=[out], cost=self.vcost("dve", in0.ap.free_size()))

    def recip(self, out, in_):
        return self.add("dve", lambda e: e.reciprocal(out.ap, in_.ap), reads=[in_], writes=[out],
                        cost=self.vcost("dve", in_.ap.free_size()))

    def copy(self, out, in_, eng="dve"):
        if eng == "act":
            return self.add("act", lambda e: e.copy(out.ap, in_.ap), reads=[in_], writes=[out],
                            cost=(in_.ap.free_size() + 260) / 1400.0)
        return self.add(eng, lambda e: e.tensor_copy(out.ap, in_.ap), reads=[in_], writes=[out],
                        cost=self.vcost(eng, in_.ap.free_size()))

    def scan(self, out, d0, d1, init, op0, op1):
        return self.add("dve", lambda e: e.tensor_tensor_scan(out.ap, d0.ap, d1.ap, init.ap, op0, op1),
                        reads=[d0, d1, init], writes=[out], cost=2 * d0.ap.free_size() / 960.0 + 0.15)

    def memset(self, out, val, eng="pool"):
        return self.add(eng, lambda e: e.memset(out.ap, val), writes=[out], cost=self.vcost(eng, out.ap.free_size()))

    @staticmethod
    def vcost(eng, n):
        if eng == "pool":
            return n / 600.0 + 0.25
        return n / 960.0 + 0.15

    def dma(self, eng, out, in_, key, **kw):
        reads = [in_] if isinstance(in_, V) else []
        writes = [out] if isinstance(out, V) else []
        oa = out.ap if isinstance(out, V) else out
        ia = in_.ap if isinstance(in_, V) else in_
        sb = out if isinstance(out, V) else in_
        nb_ = sb.ap.partition_size() * sb.ap.free_size() * 4
        op = self.add(eng, lambda e: e.dma_start(out=oa, in_=ia, **kw), reads=reads, writes=writes, dma_key=key,
                      nbytes=nb_)
        if not isinstance(out, V):
            self.out_dmas.append(op)
        return op

    def emit(self, nc, final_eng="sp"):
        if os.environ.get("MK_SCHED", "1") == "1":
            self.schedule()
            print("scheduler estimate us", round(self.est_time, 1))
        fin = Op()
        fin.eng = final_eng
        fin.fn = None
        fin.idx = len(self.ops[final_eng])
        fin.is_dma = False
        fin.dma_key = None
        fin.signal = False
        fin.cnt = 0
        fin.deps = []
        fin.ddeps = {("d", o.dma_key): self.dma_cnt[o.dma_key] for o in self.out_dmas}
        self.ops[final_eng].append(fin)

        for eng in ENGS:
            seen = {}
            for op in self.ops[eng]:
                w = {}
                for k, val in op.ddeps.items():
                    if seen.get(k, 0) >= val:
                        continue
                    w[k] = (val, None)
                for d in op.deps:
                    k = ("e", d.eng)
                    if seen.get(k, -1) >= d.idx:
                        continue
                    if k not in w or w[k][0] < d.idx:
                        w[k] = (d.idx, d)
                for k, (val, d) in w.items():
                    seen[k] = val
                    if d is not None:
                        d.signal = True
                op.waits = list(w.items())
        for eng in ENGS:
            c = 0
            for op in self.ops[eng]:
                if op.signal:
                    c += 1
                    op.cnt = c

        with ExitStack() as st:
            esem = {e: st.enter_context(nc.semaphore("s_" + e)) for e in ENGS}
            dsem = {k: st.enter_context(nc.semaphore("d_" + str(k))) for k in self.dma_cnt}
            block = st.enter_context(nc.Block())

            def run(eng, e):
                for op in self.ops[eng]:
                    for k, (val, d) in op.waits:
                        if k[0] == "d":
                            e.wait_ge(dsem[k[1]], val)
                        else:
                            e.wait_ge(esem[k[1]], d.cnt)
                    if op.fn is None:
                        continue
                    ins = op.fn(e)
                    if op.is_dma:
                        ins.then_inc(dsem[op.dma_key], 16)
                    elif op.signal:
                        ins.then_inc(esem[eng], 1)

            @block.tensor
            def _(e):
                run("pe", e)

            @block.scalar
            def _(e):
                run("act", e)

            @block.vector
            def _(e):
                run("dve", e)

            @block.gpsimd
            def _(e):
                run("pool", e)

            @block.sync
            def _(e):
                run("sp", e)


class Arena:
    def __init__(self, arena_ap, nbytes):
        self.ap = arena_ap
        self.n = nbytes
        self.top = 0
        self.peak = 0

    def alloc(self, free_shape, dtype):
        es = 2 if dtype == BF16 else 4
        n = es
        for d in free_shape:
            n *= d
        off = (self.top + 63) // 64 * 64
        assert off + n <= self.n, ("arena overflow", off, n, self.n)
        self.top = off + n
        self.peak = max(self.peak, self.top)
        ap = self.ap[:, off // 4:(off + n + 3) // 4]
        if dtype == BF16:
            ap = ap.bitcast(BF16)
        if len(free_shape) == 2:
            ap = ap.rearrange("p (a b) -> p a b", a=free_shape[0])
        elif len(free_shape) == 3:
            ap = ap.rearrange("p (a b c) -> p a b c", a=free_shape[0], b=free_shape[1])
        elif len(free_shape) == 4:
            ap = ap.rearrange("p (a b c d) -> p a b c d", a=free_shape[0], b=free_shape[1], c=free_shape[2])
        return Buf(ap, "S", off, es, free_shape)

    def view_at(self, off, free_shape, dtype):
        es = 2 if dtype == BF16 else 4
        n = es
        for d in free_shape:
            n *= d
        ap = self.ap[:, off // 4:(off + n + 3) // 4]
        if dtype == BF16:
            ap = ap.bitcast(BF16)
        if len(free_shape) == 2:
            ap = ap.rearrange("p (a b) -> p a b", a=free_shape[0])
        return Buf(ap, "S", off, es, free_shape)

    def mark(self):
        return self.top

    def release(self, m):
        self.top = m


ARENA_BYTES = 212480
SLOT_BYTES = 8192
NSLOTS = 4

PP_GAIN = 0
PP_CONVW = 48
PP_CONVB = 80
PP_BA = 88
PP_BX = 96
PP_LAM = 104
PP_SINK = 112
NPP = 128


def build_program(stage):
    SUB = int(os.environ.get("MK_SUB", "9"))
    M2L = int(os.environ.get("MK_M2", "9"))
    M3L = int(os.environ.get("MK_M3", "9"))
    nc = bass.Bass("TRN2", target_bir_lowering=False)
    dram = {}

    def din(name, shape, dt=F32):
        dram[name] = nc.dram_tensor(name, list(shape), dt, kind="ExternalInput").ap()
        return dram[name]

    xT = din("xT", (D, S))
    pp_d = din("pp", (128, NPP))
    w_gu1 = din("w_gu1", (D, 2 * DFF))
    w_dn1 = din("w_dn1", (DFF, D))
    w_gu2 = din("w_gu2", (D, 2 * DFF))
    w_dn2 = din("w_dn2", (DFF, D))
    w_in = din("w_in", (D, 5632))
    w_kv = din("w_kv", (D, 1024))
    w_pl = din("w_pl", (D, D))
    w_pa = din("w_pa", (D, D))
    w_o = din("w_o", (D, D))
    lw_a = din("lw_a", (16, 64, 64))
    lw_x = din("lw_x", (16, 64, 64))
    rope_d = din("rope", (2, 128, S))
    cst_d = din("cst", (128, 128 + 1024))
    outT = nc.dram_tensor("outT", [D, S], F32, kind="ExternalOutput").ap()

    P = Prog()
    with ExitStack() as st:
        arena_t = st.enter_context(nc.sbuf_tensor("arena", [128, ARENA_BYTES // 4], F32))
        psum_t = st.enter_context(nc.psum_tensor("psum", [128, 4096], F32))
        A = Arena(arena_t, ARENA_BYTES)

        def bank(i, n=1):
            return Buf(psum_t[:, i * 512:(i + n) * 512], "P", i * 2048, 4, (n * 512,))

        banks = [bank(i) for i in range(8)]

        hT = A.alloc((8, S), F32)
        pp = A.alloc((NPP,), F32)
        hp = A.alloc((48,), F32)
        cf = A.alloc((8,), F32)
        lc = A.alloc((64,), F32)
        ones = A.alloc((128,), BF16)
        BDa = A.alloc((8, 128), BF16)
        BDx = A.alloc((8, 128), BF16)
        slots = [A.alloc((SLOT_BYTES // 2,), BF16) for _ in range(NSLOTS)]
        slot_i = [0]

        def next_slot(shape):
            i = slot_i[0] % NSLOTS
            slot_i[0] += 1
            sb = slots[i]
            n = 1
            for d in shape:
                n *= d
            assert n * 2 <= SLOT_BYTES
            ap = sb.ap[:, 0:n]
            if len(shape) == 2:
                ap = ap.rearrange("p (a b) -> p a b", a=shape[0])
            return Buf(ap, "S", sb.off, 2, shape), "w%d" % i

        def wload(dst, src, key):
            P.dma("pool", dst, src, key)

        def kp(ap):
            return ap.rearrange("(k p) n -> p k n", p=128)

        P.dma("sp", pp.all(), pp_d, "pp")
        for c in range(8):
            P.dma("sp", hT[:, c, :], xT[c * 128:(c + 1) * 128, :], "x%d" % c)
        P.memset(ones.all(), 1.0)
        P.memset(cf[:, 0:1], EPS)
        P.memset(cf[:, 1:2], 1.0)
        P.ts(hp.all(), pp[:, 0:48], 0.5, None, ALU.mult)
        epsv = cf[:, 0:1]
        onev = cf[:, 1:2]

        def gain(i, c):
            return pp[:, PP_GAIN + i * 8 + c:PP_GAIN + i * 8 + c + 1]

        def hgain(i, c):
            return hp[:, i * 8 + c:i * 8 + c + 1]

        if stage >= 2:
            P.memset(BDa.all(), 0.0)
            P.memset(BDx.all(), 0.0)
            for (bd, lw, key) in ((BDa, lw_a, "bda"), (BDx, lw_x, "bdx")):
                src = lw.rearrange("(c two) i o -> two i c o", two=2)
                for two in range(2):
                    P.dma("pool", bd[two * 64:(two + 1) * 64, :, two * 64:(two + 1) * 64], src[two], key)
            X = lc[:, 0:8]
            T_ = lc[:, 8:16]
            CL = lc[:, 16:24]
            HCL = lc[:, 24:32]
            P.act(X, pp[:, PP_LAM:PP_LAM + 8], AF.Exp, scale=-1.0)
            P.ts(T_, X, 1.0 / 3.0, -0.5, ALU.mult, ALU.add)
            P.tt(T_, T_, X, ALU.mult)
            P.ts(T_, T_, 1.0, None, ALU.add)
            P.tt(T_, T_, X, ALU.mult)
            P.ts(CL, T_, -8.0, None, ALU.mult)
            P.ts(HCL, T_, -4.0, None, ALU.mult)
            P.ts(lc[:, 32:48], pp[:, PP_BA:PP_BA + 16], 0.5, None, ALU.mult)
            P.act(lc[:, 48:64], pp[:, PP_SINK:PP_SINK + 16], AF.Exp)

        def prenorm(gi, uT, t0, ntok, sq_bufs, rstd, ssb):
            k = 0
            for sub in range(ntok // 512):
                ts_ = t0 + sub * 512
                ss = banks[ssb[sub % len(ssb)]]
                rs = rstd[:, sub * 512:(sub + 1) * 512]
                for c in range(8):
                    sq = sq_bufs[k % len(sq_bufs)]
                    k += 1
                    P.act(sq.all(), hT[:, c, ts_:ts_ + 512], AF.Square)
                    P.mm(ss.all(), ones.all(), sq.all(), start=(c == 0), stop=(c == 7))
                P.act(rs, ss.all(), AF.Sqrt, bias=epsv, scale=1.0 / D)
                P.recip(rs, rs)
                for c in range(8):
                    P.stt(uT[:, c, sub * 512:(sub + 1) * 512], hT[:, c, ts_:ts_ + 512], gain(gi, c),
                          rs, ALU.mult, ALU.mult)

        def ffn(gi_pre, gi_post, w_gu, w_dn):
            m = A.mark()
            TB = 1024
            uT = A.alloc((8, TB), BF16)
            actT = A.alloc((NFF, TB), BF16)
            fsb = A.alloc((8, TB), F32)
            sqb = [A.alloc((512,), BF16) for _ in range(2)]
            rstd = A.alloc((TB,), F32)
            sgb = [A.alloc((512,), F32) for _ in range(2)]
            tmpb = [A.alloc((512,), F32) for _ in range(2)]
            for tb in range(S // TB):
                T0 = tb * TB
                prenorm(gi_pre, uT, T0, TB, sqb, rstd, (6, 7))
                k = 0
                for grp in range(NFF // 2):
                    W, key = next_slot((8, 512))
                    wload(W[:, :, 0:256], kp(w_gu[:, grp * 256:(grp + 1) * 256]), key)
                    wload(W[:, :, 256:512], kp(w_gu[:, DFF + grp * 256:DFF + (grp + 1) * 256]), key)
                    for f2 in range(2):
                        ffc = grp * 2 + f2
                        for sub in range(2):
                            pg = banks[(k % 2) * 2]
                            pu = banks[(k % 2) * 2 + 1]
                            sg = sgb[k % 2]
                            k += 1
                            for dc in range(8):
                                P.mm(pg.all(), W[:, dc, f2 * 128:(f2 + 1) * 128], uT[:, dc, sub * 512:(sub + 1) * 512],
                                     start=(dc == 0), stop=(dc == 7))
                            for dc in range(8):
                                P.mm(pu.all(), W[:, dc, 256 + f2 * 128:256 + (f2 + 1) * 128],
                                     uT[:, dc, sub * 512:(sub + 1) * 512], start=(dc == 0), stop=(dc == 7))
                            P.act(sg.all(), pg.all(), AF.Silu)
                            P.tt(actT[:, ffc, sub * 512:(sub + 1) * 512], sg.all(), pu.all(), ALU.mult)
                k = 0
                for dc in range(8):
                    W, key = next_slot((NFF, 128))
                    wload(W.all(), kp(w_dn[:, dc * 128:(dc + 1) * 128]), key)
                    for sub in range(2):
                        pf = banks[4 + (k % 2)]
                        sq = sqb[k % 2]
                        k += 1
                        for ffc in range(NFF):
                            P.mm(pf.all(), W[:, ffc, :], actT[:, ffc, sub * 512:(sub + 1) * 512],
                                 start=(ffc == 0), stop=(ffc == NFF - 1))
                        P.copy(fsb[:, dc, sub * 512:(sub + 1) * 512], pf.all(), eng="act")
                        P.act(sq.all(), pf.all(), AF.Square)
                        P.mm(banks[6 + sub].all(), ones.all(), sq.all(), start=(dc == 0), stop=(dc == 7))
                for sub in range(2):
                    rs = rstd[:, sub * 512:(sub + 1) * 512]
                    P.act(rs, banks[6 + sub].all(), AF.Sqrt, bias=epsv, scale=1.0 / D)
                    P.recip(rs, rs)
                k = 0
                for sub in range(2):
                    ts_ = T0 + sub * 512
                    for c in range(8):
                        tmp = tmpb[k % 2]
                        k += 1
                        P.stt(tmp.all(), fsb[:, c, sub * 512:(sub + 1) * 512], hgain(gi_post, c),
                              rstd[:, sub * 512:(sub + 1) * 512], ALU.mult, ALU.mult)
                        P.tt(hT[:, c, ts_:ts_ + 512], hT[:, c, ts_:ts_ + 512], tmp.all(), ALU.add, eng="pool")
            A.release(m)

        def mixer():
            m = A.mark()
            TB = 512
            uT = A.alloc((8, TB), BF16)
            y_lru = A.alloc((8, TB), BF16)
            qy = A.alloc((16, TB), BF16)
            msb = A.view_at(qy.off, (8, TB), F32)
            kT = A.alloc((4, 640), BF16)
            vS = A.alloc((5, 512), BF16)
            merged = A.alloc((8, TB), BF16)
            ropeC = A.alloc((TB,), F32)
            ropeS = A.alloc((TB,), F32)
            NT = 14
            tmp = [A.alloc((516,), F32) for _ in range(NT)]
            Eb = [A.alloc((1024,), BF16) for _ in range(2)]
            qbb = [A.alloc((512,), BF16) for _ in range(2)]
            sqb = [A.alloc((512,), BF16) for _ in range(2)]
            rstd = A.alloc((TB,), F32)
            Psw = A.alloc((128,), BF16)
            mask = A.alloc((1024,), BF16)
            xcar = A.alloc((8, 4), F32)
            hcar = A.alloc((8,), F32)
            ti = [0]
            bi = [0]
            qi = [0]

            def T():
                t = tmp[ti[0] % NT]
                ti[0] += 1
                return t

            def nb():
                b = banks[bi[0] % 6]
                bi[0] += 1
                return b

            def QB():
                b = qbb[qi[0] % 2]
                qi[0] += 1
                return b

            P.dma("pool", Psw.all(), cst_d[:, 0:128], "psw")
            P.dma("pool", mask.all(), cst_d[:, 128:1152], "msk")
            P.memset(xcar.all(), 0.0)
            P.memset(hcar.all(), 0.0)
            C1 = 0.7978845608028654
            C2 = C1 * 0.044715
            cvw = lambda k, c: pp[:, PP_CONVW + k * 8 + c:PP_CONVW + k * 8 + c + 1]
            cvb = lambda c: pp[:, PP_CONVB + c:PP_CONVB + c + 1]
            lcv = lambda base, c: lc[:, base + c:base + c + 1]
            F = slice(0, 512)
            pstride = ARENA_BYTES // 4

            for tb in range(S // TB):
                t0 = tb * TB
                prenorm(2, uT, t0, TB, sqb, rstd, (6, 7))
                P.dma("sp", ropeC.all(), rope_d[0][:, t0:t0 + TB], "rc")
                P.dma("sp", ropeS.all(), rope_d[1][:, t0:t0 + TB], "rs")

                for cp in range(4 if SUB >= 1 else 0):
                    W, key = next_slot((8, 512))
                    wload(W[:, :, 0:256], kp(w_in[:, cp * 256:(cp + 1) * 256]), key)
                    wload(W[:, :, 256:512], kp(w_in[:, 1024 + cp * 256:1024 + (cp + 1) * 256]), key)
                    for c2 in range(2):
                        c = cp * 2 + c2
                        pg = nb()
                        px = nb()
                        for dc in range(8):
                            P.mm(px.all(), W[:, dc, 256 + c2 * 128:256 + (c2 + 1) * 128], uT[:, dc, :],
                                 start=(dc == 0), stop=(dc == 7))
                        for dc in range(8):
                            P.mm(pg.all(), W[:, dc, c2 * 128:(c2 + 1) * 128], uT[:, dc, :],
                                 start=(dc == 0), stop=(dc == 7))
                        xf = T()
                        P.copy(xf[:, 0:3], xcar[:, c, 0:3], eng="pool")
                        P.copy(xf[:, 3:515], px.all(), eng="act")
                        P.copy(xcar[:, c, 0:3], xf[:, 512:515], eng="pool")
                        if M2L < 2:
                            continue
                        xc = T()
                        P.ts(xc[:, F], xf[:, 0:512], cvw(0, c), cvb(c), ALU.mult, ALU.add)
                        for k in range(1, 4):
                            P.stt(xc[:, F], xf[:, k:k + 512], cvw(k, c), xc[:, F], ALU.mult, ALU.add)
                        xcb = QB()
                        P.copy(xcb.all(), xc[:, F], eng="act")
                        pr = nb()
                        pi = nb()
                        P.mm(pr.all(), BDa[:, c, :], xcb.all())
                        P.mm(pi.all(), BDx[:, c, :], xcb.all())
                        if M2L < 3:
                            continue
                        thr = T()
                        thi = T()
                        P.act(thr[:, F], pr.all(), AF.Tanh, bias=lcv(32, c), scale=0.5)
                        P.act(thi[:, F], pi.all(), AF.Tanh, bias=lcv(40, c), scale=0.5)
                        a = T()
                        mu = T()
                        P.act(a[:, F], thr[:, F], AF.Exp, bias=lcv(24, c), scale=lcv(24, c))
                        P.act(mu[:, F], thr[:, F], AF.Exp, bias=lcv(16, c), scale=lcv(16, c))
                        sq = T()
                        P.act(sq[:, F], pg.all(), AF.Square)
                        P.ts(sq[:, F], sq[:, F], C2, C1, ALU.mult, ALU.add)
                        P.tt(sq[:, F], sq[:, F], pg.all(), ALU.mult)
                        P.act(sq[:, F], sq[:, F], AF.Tanh)
                        P.act(mu[:, F], mu[:, F], AF.Sqrt, bias=onev, scale=-1.0)
                        if M2L < 4:
                            continue
                        t1 = T()
                        P.stt(t1[:, F], thi[:, F], 1.0, xc[:, F], ALU.add, ALU.mult)
                        P.stt(t1[:, F], t1[:, F], 0.5, mu[:, F], ALU.mult, ALU.mult)
                        if M2L < 5:
                            continue
                        hs = T()
                        P.scan(hs[:, F], a[:, F], t1[:, F], hcar[:, c:c + 1], ALU.mult, ALU.add)
                        if M2L < 6:
                            continue
                        P.copy(hcar[:, c:c + 1], hs[:, 511:512], eng="dve")
                        P.stt(sq[:, F], sq[:, F], 1.0, pg.all(), ALU.add, ALU.mult)
                        P.stt(y_lru[:, c, :], sq[:, F], 0.5, hs[:, F], ALU.mult, ALU.mult)

                if SUB < 2:
                    continue
                RCL = int(os.environ.get("MK_RC", "9"))

                def rope_chunk(pq, outv):
                    if RCL < 1:
                        return
                    qb = QB()
                    P.copy(qb.all(), pq.all(), eng="act")
                    ps = nb()
                    P.mm(ps.all(), (ones if os.environ.get("MK_X") == "1" else Psw).all(), qb.all())
                    if RCL < 2:
                        return
                    r1 = T()
                    r2 = T()
                    P.tt(r1[:, F], ropeC.all(), pq.all(), ALU.mult)
                    P.tt(r2[:, F], ropeS.all(), ps.all(), ALU.mult)
                    if RCL >= 3:
                        P.tt(outv, r1[:, F], r2[:, F], ALU.add, eng=("pool" if os.environ.get("MK_Y") == "1" else "dve"))

                for qp in range(2):
                    W, key = next_slot((8, 512))
                    wload(W.all(), kp(w_in[:, 2048 + qp * 512:2048 + (qp + 1) * 512]), key)
                    for c4 in range(4):
                        pq = nb()
                        for dc in range(8):
                            P.mm(pq.all(), W[:, dc, c4 * 128:(c4 + 1) * 128], uT[:, dc, :], start=(dc == 0), stop=(dc == 7))
                        rope_chunk(pq, qy[:, qp * 4 + c4, :])
                W, key = next_slot((8, 512))
                wload(W.all(), kp(w_kv[:, 0:512]), key)
                for j in range(4 if M3L >= 2 else 0):
                    pk = nb()
                    for dc in range(8):
                        P.mm(pk.all(), W[:, dc, j * 128:(j + 1) * 128], uT[:, dc, :], start=(dc == 0), stop=(dc == 7))
                    rope_chunk(pk, kT[:, j, 128:640])
                W, key = next_slot((8, 512))
                wload(W.all(), kp(w_kv[:, 512:1024]), key)
                for i in range(4 if M3L >= 3 else 0):
                    pv = nb()
                    for dc in range(8):
                        P.mm(pv.all(), uT[:, dc, i * 128:(i + 1) * 128], W[:, dc, :], start=(dc == 0), stop=(dc == 7))
                    P.copy(vS[:, 1 + i, :], pv.all(), eng="act")

                k_ = 0
                for n in range(4 if SUB >= 3 else 0):
                    nglob = tb * 4 + n
                    kbs = (0, 1) if nglob > 0 else (1,)
                    for j in range(4):
                        pb = (k_ % 2) * 2
                        S2 = Buf(psum_t[:, pb * 512:(pb + 2) * 512].rearrange("p (h k g q) -> p h k g q", h=2, k=2, g=2),
                                 "P", pb * 2048, 4, (2, 2, 2, 128))
                        po = banks[4 + (k_ % 2)]
                        den = banks[6 + (k_ % 2)]
                        E = Eb[k_ % 2]
                        k_ += 1
                        for kb in kbs:
                            kc = slice((n + kb) * 128, (n + kb + 1) * 128)
                            for hh2 in range(2):
                                for half in range(2):
                                    rows = slice(half * 64, half * 64 + 64)
                                    P.mm(S2[:, half, kb, hh2, :], kT[rows, j, kc],
                                         qy[rows, 2 * j + hh2, n * 128:(n + 1) * 128])
                        P.act(E.all(), S2.all(), AF.Exp, scale=0.125)
                        P.tt(E.all(), E.all(), mask.all(), ALU.mult)
                        E4 = Buf(E.ap.rearrange("p (h k g q) -> p h k g q", h=2, k=2, g=2), "S", E.off, 2, (2, 2, 2, 128))
                        for idx, kb in enumerate(kbs):
                            P.mm(po.all(), vS[:, n + kb, j * 128:(j + 1) * 128], E4[:, :, kb, :, :],
                                 start=(idx == 0), stop=(idx == len(kbs) - 1))
                        for idx, kb in enumerate(kbs):
                            P.mm(den.all(), ones.all(), E4[:, :, kb, :, :],
                                 start=(idx == 0), stop=(idx == len(kbs) - 1))
                        rec = T()
                        sink_ap = bass.AP(arena_t, lc.off // 4 + 48 + 4 * j, [[pstride, 128], [1, 2], [2, 2], [0, 128]])
                        sinkv = V(sink_ap, lc[:, 48:64].regs)
                        rec4 = Buf(rec.ap[:, 0:512].rearrange("p (h g q) -> p h g q", h=2, g=2), "S", rec.off, 4, (2, 2, 128))
                        den4 = Buf(den.ap.rearrange("p (h g q) -> p h g q", h=2, g=2), "P", den.off, 4, (2, 2, 128))
                        po4 = Buf(po.ap.rearrange("p (h g q) -> p h g q", h=2, g=2), "P", po.off, 4, (2, 2, 128))
                        P.tt(rec4.all(), den4.all(), sinkv, ALU.add)
                        P.recip(rec[:, F], rec[:, F])
                        for half in range(2):
                            rows = slice(half * 64, half * 64 + 64)
                            P.tt(qy[rows, 8 + 2 * j:8 + 2 * j + 2, n * 128:(n + 1) * 128], po4[rows, half, :, :],
                                 rec4[rows, half, :, :], ALU.mult)
                if tb < S // TB - 1:
                    P.copy(kT[:, :, 0:128], kT[:, :, 512:640], eng="pool")
                    P.copy(vS[:, 0, :], vS[:, 4, :], eng="pool")

                if SUB < 4:
                    continue
                for g in range(2):
                    sg = [T() for _ in range(4)]
                    sa = [T() for _ in range(4)]
                    W, key = next_slot((8, 512))
                    wload(W.all(), kp(w_in[:, 3584 + g * 512:3584 + (g + 1) * 512]), key)
                    for d4 in range(4):
                        p_ = nb()
                        for dc in range(8):
                            P.mm(p_.all(), W[:, dc, d4 * 128:(d4 + 1) * 128], uT[:, dc, :], start=(dc == 0), stop=(dc == 7))
                        P.act(sg[d4][:, F], p_.all(), AF.Sigmoid)
                    W, key = next_slot((8, 512))
                    wload(W.all(), kp(w_pl[:, g * 512:(g + 1) * 512]), key)
                    for d4 in range(4):
                        p_ = nb()
                        for dc in range(8):
                            P.mm(p_.all(), W[:, dc, d4 * 128:(d4 + 1) * 128], y_lru[:, dc, :], start=(dc == 0), stop=(dc == 7))
                        P.tt(sg[d4][:, F], sg[d4][:, F], p_.all(), ALU.mult)
                    W, key = next_slot((8, 512))
                    wload(W.all(), kp(w_in[:, 4608 + g * 512:4608 + (g + 1) * 512]), key)
                    for d4 in range(4):
                        p_ = nb()
                        for dc in range(8):
                            P.mm(p_.all(), W[:, dc, d4 * 128:(d4 + 1) * 128], uT[:, dc, :], start=(dc == 0), stop=(dc == 7))
                        P.act(sa[d4][:, F], p_.all(), AF.Sigmoid)
                    W, key = next_slot((8, 512))
                    wload(W.all(), kp(w_pa[:, g * 512:(g + 1) * 512]), key)
                    for d4 in range(4):
                        p_ = nb()
                        for dc in range(8):
                            P.mm(p_.all(), W[:, dc, d4 * 128:(d4 + 1) * 128], qy[:, 8 + dc, :], start=(dc == 0), stop=(dc == 7))
                        P.tt(sa[d4][:, F], sa[d4][:, F], p_.all(), ALU.mult)
                        P.tt(merged[:, g * 4 + d4, :], sa[d4][:, F], sg[d4][:, F], ALU.add)

                k_ = 0
                for g in range(2):
                    W, key = next_slot((8, 512))
                    wload(W.all(), kp(w_o[:, g * 512:(g + 1) * 512]), key)
                    for d4 in range(4):
                        dcp = g * 4 + d4
                        p_ = nb()
                        sq = sqb[k_ % 2]
                        k_ += 1
                        for dc in range(8):
                            P.mm(p_.all(), W[:, dc, d4 * 128:(d4 + 1) * 128], merged[:, dc, :], start=(dc == 0), stop=(dc == 7))
                        P.copy(msb[:, dcp, :], p_.all(), eng="act")
                        P.act(sq.all(), p_.all(), AF.Square)
                        P.mm(banks[6].all(), ones.all(), sq.all(), start=(dcp == 0), stop=(dcp == 7))
                P.act(rstd.all(), banks[6].all(), AF.Sqrt, bias=epsv, scale=1.0 / D)
                P.recip(rstd.all(), rstd.all())
                for c in range(8):
                    tm = T()
                    P.stt(tm[:, F], msb[:, c, :], gain(3, c), rstd.all(), ALU.mult, ALU.mult)
                    P.tt(hT[:, c, t0:t0 + TB], hT[:, c, t0:t0 + TB], tm[:, F], ALU.add, eng="pool")
            A.release(m)

        if stage >= 1:
            ffn(0, 1, w_gu1, w_dn1)
        if stage >= 2:
            mixer()
        if stage >= 3:
            ffn(4, 5, w_gu2, w_dn2)

        for c in range(8):
            P.dma("sp", outT[c * 128:(c + 1) * 128, :], hT[:, c, :], "o%d" % c)

        P.emit(nc)
        print("arena peak bytes", A.peak, "ops", {e: len(P.ops[e]) for e in ENGS})
    return nc


_CACHE = {}


def _rope_tables():
    half = 32
    inv_freq = (np.float32(10000.0) ** (-np.arange(half, dtype=np.float32) / np.float32(half))).astype(np.float32)
    ang = (np.arange(S, dtype=np.float32)[:, None] * inv_freq[None, :]).astype(np.float32)
    cos = np.cos(ang).astype(np.float32).T
    sin = np.sin(ang).astype(np.float32).T
    C = np.zeros((128, S), np.float32)
    Sg = np.zeros((128, S), np.float32)
    for p in range(128):
        i = p % 32
        C[p] = cos[i]
        Sg[p] = -sin[i] if (p % 64) < 32 else sin[i]
    return np.stack([C, Sg], 0)


def _consts():
    c = np.zeros((128, 128 + 1024), np.float32)
    for m in range(128):
        k = m + 32 if (m % 64) < 32 else m - 32
        c[k, m] = 1.0
    s = np.arange(128)[:, None]
    q = np.arange(128)[None, :]
    prev = (q < s).astype(np.float32)
    cur = (q >= s).astype(np.float32)
    mk = np.concatenate([prev, prev, cur, cur, prev, prev, cur, cur], axis=1)
    c[:, 128:] = mk
    return c


def kernel(**inp):
    stage = int(os.environ.get("MK_STAGE", "3"))
    if stage not in _CACHE:
        _CACHE[stage] = build_program(stage)
    nc = _CACHE[stage]
    f = lambda a: np.ascontiguousarray(np.asarray(a, dtype=np.float32))
    x = f(inp["x"])

    def col(v):
        return f(v).reshape(8, 128).T

    pp = np.zeros((128, NPP), np.float32)
    for i, nm in enumerate(["ffn1_pre_g", "ffn1_post_g", "mix_pre_g", "mix_post_g", "ffn2_pre_g", "ffn2_post_g"]):
        pp[:, PP_GAIN + i * 8:PP_GAIN + (i + 1) * 8] = col(inp[nm][0])
    cw = f(inp["conv_w"][0])
    for k in range(4):
        pp[:, PP_CONVW + k * 8:PP_CONVW + (k + 1) * 8] = col(cw[k])
    pp[:, PP_CONVB:PP_CONVB + 8] = col(inp["conv_b"][0])
    pp[:, PP_BA:PP_BA + 8] = col(inp["lru_b_a"][0])
    pp[:, PP_BX:PP_BX + 8] = col(inp["lru_b_x"][0])
    pp[:, PP_LAM:PP_LAM + 8] = col(inp["lru_lambda"][0])
    pp[:, PP_SINK:PP_SINK + 16] = np.broadcast_to(f(inp["attn_sinks"][0])[None, :], (128, 16))

    w_in = f(inp["w_in"][0])
    kcols = w_in[:, 3072:3328].reshape(D, 4, 64)
    vcols = w_in[:, 3328:3584].reshape(D, 4, 64)
    w_kv = np.concatenate([np.repeat(kcols[:, :, None, :], 2, axis=2).reshape(D, 512),
                           np.repeat(vcols[:, :, None, :], 2, axis=2).reshape(D, 512)], axis=1)
    shared = {
        "pp": pp,
        "w_gu1": f(inp["ffn1_w_gu"][0]), "w_dn1": f(inp["ffn1_w_down"][0]),
        "w_gu2": f(inp["ffn2_w_gu"][0]), "w_dn2": f(inp["ffn2_w_down"][0]),
        "w_in": w_in, "w_kv": f(w_kv),
        "w_pl": f(inp["w_proj_lru"][0]), "w_pa": f(inp["w_proj_attn"][0]), "w_o": f(inp["w_out"][0]),
        "lw_a": f(inp["lru_w_a"][0]), "lw_x": f(inp["lru_w_x"][0]),
        "rope": _rope_tables(), "cst": _consts(),
    }
    in_maps = []
    for b in range(NCORES):
        m = dict(shared)
        m["xT"] = np.ascontiguousarray(x[b].T)
        in_maps.append(m)
    res = run_bass_kernel_spmd(nc, in_maps, core_ids=list(range(NCORES)))
    out = np.stack([np.asarray(res.results[b]["outT"]).T for b in range(NCORES)], axis=0)
    return np.ascontiguousarray(out.astype(np.float32))
```

```python
import os
from contextlib import ExitStack

import numpy as np
import concourse.bass as bass
import concourse.mybir as mybir
from concourse.bass_utils import run_bass_kernel_spmd

F32 = mybir.dt.float32
BF16 = mybir.dt.bfloat16
AF = mybir.ActivationFunctionType
ALU = mybir.AluOpType

D = 1024
S = 2048
B = 8
DFF = 2816
NFF = DFF // 128
EPS = 1e-6
NCORES = 8

ENGS = ("pe", "act", "dve", "pool", "sp")


class V:
    __slots__ = ("ap", "regs")

    def __init__(self, ap, regs):
        self.ap = ap
        self.regs = regs


class Buf:
    def __init__(self, ap, space, byte_off, esize, free_shape):
        self.ap = ap
        self.space = space
        self.off = byte_off
        self.esize = esize
        self.fs = tuple(free_shape)
        st = []
        acc = 1
        for d in reversed(self.fs):
            st.append(acc)
            acc *= d
        self.strides = tuple(reversed(st))
        self.nbytes = acc * esize

    def __getitem__(self, key):
        if not isinstance(key, tuple):
            key = (key,)
        ap = self.ap[key]
        fk = key[1:]
        rng = []
        for i, d in enumerate(self.fs):
            if i < len(fk):
                k = fk[i]
                if isinstance(k, slice):
                    a = 0 if k.start is None else k.start
                    b = d if k.stop is None else k.stop
                else:
                    a, b = k, k + 1
            else:
                a, b = 0, d
            assert 0 <= a < b <= d, (key, self.fs)
            rng.append((a, b))
        j = -1
        for i, (a, b) in enumerate(rng):
            if (a, b) != (0, self.fs[i]):
                j = i
        if j < 0:
            return self.all()
        combos = [0]
        for i in range(j):
            a, b = rng[i]
            combos = [c + x * self.strides[i] for c in combos for x in range(a, b)]
            if len(combos) > 64:
                combos = None
                break
        if combos is None:
            lo = sum(a * s for (a, b), s in zip(rng, self.strides))
            hi = sum((b - 1) * s for (a, b), s in zip(rng, self.strides))
            return V(ap, [(self.space, self.off + lo * self.esize, self.off + (hi + 1) * self.esize)])
        a, b = rng[j]
        st = self.strides[j]
        regs = [(self.space, self.off + (c + a * st) * self.esize, self.off + (c + b * st) * self.esize)
                for c in combos]
        regs.sort()
        out = [regs[0]]
        for r in regs[1:]:
            if r[1] == out[-1][2]:
                out[-1] = (out[-1][0], out[-1][1], r[2])
            else:
                out.append(r)
        return V(ap, out)

    def all(self):
        return V(self.ap, [(self.space, self.off, self.off + self.nbytes)])


class Op:
    __slots__ = ("eng", "fn", "idx", "deps", "ddeps", "is_dma", "dma_key", "dma_cnt", "signal", "cnt", "waits",
                 "gid", "odeps", "dmadeps", "cost", "nbytes", "fin", "succ", "npred", "deps_all", "phase", "st", "why", "tag", "tset")


class Rec:
    __slots__ = ("lo", "hi", "op", "w")

    def __init__(self, lo, hi, op, w):
        self.lo = lo
        self.hi = hi
        self.op = op
        self.w = w


class Prog:
    def __init__(self):
        self.ops = {e: [] for e in ENGS}
        self.recs = {"S": [], "P": []}
        self.dma_cnt = {}
        self.out_dmas = []
        self.ngid = 0
        self.phase = "setup"
        self.last_dma = {}

    def add(self, eng, fn, reads=(), writes=(), dma_key=None, cost=0.3, nbytes=0):
        op = Op()
        op.eng = eng
        op.fn = fn
        op.gid = self.ngid
        op.phase = self.phase
        self.ngid += 1
        op.cost = cost
        op.nbytes = nbytes
        op.odeps = []
        op.dmadeps = []
        op.tset = None
        op.idx = len(self.ops[eng])
        op.is_dma = dma_key is not None
        op.dma_key = dma_key
        op.signal = False
        op.cnt = 0
        op.waits = None
        if op.is_dma:
            self.dma_cnt[dma_key] = self.dma_cnt.get(dma_key, 0) + 16
            op.dma_cnt = self.dma_cnt[dma_key]
        deps = {}
        ddeps = {}

        def add_dep(d, raw):
            if d is op:
                return
            if d.is_dma:
                k = ("d", d.dma_key)
                v = self.dma_cnt[d.dma_key] - (16 if (op.is_dma and op.dma_key == d.dma_key) else 0)
                if v > ddeps.get(k, 0):
                    ddeps[k] = v
                op.dmadeps.append(d)
                return
            if d.eng == eng and not op.is_dma:
                if eng == "pe":
                    op.odeps.append(d)
                    return
            deps[id(d)] = d

        for v in reads:
            for (sp, lo, hi) in v.regs:
                for r in self.recs[sp]:
                    if r.w and r.lo < hi and lo < r.hi:
                        add_dep(r.op, True)
                    elif (sp == "P" and not r.w and r.op.eng != eng
                          and r.lo // 2048 <= (hi - 1) // 2048 and lo // 2048 <= (r.hi - 1) // 2048):
                        add_dep(r.op, True)
        for v in writes:
            for (sp, lo, hi) in v.regs:
                for r in self.recs[sp]:
                    if r.lo < hi and lo < r.hi:
                        add_dep(r.op, False)
        op.deps = list(deps.values())
        op.ddeps = ddeps
        inorder = not op.is_dma
        for v in reads:
            for (sp, lo, hi) in v.regs:
                lst = self.recs[sp]
                if inorder:
                    keep = []
                    for r in lst:
                        if (not r.w) and (not r.op.is_dma) and r.op.eng == eng and lo <= r.lo and r.hi <= hi:
                            if r.op is not op:
                                op.odeps.append(r.op)
                        else:
                            keep.append(r)
                    lst[:] = keep
                lst.append(Rec(lo, hi, op, False))
        for v in writes:
            for (sp, lo, hi) in v.regs:
                lst = self.recs[sp]
                lst[:] = [r for r in lst if not (lo <= r.lo and r.hi <= hi)]
                lst.append(Rec(lo, hi, op, True))
        if op.is_dma:
            prev = self.last_dma.get(eng)
            if prev is not None:
                op.odeps.append(prev)
            self.last_dma[eng] = op
        self.ops[eng].append(op)
        return op

    def schedule(self, window=int(os.environ.get("MK_WIN", "600"))):
        import heapq
        allops = [op for e in ENGS for op in self.ops[e]]
        for op in allops:
            op.succ = []
            op.fin = None
        for op in allops:
            preds = {}
            for d in op.deps:
                preds[id(d)] = d
            for d in op.odeps:
                preds[id(d)] = d
            for d in op.dmadeps:
                preds[id(d)] = d
            op.npred = len(preds)
            for d in preds.values():
                d.succ.append(op)
            op.deps_all = None
        ready = {e: [] for e in ENGS}
        for op in allops:
            if op.npred == 0:
                heapq.heappush(ready[op.eng], (op.gid, id(op), op))
        efree = {e: 0.0 for e in ENGS}
        dma_free = [0.0]
        order = {e: [] for e in ENGS}
        LAT = 0.25
        nleft = len(allops)
        mingid = {e: 0 for e in ENGS}

        elast = {e: None for e in ENGS}
        cur_set = [None]
        def set_pen(op):
            if op.eng != "act" or op.tset is None or cur_set[0] is None:
                return 0.0
            a, b = cur_set[0], op.tset
            if a == b or (a in "ET" and b in "ET") or (a in "GT" and b in "GT"):
                return 0.0
            return 1.3

        def est_start(op, why=False):
            t = efree[op.eng]
            w = ("eng", elast[op.eng])
            for d in op.deps:
                x = d.fin + (LAT if d.eng != op.eng else 0.05)
                if x > t:
                    t = x
                    w = ("dep", d)
            for d in op.dmadeps:
                x = d.fin + LAT
                if x > t:
                    t = x
                    w = ("dma", d)
            for d in op.odeps:
                x = d.fin if not d.is_dma else d.cnt
                if x > t:
                    t = x
                    w = ("ord", d)
            if why:
                op.why = w
            return t

        while nleft:
            best = None
            for e in ENGS:
                h = ready[e]
                if not h:
                    continue
                g0 = h[0][0]
                cand = heapq.nsmallest(int(os.environ.get("MK_CAND", "12")), h)
                for (g, _, op) in cand:
                    if g - g0 > window:
                        break
                    s = est_start(op) + set_pen(op)
                    if best is None or (s, g) < (best[0], best[1]):
                        best = (s, g, op)
            s, g, op = best
            h = ready[op.eng]
            h.remove((op.gid, id(op), op))
            heapq.heapify(h)
            op.st = s
            if op.eng == "act" and op.tset is not None:
                cur_set[0] = op.tset
            est_start(op, True)
            elast[op.eng] = op
            if op.is_dma:
                issue = 1.15 if op.eng == "pool" else 0.15
                efree[op.eng] = s + issue
                op.cnt = s + issue
                t0 = max(s + issue, dma_free[0])
                dur = op.nbytes / 300e3
                dma_free[0] = t0 + dur
                op.fin = t0 + dur + 2.0
            else:
                op.fin = s + op.cost
                efree[op.eng] = op.fin
            order[op.eng].append(op)
            nleft -= 1
            for sc in op.succ:
                sc.npred -= 1
                if sc.npred == 0:
                    heapq.heappush(ready[sc.eng], (sc.gid, id(sc), sc))
        for e in ENGS:
            assert len(order[e]) == len(self.ops[e])
            self.ops[e] = order[e]
            for i, op in enumerate(order[e]):
                op.idx = i
                op.cnt = 0
        self.est_time = max(op.fin for op in allops)
        if os.environ.get("MK_CRIT") == "1":
            last = max(allops, key=lambda o: o.fin)
            agg = {}
            o = last
            n = 0
            while o is not None and n < 200000:
                kind, p = o.why
                k = (o.phase, o.eng, kind)
                agg[k] = agg.get(k, 0.0) + (o.fin - (p.fin if p is not None else 0.0))
                o = p
                n += 1
            for k in sorted(agg, key=lambda k: -agg[k])[:40]:
                print("CRIT %-10s %-5s %-4s %7.1f" % (k[0], k[1], k[2], agg[k]))
        if os.environ.get("MK_REPORT") == "1":
            ph = {}
            for op in allops:
                d = ph.setdefault(op.phase, {"t0": 1e18, "t1": 0.0, "pe": 0.0, "act": 0.0, "dve": 0.0, "pool": 0.0, "sp": 0.0})
                d["t0"] = min(d["t0"], op.st)
                if not op.is_dma:
                    d["t1"] = max(d["t1"], op.fin)
                    d[op.eng] += op.cost
            for k, d in ph.items():
                print("%-10s t0 %7.1f t1 %7.1f span %6.1f | pe %6.1f act %6.1f dve %6.1f pool %6.1f" % (
                    k, d["t0"], d["t1"], d["t1"] - d["t0"], d["pe"], d["act"], d["dve"], d["pool"]))

    def mm(self, out, lhsT, rhs, start=True, stop=True, **kw):
        n = rhs.ap.free_size()
        return self.add("pe", lambda e: e.matmul(out.ap, lhsT.ap, rhs.ap, start=start, stop=stop, **kw),
                        reads=[lhsT, rhs], writes=[out], cost=(0.276 * n / 512.0 if n >= 256 else 0.105))

    def act(self, out, in_, func, bias=None, scale=None, eng="act"):
        reads = [in_]
        kw = {}
        if bias is not None:
            if isinstance(bias, V):
                reads.append(bias)
                kw["bias"] = bias.ap
            else:
                kw["bias"] = float(bias)
        if scale is not None:
            if isinstance(scale, V):
                reads.append(scale)
                kw["scale"] = scale.ap
            else:
                kw["scale"] = float(scale)
        op = self.add(eng, lambda e: e.activation(out.ap, in_.ap, func, **kw), reads=reads, writes=[out],
                      cost=(in_.ap.free_size() + 260) / 1400.0)
        op.tset = {AF.Exp: "E", AF.Ln: "E", AF.Tanh: "T", AF.Sqrt: "Q", AF.Silu: "U", AF.Sigmoid: "G"}.get(func)
        return op

    def tt(self, out, in0, in1, op, eng="dve"):
        return self.add(eng, lambda e: e.tensor_tensor(out.ap, in0.ap, in1.ap, op), reads=[in0, in1], writes=[out],
                        cost=self.vcost(eng, in0.ap.free_size()))

    def ts(self, out, in0, s1, s2, op0, op1=None, eng="dve"):
        reads = [in0]
        a1 = s1
        a2 = s2
        if isinstance(s1, V):
            reads.append(s1)
            a1 = s1.ap
        if isinstance(s2, V):
            reads.append(s2)
            a2 = s2.ap
        c = self.vcost(eng, in0.ap.free_size())
        if op1 is None:
            return self.add(eng, lambda e: e.tensor_scalar(out.ap, in0.ap, a1, None, op0), reads=reads, writes=[out], cost=c)
        return self.add(eng, lambda e: e.tensor_scalar(out.ap, in0.ap, a1, a2, op0, op1), reads=reads, writes=[out], cost=c)

    def stt(self, out, in0, scalar, in1, op0, op1):
        reads = [in0, in1]
        a = scalar
        if isinstance(scalar, V):
            reads.append(scalar)
            a = scalar.ap
        return self.add("dve", lambda e: e.scalar_tensor_tensor(out.ap, in0.ap, a, in1.ap, op0, op1),
                        reads=reads, writes=[out], cost=in0.ap.free_size() / 640.0 + 0.1)

    def recip(self, out, in_):
        return self.add("dve", lambda e: e.reciprocal(out.ap, in_.ap), reads=[in_], writes=[out],
                        cost=in_.ap.free_size() / 156.0 + 0.05)

    def copy(self, out, in_, eng="dve"):
        if eng == "act":
            return self.add("act", lambda e: e.copy(out.ap, in_.ap), reads=[in_], writes=[out],
                            cost=(in_.ap.free_size() + 260) / 1400.0)
        return self.add(eng, lambda e: e.tensor_copy(out.ap, in_.ap), reads=[in_], writes=[out],
                        cost=self.vcost(eng, in_.ap.free_size()))

    def scan(self, out, d0, d1, init, op0, op1):
        return self.add("dve", lambda e: e.tensor_tensor_scan(out.ap, d0.ap, d1.ap, init.ap, op0, op1),
                        reads=[d0, d1, init], writes=[out], cost=2 * d0.ap.free_size() / 960.0 + 0.15)

    def memset(self, out, val, eng="pool"):
        return self.add(eng, lambda e: e.memset(out.ap, val), writes=[out], cost=self.vcost(eng, out.ap.free_size()))

    @staticmethod
    def vcost(eng, n):
        if eng == "pool":
            return n / 450.0 + 0.25
        return n / 1000.0 + 0.09

    def dma(self, eng, out, in_, key, **kw):
        reads = [in_] if isinstance(in_, V) else []
        writes = [out] if isinstance(out, V) else []
        oa = out.ap if isinstance(out, V) else out
        ia = in_.ap if isinstance(in_, V) else in_
        sb = out if isinstance(out, V) else in_
        nb_ = sb.ap.partition_size() * sb.ap.free_size() * 4
        op = self.add(eng, lambda e: e.dma_start(out=oa, in_=ia, **kw), reads=reads, writes=writes, dma_key=key,
                      nbytes=nb_)
        if not isinstance(out, V):
            self.out_dmas.append(op)
        return op

    def emit(self, nc, final_eng="sp"):
        if os.environ.get("MK_SCHED", "1") == "1":
            self.schedule()
            print("scheduler estimate us", round(self.est_time, 1))
        fin = Op()
        fin.eng = final_eng
        fin.fn = None
        fin.idx = len(self.ops[final_eng])
        fin.is_dma = False
        fin.dma_key = None
        fin.signal = False
        fin.cnt = 0
        fin.deps = []
        fin.ddeps = {("d", o.dma_key): self.dma_cnt[o.dma_key] for o in self.out_dmas}
        self.ops[final_eng].append(fin)

        for eng in ENGS:
            seen = {}
            for op in self.ops[eng]:
                w = {}
                for k, val in op.ddeps.items():
                    if seen.get(k, 0) >= val:
                        continue
                    w[k] = (val, None)
                for d in op.deps:
                    k = ("e", d.eng)
                    if seen.get(k, -1) >= d.idx:
                        continue
                    if k not in w or w[k][0] < d.idx:
                        w[k] = (d.idx, d)
                for k, (val, d) in w.items():
                    seen[k] = val
                    if d is not None:
                        d.signal = True
                op.waits = list(w.items())
        for eng in ENGS:
            c = 0
            for op in self.ops[eng]:
                if op.signal:
                    c += 1
                    op.cnt = c

        with ExitStack() as st:
            esem = {e: st.enter_context(nc.semaphore("s_" + e)) for e in ENGS}
            dsem = {k: st.enter_context(nc.semaphore("d_" + str(k))) for k in self.dma_cnt}
            block = st.enter_context(nc.Block())

            def run(eng, e):
                for op in self.ops[eng]:
                    for k, (val, d) in op.waits:
                        if k[0] == "d":
                            e.wait_ge(dsem[k[1]], val)
                        else:
                            e.wait_ge(esem[k[1]], d.cnt)
                    if op.fn is None:
                        continue
                    ins = op.fn(e)
                    if op.is_dma:
                        ins.then_inc(dsem[op.dma_key], 16)
                    elif op.signal:
                        ins.then_inc(esem[eng], 1)

            @block.tensor
            def _(e):
                run("pe", e)

            @block.scalar
            def _(e):
                run("act", e)

            @block.vector
            def _(e):
                run("dve", e)

            @block.gpsimd
            def _(e):
                run("pool", e)

            @block.sync
            def _(e):
                run("sp", e)


class Arena:
    def __init__(self, arena_ap, nbytes):
        self.ap = arena_ap
        self.n = nbytes
        self.top = 0
        self.peak = 0

    def alloc(self, free_shape, dtype):
        es = 2 if dtype == BF16 else 4
        n = es
        for d in free_shape:
            n *= d
        off = (self.top + 63) // 64 * 64
        assert off + n <= self.n, ("arena overflow", off, n, self.n)
        self.top = off + n
        self.peak = max(self.peak, self.top)
        ap = self.ap[:, off // 4:(off + n + 3) // 4]
        if dtype == BF16:
            ap = ap.bitcast(BF16)
        if len(free_shape) == 2:
            ap = ap.rearrange("p (a b) -> p a b", a=free_shape[0])
        elif len(free_shape) == 3:
            ap = ap.rearrange("p (a b c) -> p a b c", a=free_shape[0], b=free_shape[1])
        elif len(free_shape) == 4:
            ap = ap.rearrange("p (a b c d) -> p a b c d", a=free_shape[0], b=free_shape[1], c=free_shape[2])
        return Buf(ap, "S", off, es, free_shape)

    def view_at(self, off, free_shape, dtype):
        es = 2 if dtype == BF16 else 4
        n = es
        for d in free_shape:
            n *= d
        ap = self.ap[:, off // 4:(off + n + 3) // 4]
        if dtype == BF16:
            ap = ap.bitcast(BF16)
        if len(free_shape) == 2:
            ap = ap.rearrange("p (a b) -> p a b", a=free_shape[0])
        return Buf(ap, "S", off, es, free_shape)

    def mark(self):
        return self.top

    def release(self, m):
        self.top = m


ARENA_BYTES = int(os.environ.get("MK_ARENA", "212480"))
SLOT_BYTES = 8192
NSLOTS = 4

PP_GAIN = 0
PP_CONVW = 48
PP_CONVB = 80
PP_BA = 88
PP_BX = 96
PP_LAM = 104
PP_SINK = 112
NPP = 128


def build_program(stage):
    SUB = int(os.environ.get("MK_SUB", "9"))
    M2L = int(os.environ.get("MK_M2", "9"))
    M3L = int(os.environ.get("MK_M3", "9"))
    nc = bass.Bass("TRN2", target_bir_lowering=False)
    dram = {}

    def din(name, shape, dt=F32):
        dram[name] = nc.dram_tensor(name, list(shape), dt, kind="ExternalInput").ap()
        return dram[name]

    xT = din("xT", (D, S))
    pp_d = din("pp", (128, NPP))
    w_gu1 = din("w_gu1", (D, 2 * DFF))
    w_dn1 = din("w_dn1", (DFF, D))
    w_gu2 = din("w_gu2", (D, 2 * DFF))
    w_dn2 = din("w_dn2", (DFF, D))
    w_in = din("w_in", (D, 5632))
    w_kv = din("w_kv", (D, 1024))
    w_pl = din("w_pl", (D, D))
    w_pa = din("w_pa", (D, D))
    w_o = din("w_o", (D, D))
    lw_a = din("lw_a", (16, 64, 64))
    lw_x = din("lw_x", (16, 64, 64))
    rope_d = din("rope", (2, 128, S))
    cst_d = din("cst", (128, 128 + 1024))
    outT = nc.dram_tensor("outT", [D, S], F32, kind="ExternalOutput").ap()

    P = Prog()
    with ExitStack() as st:
        arena_t = st.enter_context(nc.sbuf_tensor("arena", [128, ARENA_BYTES // 4], F32))
        psum_t = st.enter_context(nc.psum_tensor("psum", [128, 4096], F32))
        A = Arena(arena_t, ARENA_BYTES)

        def bank(i, n=1):
            return Buf(psum_t[:, i * 512:(i + n) * 512], "P", i * 2048, 4, (n * 512,))

        banks = [bank(i) for i in range(8)]

        hT = A.alloc((8, S), F32)
        pp = A.alloc((NPP,), F32)
        hp = A.alloc((48,), F32)
        cf = A.alloc((8,), F32)
        lc = A.alloc((64,), F32)
        ones = A.alloc((128,), BF16)
        BDa = A.alloc((8, 128), BF16)
        BDx = A.alloc((8, 128), BF16)
        slots = [A.alloc((SLOT_BYTES // 2,), BF16) for _ in range(NSLOTS)]
        slot_i = [0]

        def next_slot(shape):
            i = slot_i[0] % NSLOTS
            slot_i[0] += 1
            sb = slots[i]
            n = 1
            for d in shape:
                n *= d
            assert n * 2 <= SLOT_BYTES
            ap = sb.ap[:, 0:n]
            if len(shape) == 2:
                ap = ap.rearrange("p (a b) -> p a b", a=shape[0])
            return Buf(ap, "S", sb.off, 2, shape), "w%d" % i

        def wload(dst, src, key):
            P.dma("pool", dst, src, key)

        def kp(ap):
            return ap.rearrange("(k p) n -> p k n", p=128)

        P.dma("sp", pp.all(), pp_d, "pp")
        for c in range(8):
            P.dma("sp", hT[:, c, :], xT[c * 128:(c + 1) * 128, :], "x%d" % c)
        P.memset(ones.all(), 1.0)
        P.memset(cf[:, 0:1], EPS)
        P.memset(cf[:, 1:2], 1.0)
        P.ts(hp.all(), pp[:, 0:48], 0.5, None, ALU.mult)
        epsv = cf[:, 0:1]
        onev = cf[:, 1:2]

        def gain(i, c):
            return pp[:, PP_GAIN + i * 8 + c:PP_GAIN + i * 8 + c + 1]

        def hgain(i, c):
            return hp[:, i * 8 + c:i * 8 + c + 1]

        if stage >= 2:
            P.memset(BDa.all(), 0.0)
            P.memset(BDx.all(), 0.0)
            for (bd, lw, key) in ((BDa, lw_a, "bda"), (BDx, lw_x, "bdx")):
                src = lw.rearrange("(c two) i o -> two i c o", two=2)
                for two in range(2):
                    P.dma("pool", bd[two * 64:(two + 1) * 64, :, two * 64:(two + 1) * 64], src[two], key)
            X = lc[:, 0:8]
            T_ = lc[:, 8:16]
            CL = lc[:, 16:24]
            HCL = lc[:, 24:32]
            P.act(X, pp[:, PP_LAM:PP_LAM + 8], AF.Exp, scale=-1.0)
            P.ts(T_, X, 1.0 / 3.0, -0.5, ALU.mult, ALU.add)
            P.tt(T_, T_, X, ALU.mult)
            P.ts(T_, T_, 1.0, None, ALU.add)
            P.tt(T_, T_, X, ALU.mult)
            P.ts(CL, T_, -8.0, None, ALU.mult)
            P.ts(HCL, T_, -4.0, None, ALU.mult)
            P.ts(lc[:, 32:48], pp[:, PP_BA:PP_BA + 16], 0.5, None, ALU.mult)
            P.act(lc[:, 48:64], pp[:, PP_SINK:PP_SINK + 16], AF.Exp)

        def prenorm(gi, uT, t0, ntok, sq_bufs, rstd, ssb):
            k = 0
            for sub in range(ntok // 512):
                ts_ = t0 + sub * 512
                ss = banks[ssb[sub % len(ssb)]]
                rs = rstd[:, sub * 512:(sub + 1) * 512]
                for c in range(8):
                    sq = sq_bufs[k % len(sq_bufs)]
                    k += 1
                    P.act(sq.all(), hT[:, c, ts_:ts_ + 512], AF.Square)
                    P.mm(ss.all(), ones.all(), sq.all(), start=(c == 0), stop=(c == 7))
                P.act(rs, ss.all(), AF.Ln, bias=epsv, scale=1.0 / D)
                P.act(rs, rs, AF.Exp, scale=-0.5)
                for c in range(8):
                    P.stt(uT[:, c, sub * 512:(sub + 1) * 512], hT[:, c, ts_:ts_ + 512], gain(gi, c),
                          rs, ALU.mult, ALU.mult)

        def ffn(gi_pre, gi_post, w_gu, w_dn):
            m = A.mark()
            TB = 1024
            uT = A.alloc((8, TB), BF16)
            actT = A.alloc((NFF, TB), BF16)
            fsb = A.alloc((8, TB), F32)
            sqb = [A.alloc((512,), BF16) for _ in range(2)]
            rstd = A.alloc((TB,), F32)
            sgb = [A.alloc((512,), F32) for _ in range(2)]
            tmpb = [A.alloc((512,), F32) for _ in range(2)]
            for tb in range(S // TB):
                T0 = tb * TB
                P.phase = "ffn%d.%d" % (gi_pre // 4 + 1, tb)
                prenorm(gi_pre, uT, T0, TB, sqb, rstd, (6, 7))
                k = 0
                for grp in range(NFF // 2):
                    W, key = next_slot((8, 512))
                    wload(W[:, :, 0:256], kp(w_gu[:, grp * 256:(grp + 1) * 256]), key)
                    wload(W[:, :, 256:512], kp(w_gu[:, DFF + grp * 256:DFF + (grp + 1) * 256]), key)
                    for f2 in range(2):
                        ffc = grp * 2 + f2
                        for sub in range(2):
                            pg = banks[(k % 2) * 2]
                            pu = banks[(k % 2) * 2 + 1]
                            sg = sgb[k % 2]
                            k += 1
                            for dc in range(8):
                                P.mm(pg.all(), W[:, dc, f2 * 128:(f2 + 1) * 128], uT[:, dc, sub * 512:(sub + 1) * 512],
                                     start=(dc == 0), stop=(dc == 7))
                            for dc in range(8):
                                P.mm(pu.all(), W[:, dc, 256 + f2 * 128:256 + (f2 + 1) * 128],
                                     uT[:, dc, sub * 512:(sub + 1) * 512], start=(dc == 0), stop=(dc == 7))
                            P.act(sg.all(), pg.all(), AF.Silu)
                            P.tt(actT[:, ffc, sub * 512:(sub + 1) * 512], sg.all(), pu.all(), ALU.mult)
                k = 0
                for dc in range(8):
                    W, key = next_slot((NFF, 128))
                    wload(W.all(), kp(w_dn[:, dc * 128:(dc + 1) * 128]), key)
                    for sub in range(2):
                        pf = banks[4 + (k % 2)]
                        sq = sqb[k % 2]
                        k += 1
                        for ffc in range(NFF):
                            P.mm(pf.all(), W[:, ffc, :], actT[:, ffc, sub * 512:(sub + 1) * 512],
                                 start=(ffc == 0), stop=(ffc == NFF - 1))
                        P.copy(fsb[:, dc, sub * 512:(sub + 1) * 512], pf.all(), eng="act")
                        P.act(sq.all(), pf.all(), AF.Square)
                        P.mm(banks[6 + sub].all(), ones.all(), sq.all(), start=(dc == 0), stop=(dc == 7))
                for sub in range(2):
                    rs = rstd[:, sub * 512:(sub + 1) * 512]
                    P.act(rs, banks[6 + sub].all(), AF.Ln, bias=epsv, scale=1.0 / D)
                    P.act(rs, rs, AF.Exp, scale=-0.5)
                k = 0
                for sub in range(2):
                    ts_ = T0 + sub * 512
                    for c in range(8):
                        tmp = tmpb[k % 2]
                        k += 1
                        P.stt(tmp.all(), fsb[:, c, sub * 512:(sub + 1) * 512], hgain(gi_post, c),
                              rstd[:, sub * 512:(sub + 1) * 512], ALU.mult, ALU.mult)
                        P.tt(hT[:, c, ts_:ts_ + 512], hT[:, c, ts_:ts_ + 512], tmp.all(), ALU.add, eng="pool")
            A.release(m)

        def mixer():
            m = A.mark()
            TB = 512
            DB = int(os.environ.get("MK_DB", "1"))
            uTs = [A.alloc((8, TB), BF16) for _ in range(DB)]
            y_lrus = [A.alloc((8, TB), BF16) for _ in range(DB)]
            qy = A.alloc((16, TB), BF16)
            msb = A.view_at(qy.off, (8, TB), F32)
            kT = A.alloc((4, 640), BF16)
            vS = A.alloc((5, 512), BF16)
            merged = A.alloc((8, TB), BF16)
            ropeC = A.alloc((TB,), F32)
            ropeS = A.alloc((TB,), F32)
            NT = int(os.environ.get("MK_NT", "19"))
            tmp = [A.alloc((516,), F32) for _ in range(NT)]
            Eb = [A.alloc((1024,), BF16) for _ in range(int(os.environ.get("MK_EB", "2")))]
            qbb = [A.alloc((512,), BF16) for _ in range(2)]
            sqb = [A.alloc((512,), BF16) for _ in range(2)]
            rstd = A.alloc((TB,), F32)
            Psw = A.alloc((128,), BF16)
            mask = A.alloc((1024,), BF16)
            xcar = A.alloc((8, 4), F32)
            hcar = A.alloc((8,), F32)
            ti = [0]
            bi = [0]
            qi = [0]

            def T():
                t = tmp[ti[0] % NT]
                ti[0] += 1
                return t

            def nb():
                b = banks[bi[0] % 6]
                bi[0] += 1
                return b

            def QB():
                b = qbb[qi[0] % 2]
                qi[0] += 1
                return b

            print("mixer arena top", A.top, "of", ARENA_BYTES)
            OFF = os.environ.get("MK_OFF", "1") == "1"
            PEX = "pool" if OFF else "dve"
            P.dma("pool", Psw.all(), cst_d[:, 0:128], "psw")
            P.dma("pool", mask.all(), cst_d[:, 128:1152], "msk")
            P.memset(xcar.all(), 0.0)
            P.memset(hcar.all(), 0.0)
            C1 = 0.7978845608028654
            C2 = C1 * 0.044715
            cvw = lambda k, c: pp[:, PP_CONVW + k * 8 + c:PP_CONVW + k * 8 + c + 1]
            cvb = lambda c: pp[:, PP_CONVB + c:PP_CONVB + c + 1]
            lcv = lambda base, c: lc[:, base + c:base + c + 1]
            F = slice(0, 512)
            pstride = ARENA_BYTES // 4

            for tb in range(S // TB):
                t0 = tb * TB
                uT = uTs[tb % DB]
                y_lru = y_lrus[tb % DB]
                P.phase = "mix%d.lru" % tb
                prenorm(2, uT, t0, TB, sqb, rstd, (6, 7))
                P.dma("sp", ropeC.all(), rope_d[0][:, t0:t0 + TB], "rc")
                P.dma("sp", ropeS.all(), rope_d[1][:, t0:t0 + TB], "rs")

                for cp in range(4 if SUB >= 1 else 0):
                    W, key = next_slot((8, 512))
                    wload(W[:, :, 0:256], kp(w_in[:, cp * 256:(cp + 1) * 256]), key)
                    wload(W[:, :, 256:512], kp(w_in[:, 1024 + cp * 256:1024 + (cp + 1) * 256]), key)
                    for c2 in range(2):
                        c = cp * 2 + c2
                        pg = nb()
                        px = nb()
                        for dc in range(8):
                            P.mm(px.all(), W[:, dc, 256 + c2 * 128:256 + (c2 + 1) * 128], uT[:, dc, :],
                                 start=(dc == 0), stop=(dc == 7))
                        for dc in range(8):
                            P.mm(pg.all(), W[:, dc, c2 * 128:(c2 + 1) * 128], uT[:, dc, :],
                                 start=(dc == 0), stop=(dc == 7))
                        xf = T()
                        P.copy(xf[:, 0:3], xcar[:, c, 0:3], eng="pool")
                        P.copy(xf[:, 3:515], px.all(), eng="act")
                        P.copy(xcar[:, c, 0:3], xf[:, 512:515], eng="pool")
                        if M2L < 2:
                            continue
                        xc = T()
                        P.ts(xc[:, F], xf[:, 0:512], cvw(0, c), cvb(c), ALU.mult, ALU.add, eng=PEX)
                        for k in range(1, 4):
                            P.stt(xc[:, F], xf[:, k:k + 512], cvw(k, c), xc[:, F], ALU.mult, ALU.add)
                        xcb = QB()
                        P.copy(xcb.all(), xc[:, F], eng="act")
                        pr = nb()
                        pi = nb()
                        P.mm(pr.all(), BDa[:, c, :], xcb.all())
                        P.mm(pi.all(), BDx[:, c, :], xcb.all())
                        if M2L < 3:
                            continue
                        thr = T()
                        thi = T()
                        P.act(thr[:, F], pr.all(), AF.Tanh, bias=lcv(32, c), scale=0.5)
                        P.act(thi[:, F], pi.all(), AF.Tanh, bias=lcv(40, c), scale=0.5)
                        mu = T()
                        P.act(mu[:, F], thr[:, F], AF.Exp, bias=lcv(16, c), scale=lcv(16, c))
                        a = thr
                        P.act(a[:, F], thr[:, F], AF.Exp, bias=lcv(24, c), scale=lcv(24, c))
                        sq = T()
                        P.act(sq[:, F], pg.all(), AF.Square)
                        P.ts(sq[:, F], sq[:, F], C2, C1, ALU.mult, ALU.add, eng=PEX)
                        P.tt(sq[:, F], sq[:, F], pg.all(), ALU.mult)
                        P.act(sq[:, F], sq[:, F], AF.Tanh)
                        P.act(mu[:, F], mu[:, F], AF.Sqrt, bias=onev, scale=-1.0)
                        if M2L < 4:
                            continue
                        t1 = thi
                        P.stt(t1[:, F], thi[:, F], 1.0, xc[:, F], ALU.add, ALU.mult)
                        P.stt(t1[:, F], t1[:, F], 0.5, mu[:, F], ALU.mult, ALU.mult)
                        if M2L < 5:
                            continue
                        hs = xc
                        P.scan(hs[:, F], a[:, F], t1[:, F], hcar[:, c:c + 1], ALU.mult, ALU.add)
                        if M2L < 6:
                            continue
                        P.copy(hcar[:, c:c + 1], hs[:, 511:512], eng="dve")
                        P.stt(sq[:, F], sq[:, F], 1.0, pg.all(), ALU.add, ALU.mult)
                        P.stt(y_lru[:, c, :], sq[:, F], 0.5, hs[:, F], ALU.mult, ALU.mult)

                if SUB < 2:
                    continue
                P.phase = "mix%d.qkv" % tb
                RCL = int(os.environ.get("MK_RC", "9"))

                def rope_chunk(pq, outv):
                    if RCL < 1:
                        return
                    qb = QB()
                    P.copy(qb.all(), pq.all(), eng="act")
                    ps = nb()
                    P.mm(ps.all(), (ones if os.environ.get("MK_X") == "1" else Psw).all(), qb.all())
                    if RCL < 2:
                        return
                    r1 = T()
                    r2 = T()
                    P.tt(r1[:, F], ropeC.all(), pq.all(), ALU.mult)
                    P.tt(r2[:, F], ropeS.all(), ps.all(), ALU.mult)
                    if RCL >= 3:
                        P.tt(outv, r1[:, F], r2[:, F], ALU.add, eng=PEX)

                for qp in range(2):
                    W, key = next_slot((8, 512))
                    wload(W.all(), kp(w_in[:, 2048 + qp * 512:2048 + (qp + 1) * 512]), key)
                    for c4 in range(4):
                        pq = nb()
                        for dc in range(8):
                            P.mm(pq.all(), W[:, dc, c4 * 128:(c4 + 1) * 128], uT[:, dc, :], start=(dc == 0), stop=(dc == 7))
                        rope_chunk(pq, qy[:, qp * 4 + c4, :])
                W, key = next_slot((8, 512))
                wload(W.all(), kp(w_kv[:, 0:512]), key)
                for j in range(4 if M3L >= 2 else 0):
                    pk = nb()
                    for dc in range(8):
                        P.mm(pk.all(), W[:, dc, j * 128:(j + 1) * 128], uT[:, dc, :], start=(dc == 0), stop=(dc == 7))
                    rope_chunk(pk, kT[:, j, 128:640])
                W, key = next_slot((8, 512))
                wload(W.all(), kp(w_kv[:, 512:1024]), key)
                for i in range(4 if M3L >= 3 else 0):
                    pv = nb()
                    for dc in range(8):
                        P.mm(pv.all(), uT[:, dc, i * 128:(i + 1) * 128], W[:, dc, :], start=(dc == 0), stop=(dc == 7))
                    P.copy(vS[:, 1 + i, :], pv.all(), eng="act")

                P.phase = "mix%d.att" % tb
                k_ = 0
                for n in range(4 if SUB >= 3 else 0):
                    nglob = tb * 4 + n
                    kbs = (0, 1) if nglob > 0 else (1,)
                    for j in range(4):
                        pb = (k_ % 2) * 2
                        S2 = Buf(psum_t[:, pb * 512:(pb + 2) * 512].rearrange("p (h k g q) -> p h k g q", h=2, k=2, g=2),
                                 "P", pb * 2048, 4, (2, 2, 2, 128))
                        po = banks[4 + (k_ % 2)]
                        den = banks[6 + (k_ % 2)]
                        E = Eb[k_ % len(Eb)]
                        k_ += 1
                        for kb in kbs:
                            kc = slice((n + kb) * 128, (n + kb + 1) * 128)
                            for hh2 in range(2):
                                for half in range(2):
                                    rows = slice(half * 64, half * 64 + 64)
                                    P.mm(S2[:, half, kb, hh2, :], kT[rows, j, kc],
                                         qy[rows, 2 * j + hh2, n * 128:(n + 1) * 128])
                        P.act(E.all(), S2.all(), AF.Exp, scale=0.125)
                        P.tt(E.all(), E.all(), mask.all(), ALU.mult, eng=PEX)
                        E4 = Buf(E.ap.rearrange("p (h k g q) -> p h k g q", h=2, k=2, g=2), "S", E.off, 2, (2, 2, 2, 128))
                        for idx, kb in enumerate(kbs):
                            P.mm(po.all(), vS[:, n + kb, j * 128:(j + 1) * 128], E4[:, :, kb, :, :],
                                 start=(idx == 0), stop=(idx == len(kbs) - 1))
                        for idx, kb in enumerate(kbs):
                            P.mm(den.all(), ones.all(), E4[:, :, kb, :, :],
                                 start=(idx == 0), stop=(idx == len(kbs) - 1))
                        rec = T()
                        sink_ap = bass.AP(arena_t, lc.off // 4 + 48 + 4 * j, [[pstride, 128], [1, 2], [2, 2], [0, 128]])
                        sinkv = V(sink_ap, lc[:, 48:64].regs)
                        rec4 = Buf(rec.ap[:, 0:512].rearrange("p (h g q) -> p h g q", h=2, g=2), "S", rec.off, 4, (2, 2, 128))
                        den4 = Buf(den.ap.rearrange("p (h g q) -> p h g q", h=2, g=2), "P", den.off, 4, (2, 2, 128))
                        po4 = Buf(po.ap.rearrange("p (h g q) -> p h g q", h=2, g=2), "P", po.off, 4, (2, 2, 128))
                        P.tt(rec4.all(), den4.all(), sinkv, ALU.add)
                        P.act(rec[:, F], rec[:, F], AF.Ln)
                        P.act(rec[:, F], rec[:, F], AF.Exp, scale=-1.0)
                        for half in range(2):
                            rows = slice(half * 64, half * 64 + 64)
                            P.tt(qy[rows, 8 + 2 * j:8 + 2 * j + 2, n * 128:(n + 1) * 128], po4[rows, half, :, :],
                                 rec4[rows, half, :, :], ALU.mult)
                if tb < S // TB - 1:
                    P.copy(kT[:, :, 0:128], kT[:, :, 512:640], eng="pool")
                    P.copy(vS[:, 0, :], vS[:, 4, :], eng="pool")

                if SUB < 4:
                    continue
                P.phase = "mix%d.mrg" % tb
                for g in range(2):
                    sg = [T() for _ in range(4)]
                    sa = [T() for _ in range(4)]
                    W, key = next_slot((8, 512))
                    wload(W.all(), kp(w_in[:, 3584 + g * 512:3584 + (g + 1) * 512]), key)
                    for d4 in range(4):
                        p_ = nb()
                        for dc in range(8):
                            P.mm(p_.all(), W[:, dc, d4 * 128:(d4 + 1) * 128], uT[:, dc, :], start=(dc == 0), stop=(dc == 7))
                        P.act(sg[d4][:, F], p_.all(), AF.Sigmoid)
                    W, key = next_slot((8, 512))
                    wload(W.all(), kp(w_pl[:, g * 512:(g + 1) * 512]), key)
                    for d4 in range(4):
                        p_ = nb()
                        for dc in range(8):
                            P.mm(p_.all(), W[:, dc, d4 * 128:(d4 + 1) * 128], y_lru[:, dc, :], start=(dc == 0), stop=(dc == 7))
                        P.tt(sg[d4][:, F], sg[d4][:, F], p_.all(), ALU.mult)
                    W, key = next_slot((8, 512))
                    wload(W.all(), kp(w_in[:, 4608 + g * 512:4608 + (g + 1) * 512]), key)
                    for d4 in range(4):
                        p_ = nb()
                        for dc in range(8):
                            P.mm(p_.all(), W[:, dc, d4 * 128:(d4 + 1) * 128], uT[:, dc, :], start=(dc == 0), stop=(dc == 7))
                        P.act(sa[d4][:, F], p_.all(), AF.Sigmoid)
                    W, key = next_slot((8, 512))
                    wload(W.all(), kp(w_pa[:, g * 512:(g + 1) * 512]), key)
                    for d4 in range(4):
                        p_ = nb()
                        for dc in range(8):
                            P.mm(p_.all(), W[:, dc, d4 * 128:(d4 + 1) * 128], qy[:, 8 + dc, :], start=(dc == 0), stop=(dc == 7))
                        P.tt(sa[d4][:, F], sa[d4][:, F], p_.all(), ALU.mult)
                        P.tt(merged[:, g * 4 + d4, :], sa[d4][:, F], sg[d4][:, F], ALU.add, eng=PEX)

                P.phase = "mix%d.out" % tb
                k_ = 0
                for g in range(2):
                    W, key = next_slot((8, 512))
                    wload(W.all(), kp(w_o[:, g * 512:(g + 1) * 512]), key)
                    for d4 in range(4):
                        dcp = g * 4 + d4
                        p_ = nb()
                        sq = sqb[k_ % 2]
                        k_ += 1
                        for dc in range(8):
                            P.mm(p_.all(), W[:, dc, d4 * 128:(d4 + 1) * 128], merged[:, dc, :], start=(dc == 0), stop=(dc == 7))
                        P.copy(msb[:, dcp, :], p_.all(), eng="act")
                        P.act(sq.all(), p_.all(), AF.Square)
                        P.mm(banks[6].all(), ones.all(), sq.all(), start=(dcp == 0), stop=(dcp == 7))
                P.act(rstd.all(), banks[6].all(), AF.Ln, bias=epsv, scale=1.0 / D)
                P.act(rstd.all(), rstd.all(), AF.Exp, scale=-0.5)
                for c in range(8):
                    tm = T()
                    P.stt(tm[:, F], msb[:, c, :], gain(3, c), rstd.all(), ALU.mult, ALU.mult)
                    P.tt(hT[:, c, t0:t0 + TB], hT[:, c, t0:t0 + TB], tm[:, F], ALU.add, eng="pool")
            A.release(m)

        if stage >= 1:
            ffn(0, 1, w_gu1, w_dn1)
        if stage >= 2:
            mixer()
        if stage >= 3:
            ffn(4, 5, w_gu2, w_dn2)

        for c in range(8):
            P.dma("sp", outT[c * 128:(c + 1) * 128, :], hT[:, c, :], "o%d" % c)

        P.emit(nc)
        print("arena peak bytes", A.peak, "ops", {e: len(P.ops[e]) for e in ENGS})
    return nc


_CACHE = {}


def _rope_tables():
    half = 32
    inv_freq = 10000.0 ** (-np.arange(half, dtype=np.float64) / half)
    ang = np.arange(S, dtype=np.float64)[:, None] * inv_freq[None, :]
    cos = np.cos(ang).T
    sin = np.sin(ang).T
    C = np.zeros((128, S), np.float32)
    Sg = np.zeros((128, S), np.float32)
    for p in range(128):
        i = p % 32
        C[p] = cos[i]
        Sg[p] = -sin[i] if (p % 64) < 32 else sin[i]
    return np.stack([C, Sg], 0)


def _consts():
    c = np.zeros((128, 128 + 1024), np.float32)
    for m in range(128):
        k = m + 32 if (m % 64) < 32 else m - 32
        c[k, m] = 1.0
    s = np.arange(128)[:, None]
    q = np.arange(128)[None, :]
    prev = (q < s).astype(np.float32)
    cur = (q >= s).astype(np.float32)
    mk = np.concatenate([prev, prev, cur, cur, prev, prev, cur, cur], axis=1)
    c[:, 128:] = mk
    return c


def kernel(**inp):
    stage = int(os.environ.get("MK_STAGE", "3"))
    if stage not in _CACHE:
        _CACHE[stage] = build_program(stage)
    nc = _CACHE[stage]
    f = lambda a: np.ascontiguousarray(np.asarray(a, dtype=np.float32))
    x = f(inp["x"])

    def col(v):
        return f(v).reshape(8, 128).T

    pp = np.zeros((128, NPP), np.float32)
    for i, nm in enumerate(["ffn1_pre_g", "ffn1_post_g", "mix_pre_g", "mix_post_g", "ffn2_pre_g", "ffn2_post_g"]):
        pp[:, PP_GAIN + i * 8:PP_GAIN + (i + 1) * 8] = col(inp[nm][0])
    cw = f(inp["conv_w"][0])
    for k in range(4):
        pp[:, PP_CONVW + k * 8:PP_CONVW + (k + 1) * 8] = col(cw[k])
    pp[:, PP_CONVB:PP_CONVB + 8] = col(inp["conv_b"][0])
    pp[:, PP_BA:PP_BA + 8] = col(inp["lru_b_a"][0])
    pp[:, PP_BX:PP_BX + 8] = col(inp["lru_b_x"][0])
    pp[:, PP_LAM:PP_LAM + 8] = col(inp["lru_lambda"][0])
    pp[:, PP_SINK:PP_SINK + 16] = np.broadcast_to(f(inp["attn_sinks"][0])[None, :], (128, 16))

    w_in = f(inp["w_in"][0])
    kcols = w_in[:, 3072:3328].reshape(D, 4, 64)
    vcols = w_in[:, 3328:3584].reshape(D, 4, 64)
    w_kv = np.concatenate([np.repeat(kcols[:, :, None, :], 2, axis=2).reshape(D, 512),
                           np.repeat(vcols[:, :, None, :], 2, axis=2).reshape(D, 512)], axis=1)
    shared = {
        "pp": pp,
        "w_gu1": f(inp["ffn1_w_gu"][0]), "w_dn1": f(inp["ffn1_w_down"][0]),
        "w_gu2": f(inp["ffn2_w_gu"][0]), "w_dn2": f(inp["ffn2_w_down"][0]),
        "w_in": w_in, "w_kv": f(w_kv),
        "w_pl": f(inp["w_proj_lru"][0]), "w_pa": f(inp["w_proj_attn"][0]), "w_o": f(inp["w_out"][0]),
        "lw_a": f(inp["lru_w_a"][0]), "lw_x": f(inp["lru_w_x"][0]),
        "rope": _rope_tables(), "cst": _consts(),
    }
    in_maps = []
    for b in range(NCORES):
        m = dict(shared)
        m["xT"] = np.ascontiguousarray(x[b].T)
        in_maps.append(m)
    res = run_bass_kernel_spmd(nc, in_maps, core_ids=list(range(NCORES)))
    out = np.stack([np.asarray(res.results[b]["outT"]).T for b in range(NCORES)], axis=0)
    return np.ascontiguousarray(out.astype(np.float32))
```

```python
import os
from contextlib import ExitStack

import numpy as np
import concourse.bass as bass
import concourse.mybir as mybir
from concourse.bass_utils import run_bass_kernel_spmd

F32 = mybir.dt.float32
BF16 = mybir.dt.bfloat16
AF = mybir.ActivationFunctionType
ALU = mybir.AluOpType

D = 1024
S = 2048
B = 8
DFF = 2816
NFF = DFF // 128
EPS = 1e-6
NCORES = 8

ENGS = ("pe", "act", "dve", "pool", "sp")


class V:
    __slots__ = ("ap", "regs")

    def __init__(self, ap, regs):
        self.ap = ap
        self.regs = regs


class Buf:
    def __init__(self, ap, space, byte_off, esize, free_shape):
        self.ap = ap
        self.space = space
        self.off = byte_off
        self.esize = esize
        self.fs = tuple(free_shape)
        st = []
        acc = 1
        for d in reversed(self.fs):
            st.append(acc)
            acc *= d
        self.strides = tuple(reversed(st))
        self.nbytes = acc * esize

    def __getitem__(self, key):
        if not isinstance(key, tuple):
            key = (key,)
        ap = self.ap[key]
        fk = key[1:]
        rng = []
        for i, d in enumerate(self.fs):
            if i < len(fk):
                k = fk[i]
                if isinstance(k, slice):
                    a = 0 if k.start is None else k.start
                    b = d if k.stop is None else k.stop
                else:
                    a, b = k, k + 1
            else:
                a, b = 0, d
            assert 0 <= a < b <= d, (key, self.fs)
            rng.append((a, b))
        j = -1
        for i, (a, b) in enumerate(rng):
            if (a, b) != (0, self.fs[i]):
                j = i
        if j < 0:
            return self.all()
        combos = [0]
        for i in range(j):
            a, b = rng[i]
            combos = [c + x * self.strides[i] for c in combos for x in range(a, b)]
            if len(combos) > 64:
                combos = None
                break
        if combos is None:
            lo = sum(a * s for (a, b), s in zip(rng, self.strides))
            hi = sum((b - 1) * s for (a, b), s in zip(rng, self.strides))
            return V(ap, [(self.space, self.off + lo * self.esize, self.off + (hi + 1) * self.esize)])
        a, b = rng[j]
        st = self.strides[j]
        regs = [(self.space, self.off + (c + a * st) * self.esize, self.off + (c + b * st) * self.esize)
                for c in combos]
        regs.sort()
        out = [regs[0]]
        for r in regs[1:]:
            if r[1] == out[-1][2]:
                out[-1] = (out[-1][0], out[-1][1], r[2])
            else:
                out.append(r)
        return V(ap, out)

    def all(self):
        return V(self.ap, [(self.space, self.off, self.off + self.nbytes)])


class Op:
    __slots__ = ("eng", "fn", "idx", "deps", "ddeps", "is_dma", "dma_key", "dma_cnt", "signal", "cnt", "waits",
                 "gid", "odeps", "dmadeps", "cost", "nbytes", "fin", "succ", "npred", "deps_all", "phase", "st", "why", "tag", "tset")


class Rec:
    __slots__ = ("lo", "hi", "op", "w")

    def __init__(self, lo, hi, op, w):
        self.lo = lo
        self.hi = hi
        self.op = op
        self.w = w


class Prog:
    def __init__(self):
        self.ops = {e: [] for e in ENGS}
        self.recs = {"S": [], "P": []}
        self.dma_cnt = {}
        self.out_dmas = []
        self.ngid = 0
        self.phase = "setup"
        self.last_dma = {}

    def add(self, eng, fn, reads=(), writes=(), dma_key=None, cost=0.3, nbytes=0):
        op = Op()
        op.eng = eng
        op.fn = fn
        op.gid = self.ngid
        op.phase = self.phase
        self.ngid += 1
        op.cost = cost
        op.nbytes = nbytes
        op.odeps = []
        op.dmadeps = []
        op.tset = None
        op.idx = len(self.ops[eng])
        op.is_dma = dma_key is not None
        op.dma_key = dma_key
        op.signal = False
        op.cnt = 0
        op.waits = None
        if op.is_dma:
            self.dma_cnt[dma_key] = self.dma_cnt.get(dma_key, 0) + 16
            op.dma_cnt = self.dma_cnt[dma_key]
        deps = {}
        ddeps = {}

        def add_dep(d, raw):
            if d is op:
                return
            if d.is_dma:
                k = ("d", d.dma_key)
                v = self.dma_cnt[d.dma_key] - (16 if (op.is_dma and op.dma_key == d.dma_key) else 0)
                if v > ddeps.get(k, 0):
                    ddeps[k] = v
                op.dmadeps.append(d)
                return
            if d.eng == eng and not op.is_dma:
                if eng == "pe":
                    op.odeps.append(d)
                    return
            deps[id(d)] = d

        for v in reads:
            for (sp, lo, hi) in v.regs:
                for r in self.recs[sp]:
                    if r.w and r.lo < hi and lo < r.hi:
                        add_dep(r.op, True)
                    elif (sp == "P" and not r.w and r.op.eng != eng
                          and r.lo // 2048 <= (hi - 1) // 2048 and lo // 2048 <= (r.hi - 1) // 2048):
                        add_dep(r.op, True)
        for v in writes:
            for (sp, lo, hi) in v.regs:
                for r in self.recs[sp]:
                    if r.lo < hi and lo < r.hi:
                        add_dep(r.op, False)
        op.deps = list(deps.values())
        op.ddeps = ddeps
        inorder = not op.is_dma
        for v in reads:
            for (sp, lo, hi) in v.regs:
                lst = self.recs[sp]
                if inorder:
                    keep = []
                    for r in lst:
                        if (not r.w) and (not r.op.is_dma) and r.op.eng == eng and lo <= r.lo and r.hi <= hi:
                            if r.op is not op:
                                op.odeps.append(r.op)
                        else:
                            keep.append(r)
                    lst[:] = keep
                lst.append(Rec(lo, hi, op, False))
        for v in writes:
            for (sp, lo, hi) in v.regs:
                lst = self.recs[sp]
                lst[:] = [r for r in lst if not (lo <= r.lo and r.hi <= hi)]
                lst.append(Rec(lo, hi, op, True))
        if op.is_dma:
            prev = self.last_dma.get(eng)
            if prev is not None:
                op.odeps.append(prev)
            self.last_dma[eng] = op
        self.ops[eng].append(op)
        return op

    def schedule(self, window=int(os.environ.get("MK_WIN", "600"))):
        import heapq
        allops = [op for e in ENGS for op in self.ops[e]]
        for op in allops:
            op.succ = []
            op.fin = None
        for op in allops:
            preds = {}
            for d in op.deps:
                preds[id(d)] = d
            for d in op.odeps:
                preds[id(d)] = d
            for d in op.dmadeps:
                preds[id(d)] = d
            op.npred = len(preds)
            for d in preds.values():
                d.succ.append(op)
            op.deps_all = None
        ready = {e: [] for e in ENGS}
        for op in allops:
            if op.npred == 0:
                heapq.heappush(ready[op.eng], (op.gid, id(op), op))
        efree = {e: 0.0 for e in ENGS}
        dma_free = [0.0]
        order = {e: [] for e in ENGS}
        LAT = 0.25
        nleft = len(allops)
        mingid = {e: 0 for e in ENGS}

        elast = {e: None for e in ENGS}
        cur_set = [None]
        def set_pen(op):
            if op.eng != "act" or op.tset is None or cur_set[0] is None:
                return 0.0
            a, b = cur_set[0], op.tset
            if a == b or (a in "ET" and b in "ET") or (a in "GT" and b in "GT"):
                return 0.0
            return 1.3

        def est_start(op, why=False):
            t = efree[op.eng]
            w = ("eng", elast[op.eng])
            for d in op.deps:
                x = d.fin + (LAT if d.eng != op.eng else 0.05)
                if x > t:
                    t = x
                    w = ("dep", d)
            for d in op.dmadeps:
                x = d.fin + LAT
                if x > t:
                    t = x
                    w = ("dma", d)
            for d in op.odeps:
                x = d.fin if not d.is_dma else d.cnt
                if x > t:
                    t = x
                    w = ("ord", d)
            if why:
                op.why = w
            return t

        while nleft:
            best = None
            for e in ENGS:
                h = ready[e]
                if not h:
                    continue
                g0 = h[0][0]
                cand = heapq.nsmallest(int(os.environ.get("MK_CAND", "12")), h)
                for (g, _, op) in cand:
                    if g - g0 > window:
                        break
                    s = est_start(op) + set_pen(op)
                    if best is None or (s, g) < (best[0], best[1]):
                        best = (s, g, op)
            s, g, op = best
            h = ready[op.eng]
            h.remove((op.gid, id(op), op))
            heapq.heapify(h)
            op.st = s
            if op.eng == "act" and op.tset is not None:
                cur_set[0] = op.tset
            est_start(op, True)
            elast[op.eng] = op
            if op.is_dma:
                issue = 1.15 if op.eng == "pool" else 0.15
                efree[op.eng] = s + issue
                op.cnt = s + issue
                t0 = max(s + issue, dma_free[0])
                dur = op.nbytes / 300e3
                dma_free[0] = t0 + dur
                op.fin = t0 + dur + 2.0
            else:
                op.fin = s + op.cost
                efree[op.eng] = op.fin
            order[op.eng].append(op)
            nleft -= 1
            for sc in op.succ:
                sc.npred -= 1
                if sc.npred == 0:
                    heapq.heappush(ready[sc.eng], (sc.gid, id(sc), sc))
        for e in ENGS:
            assert len(order[e]) == len(self.ops[e])
            self.ops[e] = order[e]
            for i, op in enumerate(order[e]):
                op.idx = i
                op.cnt = 0
        self.est_time = max(op.fin for op in allops)
        if os.environ.get("MK_CRIT") == "1":
            last = max(allops, key=lambda o: o.fin)
            agg = {}
            o = last
            n = 0
            while o is not None and n < 200000:
                kind, p = o.why
                k = (o.phase, o.eng, kind)
                agg[k] = agg.get(k, 0.0) + (o.fin - (p.fin if p is not None else 0.0))
                o = p
                n += 1
            for k in sorted(agg, key=lambda k: -agg[k])[:40]:
                print("CRIT %-10s %-5s %-4s %7.1f" % (k[0], k[1], k[2], agg[k]))
        if os.environ.get("MK_REPORT") == "1":
            ph = {}
            for op in allops:
                d = ph.setdefault(op.phase, {"t0": 1e18, "t1": 0.0, "pe": 0.0, "act": 0.0, "dve": 0.0, "pool": 0.0, "sp": 0.0})
                d["t0"] = min(d["t0"], op.st)
                if not op.is_dma:
                    d["t1"] = max(d["t1"], op.fin)
                    d[op.eng] += op.cost
            for k, d in ph.items():
                print("%-10s t0 %7.1f t1 %7.1f span %6.1f | pe %6.1f act %6.1f dve %6.1f pool %6.1f" % (
                    k, d["t0"], d["t1"], d["t1"] - d["t0"], d["pe"], d["act"], d["dve"], d["pool"]))

    def mm(self, out, lhsT, rhs, start=True, stop=True, **kw):
        n = rhs.ap.free_size()
        return self.add("pe", lambda e: e.matmul(out.ap, lhsT.ap, rhs.ap, start=start, stop=stop, **kw),
                        reads=[lhsT, rhs], writes=[out], cost=(0.276 * n / 512.0 if n >= 256 else 0.105))

    def act(self, out, in_, func, bias=None, scale=None, eng="act"):
        reads = [in_]
        kw = {}
        if bias is not None:
            if isinstance(bias, V):
                reads.append(bias)
                kw["bias"] = bias.ap
            else:
                kw["bias"] = float(bias)
        if scale is not None:
            if isinstance(scale, V):
                reads.append(scale)
                kw["scale"] = scale.ap
            else:
                kw["scale"] = float(scale)
        op = self.add(eng, lambda e: e.activation(out.ap, in_.ap, func, **kw), reads=reads, writes=[out],
                      cost=(in_.ap.free_size() + 260) / 1400.0)
        op.tset = {AF.Exp: "E", AF.Ln: "E", AF.Tanh: "T", AF.Sqrt: "Q", AF.Silu: "U", AF.Sigmoid: "G"}.get(func)
        return op

    def tt(self, out, in0, in1, op, eng="dve"):
        return self.add(eng, lambda e: e.tensor_tensor(out.ap, in0.ap, in1.ap, op), reads=[in0, in1], writes=[out],
                        cost=self.vcost(eng, in0.ap.free_size()))

    def ts(self, out, in0, s1, s2, op0, op1=None, eng="dve"):
        reads = [in0]
        a1 = s1
        a2 = s2
        if isinstance(s1, V):
            reads.append(s1)
            a1 = s1.ap
        if isinstance(s2, V):
            reads.append(s2)
            a2 = s2.ap
        c = self.vcost(eng, in0.ap.free_size())
        if op1 is None:
            return self.add(eng, lambda e: e.tensor_scalar(out.ap, in0.ap, a1, None, op0), reads=reads, writes=[out], cost=c)
        return self.add(eng, lambda e: e.tensor_scalar(out.ap, in0.ap, a1, a2, op0, op1), reads=reads, writes=[out], cost=c)

    def stt(self, out, in0, scalar, in1, op0, op1):
        reads = [in0, in1]
        a = scalar
        if isinstance(scalar, V):
            reads.append(scalar)
            a = scalar.ap
        return self.add("dve", lambda e: e.scalar_tensor_tensor(out.ap, in0.ap, a, in1.ap, op0, op1),
                        reads=reads, writes=[out], cost=in0.ap.free_size() / 640.0 + 0.1)

    def recip(self, out, in_):
        return self.add("dve", lambda e: e.reciprocal(out.ap, in_.ap), reads=[in_], writes=[out],
                        cost=in_.ap.free_size() / 156.0 + 0.05)

    def copy(self, out, in_, eng="dve"):
        if eng == "act":
            return self.add("act", lambda e: e.copy(out.ap, in_.ap), reads=[in_], writes=[out],
                            cost=(in_.ap.free_size() + 260) / 1400.0)
        return self.add(eng, lambda e: e.tensor_copy(out.ap, in_.ap), reads=[in_], writes=[out],
                        cost=self.vcost(eng, in_.ap.free_size()))

    def scan(self, out, d0, d1, init, op0, op1):
        return self.add("dve", lambda e: e.tensor_tensor_scan(out.ap, d0.ap, d1.ap, init.ap, op0, op1),
                        reads=[d0, d1, init], writes=[out], cost=2 * d0.ap.free_size() / 960.0 + 0.15)

    def memset(self, out, val, eng="pool"):
        return self.add(eng, lambda e: e.memset(out.ap, val), writes=[out], cost=self.vcost(eng, out.ap.free_size()))

    @staticmethod
    def vcost(eng, n):
        if eng == "pool":
            return n / 450.0 + 0.25
        return n / 1000.0 + 0.09

    def dma(self, eng, out, in_, key, **kw):
        reads = [in_] if isinstance(in_, V) else []
        writes = [out] if isinstance(out, V) else []
        oa = out.ap if isinstance(out, V) else out
        ia = in_.ap if isinstance(in_, V) else in_
        sb = out if isinstance(out, V) else in_
        nb_ = sb.ap.partition_size() * sb.ap.free_size() * 4
        op = self.add(eng, lambda e: e.dma_start(out=oa, in_=ia, **kw), reads=reads, writes=writes, dma_key=key,
                      nbytes=nb_)
        if not isinstance(out, V):
            self.out_dmas.append(op)
        return op

    def emit(self, nc, final_eng="sp"):
        if os.environ.get("MK_SCHED", "1") == "1":
            self.schedule()
            print("scheduler estimate us", round(self.est_time, 1))
        fin = Op()
        fin.eng = final_eng
        fin.fn = None
        fin.idx = len(self.ops[final_eng])
        fin.is_dma = False
        fin.dma_key = None
        fin.signal = False
        fin.cnt = 0
        fin.deps = []
        fin.ddeps = {("d", o.dma_key): self.dma_cnt[o.dma_key] for o in self.out_dmas}
        self.ops[final_eng].append(fin)

        for eng in ENGS:
            seen = {}
            for op in self.ops[eng]:
                w = {}
                for k, val in op.ddeps.items():
                    if seen.get(k, 0) >= val:
                        continue
                    w[k] = (val, None)
                for d in op.deps:
                    k = ("e", d.eng)
                    if seen.get(k, -1) >= d.idx:
                        continue
                    if k not in w or w[k][0] < d.idx:
                        w[k] = (d.idx, d)
                for k, (val, d) in w.items():
                    seen[k] = val
                    if d is not None:
                        d.signal = True
                op.waits = list(w.items())
        for eng in ENGS:
            c = 0
            for op in self.ops[eng]:
                if op.signal:
                    c += 1
                    op.cnt = c

        with ExitStack() as st:
            esem = {e: st.enter_context(nc.semaphore("s_" + e)) for e in ENGS}
            dsem = {k: st.enter_context(nc.semaphore("d_" + str(k))) for k in self.dma_cnt}
            block = st.enter_context(nc.Block())

            def run(eng, e):
                for op in self.ops[eng]:
                    for k, (val, d) in op.waits:
                        if k[0] == "d":
                            e.wait_ge(dsem[k[1]], val)
                        else:
                            e.wait_ge(esem[k[1]], d.cnt)
                    if op.fn is None:
                        continue
                    ins = op.fn(e)
                    if op.is_dma:
                        ins.then_inc(dsem[op.dma_key], 16)
                    elif op.signal:
                        ins.then_inc(esem[eng], 1)

            @block.tensor
            def _(e):
                run("pe", e)

            @block.scalar
            def _(e):
                run("act", e)

            @block.vector
            def _(e):
                run("dve", e)

            @block.gpsimd
            def _(e):
                run("pool", e)

            @block.sync
            def _(e):
                run("sp", e)


class Arena:
    def __init__(self, arena_ap, nbytes):
        self.ap = arena_ap
        self.n = nbytes
        self.top = 0
        self.peak = 0

    def alloc(self, free_shape, dtype):
        es = 2 if dtype == BF16 else 4
        n = es
        for d in free_shape:
            n *= d
        off = (self.top + 63) // 64 * 64
        assert off + n <= self.n, ("arena overflow", off, n, self.n)
        self.top = off + n
        self.peak = max(self.peak, self.top)
        ap = self.ap[:, off // 4:(off + n + 3) // 4]
        if dtype == BF16:
            ap = ap.bitcast(BF16)
        if len(free_shape) == 2:
            ap = ap.rearrange("p (a b) -> p a b", a=free_shape[0])
        elif len(free_shape) == 3:
            ap = ap.rearrange("p (a b c) -> p a b c", a=free_shape[0], b=free_shape[1])
        elif len(free_shape) == 4:
            ap = ap.rearrange("p (a b c d) -> p a b c d", a=free_shape[0], b=free_shape[1], c=free_shape[2])
        return Buf(ap, "S", off, es, free_shape)

    def view_at(self, off, free_shape, dtype):
        es = 2 if dtype == BF16 else 4
        n = es
        for d in free_shape:
            n *= d
        ap = self.ap[:, off // 4:(off + n + 3) // 4]
        if dtype == BF16:
            ap = ap.bitcast(BF16)
        if len(free_shape) == 2:
            ap = ap.rearrange("p (a b) -> p a b", a=free_shape[0])
        return Buf(ap, "S", off, es, free_shape)

    def mark(self):
        return self.top

    def release(self, m):
        self.top = m


ARENA_BYTES = int(os.environ.get("MK_ARENA", "212480"))
SLOT_BYTES = 8192
NSLOTS = 4

PP_GAIN = 0
PP_CONVW = 48
PP_CONVB = 80
PP_BA = 88
PP_BX = 96
PP_LAM = 104
PP_SINK = 112
NPP = 128


def build_program(stage):
    SUB = int(os.environ.get("MK_SUB", "9"))
    M2L = int(os.environ.get("MK_M2", "9"))
    M3L = int(os.environ.get("MK_M3", "9"))
    nc = bass.Bass("TRN2", target_bir_lowering=False)
    dram = {}

    def din(name, shape, dt=F32):
        dram[name] = nc.dram_tensor(name, list(shape), dt, kind="ExternalInput").ap()
        return dram[name]

    xT = din("xT", (D, S))
    pp_d = din("pp", (128, NPP))
    w_gu1 = din("w_gu1", (D, 2 * DFF))
    w_dn1 = din("w_dn1", (DFF, D))
    w_gu2 = din("w_gu2", (D, 2 * DFF))
    w_dn2 = din("w_dn2", (DFF, D))
    w_in = din("w_in", (D, 5632))
    w_kv = din("w_kv", (D, 1024))
    w_pl = din("w_pl", (D, D))
    w_pa = din("w_pa", (D, D))
    w_o = din("w_o", (D, D))
    lw_a = din("lw_a", (16, 64, 64))
    lw_x = din("lw_x", (16, 64, 64))
    rope_d = din("rope", (2, 128, S))
    cst_d = din("cst", (128, 128 + 1024))
    outT = nc.dram_tensor("outT", [D, S], F32, kind="ExternalOutput").ap()

    P = Prog()
    with ExitStack() as st:
        arena_t = st.enter_context(nc.sbuf_tensor("arena", [128, ARENA_BYTES // 4], F32))
        psum_t = st.enter_context(nc.psum_tensor("psum", [128, 4096], F32))
        A = Arena(arena_t, ARENA_BYTES)

        def bank(i, n=1):
            return Buf(psum_t[:, i * 512:(i + n) * 512], "P", i * 2048, 4, (n * 512,))

        banks = [bank(i) for i in range(8)]

        hT = A.alloc((8, S), F32)
        pp = A.alloc((NPP,), F32)
        hp = A.alloc((48,), F32)
        cf = A.alloc((8,), F32)
        lc = A.alloc((64,), F32)
        ones = A.alloc((128,), BF16)
        BDa = A.alloc((8, 128), BF16)
        BDx = A.alloc((8, 128), BF16)
        slots = [A.alloc((SLOT_BYTES // 2,), BF16) for _ in range(NSLOTS)]
        slot_i = [0]

        def next_slot(shape):
            i = slot_i[0] % NSLOTS
            slot_i[0] += 1
            sb = slots[i]
            n = 1
            for d in shape:
                n *= d
            assert n * 2 <= SLOT_BYTES
            ap = sb.ap[:, 0:n]
            if len(shape) == 2:
                ap = ap.rearrange("p (a b) -> p a b", a=shape[0])
            return Buf(ap, "S", sb.off, 2, shape), "w%d" % i

        def wload(dst, src, key):
            P.dma("pool", dst, src, key)

        def kp(ap):
            return ap.rearrange("(k p) n -> p k n", p=128)

        P.dma("sp", pp.all(), pp_d, "pp")
        for c in range(8):
            P.dma("sp", hT[:, c, :], xT[c * 128:(c + 1) * 128, :], "x%d" % c)
        P.memset(ones.all(), 1.0)
        P.memset(cf[:, 0:1], EPS)
        P.memset(cf[:, 1:2], 1.0)
        P.ts(hp.all(), pp[:, 0:48], 0.5, None, ALU.mult)
        epsv = cf[:, 0:1]
        onev = cf[:, 1:2]

        def gain(i, c):
            return pp[:, PP_GAIN + i * 8 + c:PP_GAIN + i * 8 + c + 1]

        def hgain(i, c):
            return hp[:, i * 8 + c:i * 8 + c + 1]

        if stage >= 2:
            P.memset(BDa.all(), 0.0)
            P.memset(BDx.all(), 0.0)
            for (bd, lw, key) in ((BDa, lw_a, "bda"), (BDx, lw_x, "bdx")):
                src = lw.rearrange("(c two) i o -> two i c o", two=2)
                for two in range(2):
                    P.dma("pool", bd[two * 64:(two + 1) * 64, :, two * 64:(two + 1) * 64], src[two], key)
            X = lc[:, 0:8]
            T_ = lc[:, 8:16]
            CL = lc[:, 16:24]
            HCL = lc[:, 24:32]
            P.act(X, pp[:, PP_LAM:PP_LAM + 8], AF.Exp, scale=-1.0)
            P.ts(T_, X, 1.0 / 3.0, -0.5, ALU.mult, ALU.add)
            P.tt(T_, T_, X, ALU.mult)
            P.ts(T_, T_, 1.0, None, ALU.add)
            P.tt(T_, T_, X, ALU.mult)
            P.ts(CL, T_, -8.0, None, ALU.mult)
            P.ts(HCL, T_, -4.0, None, ALU.mult)
            P.ts(lc[:, 32:48], pp[:, PP_BA:PP_BA + 16], 0.5, None, ALU.mult)
            P.act(lc[:, 48:64], pp[:, PP_SINK:PP_SINK + 16], AF.Exp)

        def prenorm(gi, uT, t0, ntok, sq_bufs, rstd, ssb):
            k = 0
            for sub in range(ntok // 512):
                ts_ = t0 + sub * 512
                ss = banks[ssb[sub % len(ssb)]]
                rs = rstd[:, sub * 512:(sub + 1) * 512]
                for c in range(8):
                    sq = sq_bufs[k % len(sq_bufs)]
                    k += 1
                    P.act(sq.all(), hT[:, c, ts_:ts_ + 512], AF.Square)
                    P.mm(ss.all(), ones.all(), sq.all(), start=(c == 0), stop=(c == 7))
                P.act(rs, ss.all(), AF.Ln, bias=epsv, scale=1.0 / D)
                P.act(rs, rs, AF.Exp, scale=-0.5)
                for c in range(8):
                    P.stt(uT[:, c, sub * 512:(sub + 1) * 512], hT[:, c, ts_:ts_ + 512], gain(gi, c),
                          rs, ALU.mult, ALU.mult)

        def ffn(gi_pre, gi_post, w_gu, w_dn):
            m = A.mark()
            TB = 1024
            uT = A.alloc((8, TB), BF16)
            actT = A.alloc((NFF, TB), BF16)
            fsb = A.alloc((8, TB), F32)
            sqb = [A.alloc((512,), BF16) for _ in range(2)]
            rstd = A.alloc((TB,), F32)
            sgb = [A.alloc((512,), F32) for _ in range(2)]
            tmpb = [A.alloc((512,), F32) for _ in range(2)]
            for tb in range(S // TB):
                T0 = tb * TB
                P.phase = "ffn%d.%d" % (gi_pre // 4 + 1, tb)
                prenorm(gi_pre, uT, T0, TB, sqb, rstd, (6, 7))
                k = 0
                for grp in range(NFF // 2):
                    W, key = next_slot((8, 512))
                    wload(W[:, :, 0:256], kp(w_gu[:, grp * 256:(grp + 1) * 256]), key)
                    wload(W[:, :, 256:512], kp(w_gu[:, DFF + grp * 256:DFF + (grp + 1) * 256]), key)
                    for f2 in range(2):
                        ffc = grp * 2 + f2
                        for sub in range(2):
                            pg = banks[(k % 2) * 2]
                            pu = banks[(k % 2) * 2 + 1]
                            sg = sgb[k % 2]
                            k += 1
                            for dc in range(8):
                                P.mm(pg.all(), W[:, dc, f2 * 128:(f2 + 1) * 128], uT[:, dc, sub * 512:(sub + 1) * 512],
                                     start=(dc == 0), stop=(dc == 7))
                            for dc in range(8):
                                P.mm(pu.all(), W[:, dc, 256 + f2 * 128:256 + (f2 + 1) * 128],
                                     uT[:, dc, sub * 512:(sub + 1) * 512], start=(dc == 0), stop=(dc == 7))
                            P.act(sg.all(), pg.all(), AF.Silu)
                            P.tt(actT[:, ffc, sub * 512:(sub + 1) * 512], sg.all(), pu.all(), ALU.mult)
                k = 0
                for dc in range(8):
                    W, key = next_slot((NFF, 128))
                    wload(W.all(), kp(w_dn[:, dc * 128:(dc + 1) * 128]), key)
                    for sub in range(2):
                        pf = banks[4 + (k % 2)]
                        sq = sqb[k % 2]
                        k += 1
                        for ffc in range(NFF):
                            P.mm(pf.all(), W[:, ffc, :], actT[:, ffc, sub * 512:(sub + 1) * 512],
                                 start=(ffc == 0), stop=(ffc == NFF - 1))
                        P.copy(fsb[:, dc, sub * 512:(sub + 1) * 512], pf.all(), eng="act")
                        P.act(sq.all(), pf.all(), AF.Square)
                        P.mm(banks[6 + sub].all(), ones.all(), sq.all(), start=(dc == 0), stop=(dc == 7))
                for sub in range(2):
                    rs = rstd[:, sub * 512:(sub + 1) * 512]
                    P.act(rs, banks[6 + sub].all(), AF.Ln, bias=epsv, scale=1.0 / D)
                    P.act(rs, rs, AF.Exp, scale=-0.5)
                k = 0
                for sub in range(2):
                    ts_ = T0 + sub * 512
                    for c in range(8):
                        tmp = tmpb[k % 2]
                        k += 1
                        P.stt(tmp.all(), fsb[:, c, sub * 512:(sub + 1) * 512], hgain(gi_post, c),
                              rstd[:, sub * 512:(sub + 1) * 512], ALU.mult, ALU.mult)
                        P.tt(hT[:, c, ts_:ts_ + 512], hT[:, c, ts_:ts_ + 512], tmp.all(), ALU.add, eng="pool")
            A.release(m)

        def mixer():
            m = A.mark()
            TB = 512
            DB = int(os.environ.get("MK_DB", "1"))
            uTs = [A.alloc((8, TB), BF16) for _ in range(DB)]
            y_lrus = [A.alloc((8, TB), BF16) for _ in range(int(os.environ.get("MK_DBY", "1")))]
            qy = A.alloc((16, TB), BF16)
            msb = A.view_at(qy.off, (8, TB), F32)
            kT = A.alloc((4, 640), BF16)
            vS = A.alloc((5, 512), BF16)
            merged = A.alloc((8, TB), BF16)
            ropeC = A.alloc((TB,), F32)
            ropeS = A.alloc((TB,), F32)
            NT = int(os.environ.get("MK_NT", "19"))
            tmp = [A.alloc((516,), F32) for _ in range(NT)]
            Eb = [A.alloc((1024,), BF16) for _ in range(int(os.environ.get("MK_EB", "2")))]
            qbb = [A.alloc((512,), BF16) for _ in range(2)]
            sqb = [A.alloc((512,), BF16) for _ in range(2)]
            rstd = A.alloc((TB,), F32)
            Psw = A.alloc((128,), BF16)
            mask = A.alloc((1024,), BF16)
            xcar = A.alloc((8, 4), F32)
            hcar = A.alloc((8,), F32)
            ti = [0]
            bi = [0]
            qi = [0]

            def T():
                t = tmp[ti[0] % NT]
                ti[0] += 1
                return t

            def nb():
                b = banks[bi[0] % 6]
                bi[0] += 1
                return b

            def QB():
                b = qbb[qi[0] % 2]
                qi[0] += 1
                return b

            print("mixer arena top", A.top, "of", ARENA_BYTES)
            OFF = os.environ.get("MK_OFF", "0") == "1"
            PEX = "pool" if OFF else "dve"
            P.dma("pool", Psw.all(), cst_d[:, 0:128], "psw")
            P.dma("pool", mask.all(), cst_d[:, 128:1152], "msk")
            P.memset(xcar.all(), 0.0)
            P.memset(hcar.all(), 0.0)
            C1 = 0.7978845608028654
            C2 = C1 * 0.044715
            cvw = lambda k, c: pp[:, PP_CONVW + k * 8 + c:PP_CONVW + k * 8 + c + 1]
            cvb = lambda c: pp[:, PP_CONVB + c:PP_CONVB + c + 1]
            lcv = lambda base, c: lc[:, base + c:base + c + 1]
            F = slice(0, 512)
            pstride = ARENA_BYTES // 4

            for tb in range(S // TB):
                t0 = tb * TB
                uT = uTs[tb % DB]
                y_lru = y_lrus[tb % len(y_lrus)]
                P.phase = "mix%d.lru" % tb
                prenorm(2, uT, t0, TB, sqb, rstd, (6, 7))
                P.dma("sp", ropeC.all(), rope_d[0][:, t0:t0 + TB], "rc")
                P.dma("sp", ropeS.all(), rope_d[1][:, t0:t0 + TB], "rs")

                for cp in range(4 if SUB >= 1 else 0):
                    W, key = next_slot((8, 512))
                    wload(W[:, :, 0:256], kp(w_in[:, cp * 256:(cp + 1) * 256]), key)
                    wload(W[:, :, 256:512], kp(w_in[:, 1024 + cp * 256:1024 + (cp + 1) * 256]), key)
                    for c2 in range(2):
                        c = cp * 2 + c2
                        pg = nb()
                        px = nb()
                        for dc in range(8):
                            P.mm(px.all(), W[:, dc, 256 + c2 * 128:256 + (c2 + 1) * 128], uT[:, dc, :],
                                 start=(dc == 0), stop=(dc == 7))
                        for dc in range(8):
                            P.mm(pg.all(), W[:, dc, c2 * 128:(c2 + 1) * 128], uT[:, dc, :],
                                 start=(dc == 0), stop=(dc == 7))
                        xf = T()
                        P.copy(xf[:, 0:3], xcar[:, c, 0:3], eng="pool")
                        P.copy(xf[:, 3:515], px.all(), eng="act")
                        P.copy(xcar[:, c, 0:3], xf[:, 512:515], eng="pool")
                        if M2L < 2:
                            continue
                        xc = T()
                        P.ts(xc[:, F], xf[:, 0:512], cvw(0, c), cvb(c), ALU.mult, ALU.add, eng=PEX)
                        for k in range(1, 4):
                            P.stt(xc[:, F], xf[:, k:k + 512], cvw(k, c), xc[:, F], ALU.mult, ALU.add)
                        xcb = QB()
                        P.copy(xcb.all(), xc[:, F], eng="act")
                        pr = nb()
                        pi = nb()
                        P.mm(pr.all(), BDa[:, c, :], xcb.all())
                        P.mm(pi.all(), BDx[:, c, :], xcb.all())
                        if M2L < 3:
                            continue
                        thr = T()
                        thi = T()
                        P.act(thr[:, F], pr.all(), AF.Tanh, bias=lcv(32, c), scale=0.5)
                        P.act(thi[:, F], pi.all(), AF.Tanh, bias=lcv(40, c), scale=0.5)
                        mu = T()
                        P.act(mu[:, F], thr[:, F], AF.Exp, bias=lcv(16, c), scale=lcv(16, c))
                        a = thr
                        P.act(a[:, F], thr[:, F], AF.Exp, bias=lcv(24, c), scale=lcv(24, c))
                        sq = T()
                        P.act(sq[:, F], pg.all(), AF.Square)
                        P.ts(sq[:, F], sq[:, F], C2, C1, ALU.mult, ALU.add, eng=PEX)
                        P.tt(sq[:, F], sq[:, F], pg.all(), ALU.mult)
                        P.act(sq[:, F], sq[:, F], AF.Tanh)
                        P.act(mu[:, F], mu[:, F], AF.Sqrt, bias=onev, scale=-1.0)
                        if M2L < 4:
                            continue
                        t1 = thi
                        P.stt(t1[:, F], thi[:, F], 1.0, xc[:, F], ALU.add, ALU.mult)
                        P.stt(t1[:, F], t1[:, F], 0.5, mu[:, F], ALU.mult, ALU.mult)
                        if M2L < 5:
                            continue
                        hs = xc
                        P.scan(hs[:, F], a[:, F], t1[:, F], hcar[:, c:c + 1], ALU.mult, ALU.add)
                        if M2L < 6:
                            continue
                        P.copy(hcar[:, c:c + 1], hs[:, 511:512], eng="dve")
                        P.stt(sq[:, F], sq[:, F], 1.0, pg.all(), ALU.add, ALU.mult)
                        P.stt(y_lru[:, c, :], sq[:, F], 0.5, hs[:, F], ALU.mult, ALU.mult)

                if SUB < 2:
                    continue
                P.phase = "mix%d.qkv" % tb
                RCL = int(os.environ.get("MK_RC", "9"))

                def rope_chunk(pq, outv):
                    if RCL < 1:
                        return
                    qb = QB()
                    P.copy(qb.all(), pq.all(), eng="act")
                    ps = nb()
                    P.mm(ps.all(), (ones if os.environ.get("MK_X") == "1" else Psw).all(), qb.all())
                    if RCL < 2:
                        return
                    r1 = T()
                    r2 = T()
                    P.tt(r1[:, F], ropeC.all(), pq.all(), ALU.mult)
                    P.tt(r2[:, F], ropeS.all(), ps.all(), ALU.mult)
                    if RCL >= 3:
                        P.tt(outv, r1[:, F], r2[:, F], ALU.add, eng=PEX)

                for qp in range(2):
                    W, key = next_slot((8, 512))
                    wload(W.all(), kp(w_in[:, 2048 + qp * 512:2048 + (qp + 1) * 512]), key)
                    for c4 in range(4):
                        pq = nb()
                        for dc in range(8):
                            P.mm(pq.all(), W[:, dc, c4 * 128:(c4 + 1) * 128], uT[:, dc, :], start=(dc == 0), stop=(dc == 7))
                        rope_chunk(pq, qy[:, qp * 4 + c4, :])
                W, key = next_slot((8, 512))
                wload(W.all(), kp(w_kv[:, 0:512]), key)
                for j in range(4 if M3L >= 2 else 0):
                    pk = nb()
                    for dc in range(8):
                        P.mm(pk.all(), W[:, dc, j * 128:(j + 1) * 128], uT[:, dc, :], start=(dc == 0), stop=(dc == 7))
                    rope_chunk(pk, kT[:, j, 128:640])
                W, key = next_slot((8, 512))
                wload(W.all(), kp(w_kv[:, 512:1024]), key)
                for i in range(4 if M3L >= 3 else 0):
                    pv = nb()
                    for dc in range(8):
                        P.mm(pv.all(), uT[:, dc, i * 128:(i + 1) * 128], W[:, dc, :], start=(dc == 0), stop=(dc == 7))
                    P.copy(vS[:, 1 + i, :], pv.all(), eng="act")

                P.phase = "mix%d.att" % tb
                k_ = 0
                for n in range(4 if SUB >= 3 else 0):
                    nglob = tb * 4 + n
                    kbs = (0, 1) if nglob > 0 else (1,)
                    for j in range(4):
                        pb = (k_ % 2) * 2
                        S2 = Buf(psum_t[:, pb * 512:(pb + 2) * 512].rearrange("p (h k g q) -> p h k g q", h=2, k=2, g=2),
                                 "P", pb * 2048, 4, (2, 2, 2, 128))
                        po = banks[4 + (k_ % 2)]
                        den = banks[6 + (k_ % 2)]
                        E = Eb[k_ % len(Eb)]
                        k_ += 1
                        for kb in kbs:
                            kc = slice((n + kb) * 128, (n + kb + 1) * 128)
                            for hh2 in range(2):
                                for half in range(2):
                                    rows = slice(half * 64, half * 64 + 64)
                                    P.mm(S2[:, half, kb, hh2, :], kT[rows, j, kc],
                                         qy[rows, 2 * j + hh2, n * 128:(n + 1) * 128])
                        P.act(E.all(), S2.all(), AF.Exp, scale=0.125)
                        P.tt(E.all(), E.all(), mask.all(), ALU.mult, eng=("pool" if os.environ.get("MK_MASKPOOL", "1") == "1" else "dve"))
                        E4 = Buf(E.ap.rearrange("p (h k g q) -> p h k g q", h=2, k=2, g=2), "S", E.off, 2, (2, 2, 2, 128))
                        for idx, kb in enumerate(kbs):
                            P.mm(po.all(), vS[:, n + kb, j * 128:(j + 1) * 128], E4[:, :, kb, :, :],
                                 start=(idx == 0), stop=(idx == len(kbs) - 1))
                        for idx, kb in enumerate(kbs):
                            P.mm(den.all(), ones.all(), E4[:, :, kb, :, :],
                                 start=(idx == 0), stop=(idx == len(kbs) - 1))
                        rec = T()
                        sink_ap = bass.AP(arena_t, lc.off // 4 + 48 + 4 * j, [[pstride, 128], [1, 2], [2, 2], [0, 128]])
                        sinkv = V(sink_ap, lc[:, 48:64].regs)
                        rec4 = Buf(rec.ap[:, 0:512].rearrange("p (h g q) -> p h g q", h=2, g=2), "S", rec.off, 4, (2, 2, 128))
                        den4 = Buf(den.ap.rearrange("p (h g q) -> p h g q", h=2, g=2), "P", den.off, 4, (2, 2, 128))
                        po4 = Buf(po.ap.rearrange("p (h g q) -> p h g q", h=2, g=2), "P", po.off, 4, (2, 2, 128))
                        P.tt(rec4.all(), den4.all(), sinkv, ALU.add)
                        P.act(rec[:, F], rec[:, F], AF.Ln)
                        P.act(rec[:, F], rec[:, F], AF.Exp, scale=-1.0)
                        for half in range(2):
                            rows = slice(half * 64, half * 64 + 64)
                            P.tt(qy[rows, 8 + 2 * j:8 + 2 * j + 2, n * 128:(n + 1) * 128], po4[rows, half, :, :],
                                 rec4[rows, half, :, :], ALU.mult)
                if tb < S // TB - 1:
                    P.copy(kT[:, :, 0:128], kT[:, :, 512:640], eng="pool")
                    P.copy(vS[:, 0, :], vS[:, 4, :], eng="pool")

                if SUB < 4:
                    continue
                P.phase = "mix%d.mrg" % tb
                for g in range(2):
                    sg = [T() for _ in range(4)]
                    sa = [T() for _ in range(4)]
                    W, key = next_slot((8, 512))
                    wload(W.all(), kp(w_in[:, 3584 + g * 512:3584 + (g + 1) * 512]), key)
                    for d4 in range(4):
                        p_ = nb()
                        for dc in range(8):
                            P.mm(p_.all(), W[:, dc, d4 * 128:(d4 + 1) * 128], uT[:, dc, :], start=(dc == 0), stop=(dc == 7))
                        P.act(sg[d4][:, F], p_.all(), AF.Sigmoid)
                    W, key = next_slot((8, 512))
                    wload(W.all(), kp(w_pl[:, g * 512:(g + 1) * 512]), key)
                    for d4 in range(4):
                        p_ = nb()
                        for dc in range(8):
                            P.mm(p_.all(), W[:, dc, d4 * 128:(d4 + 1) * 128], y_lru[:, dc, :], start=(dc == 0), stop=(dc == 7))
                        P.tt(sg[d4][:, F], sg[d4][:, F], p_.all(), ALU.mult)
                    W, key = next_slot((8, 512))
                    wload(W.all(), kp(w_in[:, 4608 + g * 512:4608 + (g + 1) * 512]), key)
                    for d4 in range(4):
                        p_ = nb()
                        for dc in range(8):
                            P.mm(p_.all(), W[:, dc, d4 * 128:(d4 + 1) * 128], uT[:, dc, :], start=(dc == 0), stop=(dc == 7))
                        P.act(sa[d4][:, F], p_.all(), AF.Sigmoid)
                    W, key = next_slot((8, 512))
                    wload(W.all(), kp(w_pa[:, g * 512:(g + 1) * 512]), key)
                    for d4 in range(4):
                        p_ = nb()
                        for dc in range(8):
                            P.mm(p_.all(), W[:, dc, d4 * 128:(d4 + 1) * 128], qy[:, 8 + dc, :], start=(dc == 0), stop=(dc == 7))
                        P.tt(sa[d4][:, F], sa[d4][:, F], p_.all(), ALU.mult)
                        P.tt(merged[:, g * 4 + d4, :], sa[d4][:, F], sg[d4][:, F], ALU.add, eng=PEX)

                P.phase = "mix%d.out" % tb
                k_ = 0
                for g in range(2):
                    W, key = next_slot((8, 512))
                    wload(W.all(), kp(w_o[:, g * 512:(g + 1) * 512]), key)
                    for d4 in range(4):
                        dcp = g * 4 + d4
                        p_ = nb()
                        sq = sqb[k_ % 2]
                        k_ += 1
                        for dc in range(8):
                            P.mm(p_.all(), W[:, dc, d4 * 128:(d4 + 1) * 128], merged[:, dc, :], start=(dc == 0), stop=(dc == 7))
                        P.copy(msb[:, dcp, :], p_.all(), eng="act")
                        P.act(sq.all(), p_.all(), AF.Square)
                        P.mm(banks[6].all(), ones.all(), sq.all(), start=(dcp == 0), stop=(dcp == 7))
                P.act(rstd.all(), banks[6].all(), AF.Ln, bias=epsv, scale=1.0 / D)
                P.act(rstd.all(), rstd.all(), AF.Exp, scale=-0.5)
                for c in range(8):
                    tm = T()
                    P.stt(tm[:, F], msb[:, c, :], gain(3, c), rstd.all(), ALU.mult, ALU.mult)
                    P.tt(hT[:, c, t0:t0 + TB], hT[:, c, t0:t0 + TB], tm[:, F], ALU.add, eng="pool")
            A.release(m)

        if stage >= 1:
            ffn(0, 1, w_gu1, w_dn1)
        if stage >= 2:
            mixer()
        if stage >= 3:
            ffn(4, 5, w_gu2, w_dn2)

        for c in range(8):
            P.dma("sp", outT[c * 128:(c + 1) * 128, :], hT[:, c, :], "o%d" % c)

        P.emit(nc)
        print("arena peak bytes", A.peak, "ops", {e: len(P.ops[e]) for e in ENGS})
    return nc


_CACHE = {}


def _rope_tables():
    half = 32
    inv_freq = 10000.0 ** (-np.arange(half, dtype=np.float64) / half)
    ang = np.arange(S, dtype=np.float64)[:, None] * inv_freq[None, :]
    cos = np.cos(ang).T
    sin = np.sin(ang).T
    C = np.zeros((128, S), np.float32)
    Sg = np.zeros((128, S), np.float32)
    for p in range(128):
        i = p % 32
        C[p] = cos[i]
        Sg[p] = -sin[i] if (p % 64) < 32 else sin[i]
    return np.stack([C, Sg], 0)


def _consts():
    c = np.zeros((128, 128 + 1024), np.float32)
    for m in range(128):
        k = m + 32 if (m % 64) < 32 else m - 32
        c[k, m] = 1.0
    s = np.arange(128)[:, None]
    q = np.arange(128)[None, :]
    prev = (q < s).astype(np.float32)
    cur = (q >= s).astype(np.float32)
    mk = np.concatenate([prev, prev, cur, cur, prev, prev, cur, cur], axis=1)
    c[:, 128:] = mk
    return c


def kernel(**inp):
    stage = int(os.environ.get("MK_STAGE", "3"))
    if stage not in _CACHE:
        _CACHE[stage] = build_program(stage)
    nc = _CACHE[stage]
    f = lambda a: np.ascontiguousarray(np.asarray(a, dtype=np.float32))
    x = f(inp["x"])

    def col(v):
        return f(v).reshape(8, 128).T

    pp = np.zeros((128, NPP), np.float32)
    for i, nm in enumerate(["ffn1_pre_g", "ffn1_post_g", "mix_pre_g", "mix_post_g", "ffn2_pre_g", "ffn2_post_g"]):
        pp[:, PP_GAIN + i * 8:PP_GAIN + (i + 1) * 8] = col(inp[nm][0])
    cw = f(inp["conv_w"][0])
    for k in range(4):
        pp[:, PP_CONVW + k * 8:PP_CONVW + (k + 1) * 8] = col(cw[k])
    pp[:, PP_CONVB:PP_CONVB + 8] = col(inp["conv_b"][0])
    pp[:, PP_BA:PP_BA + 8] = col(inp["lru_b_a"][0])
    pp[:, PP_BX:PP_BX + 8] = col(inp["lru_b_x"][0])
    pp[:, PP_LAM:PP_LAM + 8] = col(inp["lru_lambda"][0])
    pp[:, PP_SINK:PP_SINK + 16] = np.broadcast_to(f(inp["attn_sinks"][0])[None, :], (128, 16))

    w_in = f(inp["w_in"][0])
    kcols = w_in[:, 3072:3328].reshape(D, 4, 64)
    vcols = w_in[:, 3328:3584].reshape(D, 4, 64)
    w_kv = np.concatenate([np.repeat(kcols[:, :, None, :], 2, axis=2).reshape(D, 512),
                           np.repeat(vcols[:, :, None, :], 2, axis=2).reshape(D, 512)], axis=1)
    shared = {
        "pp": pp,
        "w_gu1": f(inp["ffn1_w_gu"][0]), "w_dn1": f(inp["ffn1_w_down"][0]),
        "w_gu2": f(inp["ffn2_w_gu"][0]), "w_dn2": f(inp["ffn2_w_down"][0]),
        "w_in": w_in, "w_kv": f(w_kv),
        "w_pl": f(inp["w_proj_lru"][0]), "w_pa": f(inp["w_proj_attn"][0]), "w_o": f(inp["w_out"][0]),
        "lw_a": f(inp["lru_w_a"][0]), "lw_x": f(inp["lru_w_x"][0]),
        "rope": _rope_tables(), "cst": _consts(),
    }
    in_maps = []
    for b in range(NCORES):
        m = dict(shared)
        m["xT"] = np.ascontiguousarray(x[b].T)
        in_maps.append(m)
    res = run_bass_kernel_spmd(nc, in_maps, core_ids=list(range(NCORES)))
    out = np.stack([np.asarray(res.results[b]["outT"]).T for b in range(NCORES)], axis=0)
    return np.ascontiguousarray(out.astype(np.float32))
```

```python
import os
from contextlib import ExitStack

import numpy as np
import concourse.bass as bass
import concourse.mybir as mybir
from concourse.bass_utils import run_bass_kernel_spmd

F32 = mybir.dt.float32
BF16 = mybir.dt.bfloat16
AF = mybir.ActivationFunctionType
ALU = mybir.AluOpType

D = 1024
S = 2048
B = 8
DFF = 2816
NFF = DFF // 128
EPS = 1e-6
NCORES = 8

ENGS = ("pe", "act", "dve", "pool", "sp")


class V:
    __slots__ = ("ap", "regs")

    def __init__(self, ap, regs):
        self.ap = ap
        self.regs = regs


class Buf:
    def __init__(self, ap, space, byte_off, esize, free_shape):
        self.ap = ap
        self.space = space
        self.off = byte_off
        self.esize = esize
        self.fs = tuple(free_shape)
        st = []
        acc = 1
        for d in reversed(self.fs):
            st.append(acc)
            acc *= d
        self.strides = tuple(reversed(st))
        self.nbytes = acc * esize

    def __getitem__(self, key):
        if not isinstance(key, tuple):
            key = (key,)
        ap = self.ap[key]
        fk = key[1:]
        rng = []
        for i, d in enumerate(self.fs):
            if i < len(fk):
                k = fk[i]
                if isinstance(k, slice):
                    a = 0 if k.start is None else k.start
                    b = d if k.stop is None else k.stop
                else:
                    a, b = k, k + 1
            else:
                a, b = 0, d
            assert 0 <= a < b <= d, (key, self.fs)
            rng.append((a, b))
        j = -1
        for i, (a, b) in enumerate(rng):
            if (a, b) != (0, self.fs[i]):
                j = i
        if j < 0:
            return self.all()
        combos = [0]
        for i in range(j):
            a, b = rng[i]
            combos = [c + x * self.strides[i] for c in combos for x in range(a, b)]
            if len(combos) > 64:
                combos = None
                break
        if combos is None:
            lo = sum(a * s for (a, b), s in zip(rng, self.strides))
            hi = sum((b - 1) * s for (a, b), s in zip(rng, self.strides))
            return V(ap, [(self.space, self.off + lo * self.esize, self.off + (hi + 1) * self.esize)])
        a, b = rng[j]
        st = self.strides[j]
        regs = [(self.space, self.off + (c + a * st) * self.esize, self.off + (c + b * st) * self.esize)
                for c in combos]
        regs.sort()
        out = [regs[0]]
        for r in regs[1:]:
            if r[1] == out[-1][2]:
                out[-1] = (out[-1][0], out[-1][1], r[2])
            else:
                out.append(r)
        return V(ap, out)

    def all(self):
        return V(self.ap, [(self.space, self.off, self.off + self.nbytes)])


class Op:
    __slots__ = ("eng", "fn", "idx", "deps", "ddeps", "is_dma", "dma_key", "dma_cnt", "signal", "cnt", "waits",
                 "gid", "odeps", "dmadeps", "cost", "nbytes", "fin", "succ", "npred", "deps_all", "phase", "st", "why", "tag", "tset")


class Rec:
    __slots__ = ("lo", "hi", "op", "w")

    def __init__(self, lo, hi, op, w):
        self.lo = lo
        self.hi = hi
        self.op = op
        self.w = w


class Prog:
    def __init__(self):
        self.ops = {e: [] for e in ENGS}
        self.recs = {"S": [], "P": []}
        self.dma_cnt = {}
        self.out_dmas = []
        self.ngid = 0
        self.phase = "setup"
        self.last_dma = {}

    def add(self, eng, fn, reads=(), writes=(), dma_key=None, cost=0.3, nbytes=0):
        op = Op()
        op.eng = eng
        op.fn = fn
        op.gid = self.ngid
        op.phase = self.phase
        self.ngid += 1
        op.cost = cost
        op.nbytes = nbytes
        op.odeps = []
        op.dmadeps = []
        op.tset = None
        op.idx = len(self.ops[eng])
        op.is_dma = dma_key is not None
        op.dma_key = dma_key
        op.signal = False
        op.cnt = 0
        op.waits = None
        if op.is_dma:
            self.dma_cnt[dma_key] = self.dma_cnt.get(dma_key, 0) + 16
            op.dma_cnt = self.dma_cnt[dma_key]
        deps = {}
        ddeps = {}

        def add_dep(d, raw):
            if d is op:
                return
            if d.is_dma:
                k = ("d", d.dma_key)
                v = self.dma_cnt[d.dma_key] - (16 if (op.is_dma and op.dma_key == d.dma_key) else 0)
                if v > ddeps.get(k, 0):
                    ddeps[k] = v
                op.dmadeps.append(d)
                return
            if d.eng == eng and not op.is_dma:
                if eng == "pe":
                    op.odeps.append(d)
                    return
            deps[id(d)] = d

        for v in reads:
            for (sp, lo, hi) in v.regs:
                for r in self.recs[sp]:
                    if r.w and r.lo < hi and lo < r.hi:
                        add_dep(r.op, True)
                    elif (sp == "P" and not r.w and r.op.eng != eng
                          and r.lo // 2048 <= (hi - 1) // 2048 and lo // 2048 <= (r.hi - 1) // 2048):
                        add_dep(r.op, True)
        for v in writes:
            for (sp, lo, hi) in v.regs:
                for r in self.recs[sp]:
                    if r.lo < hi and lo < r.hi:
                        add_dep(r.op, False)
        op.deps = list(deps.values())
        op.ddeps = ddeps
        inorder = not op.is_dma
        for v in reads:
            for (sp, lo, hi) in v.regs:
                lst = self.recs[sp]
                if inorder:
                    keep = []
                    for r in lst:
                        if (not r.w) and (not r.op.is_dma) and r.op.eng == eng and lo <= r.lo and r.hi <= hi:
                            if r.op is not op:
                                op.odeps.append(r.op)
                        else:
                            keep.append(r)
                    lst[:] = keep
                lst.append(Rec(lo, hi, op, False))
        for v in writes:
            for (sp, lo, hi) in v.regs:
                lst = self.recs[sp]
                lst[:] = [r for r in lst if not (lo <= r.lo and r.hi <= hi)]
                lst.append(Rec(lo, hi, op, True))
        if op.is_dma:
            prev = self.last_dma.get(eng)
            if prev is not None:
                op.odeps.append(prev)
            self.last_dma[eng] = op
        self.ops[eng].append(op)
        return op

    def schedule(self, window=int(os.environ.get("MK_WIN", "600"))):
        import heapq
        allops = [op for e in ENGS for op in self.ops[e]]
        for op in allops:
            op.succ = []
            op.fin = None
        for op in allops:
            preds = {}
            for d in op.deps:
                preds[id(d)] = d
            for d in op.odeps:
                preds[id(d)] = d
            for d in op.dmadeps:
                preds[id(d)] = d
            op.npred = len(preds)
            for d in preds.values():
                d.succ.append(op)
            op.deps_all = None
        ready = {e: [] for e in ENGS}
        for op in allops:
            if op.npred == 0:
                heapq.heappush(ready[op.eng], (op.gid, id(op), op))
        efree = {e: 0.0 for e in ENGS}
        dma_free = [0.0]
        order = {e: [] for e in ENGS}
        LAT = 0.25
        nleft = len(allops)
        mingid = {e: 0 for e in ENGS}

        elast = {e: None for e in ENGS}
        cur_set = [None]
        def set_pen(op):
            if op.eng != "act" or op.tset is None or cur_set[0] is None:
                return 0.0
            a, b = cur_set[0], op.tset
            if a == b or (a in "ET" and b in "ET") or (a in "GT" and b in "GT"):
                return 0.0
            return 1.3

        def est_start(op, why=False):
            t = efree[op.eng]
            w = ("eng", elast[op.eng])
            for d in op.deps:
                x = d.fin + (LAT if d.eng != op.eng else 0.05)
                if x > t:
                    t = x
                    w = ("dep", d)
            for d in op.dmadeps:
                x = d.fin + LAT
                if x > t:
                    t = x
                    w = ("dma", d)
            for d in op.odeps:
                x = d.fin if not d.is_dma else d.cnt
                if x > t:
                    t = x
                    w = ("ord", d)
            if why:
                op.why = w
            return t

        while nleft:
            best = None
            for e in ENGS:
                h = ready[e]
                if not h:
                    continue
                g0 = h[0][0]
                cand = heapq.nsmallest(int(os.environ.get("MK_CAND", "12")), h)
                for (g, _, op) in cand:
                    if g - g0 > window:
                        break
                    s = est_start(op) + set_pen(op)
                    if best is None or (s, g) < (best[0], best[1]):
                        best = (s, g, op)
            s, g, op = best
            h = ready[op.eng]
            h.remove((op.gid, id(op), op))
            heapq.heapify(h)
            op.st = s
            if op.eng == "act" and op.tset is not None:
                cur_set[0] = op.tset
            est_start(op, True)
            elast[op.eng] = op
            if op.is_dma:
                issue = 1.15 if op.eng == "pool" else 0.15
                efree[op.eng] = s + issue
                op.cnt = s + issue
                t0 = max(s + issue, dma_free[0])
                dur = op.nbytes / 300e3
                dma_free[0] = t0 + dur
                op.fin = t0 + dur + 2.0
            else:
                op.fin = s + op.cost
                efree[op.eng] = op.fin
            order[op.eng].append(op)
            nleft -= 1
            for sc in op.succ:
                sc.npred -= 1
                if sc.npred == 0:
                    heapq.heappush(ready[sc.eng], (sc.gid, id(sc), sc))
        for e in ENGS:
            assert len(order[e]) == len(self.ops[e])
            self.ops[e] = order[e]
            for i, op in enumerate(order[e]):
                op.idx = i
                op.cnt = 0
        self.est_time = max(op.fin for op in allops)
        if os.environ.get("MK_CRIT") == "1":
            last = max(allops, key=lambda o: o.fin)
            agg = {}
            o = last
            n = 0
            while o is not None and n < 200000:
                kind, p = o.why
                k = (o.phase, o.eng, kind)
                agg[k] = agg.get(k, 0.0) + (o.fin - (p.fin if p is not None else 0.0))
                o = p
                n += 1
            for k in sorted(agg, key=lambda k: -agg[k])[:40]:
                print("CRIT %-10s %-5s %-4s %7.1f" % (k[0], k[1], k[2], agg[k]))
        if os.environ.get("MK_REPORT") == "1":
            ph = {}
            for op in allops:
                d = ph.setdefault(op.phase, {"t0": 1e18, "t1": 0.0, "pe": 0.0, "act": 0.0, "dve": 0.0, "pool": 0.0, "sp": 0.0})
                d["t0"] = min(d["t0"], op.st)
                if not op.is_dma:
                    d["t1"] = max(d["t1"], op.fin)
                    d[op.eng] += op.cost
            for k, d in ph.items():
                print("%-10s t0 %7.1f t1 %7.1f span %6.1f | pe %6.1f act %6.1f dve %6.1f pool %6.1f" % (
                    k, d["t0"], d["t1"], d["t1"] - d["t0"], d["pe"], d["act"], d["dve"], d["pool"]))

    def mm(self, out, lhsT, rhs, start=True, stop=True, **kw):
        n = rhs.ap.free_size()
        return self.add("pe", lambda e: e.matmul(out.ap, lhsT.ap, rhs.ap, start=start, stop=stop, **kw),
                        reads=[lhsT, rhs], writes=[out], cost=(0.276 * n / 512.0 if n >= 256 else 0.105))

    def act(self, out, in_, func, bias=None, scale=None, eng="act"):
        reads = [in_]
        kw = {}
        if bias is not None:
            if isinstance(bias, V):
                reads.append(bias)
                kw["bias"] = bias.ap
            else:
                kw["bias"] = float(bias)
        if scale is not None:
            if isinstance(scale, V):
                reads.append(scale)
                kw["scale"] = scale.ap
            else:
                kw["scale"] = float(scale)
        op = self.add(eng, lambda e: e.activation(out.ap, in_.ap, func, **kw), reads=reads, writes=[out],
                      cost=(in_.ap.free_size() + 260) / 1400.0)
        op.tset = {AF.Exp: "E", AF.Ln: "E", AF.Tanh: "T", AF.Sqrt: "Q", AF.Silu: "U", AF.Sigmoid: "G"}.get(func)
        return op

    def tt(self, out, in0, in1, op, eng="dve"):
        return self.add(eng, lambda e: e.tensor_tensor(out.ap, in0.ap, in1.ap, op), reads=[in0, in1], writes=[out],
                        cost=self.vcost(eng, in0.ap.free_size()))

    def ts(self, out, in0, s1, s2, op0, op1=None, eng="dve"):
        reads = [in0]
        a1 = s1
        a2 = s2
        if isinstance(s1, V):
            reads.append(s1)
            a1 = s1.ap
        if isinstance(s2, V):
            reads.append(s2)
            a2 = s2.ap
        c = self.vcost(eng, in0.ap.free_size())
        if op1 is None:
            return self.add(eng, lambda e: e.tensor_scalar(out.ap, in0.ap, a1, None, op0), reads=reads, writes=[out], cost=c)
        return self.add(eng, lambda e: e.tensor_scalar(out.ap, in0.ap, a1, a2, op0, op1), reads=reads, writes=[out], cost=c)

    def stt(self, out, in0, scalar, in1, op0, op1):
        reads = [in0, in1]
        a = scalar
        if isinstance(scalar, V):
            reads.append(scalar)
            a = scalar.ap
        return self.add("dve", lambda e: e.scalar_tensor_tensor(out.ap, in0.ap, a, in1.ap, op0, op1),
                        reads=reads, writes=[out], cost=in0.ap.free_size() / 640.0 + 0.1)

    def recip(self, out, in_):
        return self.add("dve", lambda e: e.reciprocal(out.ap, in_.ap), reads=[in_], writes=[out],
                        cost=in_.ap.free_size() / 156.0 + 0.05)

    def copy(self, out, in_, eng="dve"):
        if eng == "act":
            return self.add("act", lambda e: e.copy(out.ap, in_.ap), reads=[in_], writes=[out],
                            cost=(in_.ap.free_size() + 260) / 1400.0)
        return self.add(eng, lambda e: e.tensor_copy(out.ap, in_.ap), reads=[in_], writes=[out],
                        cost=self.vcost(eng, in_.ap.free_size()))

    def scan(self, out, d0, d1, init, op0, op1):
        return self.add("dve", lambda e: e.tensor_tensor_scan(out.ap, d0.ap, d1.ap, init.ap, op0, op1),
                        reads=[d0, d1, init], writes=[out], cost=2 * d0.ap.free_size() / 960.0 + 0.15)

    def memset(self, out, val, eng="pool"):
        return self.add(eng, lambda e: e.memset(out.ap, val), writes=[out], cost=self.vcost(eng, out.ap.free_size()))

    @staticmethod
    def vcost(eng, n):
        if eng == "pool":
            return n / 450.0 + 0.25
        return n / 1000.0 + 0.09

    def dma(self, eng, out, in_, key, **kw):
        reads = [in_] if isinstance(in_, V) else []
        writes = [out] if isinstance(out, V) else []
        oa = out.ap if isinstance(out, V) else out
        ia = in_.ap if isinstance(in_, V) else in_
        sb = out if isinstance(out, V) else in_
        nb_ = sb.ap.partition_size() * sb.ap.free_size() * 4
        op = self.add(eng, lambda e: e.dma_start(out=oa, in_=ia, **kw), reads=reads, writes=writes, dma_key=key,
                      nbytes=nb_)
        if not isinstance(out, V):
            self.out_dmas.append(op)
        return op

    def emit(self, nc, final_eng="sp"):
        if os.environ.get("MK_SCHED", "1") == "1":
            self.schedule()
            print("scheduler estimate us", round(self.est_time, 1))
        fin = Op()
        fin.eng = final_eng
        fin.fn = None
        fin.idx = len(self.ops[final_eng])
        fin.is_dma = False
        fin.dma_key = None
        fin.signal = False
        fin.cnt = 0
        fin.deps = []
        fin.ddeps = {("d", o.dma_key): self.dma_cnt[o.dma_key] for o in self.out_dmas}
        self.ops[final_eng].append(fin)

        for eng in ENGS:
            seen = {}
            for op in self.ops[eng]:
                w = {}
                for k, val in op.ddeps.items():
                    if seen.get(k, 0) >= val:
                        continue
                    w[k] = (val, None)
                for d in op.deps:
                    k = ("e", d.eng)
                    if seen.get(k, -1) >= d.idx:
                        continue
                    if k not in w or w[k][0] < d.idx:
                        w[k] = (d.idx, d)
                for k, (val, d) in w.items():
                    seen[k] = val
                    if d is not None:
                        d.signal = True
                op.waits = list(w.items())
        for eng in ENGS:
            c = 0
            for op in self.ops[eng]:
                if op.signal:
                    c += 1
                    op.cnt = c

        with ExitStack() as st:
            esem = {e: st.enter_context(nc.semaphore("s_" + e)) for e in ENGS}
            dsem = {k: st.enter_context(nc.semaphore("d_" + str(k))) for k in self.dma_cnt}
            block = st.enter_context(nc.Block())

            def run(eng, e):
                for op in self.ops[eng]:
                    for k, (val, d) in op.waits:
                        if k[0] == "d":
                            e.wait_ge(dsem[k[1]], val)
                        else:
                            e.wait_ge(esem[k[1]], d.cnt)
                    if op.fn is None:
                        continue
                    ins = op.fn(e)
                    if op.is_dma:
                        ins.then_inc(dsem[op.dma_key], 16)
                    elif op.signal:
                        ins.then_inc(esem[eng], 1)

            @block.tensor
            def _(e):
                run("pe", e)

            @block.scalar
            def _(e):
                run("act", e)

            @block.vector
            def _(e):
                run("dve", e)

            @block.gpsimd
            def _(e):
                run("pool", e)

            @block.sync
            def _(e):
                run("sp", e)


class Arena:
    def __init__(self, arena_ap, nbytes):
        self.ap = arena_ap
        self.n = nbytes
        self.top = 0
        self.peak = 0

    def alloc(self, free_shape, dtype):
        es = 2 if dtype == BF16 else 4
        n = es
        for d in free_shape:
            n *= d
        off = (self.top + 63) // 64 * 64
        assert off + n <= self.n, ("arena overflow", off, n, self.n)
        self.top = off + n
        self.peak = max(self.peak, self.top)
        ap = self.ap[:, off // 4:(off + n + 3) // 4]
        if dtype == BF16:
            ap = ap.bitcast(BF16)
        if len(free_shape) == 2:
            ap = ap.rearrange("p (a b) -> p a b", a=free_shape[0])
        elif len(free_shape) == 3:
            ap = ap.rearrange("p (a b c) -> p a b c", a=free_shape[0], b=free_shape[1])
        elif len(free_shape) == 4:
            ap = ap.rearrange("p (a b c d) -> p a b c d", a=free_shape[0], b=free_shape[1], c=free_shape[2])
        return Buf(ap, "S", off, es, free_shape)

    def view_at(self, off, free_shape, dtype):
        es = 2 if dtype == BF16 else 4
        n = es
        for d in free_shape:
            n *= d
        ap = self.ap[:, off // 4:(off + n + 3) // 4]
        if dtype == BF16:
            ap = ap.bitcast(BF16)
        if len(free_shape) == 2:
            ap = ap.rearrange("p (a b) -> p a b", a=free_shape[0])
        return Buf(ap, "S", off, es, free_shape)

    def mark(self):
        return self.top

    def release(self, m):
        self.top = m


ARENA_BYTES = int(os.environ.get("MK_ARENA", "212480"))
SLOT_BYTES = 8192
NSLOTS = 4

PP_GAIN = 0
PP_CONVW = 48
PP_CONVB = 80
PP_BA = 88
PP_BX = 96
PP_LAM = 104
PP_SINK = 112
NPP = 128


def build_program(stage):
    SUB = int(os.environ.get("MK_SUB", "9"))
    M2L = int(os.environ.get("MK_M2", "9"))
    M3L = int(os.environ.get("MK_M3", "9"))
    nc = bass.Bass("TRN2", target_bir_lowering=False)
    dram = {}

    def din(name, shape, dt=F32):
        dram[name] = nc.dram_tensor(name, list(shape), dt, kind="ExternalInput").ap()
        return dram[name]

    xT = din("xT", (D, S))
    pp_d = din("pp", (128, NPP))
    w_gu1 = din("w_gu1", (D, 2 * DFF))
    w_dn1 = din("w_dn1", (DFF, D))
    w_gu2 = din("w_gu2", (D, 2 * DFF))
    w_dn2 = din("w_dn2", (DFF, D))
    w_in = din("w_in", (D, 5632))
    w_kv = din("w_kv", (D, 1024))
    w_pl = din("w_pl", (D, D))
    w_pa = din("w_pa", (D, D))
    w_o = din("w_o", (D, D))
    lw_a = din("lw_a", (16, 64, 64))
    lw_x = din("lw_x", (16, 64, 64))
    rope_d = din("rope", (2, 128, S))
    cst_d = din("cst", (128, 128 + 1024))
    outT = nc.dram_tensor("outT", [D, S], F32, kind="ExternalOutput").ap()

    P = Prog()
    with ExitStack() as st:
        arena_t = st.enter_context(nc.sbuf_tensor("arena", [128, ARENA_BYTES // 4], F32))
        psum_t = st.enter_context(nc.psum_tensor("psum", [128, 4096], F32))
        A = Arena(arena_t, ARENA_BYTES)

        def bank(i, n=1):
            return Buf(psum_t[:, i * 512:(i + n) * 512], "P", i * 2048, 4, (n * 512,))

        banks = [bank(i) for i in range(8)]

        hT = A.alloc((8, S), F32)
        pp = A.alloc((NPP,), F32)
        hp = A.alloc((48,), F32)
        cf = A.alloc((8,), F32)
        lc = A.alloc((64,), F32)
        ones = A.alloc((128,), BF16)
        BDa = A.alloc((8, 128), BF16)
        BDx = A.alloc((8, 128), BF16)
        slots = [A.alloc((SLOT_BYTES // 2,), BF16) for _ in range(NSLOTS)]
        slot_i = [0]

        def next_slot(shape):
            i = slot_i[0] % NSLOTS
            slot_i[0] += 1
            sb = slots[i]
            n = 1
            for d in shape:
                n *= d
            assert n * 2 <= SLOT_BYTES
            ap = sb.ap[:, 0:n]
            if len(shape) == 2:
                ap = ap.rearrange("p (a b) -> p a b", a=shape[0])
            return Buf(ap, "S", sb.off, 2, shape), "w%d" % i

        def wload(dst, src, key):
            P.dma("pool", dst, src, key)

        def kp(ap):
            return ap.rearrange("(k p) n -> p k n", p=128)

        P.dma("sp", pp.all(), pp_d, "pp")
        for c in range(8):
            P.dma("sp", hT[:, c, :], xT[c * 128:(c + 1) * 128, :], "x%d" % c)
        P.memset(ones.all(), 1.0)
        P.memset(cf[:, 0:1], EPS)
        P.memset(cf[:, 1:2], 1.0)
        P.ts(hp.all(), pp[:, 0:48], 0.5, None, ALU.mult)
        epsv = cf[:, 0:1]
        onev = cf[:, 1:2]

        def gain(i, c):
            return pp[:, PP_GAIN + i * 8 + c:PP_GAIN + i * 8 + c + 1]

        def hgain(i, c):
            return hp[:, i * 8 + c:i * 8 + c + 1]

        if stage >= 2:
            P.memset(BDa.all(), 0.0)
            P.memset(BDx.all(), 0.0)
            for (bd, lw, key) in ((BDa, lw_a, "bda"), (BDx, lw_x, "bdx")):
                src = lw.rearrange("(c two) i o -> two i c o", two=2)
                for two in range(2):
                    P.dma("pool", bd[two * 64:(two + 1) * 64, :, two * 64:(two + 1) * 64], src[two], key)
            X = lc[:, 0:8]
            T_ = lc[:, 8:16]
            CL = lc[:, 16:24]
            HCL = lc[:, 24:32]
            P.act(X, pp[:, PP_LAM:PP_LAM + 8], AF.Exp, scale=-1.0)
            P.ts(T_, X, 1.0 / 3.0, -0.5, ALU.mult, ALU.add)
            P.tt(T_, T_, X, ALU.mult)
            P.ts(T_, T_, 1.0, None, ALU.add)
            P.tt(T_, T_, X, ALU.mult)
            P.ts(CL, T_, -8.0, None, ALU.mult)
            P.ts(HCL, T_, -4.0, None, ALU.mult)
            P.ts(lc[:, 32:48], pp[:, PP_BA:PP_BA + 16], 0.5, None, ALU.mult)
            P.act(lc[:, 48:64], pp[:, PP_SINK:PP_SINK + 16], AF.Exp)

        def prenorm(gi, uT, t0, ntok, sq_bufs, rstd, ssb):
            k = 0
            for sub in range(ntok // 512):
                ts_ = t0 + sub * 512
                ss = banks[ssb[sub % len(ssb)]]
                rs = rstd[:, sub * 512:(sub + 1) * 512]
                for c in range(8):
                    sq = sq_bufs[k % len(sq_bufs)]
                    k += 1
                    P.act(sq.all(), hT[:, c, ts_:ts_ + 512], AF.Square)
                    P.mm(ss.all(), ones.all(), sq.all(), start=(c == 0), stop=(c == 7))
                P.act(rs, ss.all(), AF.Ln, bias=epsv, scale=1.0 / D)
                P.act(rs, rs, AF.Exp, scale=-0.5)
                for c in range(8):
                    P.stt(uT[:, c, sub * 512:(sub + 1) * 512], hT[:, c, ts_:ts_ + 512], gain(gi, c),
                          rs, ALU.mult, ALU.mult)

        def ffn(gi_pre, gi_post, w_gu, w_dn):
            m = A.mark()
            TB = 1024
            uT = A.alloc((8, TB), BF16)
            actT = A.alloc((NFF, TB), BF16)
            fsb = A.alloc((8, TB), F32)
            sqb = [A.alloc((512,), BF16) for _ in range(2)]
            rstd = A.alloc((TB,), F32)
            sgb = [A.alloc((512,), F32) for _ in range(2)]
            tmpb = [A.alloc((512,), F32) for _ in range(2)]
            for tb in range(S // TB):
                T0 = tb * TB
                P.phase = "ffn%d.%d" % (gi_pre // 4 + 1, tb)
                prenorm(gi_pre, uT, T0, TB, sqb, rstd, (6, 7))
                k = 0
                for grp in range(NFF // 2):
                    W, key = next_slot((8, 512))
                    wload(W[:, :, 0:256], kp(w_gu[:, grp * 256:(grp + 1) * 256]), key)
                    wload(W[:, :, 256:512], kp(w_gu[:, DFF + grp * 256:DFF + (grp + 1) * 256]), key)
                    for f2 in range(2):
                        ffc = grp * 2 + f2
                        for sub in range(2):
                            pg = banks[(k % 2) * 2]
                            pu = banks[(k % 2) * 2 + 1]
                            sg = sgb[k % 2]
                            k += 1
                            for dc in range(8):
                                P.mm(pg.all(), W[:, dc, f2 * 128:(f2 + 1) * 128], uT[:, dc, sub * 512:(sub + 1) * 512],
                                     start=(dc == 0), stop=(dc == 7))
                            for dc in range(8):
                                P.mm(pu.all(), W[:, dc, 256 + f2 * 128:256 + (f2 + 1) * 128],
                                     uT[:, dc, sub * 512:(sub + 1) * 512], start=(dc == 0), stop=(dc == 7))
                            P.act(sg.all(), pg.all(), AF.Silu)
                            P.tt(actT[:, ffc, sub * 512:(sub + 1) * 512], sg.all(), pu.all(), ALU.mult)
                k = 0
                for dc in range(8):
                    W, key = next_slot((NFF, 128))
                    wload(W.all(), kp(w_dn[:, dc * 128:(dc + 1) * 128]), key)
                    for sub in range(2):
                        pf = banks[4 + (k % 2)]
                        sq = sqb[k % 2]
                        k += 1
                        for ffc in range(NFF):
                            P.mm(pf.all(), W[:, ffc, :], actT[:, ffc, sub * 512:(sub + 1) * 512],
                                 start=(ffc == 0), stop=(ffc == NFF - 1))
                        P.copy(fsb[:, dc, sub * 512:(sub + 1) * 512], pf.all(), eng="act")
                        P.act(sq.all(), pf.all(), AF.Square)
                        P.mm(banks[6 + sub].all(), ones.all(), sq.all(), start=(dc == 0), stop=(dc == 7))
                for sub in range(2):
                    rs = rstd[:, sub * 512:(sub + 1) * 512]
                    P.act(rs, banks[6 + sub].all(), AF.Ln, bias=epsv, scale=1.0 / D)
                    P.act(rs, rs, AF.Exp, scale=-0.5)
                k = 0
                for sub in range(2):
                    ts_ = T0 + sub * 512
                    for c in range(8):
                        tmp = tmpb[k % 2]
                        k += 1
                        P.stt(tmp.all(), fsb[:, c, sub * 512:(sub + 1) * 512], hgain(gi_post, c),
                              rstd[:, sub * 512:(sub + 1) * 512], ALU.mult, ALU.mult)
                        P.tt(hT[:, c, ts_:ts_ + 512], hT[:, c, ts_:ts_ + 512], tmp.all(), ALU.add, eng="pool")
            A.release(m)

        def mixer():
            m = A.mark()
            TB = 512
            DB = int(os.environ.get("MK_DB", "1"))
            uTs = [A.alloc((8, TB), BF16) for _ in range(DB)]
            y_lrus = [A.alloc((8, TB), BF16) for _ in range(int(os.environ.get("MK_DBY", "1")))]
            qy = A.alloc((16, TB), BF16)
            msb = A.view_at(qy.off, (8, TB), F32)
            kT = A.alloc((4, 640), BF16)
            vS = A.alloc((5, 512), BF16)
            merged = A.alloc((8, TB), BF16)
            ropeC = A.alloc((TB,), F32)
            ropeS = A.alloc((TB,), F32)
            NT = int(os.environ.get("MK_NT", "19"))
            tmp = [A.alloc((516,), F32) for _ in range(NT)]
            Eb = [A.alloc((1024,), BF16) for _ in range(int(os.environ.get("MK_EB", "2")))]
            qbb = [A.alloc((512,), BF16) for _ in range(2)]
            sqb = [A.alloc((512,), BF16) for _ in range(2)]
            rstd = A.alloc((TB,), F32)
            Psw = A.alloc((128,), BF16)
            mask = A.alloc((1024,), BF16)
            xcar = A.alloc((8, 4), F32)
            hcar = A.alloc((8,), F32)
            ti = [0]
            bi = [0]
            qi = [0]

            def T():
                t = tmp[ti[0] % NT]
                ti[0] += 1
                return t

            def nb():
                b = banks[bi[0] % 6]
                bi[0] += 1
                return b

            def QB():
                b = qbb[qi[0] % 2]
                qi[0] += 1
                return b

            print("mixer arena top", A.top, "of", ARENA_BYTES)
            OFF = os.environ.get("MK_OFF", "0") == "1"
            PEX = "pool" if OFF else "dve"
            P.dma("pool", Psw.all(), cst_d[:, 0:128], "psw")
            P.dma("pool", mask.all(), cst_d[:, 128:1152], "msk")
            P.memset(xcar.all(), 0.0)
            P.memset(hcar.all(), 0.0)
            C1 = 0.7978845608028654
            C2 = C1 * 0.044715
            cvw = lambda k, c: pp[:, PP_CONVW + k * 8 + c:PP_CONVW + k * 8 + c + 1]
            cvb = lambda c: pp[:, PP_CONVB + c:PP_CONVB + c + 1]
            lcv = lambda base, c: lc[:, base + c:base + c + 1]
            F = slice(0, 512)
            pstride = ARENA_BYTES // 4

            for tb in range(S // TB):
                t0 = tb * TB
                uT = uTs[tb % DB]
                y_lru = y_lrus[tb % len(y_lrus)]
                P.phase = "mix%d.lru" % tb
                prenorm(2, uT, t0, TB, sqb, rstd, (6, 7))
                P.dma("sp", ropeC.all(), rope_d[0][:, t0:t0 + TB], "rc")
                P.dma("sp", ropeS.all(), rope_d[1][:, t0:t0 + TB], "rs")

                for cp in range(4 if SUB >= 1 else 0):
                    W, key = next_slot((8, 512))
                    wload(W[:, :, 0:256], kp(w_in[:, cp * 256:(cp + 1) * 256]), key)
                    wload(W[:, :, 256:512], kp(w_in[:, 1024 + cp * 256:1024 + (cp + 1) * 256]), key)
                    for c2 in range(2):
                        c = cp * 2 + c2
                        pg = nb()
                        px = nb()
                        for dc in range(8):
                            P.mm(px.all(), W[:, dc, 256 + c2 * 128:256 + (c2 + 1) * 128], uT[:, dc, :],
                                 start=(dc == 0), stop=(dc == 7))
                        for dc in range(8):
                            P.mm(pg.all(), W[:, dc, c2 * 128:(c2 + 1) * 128], uT[:, dc, :],
                                 start=(dc == 0), stop=(dc == 7))
                        xf = T()
                        P.copy(xf[:, 0:3], xcar[:, c, 0:3], eng="pool")
                        P.copy(xf[:, 3:515], px.all(), eng="act")
                        P.copy(xcar[:, c, 0:3], xf[:, 512:515], eng="pool")
                        if M2L < 2:
                            continue
                        xc = T()
                        P.ts(xc[:, F], xf[:, 0:512], cvw(0, c), cvb(c), ALU.mult, ALU.add, eng=PEX)
                        for k in range(1, 4):
                            P.stt(xc[:, F], xf[:, k:k + 512], cvw(k, c), xc[:, F], ALU.mult, ALU.add)
                        xcb = QB()
                        P.copy(xcb.all(), xc[:, F], eng="act")
                        pr = nb()
                        pi = nb()
                        P.mm(pr.all(), BDa[:, c, :], xcb.all())
                        P.mm(pi.all(), BDx[:, c, :], xcb.all())
                        if M2L < 3:
                            continue
                        thr = T()
                        thi = T()
                        P.act(thr[:, F], pr.all(), AF.Tanh, bias=lcv(32, c), scale=0.5)
                        P.act(thi[:, F], pi.all(), AF.Tanh, bias=lcv(40, c), scale=0.5)
                        mu = T()
                        P.act(mu[:, F], thr[:, F], AF.Exp, bias=lcv(16, c), scale=lcv(16, c))
                        a = thr
                        P.act(a[:, F], thr[:, F], AF.Exp, bias=lcv(24, c), scale=lcv(24, c))
                        sq = T()
                        P.act(sq[:, F], pg.all(), AF.Square)
                        P.ts(sq[:, F], sq[:, F], C2, C1, ALU.mult, ALU.add, eng=PEX)
                        P.tt(sq[:, F], sq[:, F], pg.all(), ALU.mult)
                        P.act(sq[:, F], sq[:, F], AF.Tanh)
                        P.act(mu[:, F], mu[:, F], AF.Sqrt, bias=onev, scale=-1.0)
                        if M2L < 4:
                            continue
                        t1 = thi
                        P.stt(t1[:, F], thi[:, F], 1.0, xc[:, F], ALU.add, ALU.mult)
                        P.stt(t1[:, F], t1[:, F], 0.5, mu[:, F], ALU.mult, ALU.mult)
                        if M2L < 5:
                            continue
                        hs = xc
                        P.scan(hs[:, F], a[:, F], t1[:, F], hcar[:, c:c + 1], ALU.mult, ALU.add)
                        if M2L < 6:
                            continue
                        P.copy(hcar[:, c:c + 1], hs[:, 511:512], eng="dve")
                        P.stt(sq[:, F], sq[:, F], 1.0, pg.all(), ALU.add, ALU.mult)
                        P.stt(y_lru[:, c, :], sq[:, F], 0.5, hs[:, F], ALU.mult, ALU.mult)

                if SUB < 2:
                    continue
                P.phase = "mix%d.qkv" % tb
                RCL = int(os.environ.get("MK_RC", "9"))

                def rope_chunk(pq, outv):
                    if RCL < 1:
                        return
                    qb = QB()
                    P.copy(qb.all(), pq.all(), eng="act")
                    ps = nb()
                    P.mm(ps.all(), (ones if os.environ.get("MK_X") == "1" else Psw).all(), qb.all())
                    if RCL < 2:
                        return
                    r1 = T()
                    r2 = T()
                    P.tt(r1[:, F], ropeC.all(), pq.all(), ALU.mult)
                    P.tt(r2[:, F], ropeS.all(), ps.all(), ALU.mult)
                    if RCL >= 3:
                        P.tt(outv, r1[:, F], r2[:, F], ALU.add, eng=PEX)

                for qp in range(2):
                    W, key = next_slot((8, 512))
                    wload(W.all(), kp(w_in[:, 2048 + qp * 512:2048 + (qp + 1) * 512]), key)
                    for c4 in range(4):
                        pq = nb()
                        for dc in range(8):
                            P.mm(pq.all(), W[:, dc, c4 * 128:(c4 + 1) * 128], uT[:, dc, :], start=(dc == 0), stop=(dc == 7))
                        rope_chunk(pq, qy[:, qp * 4 + c4, :])
                W, key = next_slot((8, 512))
                wload(W.all(), kp(w_kv[:, 0:512]), key)
                for j in range(4 if M3L >= 2 else 0):
                    pk = nb()
                    for dc in range(8):
                        P.mm(pk.all(), W[:, dc, j * 128:(j + 1) * 128], uT[:, dc, :], start=(dc == 0), stop=(dc == 7))
                    rope_chunk(pk, kT[:, j, 128:640])
                W, key = next_slot((8, 512))
                wload(W.all(), kp(w_kv[:, 512:1024]), key)
                for i in range(4 if M3L >= 3 else 0):
                    pv = nb()
                    for dc in range(8):
                        P.mm(pv.all(), uT[:, dc, i * 128:(i + 1) * 128], W[:, dc, :], start=(dc == 0), stop=(dc == 7))
                    P.copy(vS[:, 1 + i, :], pv.all(), eng="act")

                P.phase = "mix%d.att" % tb
                k_ = 0
                for n in range(4 if SUB >= 3 else 0):
                    nglob = tb * 4 + n
                    kbs = (0, 1) if nglob > 0 else (1,)
                    for j in range(4):
                        pb = (k_ % 2) * 2
                        S2 = Buf(psum_t[:, pb * 512:(pb + 2) * 512].rearrange("p (h k g q) -> p h k g q", h=2, k=2, g=2),
                                 "P", pb * 2048, 4, (2, 2, 2, 128))
                        po = banks[4 + (k_ % 2)]
                        den = banks[6 + (k_ % 2)]
                        E = Eb[k_ % len(Eb)]
                        k_ += 1
                        for kb in kbs:
                            kc = slice((n + kb) * 128, (n + kb + 1) * 128)
                            for hh2 in range(2):
                                for half in range(2):
                                    rows = slice(half * 64, half * 64 + 64)
                                    P.mm(S2[:, half, kb, hh2, :], kT[rows, j, kc],
                                         qy[rows, 2 * j + hh2, n * 128:(n + 1) * 128])
                        P.act(E.all(), S2.all(), AF.Exp, scale=0.125)
                        P.tt(E.all(), E.all(), mask.all(), ALU.mult, eng=("pool" if os.environ.get("MK_MASKPOOL", "0") == "1" else "dve"))
                        E4 = Buf(E.ap.rearrange("p (h k g q) -> p h k g q", h=2, k=2, g=2), "S", E.off, 2, (2, 2, 2, 128))
                        for idx, kb in enumerate(kbs):
                            P.mm(po.all(), vS[:, n + kb, j * 128:(j + 1) * 128], E4[:, :, kb, :, :],
                                 start=(idx == 0), stop=(idx == len(kbs) - 1))
                        for idx, kb in enumerate(kbs):
                            P.mm(den.all(), ones.all(), E4[:, :, kb, :, :],
                                 start=(idx == 0), stop=(idx == len(kbs) - 1))
                        rec = T()
                        sink_ap = bass.AP(arena_t, lc.off // 4 + 48 + 4 * j, [[pstride, 128], [1, 2], [2, 2], [0, 128]])
                        sinkv = V(sink_ap, lc[:, 48:64].regs)
                        rec4 = Buf(rec.ap[:, 0:512].rearrange("p (h g q) -> p h g q", h=2, g=2), "S", rec.off, 4, (2, 2, 128))
                        den4 = Buf(den.ap.rearrange("p (h g q) -> p h g q", h=2, g=2), "P", den.off, 4, (2, 2, 128))
                        po4 = Buf(po.ap.rearrange("p (h g q) -> p h g q", h=2, g=2), "P", po.off, 4, (2, 2, 128))
                        P.tt(rec4.all(), den4.all(), sinkv, ALU.add)
                        P.act(rec[:, F], rec[:, F], AF.Ln)
                        P.act(rec[:, F], rec[:, F], AF.Exp, scale=-1.0)
                        for half in range(2):
                            rows = slice(half * 64, half * 64 + 64)
                            P.tt(qy[rows, 8 + 2 * j:8 + 2 * j + 2, n * 128:(n + 1) * 128], po4[rows, half, :, :],
                                 rec4[rows, half, :, :], ALU.mult)
                if tb < S // TB - 1:
                    P.copy(kT[:, :, 0:128], kT[:, :, 512:640], eng="pool")
                    P.copy(vS[:, 0, :], vS[:, 4, :], eng="pool")

                if SUB < 4:
                    continue
                P.phase = "mix%d.mrg" % tb
                for g in range(2):
                    sg = [T() for _ in range(4)]
                    sa = [T() for _ in range(4)]
                    W, key = next_slot((8, 512))
                    wload(W.all(), kp(w_in[:, 3584 + g * 512:3584 + (g + 1) * 512]), key)
                    for d4 in range(4):
                        p_ = nb()
                        for dc in range(8):
                            P.mm(p_.all(), W[:, dc, d4 * 128:(d4 + 1) * 128], uT[:, dc, :], start=(dc == 0), stop=(dc == 7))
                        P.act(sg[d4][:, F], p_.all(), AF.Sigmoid)
                    W, key = next_slot((8, 512))
                    wload(W.all(), kp(w_pl[:, g * 512:(g + 1) * 512]), key)
                    for d4 in range(4):
                        p_ = nb()
                        for dc in range(8):
                            P.mm(p_.all(), W[:, dc, d4 * 128:(d4 + 1) * 128], y_lru[:, dc, :], start=(dc == 0), stop=(dc == 7))
                        P.tt(sg[d4][:, F], sg[d4][:, F], p_.all(), ALU.mult)
                    W, key = next_slot((8, 512))
                    wload(W.all(), kp(w_in[:, 4608 + g * 512:4608 + (g + 1) * 512]), key)
                    for d4 in range(4):
                        p_ = nb()
                        for dc in range(8):
                            P.mm(p_.all(), W[:, dc, d4 * 128:(d4 + 1) * 128], uT[:, dc, :], start=(dc == 0), stop=(dc == 7))
                        P.act(sa[d4][:, F], p_.all(), AF.Sigmoid)
                    W, key = next_slot((8, 512))
                    wload(W.all(), kp(w_pa[:, g * 512:(g + 1) * 512]), key)
                    for d4 in range(4):
                        p_ = nb()
                        for dc in range(8):
                            P.mm(p_.all(), W[:, dc, d4 * 128:(d4 + 1) * 128], qy[:, 8 + dc, :], start=(dc == 0), stop=(dc == 7))
                        P.tt(sa[d4][:, F], sa[d4][:, F], p_.all(), ALU.mult)
                        P.tt(merged[:, g * 4 + d4, :], sa[d4][:, F], sg[d4][:, F], ALU.add, eng=PEX)

                P.phase = "mix%d.out" % tb
                k_ = 0
                for g in range(2):
                    W, key = next_slot((8, 512))
                    wload(W.all(), kp(w_o[:, g * 512:(g + 1) * 512]), key)
                    for d4 in range(4):
                        dcp = g * 4 + d4
                        p_ = nb()
                        sq = sqb[k_ % 2]
                        k_ += 1
                        for dc in range(8):
                            P.mm(p_.all(), W[:, dc, d4 * 128:(d4 + 1) * 128], merged[:, dc, :], start=(dc == 0), stop=(dc == 7))
                        P.copy(msb[:, dcp, :], p_.all(), eng="act")
                        P.act(sq.all(), p_.all(), AF.Square)
                        P.mm(banks[6].all(), ones.all(), sq.all(), start=(dcp == 0), stop=(dcp == 7))
                P.act(rstd.all(), banks[6].all(), AF.Ln, bias=epsv, scale=1.0 / D)
                P.act(rstd.all(), rstd.all(), AF.Exp, scale=-0.5)
                for c in range(8):
                    tm = T()
                    P.stt(tm[:, F], msb[:, c, :], gain(3, c), rstd.all(), ALU.mult, ALU.mult)
                    P.tt(hT[:, c, t0:t0 + TB], hT[:, c, t0:t0 + TB], tm[:, F], ALU.add, eng="pool")
            A.release(m)

        if stage >= 1:
            ffn(0, 1, w_gu1, w_dn1)
        if stage >= 2:
            mixer()
        if stage >= 3:
            ffn(4, 5, w_gu2, w_dn2)

        for c in range(8):
            P.dma("sp", outT[c * 128:(c + 1) * 128, :], hT[:, c, :], "o%d" % c)

        P.emit(nc)
        print("arena peak bytes", A.peak, "ops", {e: len(P.ops[e]) for e in ENGS})
    return nc


_CACHE = {}


def _rope_tables():
    half = 32
    inv_freq = 10000.0 ** (-np.arange(half, dtype=np.float64) / half)
    ang = np.arange(S, dtype=np.float64)[:, None] * inv_freq[None, :]
    cos = np.cos(ang).T
    sin = np.sin(ang).T
    C = np.zeros((128, S), np.float32)
    Sg = np.zeros((128, S), np.float32)
    for p in range(128):
        i = p % 32
        C[p] = cos[i]
        Sg[p] = -sin[i] if (p % 64) < 32 else sin[i]
    return np.stack([C, Sg], 0)


def _consts():
    c = np.zeros((128, 128 + 1024), np.float32)
    for m in range(128):
        k = m + 32 if (m % 64) < 32 else m - 32
        c[k, m] = 1.0
    s = np.arange(128)[:, None]
    q = np.arange(128)[None, :]
    prev = (q < s).astype(np.float32)
    cur = (q >= s).astype(np.float32)
    mk = np.concatenate([prev, prev, cur, cur, prev, prev, cur, cur], axis=1)
    c[:, 128:] = mk
    return c


def kernel(**inp):
    stage = int(os.environ.get("MK_STAGE", "3"))
    if stage not in _CACHE:
        _CACHE[stage] = build_program(stage)
    nc = _CACHE[stage]
    f = lambda a: np.ascontiguousarray(np.asarray(a, dtype=np.float32))
    x = f(inp["x"])

    def col(v):
        return f(v).reshape(8, 128).T

    pp = np.zeros((128, NPP), np.float32)
    for i, nm in enumerate(["ffn1_pre_g", "ffn1_post_g", "mix_pre_g", "mix_post_g", "ffn2_pre_g", "ffn2_post_g"]):
        pp[:, PP_GAIN + i * 8:PP_GAIN + (i + 1) * 8] = col(inp[nm][0])
    cw = f(inp["conv_w"][0])
    for k in range(4):
        pp[:, PP_CONVW + k * 8:PP_CONVW + (k + 1) * 8] = col(cw[k])
    pp[:, PP_CONVB:PP_CONVB + 8] = col(inp["conv_b"][0])
    pp[:, PP_BA:PP_BA + 8] = col(inp["lru_b_a"][0])
    pp[:, PP_BX:PP_BX + 8] = col(inp["lru_b_x"][0])
    pp[:, PP_LAM:PP_LAM + 8] = col(inp["lru_lambda"][0])
    pp[:, PP_SINK:PP_SINK + 16] = np.broadcast_to(f(inp["attn_sinks"][0])[None, :], (128, 16))

    w_in = f(inp["w_in"][0])
    kcols = w_in[:, 3072:3328].reshape(D, 4, 64)
    vcols = w_in[:, 3328:3584].reshape(D, 4, 64)
    w_kv = np.concatenate([np.repeat(kcols[:, :, None, :], 2, axis=2).reshape(D, 512),
                           np.repeat(vcols[:, :, None, :], 2, axis=2).reshape(D, 512)], axis=1)
    shared = {
        "pp": pp,
        "w_gu1": f(inp["ffn1_w_gu"][0]), "w_dn1": f(inp["ffn1_w_down"][0]),
        "w_gu2": f(inp["ffn2_w_gu"][0]), "w_dn2": f(inp["ffn2_w_down"][0]),
        "w_in": w_in, "w_kv": f(w_kv),
        "w_pl": f(inp["w_proj_lru"][0]), "w_pa": f(inp["w_proj_attn"][0]), "w_o": f(inp["w_out"][0]),
        "lw_a": f(inp["lru_w_a"][0]), "lw_x": f(inp["lru_w_x"][0]),
        "rope": _rope_tables(), "cst": _consts(),
    }
    in_maps = []
    for b in range(NCORES):
        m = dict(shared)
        m["xT"] = np.ascontiguousarray(x[b].T)
        in_maps.append(m)
    res = run_bass_kernel_spmd(nc, in_maps, core_ids=list(range(NCORES)))
    out = np.stack([np.asarray(res.results[b]["outT"]).T for b in range(NCORES)], axis=0)
    return np.ascontiguousarray(out.astype(np.float32))
```

```python
import os
from contextlib import ExitStack

import numpy as np
import concourse.bass as bass
import concourse.mybir as mybir
from concourse.bass_utils import run_bass_kernel_spmd

F32 = mybir.dt.float32
BF16 = mybir.dt.bfloat16
AF = mybir.ActivationFunctionType
ALU = mybir.AluOpType

D = 1024
S = 2048
B = 8
DFF = 2816
NFF = DFF // 128
EPS = 1e-6
NCORES = 8

ENGS = ("pe", "act", "dve", "pool", "sp")


class V:
    __slots__ = ("ap", "regs")

    def __init__(self, ap, regs):
        self.ap = ap
        self.regs = regs


class Buf:
    def __init__(self, ap, space, byte_off, esize, free_shape):
        self.ap = ap
        self.space = space
        self.off = byte_off
        self.esize = esize
        self.fs = tuple(free_shape)
        st = []
        acc = 1
        for d in reversed(self.fs):
            st.append(acc)
            acc *= d
        self.strides = tuple(reversed(st))
        self.nbytes = acc * esize

    def __getitem__(self, key):
        if not isinstance(key, tuple):
            key = (key,)
        ap = self.ap[key]
        fk = key[1:]
        rng = []
        for i, d in enumerate(self.fs):
            if i < len(fk):
                k = fk[i]
                if isinstance(k, slice):
                    a = 0 if k.start is None else k.start
                    b = d if k.stop is None else k.stop
                else:
                    a, b = k, k + 1
            else:
                a, b = 0, d
            assert 0 <= a < b <= d, (key, self.fs)
            rng.append((a, b))
        j = -1
        for i, (a, b) in enumerate(rng):
            if (a, b) != (0, self.fs[i]):
                j = i
        if j < 0:
            return self.all()
        combos = [0]
        for i in range(j):
            a, b = rng[i]
            combos = [c + x * self.strides[i] for c in combos for x in range(a, b)]
            if len(combos) > 64:
                combos = None
                break
        if combos is None:
            lo = sum(a * s for (a, b), s in zip(rng, self.strides))
            hi = sum((b - 1) * s for (a, b), s in zip(rng, self.strides))
            return V(ap, [(self.space, self.off + lo * self.esize, self.off + (hi + 1) * self.esize)])
        a, b = rng[j]
        st = self.strides[j]
        regs = [(self.space, self.off + (c + a * st) * self.esize, self.off + (c + b * st) * self.esize)
                for c in combos]
        regs.sort()
        out = [regs[0]]
        for r in regs[1:]:
            if r[1] == out[-1][2]:
                out[-1] = (out[-1][0], out[-1][1], r[2])
            else:
                out.append(r)
        return V(ap, out)

    def all(self):
        return V(self.ap, [(self.space, self.off, self.off + self.nbytes)])


class Op:
    __slots__ = ("eng", "fn", "idx", "deps", "ddeps", "is_dma", "dma_key", "dma_cnt", "signal", "cnt", "waits",
                 "gid", "odeps", "dmadeps", "cost", "nbytes", "fin", "succ", "npred", "deps_all", "phase", "st", "why", "tag", "tset")


class Rec:
    __slots__ = ("lo", "hi", "op", "w")

    def __init__(self, lo, hi, op, w):
        self.lo = lo
        self.hi = hi
        self.op = op
        self.w = w


class Prog:
    def __init__(self):
        self.ops = {e: [] for e in ENGS}
        self.recs = {"S": [], "P": []}
        self.dma_cnt = {}
        self.out_dmas = []
        self.ngid = 0
        self.phase = "setup"
        self.last_dma = {}

    def add(self, eng, fn, reads=(), writes=(), dma_key=None, cost=0.3, nbytes=0):
        op = Op()
        op.eng = eng
        op.fn = fn
        op.gid = self.ngid
        op.phase = self.phase
        self.ngid += 1
        op.cost = cost
        op.nbytes = nbytes
        op.odeps = []
        op.dmadeps = []
        op.tset = None
        op.idx = len(self.ops[eng])
        op.is_dma = dma_key is not None
        op.dma_key = dma_key
        op.signal = False
        op.cnt = 0
        op.waits = None
        if op.is_dma:
            self.dma_cnt[dma_key] = self.dma_cnt.get(dma_key, 0) + 16
            op.dma_cnt = self.dma_cnt[dma_key]
        deps = {}
        ddeps = {}

        def add_dep(d, raw):
            if d is op:
                return
            if d.is_dma:
                k = ("d", d.dma_key)
                v = self.dma_cnt[d.dma_key] - (16 if (op.is_dma and op.dma_key == d.dma_key) else 0)
                if v > ddeps.get(k, 0):
                    ddeps[k] = v
                op.dmadeps.append(d)
                return
            if d.eng == eng and not op.is_dma:
                if eng == "pe":
                    op.odeps.append(d)
                    return
            deps[id(d)] = d

        for v in reads:
            for (sp, lo, hi) in v.regs:
                for r in self.recs[sp]:
                    if r.w and r.lo < hi and lo < r.hi:
                        add_dep(r.op, True)
                    elif (sp == "P" and not r.w and r.op.eng != eng
                          and r.lo // 2048 <= (hi - 1) // 2048 and lo // 2048 <= (r.hi - 1) // 2048):
                        add_dep(r.op, True)
        for v in writes:
            for (sp, lo, hi) in v.regs:
                for r in self.recs[sp]:
                    if r.lo < hi and lo < r.hi:
                        add_dep(r.op, False)
        op.deps = list(deps.values())
        op.ddeps = ddeps
        inorder = not op.is_dma
        for v in reads:
            for (sp, lo, hi) in v.regs:
                lst = self.recs[sp]
                if inorder:
                    keep = []
                    for r in lst:
                        if (not r.w) and (not r.op.is_dma) and r.op.eng == eng and lo <= r.lo and r.hi <= hi:
                            if r.op is not op:
                                op.odeps.append(r.op)
                        else:
                            keep.append(r)
                    lst[:] = keep
                lst.append(Rec(lo, hi, op, False))
        for v in writes:
            for (sp, lo, hi) in v.regs:
                lst = self.recs[sp]
                lst[:] = [r for r in lst if not (lo <= r.lo and r.hi <= hi)]
                lst.append(Rec(lo, hi, op, True))
        if op.is_dma:
            prev = self.last_dma.get(eng)
            if prev is not None:
                op.odeps.append(prev)
            self.last_dma[eng] = op
        self.ops[eng].append(op)
        return op

    def schedule(self, window=int(os.environ.get("MK_WIN", "600"))):
        import heapq
        allops = [op for e in ENGS for op in self.ops[e]]
        for op in allops:
            op.succ = []
            op.fin = None
        for op in allops:
            preds = {}
            for d in op.deps:
                preds[id(d)] = d
            for d in op.odeps:
                preds[id(d)] = d
            for d in op.dmadeps:
                preds[id(d)] = d
            op.npred = len(preds)
            for d in preds.values():
                d.succ.append(op)
            op.deps_all = None
        ready = {e: [] for e in ENGS}
        for op in allops:
            if op.npred == 0:
                heapq.heappush(ready[op.eng], (op.gid, id(op), op))
        efree = {e: 0.0 for e in ENGS}
        dma_free = [0.0]
        order = {e: [] for e in ENGS}
        LAT = 0.25
        nleft = len(allops)
        mingid = {e: 0 for e in ENGS}

        elast = {e: None for e in ENGS}
        cur_set = [None]
        def set_pen(op):
            if op.eng != "act" or op.tset is None or cur_set[0] is None:
                return 0.0
            a, b = cur_set[0], op.tset
            if a == b or (a in "ET" and b in "ET") or (a in "GT" and b in "GT"):
                return 0.0
            return 1.3

        def est_start(op, why=False):
            t = efree[op.eng]
            w = ("eng", elast[op.eng])
            for d in op.deps:
                x = d.fin + (LAT if d.eng != op.eng else 0.05)
                if x > t:
                    t = x
                    w = ("dep", d)
            for d in op.dmadeps:
                x = d.fin + LAT
                if x > t:
                    t = x
                    w = ("dma", d)
            for d in op.odeps:
                x = d.fin if not d.is_dma else d.cnt
                if x > t:
                    t = x
                    w = ("ord", d)
            if why:
                op.why = w
            return t

        while nleft:
            best = None
            for e in ENGS:
                h = ready[e]
                if not h:
                    continue
                g0 = h[0][0]
                cand = heapq.nsmallest(int(os.environ.get("MK_CAND", "12")), h)
                for (g, _, op) in cand:
                    if g - g0 > window:
                        break
                    s = est_start(op) + set_pen(op)
                    if best is None or (s, g) < (best[0], best[1]):
                        best = (s, g, op)
            s, g, op = best
            h = ready[op.eng]
            h.remove((op.gid, id(op), op))
            heapq.heapify(h)
            op.st = s
            if op.eng == "act" and op.tset is not None:
                cur_set[0] = op.tset
            est_start(op, True)
            elast[op.eng] = op
            if op.is_dma:
                issue = 1.15 if op.eng == "pool" else 0.15
                efree[op.eng] = s + issue
                op.cnt = s + issue
                t0 = max(s + issue, dma_free[0])
                dur = op.nbytes / 300e3
                dma_free[0] = t0 + dur
                op.fin = t0 + dur + 2.0
            else:
                op.fin = s + op.cost
                efree[op.eng] = op.fin
            order[op.eng].append(op)
            nleft -= 1
            for sc in op.succ:
                sc.npred -= 1
                if sc.npred == 0:
                    heapq.heappush(ready[sc.eng], (sc.gid, id(sc), sc))
        for e in ENGS:
            assert len(order[e]) == len(self.ops[e])
            self.ops[e] = order[e]
            for i, op in enumerate(order[e]):
                op.idx = i
                op.cnt = 0
        self.est_time = max(op.fin for op in allops)
        if os.environ.get("MK_CRIT") == "1":
            last = max(allops, key=lambda o: o.fin)
            agg = {}
            o = last
            n = 0
            while o is not None and n < 200000:
                kind, p = o.why
                k = (o.phase, o.eng, kind)
                agg[k] = agg.get(k, 0.0) + (o.fin - (p.fin if p is not None else 0.0))
                o = p
                n += 1
            for k in sorted(agg, key=lambda k: -agg[k])[:40]:
                print("CRIT %-10s %-5s %-4s %7.1f" % (k[0], k[1], k[2], agg[k]))
        if os.environ.get("MK_REPORT") == "1":
            ph = {}
            for op in allops:
                d = ph.setdefault(op.phase, {"t0": 1e18, "t1": 0.0, "pe": 0.0, "act": 0.0, "dve": 0.0, "pool": 0.0, "sp": 0.0})
                d["t0"] = min(d["t0"], op.st)
                if not op.is_dma:
                    d["t1"] = max(d["t1"], op.fin)
                    d[op.eng] += op.cost
            for k, d in ph.items():
                print("%-10s t0 %7.1f t1 %7.1f span %6.1f | pe %6.1f act %6.1f dve %6.1f pool %6.1f" % (
                    k, d["t0"], d["t1"], d["t1"] - d["t0"], d["pe"], d["act"], d["dve"], d["pool"]))

    def mm(self, out, lhsT, rhs, start=True, stop=True, **kw):
        n = rhs.ap.free_size()
        return self.add("pe", lambda e: e.matmul(out.ap, lhsT.ap, rhs.ap, start=start, stop=stop, **kw),
                        reads=[lhsT, rhs], writes=[out], cost=(0.276 * n / 512.0 if n >= 256 else 0.105))

    def act(self, out, in_, func, bias=None, scale=None, eng="act"):
        reads = [in_]
        kw = {}
        if bias is not None:
            if isinstance(bias, V):
                reads.append(bias)
                kw["bias"] = bias.ap
            else:
                kw["bias"] = float(bias)
        if scale is not None:
            if isinstance(scale, V):
                reads.append(scale)
                kw["scale"] = scale.ap
            else:
                kw["scale"] = float(scale)
        op = self.add(eng, lambda e: e.activation(out.ap, in_.ap, func, **kw), reads=reads, writes=[out],
                      cost=(in_.ap.free_size() + 260) / 1400.0)
        op.tset = {AF.Exp: "E", AF.Ln: "E", AF.Tanh: "T", AF.Sqrt: "Q", AF.Silu: "U", AF.Sigmoid: "G"}.get(func)
        return op

    def tt(self, out, in0, in1, op, eng="dve"):
        return self.add(eng, lambda e: e.tensor_tensor(out.ap, in0.ap, in1.ap, op), reads=[in0, in1], writes=[out],
                        cost=self.vcost(eng, in0.ap.free_size()))

    def ts(self, out, in0, s1, s2, op0, op1=None, eng="dve"):
        reads = [in0]
        a1 = s1
        a2 = s2
        if isinstance(s1, V):
            reads.append(s1)
            a1 = s1.ap
        if isinstance(s2, V):
            reads.append(s2)
            a2 = s2.ap
        c = self.vcost(eng, in0.ap.free_size())
        if op1 is None:
            return self.add(eng, lambda e: e.tensor_scalar(out.ap, in0.ap, a1, None, op0), reads=reads, writes=[out], cost=c)
        return self.add(eng, lambda e: e.tensor_scalar(out.ap, in0.ap, a1, a2, op0, op1), reads=reads, writes=[out], cost=c)

    def stt(self, out, in0, scalar, in1, op0, op1):
        reads = [in0, in1]
        a = scalar
        if isinstance(scalar, V):
            reads.append(scalar)
            a = scalar.ap
        return self.add("dve", lambda e: e.scalar_tensor_tensor(out.ap, in0.ap, a, in1.ap, op0, op1),
                        reads=reads, writes=[out], cost=in0.ap.free_size() / 640.0 + 0.1)

    def recip(self, out, in_):
        return self.add("dve", lambda e: e.reciprocal(out.ap, in_.ap), reads=[in_], writes=[out],
                        cost=in_.ap.free_size() / 156.0 + 0.05)

    def copy(self, out, in_, eng="dve"):
        if eng == "act":
            return self.add("act", lambda e: e.copy(out.ap, in_.ap), reads=[in_], writes=[out],
                            cost=(in_.ap.free_size() + 260) / 1400.0)
        return self.add(eng, lambda e: e.tensor_copy(out.ap, in_.ap), reads=[in_], writes=[out],
                        cost=self.vcost(eng, in_.ap.free_size()))

    def scan(self, out, d0, d1, init, op0, op1):
        return self.add("dve", lambda e: e.tensor_tensor_scan(out.ap, d0.ap, d1.ap, init.ap, op0, op1),
                        reads=[d0, d1, init], writes=[out], cost=2 * d0.ap.free_size() / 960.0 + 0.15)

    def memset(self, out, val, eng="pool"):
        return self.add(eng, lambda e: e.memset(out.ap, val), writes=[out], cost=self.vcost(eng, out.ap.free_size()))

    @staticmethod
    def vcost(eng, n):
        if eng == "pool":
            return n / 450.0 + 0.25
        return n / 1000.0 + 0.09

    def dma(self, eng, out, in_, key, **kw):
        reads = [in_] if isinstance(in_, V) else []
        writes = [out] if isinstance(out, V) else []
        oa = out.ap if isinstance(out, V) else out
        ia = in_.ap if isinstance(in_, V) else in_
        sb = out if isinstance(out, V) else in_
        nb_ = sb.ap.partition_size() * sb.ap.free_size() * 4
        op = self.add(eng, lambda e: e.dma_start(out=oa, in_=ia, **kw), reads=reads, writes=writes, dma_key=key,
                      nbytes=nb_)
        if not isinstance(out, V):
            self.out_dmas.append(op)
        return op

    def emit(self, nc, final_eng="sp"):
        if os.environ.get("MK_SCHED", "1") == "1":
            self.schedule()
            print("scheduler estimate us", round(self.est_time, 1))
        fin = Op()
        fin.eng = final_eng
        fin.fn = None
        fin.idx = len(self.ops[final_eng])
        fin.is_dma = False
        fin.dma_key = None
        fin.signal = False
        fin.cnt = 0
        fin.deps = []
        fin.ddeps = {("d", o.dma_key): self.dma_cnt[o.dma_key] for o in self.out_dmas}
        self.ops[final_eng].append(fin)

        for eng in ENGS:
            seen = {}
            for op in self.ops[eng]:
                w = {}
                for k, val in op.ddeps.items():
                    if seen.get(k, 0) >= val:
                        continue
                    w[k] = (val, None)
                for d in op.deps:
                    k = ("e", d.eng)
                    if seen.get(k, -1) >= d.idx:
                        continue
                    if k not in w or w[k][0] < d.idx:
                        w[k] = (d.idx, d)
                for k, (val, d) in w.items():
                    seen[k] = val
                    if d is not None:
                        d.signal = True
                op.waits = list(w.items())
        for eng in ENGS:
            c = 0
            for op in self.ops[eng]:
                if op.signal:
                    c += 1
                    op.cnt = c

        with ExitStack() as st:
            esem = {e: st.enter_context(nc.semaphore("s_" + e)) for e in ENGS}
            dsem = {k: st.enter_context(nc.semaphore("d_" + str(k))) for k in self.dma_cnt}
            block = st.enter_context(nc.Block())

            def run(eng, e):
                for op in self.ops[eng]:
                    for k, (val, d) in op.waits:
                        if k[0] == "d":
                            e.wait_ge(dsem[k[1]], val)
                        else:
                            e.wait_ge(esem[k[1]], d.cnt)
                    if op.fn is None:
                        continue
                    ins = op.fn(e)
                    if op.is_dma:
                        ins.then_inc(dsem[op.dma_key], 16)
                    elif op.signal:
                        ins.then_inc(esem[eng], 1)

            @block.tensor
            def _(e):
                run("pe", e)

            @block.scalar
            def _(e):
                run("act", e)

            @block.vector
            def _(e):
                run("dve", e)

            @block.gpsimd
            def _(e):
                run("pool", e)

            @block.sync
            def _(e):
                run("sp", e)


class Arena:
    def __init__(self, arena_ap, nbytes):
        self.ap = arena_ap
        self.n = nbytes
        self.top = 0
        self.peak = 0

    def alloc(self, free_shape, dtype):
        es = 2 if dtype == BF16 else 4
        n = es
        for d in free_shape:
            n *= d
        off = (self.top + 63) // 64 * 64
        assert off + n <= self.n, ("arena overflow", off, n, self.n)
        self.top = off + n
        self.peak = max(self.peak, self.top)
        ap = self.ap[:, off // 4:(off + n + 3) // 4]
        if dtype == BF16:
            ap = ap.bitcast(BF16)
        if len(free_shape) == 2:
            ap = ap.rearrange("p (a b) -> p a b", a=free_shape[0])
        elif len(free_shape) == 3:
            ap = ap.rearrange("p (a b c) -> p a b c", a=free_shape[0], b=free_shape[1])
        elif len(free_shape) == 4:
            ap = ap.rearrange("p (a b c d) -> p a b c d", a=free_shape[0], b=free_shape[1], c=free_shape[2])
        return Buf(ap, "S", off, es, free_shape)

    def view_at(self, off, free_shape, dtype):
        es = 2 if dtype == BF16 else 4
        n = es
        for d in free_shape:
            n *= d
        ap = self.ap[:, off // 4:(off + n + 3) // 4]
        if dtype == BF16:
            ap = ap.bitcast(BF16)
        if len(free_shape) == 2:
            ap = ap.rearrange("p (a b) -> p a b", a=free_shape[0])
        return Buf(ap, "S", off, es, free_shape)

    def mark(self):
        return self.top

    def release(self, m):
        self.top = m


ARENA_BYTES = int(os.environ.get("MK_ARENA", "212480"))
SLOT_BYTES = 8192
NSLOTS = 4

PP_GAIN = 0
PP_CONVW = 48
PP_CONVB = 80
PP_BA = 88
PP_BX = 96
PP_LAM = 104
PP_SINK = 112
NPP = 128


def build_program(stage):
    HUP = os.environ.get("MK_HUP", "dve")
    SUB = int(os.environ.get("MK_SUB", "9"))
    M2L = int(os.environ.get("MK_M2", "9"))
    M3L = int(os.environ.get("MK_M3", "9"))
    nc = bass.Bass("TRN2", target_bir_lowering=False)
    dram = {}

    def din(name, shape, dt=F32):
        dram[name] = nc.dram_tensor(name, list(shape), dt, kind="ExternalInput").ap()
        return dram[name]

    xT = din("xT", (D, S))
    pp_d = din("pp", (128, NPP))
    w_gu1 = din("w_gu1", (D, 2 * DFF))
    w_dn1 = din("w_dn1", (DFF, D))
    w_gu2 = din("w_gu2", (D, 2 * DFF))
    w_dn2 = din("w_dn2", (DFF, D))
    w_in = din("w_in", (D, 5632))
    w_kv = din("w_kv", (D, 1024))
    w_pl = din("w_pl", (D, D))
    w_pa = din("w_pa", (D, D))
    w_o = din("w_o", (D, D))
    lw_a = din("lw_a", (16, 64, 64))
    lw_x = din("lw_x", (16, 64, 64))
    rope_d = din("rope", (2, 128, S))
    cst_d = din("cst", (128, 128 + 1024))
    outT = nc.dram_tensor("outT", [D, S], F32, kind="ExternalOutput").ap()

    P = Prog()
    with ExitStack() as st:
        arena_t = st.enter_context(nc.sbuf_tensor("arena", [128, ARENA_BYTES // 4], F32))
        psum_t = st.enter_context(nc.psum_tensor("psum", [128, 4096], F32))
        A = Arena(arena_t, ARENA_BYTES)

        def bank(i, n=1):
            return Buf(psum_t[:, i * 512:(i + n) * 512], "P", i * 2048, 4, (n * 512,))

        banks = [bank(i) for i in range(8)]

        hT = A.alloc((8, S), F32)
        pp = A.alloc((NPP,), F32)
        hp = A.alloc((48,), F32)
        cf = A.alloc((8,), F32)
        lc = A.alloc((64,), F32)
        ones = A.alloc((128,), BF16)
        BDa = A.alloc((8, 128), BF16)
        BDx = A.alloc((8, 128), BF16)
        slots = [A.alloc((SLOT_BYTES // 2,), BF16) for _ in range(NSLOTS)]
        slot_i = [0]

        def next_slot(shape):
            i = slot_i[0] % NSLOTS
            slot_i[0] += 1
            sb = slots[i]
            n = 1
            for d in shape:
                n *= d
            assert n * 2 <= SLOT_BYTES
            ap = sb.ap[:, 0:n]
            if len(shape) == 2:
                ap = ap.rearrange("p (a b) -> p a b", a=shape[0])
            return Buf(ap, "S", sb.off, 2, shape), "w%d" % i

        def wload(dst, src, key):
            P.dma("pool", dst, src, key)

        def kp(ap):
            return ap.rearrange("(k p) n -> p k n", p=128)

        P.dma("sp", pp.all(), pp_d, "pp")
        for c in range(8):
            P.dma("sp", hT[:, c, :], xT[c * 128:(c + 1) * 128, :], "x%d" % c)
        P.memset(ones.all(), 1.0)
        P.memset(cf[:, 0:1], EPS)
        P.memset(cf[:, 1:2], 1.0)
        P.ts(hp.all(), pp[:, 0:48], 0.5, None, ALU.mult)
        epsv = cf[:, 0:1]
        onev = cf[:, 1:2]

        def gain(i, c):
            return pp[:, PP_GAIN + i * 8 + c:PP_GAIN + i * 8 + c + 1]

        def hgain(i, c):
            return hp[:, i * 8 + c:i * 8 + c + 1]

        if stage >= 2:
            P.memset(BDa.all(), 0.0)
            P.memset(BDx.all(), 0.0)
            for (bd, lw, key) in ((BDa, lw_a, "bda"), (BDx, lw_x, "bdx")):
                src = lw.rearrange("(c two) i o -> two i c o", two=2)
                for two in range(2):
                    P.dma("pool", bd[two * 64:(two + 1) * 64, :, two * 64:(two + 1) * 64], src[two], key)
            X = lc[:, 0:8]
            T_ = lc[:, 8:16]
            CL = lc[:, 16:24]
            HCL = lc[:, 24:32]
            P.act(X, pp[:, PP_LAM:PP_LAM + 8], AF.Exp, scale=-1.0)
            P.ts(T_, X, 1.0 / 3.0, -0.5, ALU.mult, ALU.add)
            P.tt(T_, T_, X, ALU.mult)
            P.ts(T_, T_, 1.0, None, ALU.add)
            P.tt(T_, T_, X, ALU.mult)
            P.ts(CL, T_, -8.0, None, ALU.mult)
            P.ts(HCL, T_, -4.0, None, ALU.mult)
            P.ts(lc[:, 32:48], pp[:, PP_BA:PP_BA + 16], 0.5, None, ALU.mult)
            P.act(lc[:, 48:64], pp[:, PP_SINK:PP_SINK + 16], AF.Exp)

        def prenorm(gi, uT, t0, ntok, sq_bufs, rstd, ssb):
            k = 0
            for sub in range(ntok // 512):
                ts_ = t0 + sub * 512
                ss = banks[ssb[sub % len(ssb)]]
                rs = rstd[:, sub * 512:(sub + 1) * 512]
                for c in range(8):
                    sq = sq_bufs[k % len(sq_bufs)]
                    k += 1
                    P.act(sq.all(), hT[:, c, ts_:ts_ + 512], AF.Square)
                    P.mm(ss.all(), ones.all(), sq.all(), start=(c == 0), stop=(c == 7))
                P.act(rs, ss.all(), AF.Ln, bias=epsv, scale=1.0 / D)
                P.act(rs, rs, AF.Exp, scale=-0.5)
                for c in range(8):
                    P.stt(uT[:, c, sub * 512:(sub + 1) * 512], hT[:, c, ts_:ts_ + 512], gain(gi, c),
                          rs, ALU.mult, ALU.mult)

        def ffn(gi_pre, gi_post, w_gu, w_dn):
            m = A.mark()
            TB = 1024
            uT = A.alloc((8, TB), BF16)
            actT = A.alloc((NFF, TB), BF16)
            fsb = A.alloc((8, TB), F32)
            sqb = [A.alloc((512,), BF16) for _ in range(2)]
            rstd = A.alloc((TB,), F32)
            sgb = [A.alloc((512,), F32) for _ in range(2)]
            tmpb = [A.alloc((512,), F32) for _ in range(2)]
            for tb in range(S // TB):
                T0 = tb * TB
                P.phase = "ffn%d.%d" % (gi_pre // 4 + 1, tb)
                prenorm(gi_pre, uT, T0, TB, sqb, rstd, (6, 7))
                k = 0
                for grp in range(NFF // 2):
                    W, key = next_slot((8, 512))
                    wload(W[:, :, 0:256], kp(w_gu[:, grp * 256:(grp + 1) * 256]), key)
                    wload(W[:, :, 256:512], kp(w_gu[:, DFF + grp * 256:DFF + (grp + 1) * 256]), key)
                    for f2 in range(2):
                        ffc = grp * 2 + f2
                        for sub in range(2):
                            pg = banks[(k % 2) * 2]
                            pu = banks[(k % 2) * 2 + 1]
                            sg = sgb[k % 2]
                            k += 1
                            for dc in range(8):
                                P.mm(pg.all(), W[:, dc, f2 * 128:(f2 + 1) * 128], uT[:, dc, sub * 512:(sub + 1) * 512],
                                     start=(dc == 0), stop=(dc == 7))
                            for dc in range(8):
                                P.mm(pu.all(), W[:, dc, 256 + f2 * 128:256 + (f2 + 1) * 128],
                                     uT[:, dc, sub * 512:(sub + 1) * 512], start=(dc == 0), stop=(dc == 7))
                            P.act(sg.all(), pg.all(), AF.Silu)
                            P.tt(actT[:, ffc, sub * 512:(sub + 1) * 512], sg.all(), pu.all(), ALU.mult)
                k = 0
                for dc in range(8):
                    W, key = next_slot((NFF, 128))
                    wload(W.all(), kp(w_dn[:, dc * 128:(dc + 1) * 128]), key)
                    for sub in range(2):
                        pf = banks[4 + (k % 2)]
                        sq = sqb[k % 2]
                        k += 1
                        for ffc in range(NFF):
                            P.mm(pf.all(), W[:, ffc, :], actT[:, ffc, sub * 512:(sub + 1) * 512],
                                 start=(ffc == 0), stop=(ffc == NFF - 1))
                        P.copy(fsb[:, dc, sub * 512:(sub + 1) * 512], pf.all(), eng="act")
                        P.act(sq.all(), pf.all(), AF.Square)
                        P.mm(banks[6 + sub].all(), ones.all(), sq.all(), start=(dc == 0), stop=(dc == 7))
                for sub in range(2):
                    rs = rstd[:, sub * 512:(sub + 1) * 512]
                    P.act(rs, banks[6 + sub].all(), AF.Ln, bias=epsv, scale=1.0 / D)
                    P.act(rs, rs, AF.Exp, scale=-0.5)
                k = 0
                for sub in range(2):
                    ts_ = T0 + sub * 512
                    for c in range(8):
                        tmp = tmpb[k % 2]
                        k += 1
                        P.stt(tmp.all(), fsb[:, c, sub * 512:(sub + 1) * 512], hgain(gi_post, c),
                              rstd[:, sub * 512:(sub + 1) * 512], ALU.mult, ALU.mult)
                        P.tt(hT[:, c, ts_:ts_ + 512], hT[:, c, ts_:ts_ + 512], tmp.all(), ALU.add, eng=HUP)
            A.release(m)

        def mixer():
            m = A.mark()
            TB = 512
            DB = int(os.environ.get("MK_DB", "1"))
            uTs = [A.alloc((8, TB), BF16) for _ in range(DB)]
            y_lrus = [A.alloc((8, TB), BF16) for _ in range(int(os.environ.get("MK_DBY", "1")))]
            qy = A.alloc((16, TB), BF16)
            msb = A.view_at(qy.off, (8, TB), F32)
            kT = A.alloc((4, 640), BF16)
            vS = A.alloc((5, 512), BF16)
            merged = A.alloc((8, TB), BF16)
            ropeC = A.alloc((TB,), F32)
            ropeS = A.alloc((TB,), F32)
            NT = int(os.environ.get("MK_NT", "19"))
            tmp = [A.alloc((516,), F32) for _ in range(NT)]
            Eb = [A.alloc((1024,), BF16) for _ in range(int(os.environ.get("MK_EB", "2")))]
            qbb = [A.alloc((512,), BF16) for _ in range(2)]
            sqb = [A.alloc((512,), BF16) for _ in range(2)]
            rstd = A.alloc((TB,), F32)
            Psw = A.alloc((128,), BF16)
            mask = A.alloc((1024,), BF16)
            xcar = A.alloc((8, 4), F32)
            hcar = A.alloc((8,), F32)
            ti = [0]
            bi = [0]
            qi = [0]

            def T():
                t = tmp[ti[0] % NT]
                ti[0] += 1
                return t

            def nb():
                b = banks[bi[0] % 6]
                bi[0] += 1
                return b

            def QB():
                b = qbb[qi[0] % 2]
                qi[0] += 1
                return b

            print("mixer arena top", A.top, "of", ARENA_BYTES)
            OFF = os.environ.get("MK_OFF", "0") == "1"
            PEX = "pool" if OFF else "dve"
            P.dma("pool", Psw.all(), cst_d[:, 0:128], "psw")
            P.dma("pool", mask.all(), cst_d[:, 128:1152], "msk")
            P.memset(xcar.all(), 0.0)
            P.memset(hcar.all(), 0.0)
            C1 = 0.7978845608028654
            C2 = C1 * 0.044715
            cvw = lambda k, c: pp[:, PP_CONVW + k * 8 + c:PP_CONVW + k * 8 + c + 1]
            cvb = lambda c: pp[:, PP_CONVB + c:PP_CONVB + c + 1]
            lcv = lambda base, c: lc[:, base + c:base + c + 1]
            F = slice(0, 512)
            pstride = ARENA_BYTES // 4

            for tb in range(S // TB):
                t0 = tb * TB
                uT = uTs[tb % DB]
                y_lru = y_lrus[tb % len(y_lrus)]
                P.phase = "mix%d.lru" % tb
                prenorm(2, uT, t0, TB, sqb, rstd, (6, 7))
                P.dma("sp", ropeC.all(), rope_d[0][:, t0:t0 + TB], "rc")
                P.dma("sp", ropeS.all(), rope_d[1][:, t0:t0 + TB], "rs")

                for cp in range(4 if SUB >= 1 else 0):
                    W, key = next_slot((8, 512))
                    wload(W[:, :, 0:256], kp(w_in[:, cp * 256:(cp + 1) * 256]), key)
                    wload(W[:, :, 256:512], kp(w_in[:, 1024 + cp * 256:1024 + (cp + 1) * 256]), key)
                    for c2 in range(2):
                        c = cp * 2 + c2
                        pg = nb()
                        px = nb()
                        for dc in range(8):
                            P.mm(px.all(), W[:, dc, 256 + c2 * 128:256 + (c2 + 1) * 128], uT[:, dc, :],
                                 start=(dc == 0), stop=(dc == 7))
                        for dc in range(8):
                            P.mm(pg.all(), W[:, dc, c2 * 128:(c2 + 1) * 128], uT[:, dc, :],
                                 start=(dc == 0), stop=(dc == 7))
                        xf = T()
                        P.copy(xf[:, 0:3], xcar[:, c, 0:3], eng="pool")
                        P.copy(xf[:, 3:515], px.all(), eng="act")
                        P.copy(xcar[:, c, 0:3], xf[:, 512:515], eng="pool")
                        if M2L < 2:
                            continue
                        xc = T()
                        P.ts(xc[:, F], xf[:, 0:512], cvw(0, c), cvb(c), ALU.mult, ALU.add, eng=PEX)
                        for k in range(1, 4):
                            P.stt(xc[:, F], xf[:, k:k + 512], cvw(k, c), xc[:, F], ALU.mult, ALU.add)
                        xcb = QB()
                        P.copy(xcb.all(), xc[:, F], eng="act")
                        pr = nb()
                        pi = nb()
                        P.mm(pr.all(), BDa[:, c, :], xcb.all())
                        P.mm(pi.all(), BDx[:, c, :], xcb.all())
                        if M2L < 3:
                            continue
                        thr = T()
                        thi = T()
                        P.act(thr[:, F], pr.all(), AF.Tanh, bias=lcv(32, c), scale=0.5)
                        P.act(thi[:, F], pi.all(), AF.Tanh, bias=lcv(40, c), scale=0.5)
                        mu = T()
                        P.act(mu[:, F], thr[:, F], AF.Exp, bias=lcv(16, c), scale=lcv(16, c))
                        a = thr
                        P.act(a[:, F], thr[:, F], AF.Exp, bias=lcv(24, c), scale=lcv(24, c))
                        sq = T()
                        P.act(sq[:, F], pg.all(), AF.Square)
                        P.ts(sq[:, F], sq[:, F], C2, C1, ALU.mult, ALU.add, eng=PEX)
                        P.tt(sq[:, F], sq[:, F], pg.all(), ALU.mult)
                        P.act(sq[:, F], sq[:, F], AF.Tanh)
                        P.act(mu[:, F], mu[:, F], AF.Sqrt, bias=onev, scale=-1.0)
                        if M2L < 4:
                            continue
                        t1 = thi
                        P.stt(t1[:, F], thi[:, F], 1.0, xc[:, F], ALU.add, ALU.mult)
                        P.stt(t1[:, F], t1[:, F], 0.5, mu[:, F], ALU.mult, ALU.mult)
                        if M2L < 5:
                            continue
                        hs = xc
                        P.scan(hs[:, F], a[:, F], t1[:, F], hcar[:, c:c + 1], ALU.mult, ALU.add)
                        if M2L < 6:
                            continue
                        P.copy(hcar[:, c:c + 1], hs[:, 511:512], eng="dve")
                        P.stt(sq[:, F], sq[:, F], 1.0, pg.all(), ALU.add, ALU.mult)
                        P.stt(y_lru[:, c, :], sq[:, F], 0.5, hs[:, F], ALU.mult, ALU.mult)

                if SUB < 2:
                    continue
                P.phase = "mix%d.qkv" % tb
                RCL = int(os.environ.get("MK_RC", "9"))

                def rope_chunk(pq, outv):
                    if RCL < 1:
                        return
                    qb = QB()
                    P.copy(qb.all(), pq.all(), eng="act")
                    ps = nb()
                    P.mm(ps.all(), (ones if os.environ.get("MK_X") == "1" else Psw).all(), qb.all())
                    if RCL < 2:
                        return
                    r1 = T()
                    r2 = T()
                    P.tt(r1[:, F], ropeC.all(), pq.all(), ALU.mult)
                    P.tt(r2[:, F], ropeS.all(), ps.all(), ALU.mult)
                    if RCL >= 3:
                        P.tt(outv, r1[:, F], r2[:, F], ALU.add, eng=PEX)

                for qp in range(2):
                    W, key = next_slot((8, 512))
                    wload(W.all(), kp(w_in[:, 2048 + qp * 512:2048 + (qp + 1) * 512]), key)
                    for c4 in range(4):
                        pq = nb()
                        for dc in range(8):
                            P.mm(pq.all(), W[:, dc, c4 * 128:(c4 + 1) * 128], uT[:, dc, :], start=(dc == 0), stop=(dc == 7))
                        rope_chunk(pq, qy[:, qp * 4 + c4, :])
                W, key = next_slot((8, 512))
                wload(W.all(), kp(w_kv[:, 0:512]), key)
                for j in range(4 if M3L >= 2 else 0):
                    pk = nb()
                    for dc in range(8):
                        P.mm(pk.all(), W[:, dc, j * 128:(j + 1) * 128], uT[:, dc, :], start=(dc == 0), stop=(dc == 7))
                    rope_chunk(pk, kT[:, j, 128:640])
                W, key = next_slot((8, 512))
                wload(W.all(), kp(w_kv[:, 512:1024]), key)
                for i in range(4 if M3L >= 3 else 0):
                    pv = nb()
                    for dc in range(8):
                        P.mm(pv.all(), uT[:, dc, i * 128:(i + 1) * 128], W[:, dc, :], start=(dc == 0), stop=(dc == 7))
                    P.copy(vS[:, 1 + i, :], pv.all(), eng="act")

                P.phase = "mix%d.att" % tb
                k_ = 0
                for n in range(4 if SUB >= 3 else 0):
                    nglob = tb * 4 + n
                    kbs = (0, 1) if nglob > 0 else (1,)
                    for j in range(4):
                        pb = (k_ % 2) * 2
                        S2 = Buf(psum_t[:, pb * 512:(pb + 2) * 512].rearrange("p (h k g q) -> p h k g q", h=2, k=2, g=2),
                                 "P", pb * 2048, 4, (2, 2, 2, 128))
                        po = banks[4 + (k_ % 2)]
                        den = banks[6 + (k_ % 2)]
                        E = Eb[k_ % len(Eb)]
                        k_ += 1
                        for kb in kbs:
                            kc = slice((n + kb) * 128, (n + kb + 1) * 128)
                            for hh2 in range(2):
                                for half in range(2):
                                    rows = slice(half * 64, half * 64 + 64)
                                    P.mm(S2[:, half, kb, hh2, :], kT[rows, j, kc],
                                         qy[rows, 2 * j + hh2, n * 128:(n + 1) * 128])
                        P.act(E.all(), S2.all(), AF.Exp, scale=0.125)
                        P.tt(E.all(), E.all(), mask.all(), ALU.mult, eng=("pool" if os.environ.get("MK_MASKPOOL", "0") == "1" else "dve"))
                        E4 = Buf(E.ap.rearrange("p (h k g q) -> p h k g q", h=2, k=2, g=2), "S", E.off, 2, (2, 2, 2, 128))
                        for idx, kb in enumerate(kbs):
                            P.mm(po.all(), vS[:, n + kb, j * 128:(j + 1) * 128], E4[:, :, kb, :, :],
                                 start=(idx == 0), stop=(idx == len(kbs) - 1))
                        for idx, kb in enumerate(kbs):
                            P.mm(den.all(), ones.all(), E4[:, :, kb, :, :],
                                 start=(idx == 0), stop=(idx == len(kbs) - 1))
                        rec = T()
                        sink_ap = bass.AP(arena_t, lc.off // 4 + 48 + 4 * j, [[pstride, 128], [1, 2], [2, 2], [0, 128]])
                        sinkv = V(sink_ap, lc[:, 48:64].regs)
                        rec4 = Buf(rec.ap[:, 0:512].rearrange("p (h g q) -> p h g q", h=2, g=2), "S", rec.off, 4, (2, 2, 128))
                        den4 = Buf(den.ap.rearrange("p (h g q) -> p h g q", h=2, g=2), "P", den.off, 4, (2, 2, 128))
                        po4 = Buf(po.ap.rearrange("p (h g q) -> p h g q", h=2, g=2), "P", po.off, 4, (2, 2, 128))
                        P.tt(rec4.all(), den4.all(), sinkv, ALU.add)
                        P.act(rec[:, F], rec[:, F], AF.Ln)
                        P.act(rec[:, F], rec[:, F], AF.Exp, scale=-1.0)
                        for half in range(2):
                            rows = slice(half * 64, half * 64 + 64)
                            P.tt(qy[rows, 8 + 2 * j:8 + 2 * j + 2, n * 128:(n + 1) * 128], po4[rows, half, :, :],
                                 rec4[rows, half, :, :], ALU.mult)
                if tb < S // TB - 1:
                    P.copy(kT[:, :, 0:128], kT[:, :, 512:640], eng="pool")
                    P.copy(vS[:, 0, :], vS[:, 4, :], eng="pool")

                if SUB < 4:
                    continue
                P.phase = "mix%d.mrg" % tb
                for g in range(2):
                    sg = [T() for _ in range(4)]
                    sa = [T() for _ in range(4)]
                    W, key = next_slot((8, 512))
                    wload(W.all(), kp(w_in[:, 3584 + g * 512:3584 + (g + 1) * 512]), key)
                    for d4 in range(4):
                        p_ = nb()
                        for dc in range(8):
                            P.mm(p_.all(), W[:, dc, d4 * 128:(d4 + 1) * 128], uT[:, dc, :], start=(dc == 0), stop=(dc == 7))
                        P.act(sg[d4][:, F], p_.all(), AF.Sigmoid)
                    W, key = next_slot((8, 512))
                    wload(W.all(), kp(w_pl[:, g * 512:(g + 1) * 512]), key)
                    for d4 in range(4):
                        p_ = nb()
                        for dc in range(8):
                            P.mm(p_.all(), W[:, dc, d4 * 128:(d4 + 1) * 128], y_lru[:, dc, :], start=(dc == 0), stop=(dc == 7))
                        P.tt(sg[d4][:, F], sg[d4][:, F], p_.all(), ALU.mult)
                    W, key = next_slot((8, 512))
                    wload(W.all(), kp(w_in[:, 4608 + g * 512:4608 + (g + 1) * 512]), key)
                    for d4 in range(4):
                        p_ = nb()
                        for dc in range(8):
                            P.mm(p_.all(), W[:, dc, d4 * 128:(d4 + 1) * 128], uT[:, dc, :], start=(dc == 0), stop=(dc == 7))
                        P.act(sa[d4][:, F], p_.all(), AF.Sigmoid)
                    W, key = next_slot((8, 512))
                    wload(W.all(), kp(w_pa[:, g * 512:(g + 1) * 512]), key)
                    for d4 in range(4):
                        p_ = nb()
                        for dc in range(8):
                            P.mm(p_.all(), W[:, dc, d4 * 128:(d4 + 1) * 128], qy[:, 8 + dc, :], start=(dc == 0), stop=(dc == 7))
                        P.tt(sa[d4][:, F], sa[d4][:, F], p_.all(), ALU.mult)
                        P.tt(merged[:, g * 4 + d4, :], sa[d4][:, F], sg[d4][:, F], ALU.add, eng=PEX)

                P.phase = "mix%d.out" % tb
                k_ = 0
                for g in range(2):
                    W, key = next_slot((8, 512))
                    wload(W.all(), kp(w_o[:, g * 512:(g + 1) * 512]), key)
                    for d4 in range(4):
                        dcp = g * 4 + d4
                        p_ = nb()
                        sq = sqb[k_ % 2]
                        k_ += 1
                        for dc in range(8):
                            P.mm(p_.all(), W[:, dc, d4 * 128:(d4 + 1) * 128], merged[:, dc, :], start=(dc == 0), stop=(dc == 7))
                        P.copy(msb[:, dcp, :], p_.all(), eng="act")
                        P.act(sq.all(), p_.all(), AF.Square)
                        P.mm(banks[6].all(), ones.all(), sq.all(), start=(dcp == 0), stop=(dcp == 7))
                P.act(rstd.all(), banks[6].all(), AF.Ln, bias=epsv, scale=1.0 / D)
                P.act(rstd.all(), rstd.all(), AF.Exp, scale=-0.5)
                for c in range(8):
                    tm = T()
                    P.stt(tm[:, F], msb[:, c, :], gain(3, c), rstd.all(), ALU.mult, ALU.mult)
                    P.tt(hT[:, c, t0:t0 + TB], hT[:, c, t0:t0 + TB], tm[:, F], ALU.add, eng=HUP)
            A.release(m)

        if stage >= 1:
            ffn(0, 1, w_gu1, w_dn1)
        if stage >= 2:
            mixer()
        if stage >= 3:
            ffn(4, 5, w_gu2, w_dn2)

        for c in range(8):
            P.dma("sp", outT[c * 128:(c + 1) * 128, :], hT[:, c, :], "o%d" % c)

        P.emit(nc)
        print("arena peak bytes", A.peak, "ops", {e: len(P.ops[e]) for e in ENGS})
    return nc


_CACHE = {}


def _rope_tables():
    half = 32
    inv_freq = 10000.0 ** (-np.arange(half, dtype=np.float64) / half)
    ang = np.arange(S, dtype=np.float64)[:, None] * inv_freq[None, :]
    cos = np.cos(ang).T
    sin = np.sin(ang).T
    C = np.zeros((128, S), np.float32)
    Sg = np.zeros((128, S), np.float32)
    for p in range(128):
        i = p % 32
        C[p] = cos[i]
        Sg[p] = -sin[i] if (p % 64) < 32 else sin[i]
    return np.stack([C, Sg], 0)


def _consts():
    c = np.zeros((128, 128 + 1024), np.float32)
    for m in range(128):
        k = m + 32 if (m % 64) < 32 else m - 32
        c[k, m] = 1.0
    s = np.arange(128)[:, None]
    q = np.arange(128)[None, :]
    prev = (q < s).astype(np.float32)
    cur = (q >= s).astype(np.float32)
    mk = np.concatenate([prev, prev, cur, cur, prev, prev, cur, cur], axis=1)
    c[:, 128:] = mk
    return c


def kernel(**inp):
    stage = int(os.environ.get("MK_STAGE", "3"))
    if stage not in _CACHE:
        _CACHE[stage] = build_program(stage)
    nc = _CACHE[stage]
    f = lambda a: np.ascontiguousarray(np.asarray(a, dtype=np.float32))
    x = f(inp["x"])

    def col(v):
        return f(v).reshape(8, 128).T

    pp = np.zeros((128, NPP), np.float32)
    for i, nm in enumerate(["ffn1_pre_g", "ffn1_post_g", "mix_pre_g", "mix_post_g", "ffn2_pre_g", "ffn2_post_g"]):
        pp[:, PP_GAIN + i * 8:PP_GAIN + (i + 1) * 8] = col(inp[nm][0])
    cw = f(inp["conv_w"][0])
    for k in range(4):
        pp[:, PP_CONVW + k * 8:PP_CONVW + (k + 1) * 8] = col(cw[k])
    pp[:, PP_CONVB:PP_CONVB + 8] = col(inp["conv_b"][0])
    pp[:, PP_BA:PP_BA + 8] = col(inp["lru_b_a"][0])
    pp[:, PP_BX:PP_BX + 8] = col(inp["lru_b_x"][0])
    pp[:, PP_LAM:PP_LAM + 8] = col(inp["lru_lambda"][0])
    pp[:, PP_SINK:PP_SINK + 16] = np.broadcast_to(f(inp["attn_sinks"][0])[None, :], (128, 16))

    w_in = f(inp["w_in"][0])
    kcols = w_in[:, 3072:3328].reshape(D, 4, 64)
    vcols = w_in[:, 3328:3584].reshape(D, 4, 64)
    w_kv = np.concatenate([np.repeat(kcols[:, :, None, :], 2, axis=2).reshape(D, 512),
                           np.repeat(vcols[:, :, None, :], 2, axis=2).reshape(D, 512)], axis=1)
    shared = {
        "pp": pp,
        "w_gu1": f(inp["ffn1_w_gu"][0]), "w_dn1": f(inp["ffn1_w_down"][0]),
        "w_gu2": f(inp["ffn2_w_gu"][0]), "w_dn2": f(inp["ffn2_w_down"][0]),
        "w_in": w_in, "w_kv": f(w_kv),
        "w_pl": f(inp["w_proj_lru"][0]), "w_pa": f(inp["w_proj_attn"][0]), "w_o": f(inp["w_out"][0]),
        "lw_a": f(inp["lru_w_a"][0]), "lw_x": f(inp["lru_w_x"][0]),
        "rope": _rope_tables(), "cst": _consts(),
    }
    in_maps = []
    for b in range(NCORES):
        m = dict(shared)
        m["xT"] = np.ascontiguousarray(x[b].T)
        in_maps.append(m)
    res = run_bass_kernel_spmd(nc, in_maps, core_ids=list(range(NCORES)))
    out = np.stack([np.asarray(res.results[b]["outT"]).T for b in range(NCORES)], axis=0)
    return np.ascontiguousarray(out.astype(np.float32))
```

```python
import os
from contextlib import ExitStack

import numpy as np
import concourse.bass as bass
import concourse.mybir as mybir
from concourse.bass_utils import run_bass_kernel_spmd

F32 = mybir.dt.float32
BF16 = mybir.dt.bfloat16
AF = mybir.ActivationFunctionType
ALU = mybir.AluOpType

D = 1024
S = 2048
B = 8
DFF = 2816
NFF = DFF // 128
EPS = 1e-6
NCORES = 8

ENGS = ("pe", "act", "dve", "pool", "sp")


class V:
    __slots__ = ("ap", "regs")

    def __init__(self, ap, regs):
        self.ap = ap
        self.regs = regs


class Buf:
    def __init__(self, ap, space, byte_off, esize, free_shape):
        self.ap = ap
        self.space = space
        self.off = byte_off
        self.esize = esize
        self.fs = tuple(free_shape)
        st = []
        acc = 1
        for d in reversed(self.fs):
            st.append(acc)
            acc *= d
        self.strides = tuple(reversed(st))
        self.nbytes = acc * esize

    def __getitem__(self, key):
        if not isinstance(key, tuple):
            key = (key,)
        ap = self.ap[key]
        fk = key[1:]
        rng = []
        for i, d in enumerate(self.fs):
            if i < len(fk):
                k = fk[i]
                if isinstance(k, slice):
                    a = 0 if k.start is None else k.start
                    b = d if k.stop is None else k.stop
                else:
                    a, b = k, k + 1
            else:
                a, b = 0, d
            assert 0 <= a < b <= d, (key, self.fs)
            rng.append((a, b))
        j = -1
        for i, (a, b) in enumerate(rng):
            if (a, b) != (0, self.fs[i]):
                j = i
        if j < 0:
            return self.all()
        combos = [0]
        for i in range(j):
            a, b = rng[i]
            combos = [c + x * self.strides[i] for c in combos for x in range(a, b)]
            if len(combos) > 64:
                combos = None
                break
        if combos is None:
            lo = sum(a * s for (a, b), s in zip(rng, self.strides))
            hi = sum((b - 1) * s for (a, b), s in zip(rng, self.strides))
            return V(ap, [(self.space, self.off + lo * self.esize, self.off + (hi + 1) * self.esize)])
        a, b = rng[j]
        st = self.strides[j]
        regs = [(self.space, self.off + (c + a * st) * self.esize, self.off + (c + b * st) * self.esize)
                for c in combos]
        regs.sort()
        out = [regs[0]]
        for r in regs[1:]:
            if r[1] == out[-1][2]:
                out[-1] = (out[-1][0], out[-1][1], r[2])
            else:
                out.append(r)
        return V(ap, out)

    def all(self):
        return V(self.ap, [(self.space, self.off, self.off + self.nbytes)])


class Op:
    __slots__ = ("eng", "fn", "idx", "deps", "ddeps", "is_dma", "dma_key", "dma_cnt", "signal", "cnt", "waits",
                 "gid", "odeps", "dmadeps", "cost", "nbytes", "fin", "succ", "npred", "deps_all", "phase", "st", "why", "tag", "tset")


class Rec:
    __slots__ = ("lo", "hi", "op", "w")

    def __init__(self, lo, hi, op, w):
        self.lo = lo
        self.hi = hi
        self.op = op
        self.w = w


class Prog:
    def __init__(self):
        self.ops = {e: [] for e in ENGS}
        self.recs = {"S": [], "P": []}
        self.dma_cnt = {}
        self.out_dmas = []
        self.ngid = 0
        self.phase = "setup"
        self.last_dma = {}

    def add(self, eng, fn, reads=(), writes=(), dma_key=None, cost=0.3, nbytes=0):
        op = Op()
        op.eng = eng
        op.fn = fn
        op.gid = self.ngid
        op.phase = self.phase
        self.ngid += 1
        op.cost = cost
        op.nbytes = nbytes
        op.odeps = []
        op.dmadeps = []
        op.tset = None
        op.idx = len(self.ops[eng])
        op.is_dma = dma_key is not None
        op.dma_key = dma_key
        op.signal = False
        op.cnt = 0
        op.waits = None
        if op.is_dma:
            self.dma_cnt[dma_key] = self.dma_cnt.get(dma_key, 0) + 16
            op.dma_cnt = self.dma_cnt[dma_key]
        deps = {}
        ddeps = {}

        def add_dep(d, raw):
            if d is op:
                return
            if d.is_dma:
                k = ("d", d.dma_key)
                v = self.dma_cnt[d.dma_key] - (16 if (op.is_dma and op.dma_key == d.dma_key) else 0)
                if v > ddeps.get(k, 0):
                    ddeps[k] = v
                op.dmadeps.append(d)
                return
            if d.eng == eng and not op.is_dma:
                if eng == "pe":
                    op.odeps.append(d)
                    return
            deps[id(d)] = d

        for v in reads:
            for (sp, lo, hi) in v.regs:
                for r in self.recs[sp]:
                    if r.w and r.lo < hi and lo < r.hi:
                        add_dep(r.op, True)
                    elif (sp == "P" and not r.w and r.op.eng != eng
                          and r.lo // 2048 <= (hi - 1) // 2048 and lo // 2048 <= (r.hi - 1) // 2048):
                        add_dep(r.op, True)
        for v in writes:
            for (sp, lo, hi) in v.regs:
                for r in self.recs[sp]:
                    if r.lo < hi and lo < r.hi:
                        add_dep(r.op, False)
        op.deps = list(deps.values())
        op.ddeps = ddeps
        inorder = not op.is_dma
        for v in reads:
            for (sp, lo, hi) in v.regs:
                lst = self.recs[sp]
                if inorder:
                    keep = []
                    for r in lst:
                        if (not r.w) and (not r.op.is_dma) and r.op.eng == eng and lo <= r.lo and r.hi <= hi:
                            if r.op is not op:
                                op.odeps.append(r.op)
                        else:
                            keep.append(r)
                    lst[:] = keep
                lst.append(Rec(lo, hi, op, False))
        for v in writes:
            for (sp, lo, hi) in v.regs:
                lst = self.recs[sp]
                lst[:] = [r for r in lst if not (lo <= r.lo and r.hi <= hi)]
                lst.append(Rec(lo, hi, op, True))
        if op.is_dma:
            prev = self.last_dma.get(eng)
            if prev is not None:
                op.odeps.append(prev)
            self.last_dma[eng] = op
        self.ops[eng].append(op)
        return op

    def schedule(self, window=int(os.environ.get("MK_WIN", "600"))):
        import heapq
        allops = [op for e in ENGS for op in self.ops[e]]
        for op in allops:
            op.succ = []
            op.fin = None
        for op in allops:
            preds = {}
            for d in op.deps:
                preds[id(d)] = d
            for d in op.odeps:
                preds[id(d)] = d
            for d in op.dmadeps:
                preds[id(d)] = d
            op.npred = len(preds)
            for d in preds.values():
                d.succ.append(op)
            op.deps_all = None
        ready = {e: [] for e in ENGS}
        for op in allops:
            if op.npred == 0:
                heapq.heappush(ready[op.eng], (op.gid, id(op), op))
        efree = {e: 0.0 for e in ENGS}
        dma_free = [0.0]
        order = {e: [] for e in ENGS}
        LAT = 0.25
        nleft = len(allops)
        mingid = {e: 0 for e in ENGS}

        elast = {e: None for e in ENGS}
        cur_set = [None]
        def set_pen(op):
            if op.eng != "act" or op.tset is None or cur_set[0] is None:
                return 0.0
            a, b = cur_set[0], op.tset
            if a == b or (a in "ET" and b in "ET") or (a in "GT" and b in "GT"):
                return 0.0
            return 1.3

        def est_start(op, why=False):
            t = efree[op.eng]
            w = ("eng", elast[op.eng])
            for d in op.deps:
                x = d.fin + (LAT if d.eng != op.eng else 0.05)
                if x > t:
                    t = x
                    w = ("dep", d)
            for d in op.dmadeps:
                x = d.fin + LAT
                if x > t:
                    t = x
                    w = ("dma", d)
            for d in op.odeps:
                x = d.fin if not d.is_dma else d.cnt
                if x > t:
                    t = x
                    w = ("ord", d)
            if why:
                op.why = w
            return t

        while nleft:
            best = None
            for e in ENGS:
                h = ready[e]
                if not h:
                    continue
                g0 = h[0][0]
                cand = heapq.nsmallest(int(os.environ.get("MK_CAND", "12")), h)
                for (g, _, op) in cand:
                    if g - g0 > window:
                        break
                    s = est_start(op) + set_pen(op)
                    if best is None or (s, g) < (best[0], best[1]):
                        best = (s, g, op)
            s, g, op = best
            h = ready[op.eng]
            h.remove((op.gid, id(op), op))
            heapq.heapify(h)
            op.st = s
            if op.eng == "act" and op.tset is not None:
                cur_set[0] = op.tset
            est_start(op, True)
            elast[op.eng] = op
            if op.is_dma:
                issue = 1.15 if op.eng == "pool" else 0.15
                efree[op.eng] = s + issue
                op.cnt = s + issue
                t0 = max(s + issue, dma_free[0])
                dur = op.nbytes / 300e3
                dma_free[0] = t0 + dur
                op.fin = t0 + dur + 2.0
            else:
                op.fin = s + op.cost
                efree[op.eng] = op.fin
            order[op.eng].append(op)
            nleft -= 1
            for sc in op.succ:
                sc.npred -= 1
                if sc.npred == 0:
                    heapq.heappush(ready[sc.eng], (sc.gid, id(sc), sc))
        for e in ENGS:
            assert len(order[e]) == len(self.ops[e])
            self.ops[e] = order[e]
            for i, op in enumerate(order[e]):
                op.idx = i
                op.cnt = 0
        self.est_time = max(op.fin for op in allops)
        if os.environ.get("MK_CRIT") == "1":
            last = max(allops, key=lambda o: o.fin)
            agg = {}
            o = last
            n = 0
            while o is not None and n < 200000:
                kind, p = o.why
                k = (o.phase, o.eng, kind)
                agg[k] = agg.get(k, 0.0) + (o.fin - (p.fin if p is not None else 0.0))
                o = p
                n += 1
            for k in sorted(agg, key=lambda k: -agg[k])[:40]:
                print("CRIT %-10s %-5s %-4s %7.1f" % (k[0], k[1], k[2], agg[k]))
        if os.environ.get("MK_REPORT") == "1":
            ph = {}
            for op in allops:
                d = ph.setdefault(op.phase, {"t0": 1e18, "t1": 0.0, "pe": 0.0, "act": 0.0, "dve": 0.0, "pool": 0.0, "sp": 0.0})
                d["t0"] = min(d["t0"], op.st)
                if not op.is_dma:
                    d["t1"] = max(d["t1"], op.fin)
                    d[op.eng] += op.cost
            for k, d in ph.items():
                print("%-10s t0 %7.1f t1 %7.1f span %6.1f | pe %6.1f act %6.1f dve %6.1f pool %6.1f" % (
                    k, d["t0"], d["t1"], d["t1"] - d["t0"], d["pe"], d["act"], d["dve"], d["pool"]))

    def mm(self, out, lhsT, rhs, start=True, stop=True, **kw):
        n = rhs.ap.free_size()
        return self.add("pe", lambda e: e.matmul(out.ap, lhsT.ap, rhs.ap, start=start, stop=stop, **kw),
                        reads=[lhsT, rhs], writes=[out], cost=(0.276 * n / 512.0 if n >= 256 else 0.105))

    def act(self, out, in_, func, bias=None, scale=None, eng="act"):
        reads = [in_]
        kw = {}
        if bias is not None:
            if isinstance(bias, V):
                reads.append(bias)
                kw["bias"] = bias.ap
            else:
                kw["bias"] = float(bias)
        if scale is not None:
            if isinstance(scale, V):
                reads.append(scale)
                kw["scale"] = scale.ap
            else:
                kw["scale"] = float(scale)
        op = self.add(eng, lambda e: e.activation(out.ap, in_.ap, func, **kw), reads=reads, writes=[out],
                      cost=(in_.ap.free_size() + 260) / 1400.0)
        op.tset = {AF.Exp: "E", AF.Ln: "E", AF.Tanh: "T", AF.Sqrt: "Q", AF.Silu: "U", AF.Sigmoid: "G"}.get(func)
        return op

    def tt(self, out, in0, in1, op, eng="dve"):
        return self.add(eng, lambda e: e.tensor_tensor(out.ap, in0.ap, in1.ap, op), reads=[in0, in1], writes=[out],
                        cost=self.vcost(eng, in0.ap.free_size()))

    def ts(self, out, in0, s1, s2, op0, op1=None, eng="dve"):
        reads = [in0]
        a1 = s1
        a2 = s2
        if isinstance(s1, V):
            reads.append(s1)
            a1 = s1.ap
        if isinstance(s2, V):
            reads.append(s2)
            a2 = s2.ap
        c = self.vcost(eng, in0.ap.free_size())
        if op1 is None:
            return self.add(eng, lambda e: e.tensor_scalar(out.ap, in0.ap, a1, None, op0), reads=reads, writes=[out], cost=c)
        return self.add(eng, lambda e: e.tensor_scalar(out.ap, in0.ap, a1, a2, op0, op1), reads=reads, writes=[out], cost=c)

    def stt(self, out, in0, scalar, in1, op0, op1):
        reads = [in0, in1]
        a = scalar
        if isinstance(scalar, V):
            reads.append(scalar)
            a = scalar.ap
        return self.add("dve", lambda e: e.scalar_tensor_tensor(out.ap, in0.ap, a, in1.ap, op0, op1),
                        reads=reads, writes=[out], cost=in0.ap.free_size() / 640.0 + 0.1)

    def recip(self, out, in_):
        return self.add("dve", lambda e: e.reciprocal(out.ap, in_.ap), reads=[in_], writes=[out],
                        cost=in_.ap.free_size() / 156.0 + 0.05)

    def copy(self, out, in_, eng="dve"):
        if eng == "act":
            return self.add("act", lambda e: e.copy(out.ap, in_.ap), reads=[in_], writes=[out],
                            cost=(in_.ap.free_size() + 260) / 1400.0)
        return self.add(eng, lambda e: e.tensor_copy(out.ap, in_.ap), reads=[in_], writes=[out],
                        cost=self.vcost(eng, in_.ap.free_size()))

    def scan(self, out, d0, d1, init, op0, op1):
        return self.add("dve", lambda e: e.tensor_tensor_scan(out.ap, d0.ap, d1.ap, init.ap, op0, op1),
                        reads=[d0, d1, init], writes=[out], cost=2 * d0.ap.free_size() / 960.0 + 0.15)

    def memset(self, out, val, eng="pool"):
        return self.add(eng, lambda e: e.memset(out.ap, val), writes=[out], cost=self.vcost(eng, out.ap.free_size()))

    @staticmethod
    def vcost(eng, n):
        if eng == "pool":
            return n / 450.0 + 0.25
        return n / 1000.0 + 0.09

    def dma(self, eng, out, in_, key, **kw):
        reads = [in_] if isinstance(in_, V) else []
        writes = [out] if isinstance(out, V) else []
        oa = out.ap if isinstance(out, V) else out
        ia = in_.ap if isinstance(in_, V) else in_
        sb = out if isinstance(out, V) else in_
        nb_ = sb.ap.partition_size() * sb.ap.free_size() * 4
        op = self.add(eng, lambda e: e.dma_start(out=oa, in_=ia, **kw), reads=reads, writes=writes, dma_key=key,
                      nbytes=nb_)
        if not isinstance(out, V):
            self.out_dmas.append(op)
        return op

    def emit(self, nc, final_eng="sp"):
        if os.environ.get("MK_SCHED", "1") == "1":
            self.schedule()
            print("scheduler estimate us", round(self.est_time, 1))
        fin = Op()
        fin.eng = final_eng
        fin.fn = None
        fin.idx = len(self.ops[final_eng])
        fin.is_dma = False
        fin.dma_key = None
        fin.signal = False
        fin.cnt = 0
        fin.deps = []
        fin.ddeps = {("d", o.dma_key): self.dma_cnt[o.dma_key] for o in self.out_dmas}
        self.ops[final_eng].append(fin)

        for eng in ENGS:
            seen = {}
            for op in self.ops[eng]:
                w = {}
                for k, val in op.ddeps.items():
                    if seen.get(k, 0) >= val:
                        continue
                    w[k] = (val, None)
                for d in op.deps:
                    k = ("e", d.eng)
                    if seen.get(k, -1) >= d.idx:
                        continue
                    if k not in w or w[k][0] < d.idx:
                        w[k] = (d.idx, d)
                for k, (val, d) in w.items():
                    seen[k] = val
                    if d is not None:
                        d.signal = True
                op.waits = list(w.items())
        for eng in ENGS:
            c = 0
            for op in self.ops[eng]:
                if op.signal:
                    c += 1
                    op.cnt = c

        with ExitStack() as st:
            esem = {e: st.enter_context(nc.semaphore("s_" + e)) for e in ENGS}
            dsem = {k: st.enter_context(nc.semaphore("d_" + str(k))) for k in self.dma_cnt}
            block = st.enter_context(nc.Block())

            def run(eng, e):
                for op in self.ops[eng]:
                    for k, (val, d) in op.waits:
                        if k[0] == "d":
                            e.wait_ge(dsem[k[1]], val)
                        else:
                            e.wait_ge(esem[k[1]], d.cnt)
                    if op.fn is None:
                        continue
                    ins = op.fn(e)
                    if op.is_dma:
                        ins.then_inc(dsem[op.dma_key], 16)
                    elif op.signal:
                        ins.then_inc(esem[eng], 1)

            @block.tensor
            def _(e):
                run("pe", e)

            @block.scalar
            def _(e):
                run("act", e)

            @block.vector
            def _(e):
                run("dve", e)

            @block.gpsimd
            def _(e):
                run("pool", e)

            @block.sync
            def _(e):
                run("sp", e)


class Arena:
    def __init__(self, arena_ap, nbytes):
        self.ap = arena_ap
        self.n = nbytes
        self.top = 0
        self.peak = 0

    def alloc(self, free_shape, dtype):
        es = 2 if dtype == BF16 else 4
        n = es
        for d in free_shape:
            n *= d
        off = (self.top + 63) // 64 * 64
        assert off + n <= self.n, ("arena overflow", off, n, self.n)
        self.top = off + n
        self.peak = max(self.peak, self.top)
        ap = self.ap[:, off // 4:(off + n + 3) // 4]
        if dtype == BF16:
            ap = ap.bitcast(BF16)
        if len(free_shape) == 2:
            ap = ap.rearrange("p (a b) -> p a b", a=free_shape[0])
        elif len(free_shape) == 3:
            ap = ap.rearrange("p (a b c) -> p a b c", a=free_shape[0], b=free_shape[1])
        elif len(free_shape) == 4:
            ap = ap.rearrange("p (a b c d) -> p a b c d", a=free_shape[0], b=free_shape[1], c=free_shape[2])
        return Buf(ap, "S", off, es, free_shape)

    def view_at(self, off, free_shape, dtype):
        es = 2 if dtype == BF16 else 4
        n = es
        for d in free_shape:
            n *= d
        ap = self.ap[:, off // 4:(off + n + 3) // 4]
        if dtype == BF16:
            ap = ap.bitcast(BF16)
        if len(free_shape) == 2:
            ap = ap.rearrange("p (a b) -> p a b", a=free_shape[0])
        return Buf(ap, "S", off, es, free_shape)

    def mark(self):
        return self.top

    def release(self, m):
        self.top = m


ARENA_BYTES = int(os.environ.get("MK_ARENA", "212480"))
SLOT_BYTES = 8192
NSLOTS = 4

PP_GAIN = 0
PP_CONVW = 48
PP_CONVB = 80
PP_BA = 88
PP_BX = 96
PP_LAM = 104
PP_SINK = 112
NPP = 128


def build_program(stage):
    HUP = os.environ.get("MK_HUP", "dve")
    SUB = int(os.environ.get("MK_SUB", "9"))
    M2L = int(os.environ.get("MK_M2", "9"))
    M3L = int(os.environ.get("MK_M3", "9"))
    nc = bass.Bass("TRN2", target_bir_lowering=False)
    dram = {}

    def din(name, shape, dt=F32):
        dram[name] = nc.dram_tensor(name, list(shape), dt, kind="ExternalInput").ap()
        return dram[name]

    xT = din("xT", (D, S))
    pp_d = din("pp", (128, NPP))
    w_gu1 = din("w_gu1", (D, 2 * DFF))
    w_dn1 = din("w_dn1", (DFF, D))
    w_gu2 = din("w_gu2", (D, 2 * DFF))
    w_dn2 = din("w_dn2", (DFF, D))
    w_in = din("w_in", (D, 5632))
    w_kv = din("w_kv", (D, 1024))
    w_pl = din("w_pl", (D, D))
    w_pa = din("w_pa", (D, D))
    w_o = din("w_o", (D, D))
    lw_a = din("lw_a", (16, 64, 64))
    lw_x = din("lw_x", (16, 64, 64))
    rope_d = din("rope", (2, 128, S))
    cst_d = din("cst", (128, 128 + 1024))
    outT = nc.dram_tensor("outT", [D, S], F32, kind="ExternalOutput").ap()

    P = Prog()
    with ExitStack() as st:
        arena_t = st.enter_context(nc.sbuf_tensor("arena", [128, ARENA_BYTES // 4], F32))
        psum_t = st.enter_context(nc.psum_tensor("psum", [128, 4096], F32))
        A = Arena(arena_t, ARENA_BYTES)

        def bank(i, n=1):
            return Buf(psum_t[:, i * 512:(i + n) * 512], "P", i * 2048, 4, (n * 512,))

        banks = [bank(i) for i in range(8)]

        hT = A.alloc((8, S), F32)
        pp = A.alloc((NPP,), F32)
        hp = A.alloc((48,), F32)
        cf = A.alloc((8,), F32)
        lc = A.alloc((64,), F32)
        ones = A.alloc((128,), BF16)
        BDa = A.alloc((8, 128), BF16)
        BDx = A.alloc((8, 128), BF16)
        slots = [A.alloc((SLOT_BYTES // 2,), BF16) for _ in range(NSLOTS)]
        slot_i = [0]

        def next_slot(shape):
            i = slot_i[0] % NSLOTS
            slot_i[0] += 1
            sb = slots[i]
            n = 1
            for d in shape:
                n *= d
            assert n * 2 <= SLOT_BYTES
            ap = sb.ap[:, 0:n]
            if len(shape) == 2:
                ap = ap.rearrange("p (a b) -> p a b", a=shape[0])
            return Buf(ap, "S", sb.off, 2, shape), "w%d" % i

        def wload(dst, src, key):
            P.dma("pool", dst, src, key)

        def kp(ap):
            return ap.rearrange("(k p) n -> p k n", p=128)

        P.dma("sp", pp.all(), pp_d, "pp")
        for sub in range(4):
            for c in range(8):
                P.dma("sp", hT[:, c, sub * 512:(sub + 1) * 512], xT[c * 128:(c + 1) * 128, sub * 512:(sub + 1) * 512],
                      "x%d" % sub)
        P.memset(ones.all(), 1.0)
        P.memset(cf[:, 0:1], EPS)
        P.memset(cf[:, 1:2], 1.0)
        P.ts(hp.all(), pp[:, 0:48], 0.5, None, ALU.mult)
        epsv = cf[:, 0:1]
        onev = cf[:, 1:2]

        def gain(i, c):
            return pp[:, PP_GAIN + i * 8 + c:PP_GAIN + i * 8 + c + 1]

        def hgain(i, c):
            return hp[:, i * 8 + c:i * 8 + c + 1]

        if stage >= 2:
            P.memset(BDa.all(), 0.0)
            P.memset(BDx.all(), 0.0)
            for (bd, lw, key) in ((BDa, lw_a, "bda"), (BDx, lw_x, "bdx")):
                src = lw.rearrange("(c two) i o -> two i c o", two=2)
                for two in range(2):
                    P.dma("pool", bd[two * 64:(two + 1) * 64, :, two * 64:(two + 1) * 64], src[two], key)
            X = lc[:, 0:8]
            T_ = lc[:, 8:16]
            CL = lc[:, 16:24]
            HCL = lc[:, 24:32]
            P.act(X, pp[:, PP_LAM:PP_LAM + 8], AF.Exp, scale=-1.0)
            P.ts(T_, X, 1.0 / 3.0, -0.5, ALU.mult, ALU.add)
            P.tt(T_, T_, X, ALU.mult)
            P.ts(T_, T_, 1.0, None, ALU.add)
            P.tt(T_, T_, X, ALU.mult)
            P.ts(CL, T_, -8.0, None, ALU.mult)
            P.ts(HCL, T_, -4.0, None, ALU.mult)
            P.ts(lc[:, 32:48], pp[:, PP_BA:PP_BA + 16], 0.5, None, ALU.mult)
            P.act(lc[:, 48:64], pp[:, PP_SINK:PP_SINK + 16], AF.Exp)

        def prenorm(gi, uT, t0, ntok, sq_bufs, rstd, ssb):
            k = 0
            for sub in range(ntok // 512):
                ts_ = t0 + sub * 512
                ss = banks[ssb[sub % len(ssb)]]
                rs = rstd[:, sub * 512:(sub + 1) * 512]
                for c in range(8):
                    sq = sq_bufs[k % len(sq_bufs)]
                    k += 1
                    P.act(sq.all(), hT[:, c, ts_:ts_ + 512], AF.Square)
                    P.mm(ss.all(), ones.all(), sq.all(), start=(c == 0), stop=(c == 7))
                P.act(rs, ss.all(), AF.Ln, bias=epsv, scale=1.0 / D)
                P.act(rs, rs, AF.Exp, scale=-0.5)
                for c in range(8):
                    P.stt(uT[:, c, sub * 512:(sub + 1) * 512], hT[:, c, ts_:ts_ + 512], gain(gi, c),
                          rs, ALU.mult, ALU.mult)

        def ffn(gi_pre, gi_post, w_gu, w_dn):
            m = A.mark()
            TB = 1024
            uT = A.alloc((8, TB), BF16)
            actT = A.alloc((NFF, TB), BF16)
            fsb = A.alloc((8, TB), F32)
            sqb = [A.alloc((512,), BF16) for _ in range(2)]
            rstd = A.alloc((TB,), F32)
            sgb = [A.alloc((512,), F32) for _ in range(2)]
            tmpb = [A.alloc((512,), F32) for _ in range(2)]
            for tb in range(S // TB):
                T0 = tb * TB
                P.phase = "ffn%d.%d" % (gi_pre // 4 + 1, tb)
                prenorm(gi_pre, uT, T0, TB, sqb, rstd, (6, 7))
                k = 0
                for grp in range(NFF // 2):
                    W, key = next_slot((8, 512))
                    wload(W[:, :, 0:256], kp(w_gu[:, grp * 256:(grp + 1) * 256]), key)
                    wload(W[:, :, 256:512], kp(w_gu[:, DFF + grp * 256:DFF + (grp + 1) * 256]), key)
                    for f2 in range(2):
                        ffc = grp * 2 + f2
                        for sub in range(2):
                            pg = banks[(k % 2) * 2]
                            pu = banks[(k % 2) * 2 + 1]
                            sg = sgb[k % 2]
                            k += 1
                            for dc in range(8):
                                P.mm(pg.all(), W[:, dc, f2 * 128:(f2 + 1) * 128], uT[:, dc, sub * 512:(sub + 1) * 512],
                                     start=(dc == 0), stop=(dc == 7))
                            for dc in range(8):
                                P.mm(pu.all(), W[:, dc, 256 + f2 * 128:256 + (f2 + 1) * 128],
                                     uT[:, dc, sub * 512:(sub + 1) * 512], start=(dc == 0), stop=(dc == 7))
                            P.act(sg.all(), pg.all(), AF.Silu)
                            P.tt(actT[:, ffc, sub * 512:(sub + 1) * 512], sg.all(), pu.all(), ALU.mult)
                k = 0
                for dc in range(8):
                    W, key = next_slot((NFF, 128))
                    wload(W.all(), kp(w_dn[:, dc * 128:(dc + 1) * 128]), key)
                    for sub in range(2):
                        pf = banks[4 + (k % 2)]
                        sq = sqb[k % 2]
                        k += 1
                        for ffc in range(NFF):
                            P.mm(pf.all(), W[:, ffc, :], actT[:, ffc, sub * 512:(sub + 1) * 512],
                                 start=(ffc == 0), stop=(ffc == NFF - 1))
                        P.copy(fsb[:, dc, sub * 512:(sub + 1) * 512], pf.all(), eng="act")
                        P.act(sq.all(), pf.all(), AF.Square)
                        P.mm(banks[6 + sub].all(), ones.all(), sq.all(), start=(dc == 0), stop=(dc == 7))
                for sub in range(2):
                    rs = rstd[:, sub * 512:(sub + 1) * 512]
                    P.act(rs, banks[6 + sub].all(), AF.Ln, bias=epsv, scale=1.0 / D)
                    P.act(rs, rs, AF.Exp, scale=-0.5)
                k = 0
                for sub in range(2):
                    ts_ = T0 + sub * 512
                    for c in range(8):
                        tmp = tmpb[k % 2]
                        k += 1
                        P.stt(tmp.all(), fsb[:, c, sub * 512:(sub + 1) * 512], hgain(gi_post, c),
                              rstd[:, sub * 512:(sub + 1) * 512], ALU.mult, ALU.mult)
                        P.tt(hT[:, c, ts_:ts_ + 512], hT[:, c, ts_:ts_ + 512], tmp.all(), ALU.add, eng=HUP)
            A.release(m)

        def mixer():
            m = A.mark()
            TB = 512
            DB = int(os.environ.get("MK_DB", "1"))
            uTs = [A.alloc((8, TB), BF16) for _ in range(DB)]
            y_lrus = [A.alloc((8, TB), BF16) for _ in range(int(os.environ.get("MK_DBY", "1")))]
            qy = A.alloc((16, TB), BF16)
            msb = A.view_at(qy.off, (8, TB), F32)
            kT = A.alloc((4, 640), BF16)
            vS = A.alloc((5, 512), BF16)
            merged = A.alloc((8, TB), BF16)
            ropeC = A.alloc((TB,), F32)
            ropeS = A.alloc((TB,), F32)
            NT = int(os.environ.get("MK_NT", "19"))
            tmp = [A.alloc((516,), F32) for _ in range(NT)]
            Eb = [A.alloc((1024,), BF16) for _ in range(int(os.environ.get("MK_EB", "2")))]
            qbb = [A.alloc((512,), BF16) for _ in range(2)]
            sqb = [A.alloc((512,), BF16) for _ in range(2)]
            rstd = A.alloc((TB,), F32)
            Psw = A.alloc((128,), BF16)
            mask = A.alloc((1024,), BF16)
            xcar = A.alloc((8, 4), F32)
            hcar = A.alloc((8,), F32)
            ti = [0]
            bi = [0]
            qi = [0]

            def T():
                t = tmp[ti[0] % NT]
                ti[0] += 1
                return t

            def nb():
                b = banks[bi[0] % 6]
                bi[0] += 1
                return b

            def QB():
                b = qbb[qi[0] % 2]
                qi[0] += 1
                return b

            print("mixer arena top", A.top, "of", ARENA_BYTES)
            OFF = os.environ.get("MK_OFF", "0") == "1"
            PEX = "pool" if OFF else "dve"
            P.dma("pool", Psw.all(), cst_d[:, 0:128], "psw")
            P.dma("pool", mask.all(), cst_d[:, 128:1152], "msk")
            P.memset(xcar.all(), 0.0)
            P.memset(hcar.all(), 0.0)
            C1 = 0.7978845608028654
            C2 = C1 * 0.044715
            cvw = lambda k, c: pp[:, PP_CONVW + k * 8 + c:PP_CONVW + k * 8 + c + 1]
            cvb = lambda c: pp[:, PP_CONVB + c:PP_CONVB + c + 1]
            lcv = lambda base, c: lc[:, base + c:base + c + 1]
            F = slice(0, 512)
            pstride = ARENA_BYTES // 4

            for tb in range(S // TB):
                t0 = tb * TB
                uT = uTs[tb % DB]
                y_lru = y_lrus[tb % len(y_lrus)]
                P.phase = "mix%d.lru" % tb
                prenorm(2, uT, t0, TB, sqb, rstd, (6, 7))
                P.dma("sp", ropeC.all(), rope_d[0][:, t0:t0 + TB], "rc")
                P.dma("sp", ropeS.all(), rope_d[1][:, t0:t0 + TB], "rs")

                for cp in range(4 if SUB >= 1 else 0):
                    W, key = next_slot((8, 512))
                    wload(W[:, :, 0:256], kp(w_in[:, cp * 256:(cp + 1) * 256]), key)
                    wload(W[:, :, 256:512], kp(w_in[:, 1024 + cp * 256:1024 + (cp + 1) * 256]), key)
                    for c2 in range(2):
                        c = cp * 2 + c2
                        pg = nb()
                        px = nb()
                        for dc in range(8):
                            P.mm(px.all(), W[:, dc, 256 + c2 * 128:256 + (c2 + 1) * 128], uT[:, dc, :],
                                 start=(dc == 0), stop=(dc == 7))
                        for dc in range(8):
                            P.mm(pg.all(), W[:, dc, c2 * 128:(c2 + 1) * 128], uT[:, dc, :],
                                 start=(dc == 0), stop=(dc == 7))
                        xf = T()
                        P.copy(xf[:, 0:3], xcar[:, c, 0:3], eng="pool")
                        P.copy(xf[:, 3:515], px.all(), eng="act")
                        P.copy(xcar[:, c, 0:3], xf[:, 512:515], eng="pool")
                        if M2L < 2:
                            continue
                        xc = T()
                        P.ts(xc[:, F], xf[:, 0:512], cvw(0, c), cvb(c), ALU.mult, ALU.add, eng=PEX)
                        for k in range(1, 4):
                            P.stt(xc[:, F], xf[:, k:k + 512], cvw(k, c), xc[:, F], ALU.mult, ALU.add)
                        xcb = QB()
                        P.copy(xcb.all(), xc[:, F], eng="act")
                        pr = nb()
                        pi = nb()
                        P.mm(pr.all(), BDa[:, c, :], xcb.all())
                        P.mm(pi.all(), BDx[:, c, :], xcb.all())
                        if M2L < 3:
                            continue
                        thr = T()
                        thi = T()
                        P.act(thr[:, F], pr.all(), AF.Tanh, bias=lcv(32, c), scale=0.5)
                        P.act(thi[:, F], pi.all(), AF.Tanh, bias=lcv(40, c), scale=0.5)
                        mu = T()
                        P.act(mu[:, F], thr[:, F], AF.Exp, bias=lcv(16, c), scale=lcv(16, c))
                        a = thr
                        P.act(a[:, F], thr[:, F], AF.Exp, bias=lcv(24, c), scale=lcv(24, c))
                        sq = T()
                        P.act(sq[:, F], pg.all(), AF.Square)
                        P.ts(sq[:, F], sq[:, F], C2, C1, ALU.mult, ALU.add, eng=PEX)
                        P.tt(sq[:, F], sq[:, F], pg.all(), ALU.mult)
                        P.act(sq[:, F], sq[:, F], AF.Tanh)
                        P.act(mu[:, F], mu[:, F], AF.Sqrt, bias=onev, scale=-1.0)
                        if M2L < 4:
                            continue
                        t1 = thi
                        P.stt(t1[:, F], thi[:, F], 1.0, xc[:, F], ALU.add, ALU.mult)
                        P.stt(t1[:, F], t1[:, F], 0.5, mu[:, F], ALU.mult, ALU.mult)
                        if M2L < 5:
                            continue
                        hs = xc
                        P.scan(hs[:, F], a[:, F], t1[:, F], hcar[:, c:c + 1], ALU.mult, ALU.add)
                        if M2L < 6:
                            continue
                        P.copy(hcar[:, c:c + 1], hs[:, 511:512], eng="dve")
                        P.stt(sq[:, F], sq[:, F], 1.0, pg.all(), ALU.add, ALU.mult)
                        P.stt(y_lru[:, c, :], sq[:, F], 0.5, hs[:, F], ALU.mult, ALU.mult)

                if SUB < 2:
                    continue
                P.phase = "mix%d.qkv" % tb
                RCL = int(os.environ.get("MK_RC", "9"))

                def rope_chunk(pq, outv):
                    if RCL < 1:
                        return
                    qb = QB()
                    P.copy(qb.all(), pq.all(), eng="act")
                    ps = nb()
                    P.mm(ps.all(), (ones if os.environ.get("MK_X") == "1" else Psw).all(), qb.all())
                    if RCL < 2:
                        return
                    r1 = T()
                    r2 = T()
                    P.tt(r1[:, F], ropeC.all(), pq.all(), ALU.mult)
                    P.tt(r2[:, F], ropeS.all(), ps.all(), ALU.mult)
                    if RCL >= 3:
                        P.tt(outv, r1[:, F], r2[:, F], ALU.add, eng=PEX)

                for qp in range(2):
                    W, key = next_slot((8, 512))
                    wload(W.all(), kp(w_in[:, 2048 + qp * 512:2048 + (qp + 1) * 512]), key)
                    for c4 in range(4):
                        pq = nb()
                        for dc in range(8):
                            P.mm(pq.all(), W[:, dc, c4 * 128:(c4 + 1) * 128], uT[:, dc, :], start=(dc == 0), stop=(dc == 7))
                        rope_chunk(pq, qy[:, qp * 4 + c4, :])
                W, key = next_slot((8, 512))
                wload(W.all(), kp(w_kv[:, 0:512]), key)
                for j in range(4 if M3L >= 2 else 0):
                    pk = nb()
                    for dc in range(8):
                        P.mm(pk.all(), W[:, dc, j * 128:(j + 1) * 128], uT[:, dc, :], start=(dc == 0), stop=(dc == 7))
                    rope_chunk(pk, kT[:, j, 128:640])
                W, key = next_slot((8, 512))
                wload(W.all(), kp(w_kv[:, 512:1024]), key)
                for i in range(4 if M3L >= 3 else 0):
                    pv = nb()
                    for dc in range(8):
                        P.mm(pv.all(), uT[:, dc, i * 128:(i + 1) * 128], W[:, dc, :], start=(dc == 0), stop=(dc == 7))
                    P.copy(vS[:, 1 + i, :], pv.all(), eng="act")

                P.phase = "mix%d.att" % tb
                k_ = 0
                for n in range(4 if SUB >= 3 else 0):
                    nglob = tb * 4 + n
                    kbs = (0, 1) if nglob > 0 else (1,)
                    for j in range(4):
                        pb = (k_ % 2) * 2
                        S2 = Buf(psum_t[:, pb * 512:(pb + 2) * 512].rearrange("p (h k g q) -> p h k g q", h=2, k=2, g=2),
                                 "P", pb * 2048, 4, (2, 2, 2, 128))
                        po = banks[4 + (k_ % 2)]
                        den = banks[6 + (k_ % 2)]
                        E = Eb[k_ % len(Eb)]
                        k_ += 1
                        for kb in kbs:
                            kc = slice((n + kb) * 128, (n + kb + 1) * 128)
                            for hh2 in range(2):
                                for half in range(2):
                                    rows = slice(half * 64, half * 64 + 64)
                                    P.mm(S2[:, half, kb, hh2, :], kT[rows, j, kc],
                                         qy[rows, 2 * j + hh2, n * 128:(n + 1) * 128])
                        P.act(E.all(), S2.all(), AF.Exp, scale=0.125)
                        P.tt(E.all(), E.all(), mask.all(), ALU.mult, eng=("pool" if os.environ.get("MK_MASKPOOL", "0") == "1" else "dve"))
                        E4 = Buf(E.ap.rearrange("p (h k g q) -> p h k g q", h=2, k=2, g=2), "S", E.off, 2, (2, 2, 2, 128))
                        for idx, kb in enumerate(kbs):
                            P.mm(po.all(), vS[:, n + kb, j * 128:(j + 1) * 128], E4[:, :, kb, :, :],
                                 start=(idx == 0), stop=(idx == len(kbs) - 1))
                        for idx, kb in enumerate(kbs):
                            P.mm(den.all(), ones.all(), E4[:, :, kb, :, :],
                                 start=(idx == 0), stop=(idx == len(kbs) - 1))
                        rec = T()
                        sink_ap = bass.AP(arena_t, lc.off // 4 + 48 + 4 * j, [[pstride, 128], [1, 2], [2, 2], [0, 128]])
                        sinkv = V(sink_ap, lc[:, 48:64].regs)
                        rec4 = Buf(rec.ap[:, 0:512].rearrange("p (h g q) -> p h g q", h=2, g=2), "S", rec.off, 4, (2, 2, 128))
                        den4 = Buf(den.ap.rearrange("p (h g q) -> p h g q", h=2, g=2), "P", den.off, 4, (2, 2, 128))
                        po4 = Buf(po.ap.rearrange("p (h g q) -> p h g q", h=2, g=2), "P", po.off, 4, (2, 2, 128))
                        P.tt(rec4.all(), den4.all(), sinkv, ALU.add)
                        P.act(rec[:, F], rec[:, F], AF.Ln)
                        P.act(rec[:, F], rec[:, F], AF.Exp, scale=-1.0)
                        for half in range(2):
                            rows = slice(half * 64, half * 64 + 64)
                            P.tt(qy[rows, 8 + 2 * j:8 + 2 * j + 2, n * 128:(n + 1) * 128], po4[rows, half, :, :],
                                 rec4[rows, half, :, :], ALU.mult)
                if tb < S // TB - 1:
                    P.copy(kT[:, :, 0:128], kT[:, :, 512:640], eng="pool")
                    P.copy(vS[:, 0, :], vS[:, 4, :], eng="pool")

                if SUB < 4:
                    continue
                P.phase = "mix%d.mrg" % tb
                for g in range(2):
                    sg = [T() for _ in range(4)]
                    sa = [T() for _ in range(4)]
                    W, key = next_slot((8, 512))
                    wload(W.all(), kp(w_in[:, 3584 + g * 512:3584 + (g + 1) * 512]), key)
                    for d4 in range(4):
                        p_ = nb()
                        for dc in range(8):
                            P.mm(p_.all(), W[:, dc, d4 * 128:(d4 + 1) * 128], uT[:, dc, :], start=(dc == 0), stop=(dc == 7))
                        P.act(sg[d4][:, F], p_.all(), AF.Sigmoid)
                    W, key = next_slot((8, 512))
                    wload(W.all(), kp(w_pl[:, g * 512:(g + 1) * 512]), key)
                    for d4 in range(4):
                        p_ = nb()
                        for dc in range(8):
                            P.mm(p_.all(), W[:, dc, d4 * 128:(d4 + 1) * 128], y_lru[:, dc, :], start=(dc == 0), stop=(dc == 7))
                        P.tt(sg[d4][:, F], sg[d4][:, F], p_.all(), ALU.mult)
                    W, key = next_slot((8, 512))
                    wload(W.all(), kp(w_in[:, 4608 + g * 512:4608 + (g + 1) * 512]), key)
                    for d4 in range(4):
                        p_ = nb()
                        for dc in range(8):
                            P.mm(p_.all(), W[:, dc, d4 * 128:(d4 + 1) * 128], uT[:, dc, :], start=(dc == 0), stop=(dc == 7))
                        P.act(sa[d4][:, F], p_.all(), AF.Sigmoid)
                    W, key = next_slot((8, 512))
                    wload(W.all(), kp(w_pa[:, g * 512:(g + 1) * 512]), key)
                    for d4 in range(4):
                        p_ = nb()
                        for dc in range(8):
                            P.mm(p_.all(), W[:, dc, d4 * 128:(d4 + 1) * 128], qy[:, 8 + dc, :], start=(dc == 0), stop=(dc == 7))
                        P.tt(sa[d4][:, F], sa[d4][:, F], p_.all(), ALU.mult)
                        P.tt(merged[:, g * 4 + d4, :], sa[d4][:, F], sg[d4][:, F], ALU.add, eng=PEX)

                P.phase = "mix%d.out" % tb
                k_ = 0
                for g in range(2):
                    W, key = next_slot((8, 512))
                    wload(W.all(), kp(w_o[:, g * 512:(g + 1) * 512]), key)
                    for d4 in range(4):
                        dcp = g * 4 + d4
                        p_ = nb()
                        sq = sqb[k_ % 2]
                        k_ += 1
                        for dc in range(8):
                            P.mm(p_.all(), W[:, dc, d4 * 128:(d4 + 1) * 128], merged[:, dc, :], start=(dc == 0), stop=(dc == 7))
                        P.copy(msb[:, dcp, :], p_.all(), eng="act")
                        P.act(sq.all(), p_.all(), AF.Square)
                        P.mm(banks[6].all(), ones.all(), sq.all(), start=(dcp == 0), stop=(dcp == 7))
                P.act(rstd.all(), banks[6].all(), AF.Ln, bias=epsv, scale=1.0 / D)
                P.act(rstd.all(), rstd.all(), AF.Exp, scale=-0.5)
                for c in range(8):
                    tm = T()
                    P.stt(tm[:, F], msb[:, c, :], gain(3, c), rstd.all(), ALU.mult, ALU.mult)
                    P.tt(hT[:, c, t0:t0 + TB], hT[:, c, t0:t0 + TB], tm[:, F], ALU.add, eng=HUP)
            A.release(m)

        if stage >= 1:
            ffn(0, 1, w_gu1, w_dn1)
        if stage >= 2:
            mixer()
        if stage >= 3:
            ffn(4, 5, w_gu2, w_dn2)

        for sub in range(4):
            for c in range(8):
                P.dma("sp", outT[c * 128:(c + 1) * 128, sub * 512:(sub + 1) * 512], hT[:, c, sub * 512:(sub + 1) * 512],
                      "o%d" % sub)

        P.emit(nc)
        print("arena peak bytes", A.peak, "ops", {e: len(P.ops[e]) for e in ENGS})
    return nc


_CACHE = {}


def _rope_tables():
    half = 32
    inv_freq = 10000.0 ** (-np.arange(half, dtype=np.float64) / half)
    ang = np.arange(S, dtype=np.float64)[:, None] * inv_freq[None, :]
    cos = np.cos(ang).T
    sin = np.sin(ang).T
    C = np.zeros((128, S), np.float32)
    Sg = np.zeros((128, S), np.float32)
    for p in range(128):
        i = p % 32
        C[p] = cos[i]
        Sg[p] = -sin[i] if (p % 64) < 32 else sin[i]
    return np.stack([C, Sg], 0)


def _consts():
    c = np.zeros((128, 128 + 1024), np.float32)
    for m in range(128):
        k = m + 32 if (m % 64) < 32 else m - 32
        c[k, m] = 1.0
    s = np.arange(128)[:, None]
    q = np.arange(128)[None, :]
    prev = (q < s).astype(np.float32)
    cur = (q >= s).astype(np.float32)
    mk = np.concatenate([prev, prev, cur, cur, prev, prev, cur, cur], axis=1)
    c[:, 128:] = mk
    return c


def kernel(**inp):
    stage = int(os.environ.get("MK_STAGE", "3"))
    if stage not in _CACHE:
        _CACHE[stage] = build_program(stage)
    nc = _CACHE[stage]
    f = lambda a: np.ascontiguousarray(np.asarray(a, dtype=np.float32))
    x = f(inp["x"])

    def col(v):
        return f(v).reshape(8, 128).T

    pp = np.zeros((128, NPP), np.float32)
    for i, nm in enumerate(["ffn1_pre_g", "ffn1_post_g", "mix_pre_g", "mix_post_g", "ffn2_pre_g", "ffn2_post_g"]):
        pp[:, PP_GAIN + i * 8:PP_GAIN + (i + 1) * 8] = col(inp[nm][0])
    cw = f(inp["conv_w"][0])
    for k in range(4):
        pp[:, PP_CONVW + k * 8:PP_CONVW + (k + 1) * 8] = col(cw[k])
    pp[:, PP_CONVB:PP_CONVB + 8] = col(inp["conv_b"][0])
    pp[:, PP_BA:PP_BA + 8] = col(inp["lru_b_a"][0])
    pp[:, PP_BX:PP_BX + 8] = col(inp["lru_b_x"][0])
    pp[:, PP_LAM:PP_LAM + 8] = col(inp["lru_lambda"][0])
    pp[:, PP_SINK:PP_SINK + 16] = np.broadcast_to(f(inp["attn_sinks"][0])[None, :], (128, 16))

    w_in = f(inp["w_in"][0])
    kcols = w_in[:, 3072:3328].reshape(D, 4, 64)
    vcols = w_in[:, 3328:3584].reshape(D, 4, 64)
    w_kv = np.concatenate([np.repeat(kcols[:, :, None, :], 2, axis=2).reshape(D, 512),
                           np.repeat(vcols[:, :, None, :], 2, axis=2).reshape(D, 512)], axis=1)
    shared = {
        "pp": pp,
        "w_gu1": f(inp["ffn1_w_gu"][0]), "w_dn1": f(inp["ffn1_w_down"][0]),
        "w_gu2": f(inp["ffn2_w_gu"][0]), "w_dn2": f(inp["ffn2_w_down"][0]),
        "w_in": w_in, "w_kv": f(w_kv),
        "w_pl": f(inp["w_proj_lru"][0]), "w_pa": f(inp["w_proj_attn"][0]), "w_o": f(inp["w_out"][0]),
        "lw_a": f(inp["lru_w_a"][0]), "lw_x": f(inp["lru_w_x"][0]),
        "rope": _rope_tables(), "cst": _consts(),
    }
    in_maps = []
    for b in range(NCORES):
        m = dict(shared)
        m["xT"] = np.ascontiguousarray(x[b].T)
        in_maps.append(m)
    res = run_bass_kernel_spmd(nc, in_maps, core_ids=list(range(NCORES)))
    out = np.stack([np.asarray(res.results[b]["outT"]).T for b in range(NCORES)], axis=0)
    return np.ascontiguousarray(out.astype(np.float32))
```

```python
import os
from contextlib import ExitStack

import numpy as np
import concourse.bass as bass
import concourse.mybir as mybir
from concourse.bass_utils import run_bass_kernel_spmd

F32 = mybir.dt.float32
BF16 = mybir.dt.bfloat16
AF = mybir.ActivationFunctionType
ALU = mybir.AluOpType

D = 1024
S = 2048
B = 8
DFF = 2816
NFF = DFF // 128
EPS = 1e-6
NCORES = 8

ENGS = ("pe", "act", "dve", "pool", "sp")


class V:
    __slots__ = ("ap", "regs")

    def __init__(self, ap, regs):
        self.ap = ap
        self.regs = regs


class Buf:
    def __init__(self, ap, space, byte_off, esize, free_shape):
        self.ap = ap
        self.space = space
        self.off = byte_off
        self.esize = esize
        self.fs = tuple(free_shape)
        st = []
        acc = 1
        for d in reversed(self.fs):
            st.append(acc)
            acc *= d
        self.strides = tuple(reversed(st))
        self.nbytes = acc * esize

    def __getitem__(self, key):
        if not isinstance(key, tuple):
            key = (key,)
        ap = self.ap[key]
        fk = key[1:]
        rng = []
        for i, d in enumerate(self.fs):
            if i < len(fk):
                k = fk[i]
                if isinstance(k, slice):
                    a = 0 if k.start is None else k.start
                    b = d if k.stop is None else k.stop
                else:
                    a, b = k, k + 1
            else:
                a, b = 0, d
            assert 0 <= a < b <= d, (key, self.fs)
            rng.append((a, b))
        j = -1
        for i, (a, b) in enumerate(rng):
            if (a, b) != (0, self.fs[i]):
                j = i
        if j < 0:
            return self.all()
        combos = [0]
        for i in range(j):
            a, b = rng[i]
            combos = [c + x * self.strides[i] for c in combos for x in range(a, b)]
            if len(combos) > 64:
                combos = None
                break
        if combos is None:
            lo = sum(a * s for (a, b), s in zip(rng, self.strides))
            hi = sum((b - 1) * s for (a, b), s in zip(rng, self.strides))
            return V(ap, [(self.space, self.off + lo * self.esize, self.off + (hi + 1) * self.esize)])
        a, b = rng[j]
        st = self.strides[j]
        regs = [(self.space, self.off + (c + a * st) * self.esize, self.off + (c + b * st) * self.esize)
                for c in combos]
        regs.sort()
        out = [regs[0]]
        for r in regs[1:]:
            if r[1] == out[-1][2]:
                out[-1] = (out[-1][0], out[-1][1], r[2])
            else:
                out.append(r)
        return V(ap, out)

    def all(self):
        return V(self.ap, [(self.space, self.off, self.off + self.nbytes)])


class Op:
    __slots__ = ("eng", "fn", "idx", "deps", "ddeps", "is_dma", "dma_key", "dma_cnt", "signal", "cnt", "waits",
                 "gid", "odeps", "dmadeps", "cost", "nbytes", "fin", "succ", "npred", "deps_all", "phase", "st", "why", "tag", "tset")


class Rec:
    __slots__ = ("lo", "hi", "op", "w")

    def __init__(self, lo, hi, op, w):
        self.lo = lo
        self.hi = hi
        self.op = op
        self.w = w


class Prog:
    def __init__(self):
        self.ops = {e: [] for e in ENGS}
        self.recs = {"S": [], "P": []}
        self.dma_cnt = {}
        self.out_dmas = []
        self.ngid = 0
        self.phase = "setup"
        self.last_dma = {}

    def add(self, eng, fn, reads=(), writes=(), dma_key=None, cost=0.3, nbytes=0):
        op = Op()
        op.eng = eng
        op.fn = fn
        op.gid = self.ngid
        op.phase = self.phase
        self.ngid += 1
        op.cost = cost
        op.nbytes = nbytes
        op.odeps = []
        op.dmadeps = []
        op.tset = None
        op.idx = len(self.ops[eng])
        op.is_dma = dma_key is not None
        op.dma_key = dma_key
        op.signal = False
        op.cnt = 0
        op.waits = None
        if op.is_dma:
            self.dma_cnt[dma_key] = self.dma_cnt.get(dma_key, 0) + 16
            op.dma_cnt = self.dma_cnt[dma_key]
        deps = {}
        ddeps = {}

        def add_dep(d, raw):
            if d is op:
                return
            if d.is_dma:
                k = ("d", d.dma_key)
                v = self.dma_cnt[d.dma_key] - (16 if (op.is_dma and op.dma_key == d.dma_key) else 0)
                if v > ddeps.get(k, 0):
                    ddeps[k] = v
                op.dmadeps.append(d)
                return
            if d.eng == eng and not op.is_dma:
                if eng == "pe":
                    op.odeps.append(d)
                    return
            deps[id(d)] = d

        for v in reads:
            for (sp, lo, hi) in v.regs:
                for r in self.recs[sp]:
                    if r.w and r.lo < hi and lo < r.hi:
                        add_dep(r.op, True)
                    elif (sp == "P" and not r.w and r.op.eng != eng
                          and r.lo // 2048 <= (hi - 1) // 2048 and lo // 2048 <= (r.hi - 1) // 2048):
                        add_dep(r.op, True)
        for v in writes:
            for (sp, lo, hi) in v.regs:
                for r in self.recs[sp]:
                    if r.lo < hi and lo < r.hi:
                        add_dep(r.op, False)
        op.deps = list(deps.values())
        op.ddeps = ddeps
        inorder = not op.is_dma
        for v in reads:
            for (sp, lo, hi) in v.regs:
                lst = self.recs[sp]
                if inorder:
                    keep = []
                    for r in lst:
                        if (not r.w) and (not r.op.is_dma) and r.op.eng == eng and lo <= r.lo and r.hi <= hi:
                            if r.op is not op:
                                op.odeps.append(r.op)
                        else:
                            keep.append(r)
                    lst[:] = keep
                lst.append(Rec(lo, hi, op, False))
        for v in writes:
            for (sp, lo, hi) in v.regs:
                lst = self.recs[sp]
                lst[:] = [r for r in lst if not (lo <= r.lo and r.hi <= hi)]
                lst.append(Rec(lo, hi, op, True))
        if op.is_dma:
            prev = self.last_dma.get(eng)
            if prev is not None:
                op.odeps.append(prev)
            self.last_dma[eng] = op
        self.ops[eng].append(op)
        return op

    def schedule(self, window=int(os.environ.get("MK_WIN", "600"))):
        import heapq
        allops = [op for e in ENGS for op in self.ops[e]]
        for op in allops:
            op.succ = []
            op.fin = None
        for op in allops:
            preds = {}
            for d in op.deps:
                preds[id(d)] = d
            for d in op.odeps:
                preds[id(d)] = d
            for d in op.dmadeps:
                preds[id(d)] = d
            op.npred = len(preds)
            for d in preds.values():
                d.succ.append(op)
            op.deps_all = None
        ready = {e: [] for e in ENGS}
        for op in allops:
            if op.npred == 0:
                heapq.heappush(ready[op.eng], (op.gid, id(op), op))
        efree = {e: 0.0 for e in ENGS}
        dma_free = [0.0]
        order = {e: [] for e in ENGS}
        LAT = 0.25
        nleft = len(allops)
        mingid = {e: 0 for e in ENGS}

        elast = {e: None for e in ENGS}
        cur_set = [None]
        def set_pen(op):
            if op.eng != "act" or op.tset is None or cur_set[0] is None:
                return 0.0
            a, b = cur_set[0], op.tset
            if a == b or (a in "ET" and b in "ET") or (a in "GT" and b in "GT"):
                return 0.0
            return 1.3

        def est_start(op, why=False):
            t = efree[op.eng]
            w = ("eng", elast[op.eng])
            for d in op.deps:
                x = d.fin + (LAT if d.eng != op.eng else 0.05)
                if x > t:
                    t = x
                    w = ("dep", d)
            for d in op.dmadeps:
                x = d.fin + LAT
                if x > t:
                    t = x
                    w = ("dma", d)
            for d in op.odeps:
                x = d.fin if not d.is_dma else d.cnt
                if x > t:
                    t = x
                    w = ("ord", d)
            if why:
                op.why = w
            return t

        while nleft:
            best = None
            for e in ENGS:
                h = ready[e]
                if not h:
                    continue
                g0 = h[0][0]
                cand = heapq.nsmallest(int(os.environ.get("MK_CAND", "12")), h)
                for (g, _, op) in cand:
                    if g - g0 > window:
                        break
                    s = est_start(op) + set_pen(op)
                    if best is None or (s, g) < (best[0], best[1]):
                        best = (s, g, op)
            s, g, op = best
            h = ready[op.eng]
            h.remove((op.gid, id(op), op))
            heapq.heapify(h)
            op.st = s
            if op.eng == "act" and op.tset is not None:
                cur_set[0] = op.tset
            est_start(op, True)
            elast[op.eng] = op
            if op.is_dma:
                issue = 1.15 if op.eng == "pool" else 0.15
                efree[op.eng] = s + issue
                op.cnt = s + issue
                t0 = max(s + issue, dma_free[0])
                dur = op.nbytes / 300e3
                dma_free[0] = t0 + dur
                op.fin = t0 + dur + 2.0
            else:
                op.fin = s + op.cost
                efree[op.eng] = op.fin
            order[op.eng].append(op)
            nleft -= 1
            for sc in op.succ:
                sc.npred -= 1
                if sc.npred == 0:
                    heapq.heappush(ready[sc.eng], (sc.gid, id(sc), sc))
        for e in ENGS:
            assert len(order[e]) == len(self.ops[e])
            self.ops[e] = order[e]
            for i, op in enumerate(order[e]):
                op.idx = i
                op.cnt = 0
        self.est_time = max(op.fin for op in allops)
        if os.environ.get("MK_CRIT") == "1":
            last = max(allops, key=lambda o: o.fin)
            agg = {}
            o = last
            n = 0
            while o is not None and n < 200000:
                kind, p = o.why
                k = (o.phase, o.eng, kind)
                agg[k] = agg.get(k, 0.0) + (o.fin - (p.fin if p is not None else 0.0))
                o = p
                n += 1
            for k in sorted(agg, key=lambda k: -agg[k])[:40]:
                print("CRIT %-10s %-5s %-4s %7.1f" % (k[0], k[1], k[2], agg[k]))
        if os.environ.get("MK_REPORT") == "1":
            ph = {}
            for op in allops:
                d = ph.setdefault(op.phase, {"t0": 1e18, "t1": 0.0, "pe": 0.0, "act": 0.0, "dve": 0.0, "pool": 0.0, "sp": 0.0})
                d["t0"] = min(d["t0"], op.st)
                if not op.is_dma:
                    d["t1"] = max(d["t1"], op.fin)
                    d[op.eng] += op.cost
            for k, d in ph.items():
                print("%-10s t0 %7.1f t1 %7.1f span %6.1f | pe %6.1f act %6.1f dve %6.1f pool %6.1f" % (
                    k, d["t0"], d["t1"], d["t1"] - d["t0"], d["pe"], d["act"], d["dve"], d["pool"]))

    def mm(self, out, lhsT, rhs, start=True, stop=True, **kw):
        n = rhs.ap.free_size()
        return self.add("pe", lambda e: e.matmul(out.ap, lhsT.ap, rhs.ap, start=start, stop=stop, **kw),
                        reads=[lhsT, rhs], writes=[out], cost=(0.276 * n / 512.0 if n >= 256 else 0.105))

    def act(self, out, in_, func, bias=None, scale=None, eng="act"):
        reads = [in_]
        kw = {}
        if bias is not None:
            if isinstance(bias, V):
                reads.append(bias)
                kw["bias"] = bias.ap
            else:
                kw["bias"] = float(bias)
        if scale is not None:
            if isinstance(scale, V):
                reads.append(scale)
                kw["scale"] = scale.ap
            else:
                kw["scale"] = float(scale)
        op = self.add(eng, lambda e: e.activation(out.ap, in_.ap, func, **kw), reads=reads, writes=[out],
                      cost=(in_.ap.free_size() + 260) / 1400.0)
        op.tset = {AF.Exp: "E", AF.Ln: "E", AF.Tanh: "T", AF.Sqrt: "Q", AF.Silu: "U", AF.Sigmoid: "G", AF.Gelu_apprx_tanh: "L"}.get(func)
        return op

    def tt(self, out, in0, in1, op, eng="dve"):
        return self.add(eng, lambda e: e.tensor_tensor(out.ap, in0.ap, in1.ap, op), reads=[in0, in1], writes=[out],
                        cost=self.vcost(eng, in0.ap.free_size()))

    def ts(self, out, in0, s1, s2, op0, op1=None, eng="dve"):
        reads = [in0]
        a1 = s1
        a2 = s2
        if isinstance(s1, V):
            reads.append(s1)
            a1 = s1.ap
        if isinstance(s2, V):
            reads.append(s2)
            a2 = s2.ap
        c = self.vcost(eng, in0.ap.free_size())
        if op1 is None:
            return self.add(eng, lambda e: e.tensor_scalar(out.ap, in0.ap, a1, None, op0), reads=reads, writes=[out], cost=c)
        return self.add(eng, lambda e: e.tensor_scalar(out.ap, in0.ap, a1, a2, op0, op1), reads=reads, writes=[out], cost=c)

    def stt(self, out, in0, scalar, in1, op0, op1):
        reads = [in0, in1]
        a = scalar
        if isinstance(scalar, V):
            reads.append(scalar)
            a = scalar.ap
        return self.add("dve", lambda e: e.scalar_tensor_tensor(out.ap, in0.ap, a, in1.ap, op0, op1),
                        reads=reads, writes=[out], cost=in0.ap.free_size() / 640.0 + 0.1)

    def recip(self, out, in_):
        return self.add("dve", lambda e: e.reciprocal(out.ap, in_.ap), reads=[in_], writes=[out],
                        cost=in_.ap.free_size() / 156.0 + 0.05)

    def copy(self, out, in_, eng="dve"):
        if eng == "act":
            return self.add("act", lambda e: e.copy(out.ap, in_.ap), reads=[in_], writes=[out],
                            cost=(in_.ap.free_size() + 260) / 1400.0)
        return self.add(eng, lambda e: e.tensor_copy(out.ap, in_.ap), reads=[in_], writes=[out],
                        cost=self.vcost(eng, in_.ap.free_size()))

    def scan(self, out, d0, d1, init, op0, op1):
        return self.add("dve", lambda e: e.tensor_tensor_scan(out.ap, d0.ap, d1.ap, init.ap, op0, op1),
                        reads=[d0, d1, init], writes=[out], cost=2 * d0.ap.free_size() / 960.0 + 0.15)

    def memset(self, out, val, eng="pool"):
        return self.add(eng, lambda e: e.memset(out.ap, val), writes=[out], cost=self.vcost(eng, out.ap.free_size()))

    @staticmethod
    def vcost(eng, n):
        if eng == "pool":
            return n / 450.0 + 0.25
        return n / 1000.0 + 0.09

    def dma(self, eng, out, in_, key, **kw):
        reads = [in_] if isinstance(in_, V) else []
        writes = [out] if isinstance(out, V) else []
        oa = out.ap if isinstance(out, V) else out
        ia = in_.ap if isinstance(in_, V) else in_
        sb = out if isinstance(out, V) else in_
        nb_ = sb.ap.partition_size() * sb.ap.free_size() * 4
        op = self.add(eng, lambda e: e.dma_start(out=oa, in_=ia, **kw), reads=reads, writes=writes, dma_key=key,
                      nbytes=nb_)
        if not isinstance(out, V):
            self.out_dmas.append(op)
        return op

    def emit(self, nc, final_eng="sp"):
        if os.environ.get("MK_SCHED", "1") == "1":
            self.schedule()
            print("scheduler estimate us", round(self.est_time, 1))
        fin = Op()
        fin.eng = final_eng
        fin.fn = None
        fin.idx = len(self.ops[final_eng])
        fin.is_dma = False
        fin.dma_key = None
        fin.signal = False
        fin.cnt = 0
        fin.deps = []
        fin.ddeps = {("d", o.dma_key): self.dma_cnt[o.dma_key] for o in self.out_dmas}
        self.ops[final_eng].append(fin)

        for eng in ENGS:
            seen = {}
            for op in self.ops[eng]:
                w = {}
                for k, val in op.ddeps.items():
                    if seen.get(k, 0) >= val:
                        continue
                    w[k] = (val, None)
                for d in op.deps:
                    k = ("e", d.eng)
                    if seen.get(k, -1) >= d.idx:
                        continue
                    if k not in w or w[k][0] < d.idx:
                        w[k] = (d.idx, d)
                for k, (val, d) in w.items():
                    seen[k] = val
                    if d is not None:
                        d.signal = True
                op.waits = list(w.items())
        for eng in ENGS:
            c = 0
            for op in self.ops[eng]:
                if op.signal:
                    c += 1
                    op.cnt = c

        with ExitStack() as st:
            esem = {e: st.enter_context(nc.semaphore("s_" + e)) for e in ENGS}
            dsem = {k: st.enter_context(nc.semaphore("d_" + str(k))) for k in self.dma_cnt}
            block = st.enter_context(nc.Block())

            def run(eng, e):
                for op in self.ops[eng]:
                    for k, (val, d) in op.waits:
                        if k[0] == "d":
                            e.wait_ge(dsem[k[1]], val)
                        else:
                            e.wait_ge(esem[k[1]], d.cnt)
                    if op.fn is None:
                        continue
                    ins = op.fn(e)
                    if op.is_dma:
                        ins.then_inc(dsem[op.dma_key], 16)
                    elif op.signal:
                        ins.then_inc(esem[eng], 1)

            @block.tensor
            def _(e):
                run("pe", e)

            @block.scalar
            def _(e):
                run("act", e)

            @block.vector
            def _(e):
                run("dve", e)

            @block.gpsimd
            def _(e):
                run("pool", e)

            @block.sync
            def _(e):
                run("sp", e)


class Arena:
    def __init__(self, arena_ap, nbytes):
        self.ap = arena_ap
        self.n = nbytes
        self.top = 0
        self.peak = 0

    def alloc(self, free_shape, dtype):
        es = 2 if dtype == BF16 else 4
        n = es
        for d in free_shape:
            n *= d
        off = (self.top + 63) // 64 * 64
        assert off + n <= self.n, ("arena overflow", off, n, self.n)
        self.top = off + n
        self.peak = max(self.peak, self.top)
        ap = self.ap[:, off // 4:(off + n + 3) // 4]
        if dtype == BF16:
            ap = ap.bitcast(BF16)
        if len(free_shape) == 2:
            ap = ap.rearrange("p (a b) -> p a b", a=free_shape[0])
        elif len(free_shape) == 3:
            ap = ap.rearrange("p (a b c) -> p a b c", a=free_shape[0], b=free_shape[1])
        elif len(free_shape) == 4:
            ap = ap.rearrange("p (a b c d) -> p a b c d", a=free_shape[0], b=free_shape[1], c=free_shape[2])
        return Buf(ap, "S", off, es, free_shape)

    def view_at(self, off, free_shape, dtype):
        es = 2 if dtype == BF16 else 4
        n = es
        for d in free_shape:
            n *= d
        ap = self.ap[:, off // 4:(off + n + 3) // 4]
        if dtype == BF16:
            ap = ap.bitcast(BF16)
        if len(free_shape) == 2:
            ap = ap.rearrange("p (a b) -> p a b", a=free_shape[0])
        return Buf(ap, "S", off, es, free_shape)

    def mark(self):
        return self.top

    def release(self, m):
        self.top = m


ARENA_BYTES = int(os.environ.get("MK_ARENA", "212480"))
SLOT_BYTES = 8192
NSLOTS = 4

PP_GAIN = 0
PP_CONVW = 48
PP_CONVB = 80
PP_BA = 88
PP_BX = 96
PP_LAM = 104
PP_SINK = 112
NPP = 128


def build_program(stage):
    HUP = os.environ.get("MK_HUP", "dve")
    SUB = int(os.environ.get("MK_SUB", "9"))
    M2L = int(os.environ.get("MK_M2", "9"))
    M3L = int(os.environ.get("MK_M3", "9"))
    nc = bass.Bass("TRN2", target_bir_lowering=False)
    dram = {}

    def din(name, shape, dt=F32):
        dram[name] = nc.dram_tensor(name, list(shape), dt, kind="ExternalInput").ap()
        return dram[name]

    xT = din("xT", (D, S))
    pp_d = din("pp", (128, NPP))
    w_gu1 = din("w_gu1", (D, 2 * DFF))
    w_dn1 = din("w_dn1", (DFF, D))
    w_gu2 = din("w_gu2", (D, 2 * DFF))
    w_dn2 = din("w_dn2", (DFF, D))
    w_in = din("w_in", (D, 5632))
    w_kv = din("w_kv", (D, 1024))
    w_pl = din("w_pl", (D, D))
    w_pa = din("w_pa", (D, D))
    w_o = din("w_o", (D, D))
    lw_a = din("lw_a", (16, 64, 64))
    lw_x = din("lw_x", (16, 64, 64))
    rope_d = din("rope", (2, 128, S))
    cst_d = din("cst", (128, 128 + 1024))
    outT = nc.dram_tensor("outT", [D, S], F32, kind="ExternalOutput").ap()

    P = Prog()
    with ExitStack() as st:
        arena_t = st.enter_context(nc.sbuf_tensor("arena", [128, ARENA_BYTES // 4], F32))
        psum_t = st.enter_context(nc.psum_tensor("psum", [128, 4096], F32))
        A = Arena(arena_t, ARENA_BYTES)

        def bank(i, n=1):
            return Buf(psum_t[:, i * 512:(i + n) * 512], "P", i * 2048, 4, (n * 512,))

        banks = [bank(i) for i in range(8)]

        hT = A.alloc((8, S), F32)
        pp = A.alloc((NPP,), F32)
        hp = A.alloc((48,), F32)
        cf = A.alloc((8,), F32)
        lc = A.alloc((64,), F32)
        ones = A.alloc((128,), BF16)
        BDa = A.alloc((8, 128), BF16)
        BDx = A.alloc((8, 128), BF16)
        slots = [A.alloc((SLOT_BYTES // 2,), BF16) for _ in range(NSLOTS)]
        slot_i = [0]

        def next_slot(shape):
            i = slot_i[0] % NSLOTS
            slot_i[0] += 1
            sb = slots[i]
            n = 1
            for d in shape:
                n *= d
            assert n * 2 <= SLOT_BYTES
            ap = sb.ap[:, 0:n]
            if len(shape) == 2:
                ap = ap.rearrange("p (a b) -> p a b", a=shape[0])
            return Buf(ap, "S", sb.off, 2, shape), "w%d" % i

        def wload(dst, src, key):
            P.dma("pool", dst, src, key)

        def kp(ap):
            return ap.rearrange("(k p) n -> p k n", p=128)

        P.dma("sp", pp.all(), pp_d, "pp")
        for sub in range(4):
            for c in range(8):
                P.dma("sp", hT[:, c, sub * 512:(sub + 1) * 512], xT[c * 128:(c + 1) * 128, sub * 512:(sub + 1) * 512],
                      "x%d" % sub)
        P.memset(ones.all(), 1.0)
        P.memset(cf[:, 0:1], EPS)
        P.memset(cf[:, 1:2], 1.0)
        P.ts(hp.all(), pp[:, 0:48], 0.5, None, ALU.mult)
        epsv = cf[:, 0:1]
        onev = cf[:, 1:2]

        def gain(i, c):
            return pp[:, PP_GAIN + i * 8 + c:PP_GAIN + i * 8 + c + 1]

        def hgain(i, c):
            return hp[:, i * 8 + c:i * 8 + c + 1]

        if stage >= 2:
            P.memset(BDa.all(), 0.0)
            P.memset(BDx.all(), 0.0)
            for (bd, lw, key) in ((BDa, lw_a, "bda"), (BDx, lw_x, "bdx")):
                src = lw.rearrange("(c two) i o -> two i c o", two=2)
                for two in range(2):
                    P.dma("pool", bd[two * 64:(two + 1) * 64, :, two * 64:(two + 1) * 64], src[two], key)
            X = lc[:, 0:8]
            T_ = lc[:, 8:16]
            CL = lc[:, 16:24]
            HCL = lc[:, 24:32]
            P.act(X, pp[:, PP_LAM:PP_LAM + 8], AF.Exp, scale=-1.0)
            P.ts(T_, X, 1.0 / 3.0, -0.5, ALU.mult, ALU.add)
            P.tt(T_, T_, X, ALU.mult)
            P.ts(T_, T_, 1.0, None, ALU.add)
            P.tt(T_, T_, X, ALU.mult)
            P.ts(CL, T_, -8.0, None, ALU.mult)
            P.ts(HCL, T_, -4.0, None, ALU.mult)
            P.ts(lc[:, 32:48], pp[:, PP_BA:PP_BA + 16], 0.5, None, ALU.mult)
            P.act(lc[:, 48:64], pp[:, PP_SINK:PP_SINK + 16], AF.Exp)

        def prenorm(gi, uT, t0, ntok, sq_bufs, rstd, ssb):
            k = 0
            for sub in range(ntok // 512):
                ts_ = t0 + sub * 512
                ss = banks[ssb[sub % len(ssb)]]
                rs = rstd[:, sub * 512:(sub + 1) * 512]
                for c in range(8):
                    sq = sq_bufs[k % len(sq_bufs)]
                    k += 1
                    P.act(sq.all(), hT[:, c, ts_:ts_ + 512], AF.Square)
                    P.mm(ss.all(), ones.all(), sq.all(), start=(c == 0), stop=(c == 7))
                P.act(rs, ss.all(), AF.Ln, bias=epsv, scale=1.0 / D)
                P.act(rs, rs, AF.Exp, scale=-0.5)
                for c in range(8):
                    P.stt(uT[:, c, sub * 512:(sub + 1) * 512], hT[:, c, ts_:ts_ + 512], gain(gi, c),
                          rs, ALU.mult, ALU.mult)

        def ffn(gi_pre, gi_post, w_gu, w_dn):
            m = A.mark()
            TB = 1024
            uT = A.alloc((8, TB), BF16)
            actT = A.alloc((NFF, TB), BF16)
            fsb = A.alloc((8, TB), F32)
            sqb = [A.alloc((512,), BF16) for _ in range(2)]
            rstd = A.alloc((TB,), F32)
            sgb = [A.alloc((512,), F32) for _ in range(2)]
            tmpb = [A.alloc((512,), F32) for _ in range(2)]
            for tb in range(S // TB):
                T0 = tb * TB
                P.phase = "ffn%d.%d" % (gi_pre // 4 + 1, tb)
                prenorm(gi_pre, uT, T0, TB, sqb, rstd, (6, 7))
                k = 0
                for grp in range(NFF // 2):
                    W, key = next_slot((8, 512))
                    wload(W[:, :, 0:256], kp(w_gu[:, grp * 256:(grp + 1) * 256]), key)
                    wload(W[:, :, 256:512], kp(w_gu[:, DFF + grp * 256:DFF + (grp + 1) * 256]), key)
                    for f2 in range(2):
                        ffc = grp * 2 + f2
                        for sub in range(2):
                            pg = banks[(k % 2) * 2]
                            pu = banks[(k % 2) * 2 + 1]
                            sg = sgb[k % 2]
                            k += 1
                            for dc in range(8):
                                P.mm(pg.all(), W[:, dc, f2 * 128:(f2 + 1) * 128], uT[:, dc, sub * 512:(sub + 1) * 512],
                                     start=(dc == 0), stop=(dc == 7))
                            for dc in range(8):
                                P.mm(pu.all(), W[:, dc, 256 + f2 * 128:256 + (f2 + 1) * 128],
                                     uT[:, dc, sub * 512:(sub + 1) * 512], start=(dc == 0), stop=(dc == 7))
                            P.act(sg.all(), pg.all(), AF.Silu)
                            P.tt(actT[:, ffc, sub * 512:(sub + 1) * 512], sg.all(), pu.all(), ALU.mult)
                k = 0
                for dc in range(8):
                    W, key = next_slot((NFF, 128))
                    wload(W.all(), kp(w_dn[:, dc * 128:(dc + 1) * 128]), key)
                    for sub in range(2):
                        pf = banks[4 + (k % 2)]
                        sq = sqb[k % 2]
                        k += 1
                        for ffc in range(NFF):
                            P.mm(pf.all(), W[:, ffc, :], actT[:, ffc, sub * 512:(sub + 1) * 512],
                                 start=(ffc == 0), stop=(ffc == NFF - 1))
                        P.copy(fsb[:, dc, sub * 512:(sub + 1) * 512], pf.all(), eng="act")
                        P.act(sq.all(), pf.all(), AF.Square)
                        P.mm(banks[6 + sub].all(), ones.all(), sq.all(), start=(dc == 0), stop=(dc == 7))
                for sub in range(2):
                    rs = rstd[:, sub * 512:(sub + 1) * 512]
                    P.act(rs, banks[6 + sub].all(), AF.Ln, bias=epsv, scale=1.0 / D)
                    P.act(rs, rs, AF.Exp, scale=-0.5)
                k = 0
                for sub in range(2):
                    ts_ = T0 + sub * 512
                    for c in range(8):
                        tmp = tmpb[k % 2]
                        k += 1
                        P.stt(tmp.all(), fsb[:, c, sub * 512:(sub + 1) * 512], hgain(gi_post, c),
                              rstd[:, sub * 512:(sub + 1) * 512], ALU.mult, ALU.mult)
                        P.tt(hT[:, c, ts_:ts_ + 512], hT[:, c, ts_:ts_ + 512], tmp.all(), ALU.add, eng=HUP)
            A.release(m)

        def mixer():
            m = A.mark()
            TB = 512
            DB = int(os.environ.get("MK_DB", "1"))
            uTs = [A.alloc((8, TB), BF16) for _ in range(DB)]
            y_lrus = [A.alloc((8, TB), BF16) for _ in range(int(os.environ.get("MK_DBY", "1")))]
            qy = A.alloc((16, TB), BF16)
            msb = A.view_at(qy.off, (8, TB), F32)
            kT = A.alloc((4, 640), BF16)
            vS = A.alloc((5, 512), BF16)
            merged = A.alloc((8, TB), BF16)
            ropeC = A.alloc((TB,), F32)
            ropeS = A.alloc((TB,), F32)
            NT = int(os.environ.get("MK_NT", "19"))
            tmp = [A.alloc((516,), F32) for _ in range(NT)]
            Eb = [A.alloc((1024,), BF16) for _ in range(int(os.environ.get("MK_EB", "2")))]
            qbb = [A.alloc((512,), BF16) for _ in range(2)]
            sqb = [A.alloc((512,), BF16) for _ in range(2)]
            rstd = A.alloc((TB,), F32)
            Psw = A.alloc((128,), BF16)
            mask = A.alloc((1024,), BF16)
            xcar = A.alloc((8, 4), F32)
            hcar = A.alloc((8,), F32)
            ti = [0]
            bi = [0]
            qi = [0]

            def T():
                t = tmp[ti[0] % NT]
                ti[0] += 1
                return t

            def nb():
                b = banks[bi[0] % 6]
                bi[0] += 1
                return b

            def QB():
                b = qbb[qi[0] % 2]
                qi[0] += 1
                return b

            print("mixer arena top", A.top, "of", ARENA_BYTES)
            OFF = os.environ.get("MK_OFF", "0") == "1"
            PEX = "pool" if OFF else "dve"
            P.dma("pool", Psw.all(), cst_d[:, 0:128], "psw")
            P.dma("pool", mask.all(), cst_d[:, 128:1152], "msk")
            P.memset(xcar.all(), 0.0)
            P.memset(hcar.all(), 0.0)
            C1 = 0.7978845608028654
            C2 = C1 * 0.044715
            cvw = lambda k, c: pp[:, PP_CONVW + k * 8 + c:PP_CONVW + k * 8 + c + 1]
            cvb = lambda c: pp[:, PP_CONVB + c:PP_CONVB + c + 1]
            lcv = lambda base, c: lc[:, base + c:base + c + 1]
            F = slice(0, 512)
            pstride = ARENA_BYTES // 4

            for tb in range(S // TB):
                t0 = tb * TB
                uT = uTs[tb % DB]
                y_lru = y_lrus[tb % len(y_lrus)]
                P.phase = "mix%d.lru" % tb
                prenorm(2, uT, t0, TB, sqb, rstd, (6, 7))
                P.dma("sp", ropeC.all(), rope_d[0][:, t0:t0 + TB], "rc")
                P.dma("sp", ropeS.all(), rope_d[1][:, t0:t0 + TB], "rs")

                for cp in range(4 if SUB >= 1 else 0):
                    W, key = next_slot((8, 512))
                    wload(W[:, :, 0:256], kp(w_in[:, cp * 256:(cp + 1) * 256]), key)
                    wload(W[:, :, 256:512], kp(w_in[:, 1024 + cp * 256:1024 + (cp + 1) * 256]), key)
                    for c2 in range(2):
                        c = cp * 2 + c2
                        pg = nb()
                        px = nb()
                        for dc in range(8):
                            P.mm(px.all(), W[:, dc, 256 + c2 * 128:256 + (c2 + 1) * 128], uT[:, dc, :],
                                 start=(dc == 0), stop=(dc == 7))
                        for dc in range(8):
                            P.mm(pg.all(), W[:, dc, c2 * 128:(c2 + 1) * 128], uT[:, dc, :],
                                 start=(dc == 0), stop=(dc == 7))
                        xf = T()
                        P.copy(xf[:, 0:3], xcar[:, c, 0:3], eng="pool")
                        P.copy(xf[:, 3:515], px.all(), eng="act")
                        P.copy(xcar[:, c, 0:3], xf[:, 512:515], eng="pool")
                        if M2L < 2:
                            continue
                        xc = T()
                        P.ts(xc[:, F], xf[:, 0:512], cvw(0, c), cvb(c), ALU.mult, ALU.add, eng=PEX)
                        for k in range(1, 4):
                            P.stt(xc[:, F], xf[:, k:k + 512], cvw(k, c), xc[:, F], ALU.mult, ALU.add)
                        xcb = QB()
                        P.copy(xcb.all(), xc[:, F], eng="act")
                        pr = nb()
                        pi = nb()
                        P.mm(pr.all(), BDa[:, c, :], xcb.all())
                        P.mm(pi.all(), BDx[:, c, :], xcb.all())
                        if M2L < 3:
                            continue
                        thr = T()
                        thi = T()
                        P.act(thr[:, F], pr.all(), AF.Tanh, bias=lcv(32, c), scale=0.5)
                        P.act(thi[:, F], pi.all(), AF.Tanh, bias=lcv(40, c), scale=0.5)
                        mu = T()
                        P.act(mu[:, F], thr[:, F], AF.Exp, bias=lcv(16, c), scale=lcv(16, c))
                        a = thr
                        P.act(a[:, F], thr[:, F], AF.Exp, bias=lcv(24, c), scale=lcv(24, c))
                        sq = T()
                        NG = os.environ.get("MK_NGELU", "1") == "1"
                        if NG:
                            P.act(sq[:, F], pg.all(), AF.Gelu_apprx_tanh)
                        else:
                            P.act(sq[:, F], pg.all(), AF.Square)
                            P.ts(sq[:, F], sq[:, F], C2, C1, ALU.mult, ALU.add, eng=PEX)
                            P.tt(sq[:, F], sq[:, F], pg.all(), ALU.mult)
                            P.act(sq[:, F], sq[:, F], AF.Tanh)
                        P.act(mu[:, F], mu[:, F], AF.Sqrt, bias=onev, scale=-1.0)
                        if M2L < 4:
                            continue
                        t1 = thi
                        P.stt(t1[:, F], thi[:, F], 1.0, xc[:, F], ALU.add, ALU.mult)
                        P.stt(t1[:, F], t1[:, F], 0.5, mu[:, F], ALU.mult, ALU.mult)
                        if M2L < 5:
                            continue
                        hs = xc
                        P.scan(hs[:, F], a[:, F], t1[:, F], hcar[:, c:c + 1], ALU.mult, ALU.add)
                        if M2L < 6:
                            continue
                        P.copy(hcar[:, c:c + 1], hs[:, 511:512], eng="dve")
                        if NG:
                            P.tt(y_lru[:, c, :], sq[:, F], hs[:, F], ALU.mult)
                        else:
                            P.stt(sq[:, F], sq[:, F], 1.0, pg.all(), ALU.add, ALU.mult)
                            P.stt(y_lru[:, c, :], sq[:, F], 0.5, hs[:, F], ALU.mult, ALU.mult)

                if SUB < 2:
                    continue
                P.phase = "mix%d.qkv" % tb
                RCL = int(os.environ.get("MK_RC", "9"))

                def rope_chunk(pq, outv):
                    if RCL < 1:
                        return
                    qb = QB()
                    P.copy(qb.all(), pq.all(), eng="act")
                    ps = nb()
                    P.mm(ps.all(), (ones if os.environ.get("MK_X") == "1" else Psw).all(), qb.all())
                    if RCL < 2:
                        return
                    r1 = T()
                    r2 = T()
                    P.tt(r1[:, F], ropeC.all(), pq.all(), ALU.mult)
                    P.tt(r2[:, F], ropeS.all(), ps.all(), ALU.mult)
                    if RCL >= 3:
                        P.tt(outv, r1[:, F], r2[:, F], ALU.add, eng=PEX)

                for qp in range(2):
                    W, key = next_slot((8, 512))
                    wload(W.all(), kp(w_in[:, 2048 + qp * 512:2048 + (qp + 1) * 512]), key)
                    for c4 in range(4):
                        pq = nb()
                        for dc in range(8):
                            P.mm(pq.all(), W[:, dc, c4 * 128:(c4 + 1) * 128], uT[:, dc, :], start=(dc == 0), stop=(dc == 7))
                        rope_chunk(pq, qy[:, qp * 4 + c4, :])
                W, key = next_slot((8, 512))
                wload(W.all(), kp(w_kv[:, 0:512]), key)
                for j in range(4 if M3L >= 2 else 0):
                    pk = nb()
                    for dc in range(8):
                        P.mm(pk.all(), W[:, dc, j * 128:(j + 1) * 128], uT[:, dc, :], start=(dc == 0), stop=(dc == 7))
                    rope_chunk(pk, kT[:, j, 128:640])
                W, key = next_slot((8, 512))
                wload(W.all(), kp(w_kv[:, 512:1024]), key)
                for i in range(4 if M3L >= 3 else 0):
                    pv = nb()
                    for dc in range(8):
                        P.mm(pv.all(), uT[:, dc, i * 128:(i + 1) * 128], W[:, dc, :], start=(dc == 0), stop=(dc == 7))
                    P.copy(vS[:, 1 + i, :], pv.all(), eng="act")

                P.phase = "mix%d.att" % tb
                k_ = 0
                for n in range(4 if SUB >= 3 else 0):
                    nglob = tb * 4 + n
                    kbs = (0, 1) if nglob > 0 else (1,)
                    for j in range(4):
                        pb = (k_ % 2) * 2
                        S2 = Buf(psum_t[:, pb * 512:(pb + 2) * 512].rearrange("p (h k g q) -> p h k g q", h=2, k=2, g=2),
                                 "P", pb * 2048, 4, (2, 2, 2, 128))
                        po = banks[4 + (k_ % 2)]
                        den = banks[6 + (k_ % 2)]
                        E = Eb[k_ % len(Eb)]
                        k_ += 1
                        for kb in kbs:
                            kc = slice((n + kb) * 128, (n + kb + 1) * 128)
                            for hh2 in range(2):
                                for half in range(2):
                                    rows = slice(half * 64, half * 64 + 64)
                                    P.mm(S2[:, half, kb, hh2, :], kT[rows, j, kc],
                                         qy[rows, 2 * j + hh2, n * 128:(n + 1) * 128])
                        P.act(E.all(), S2.all(), AF.Exp, scale=0.125)
                        P.tt(E.all(), E.all(), mask.all(), ALU.mult, eng=("pool" if os.environ.get("MK_MASKPOOL", "0") == "1" else "dve"))
                        E4 = Buf(E.ap.rearrange("p (h k g q) -> p h k g q", h=2, k=2, g=2), "S", E.off, 2, (2, 2, 2, 128))
                        for idx, kb in enumerate(kbs):
                            P.mm(po.all(), vS[:, n + kb, j * 128:(j + 1) * 128], E4[:, :, kb, :, :],
                                 start=(idx == 0), stop=(idx == len(kbs) - 1))
                        for idx, kb in enumerate(kbs):
                            P.mm(den.all(), ones.all(), E4[:, :, kb, :, :],
                                 start=(idx == 0), stop=(idx == len(kbs) - 1))
                        rec = T()
                        sink_ap = bass.AP(arena_t, lc.off // 4 + 48 + 4 * j, [[pstride, 128], [1, 2], [2, 2], [0, 128]])
                        sinkv = V(sink_ap, lc[:, 48:64].regs)
                        rec4 = Buf(rec.ap[:, 0:512].rearrange("p (h g q) -> p h g q", h=2, g=2), "S", rec.off, 4, (2, 2, 128))
                        den4 = Buf(den.ap.rearrange("p (h g q) -> p h g q", h=2, g=2), "P", den.off, 4, (2, 2, 128))
                        po4 = Buf(po.ap.rearrange("p (h g q) -> p h g q", h=2, g=2), "P", po.off, 4, (2, 2, 128))
                        P.tt(rec4.all(), den4.all(), sinkv, ALU.add)
                        P.act(rec[:, F], rec[:, F], AF.Ln)
                        P.act(rec[:, F], rec[:, F], AF.Exp, scale=-1.0)
                        for half in range(2):
                            rows = slice(half * 64, half * 64 + 64)
                            P.tt(qy[rows, 8 + 2 * j:8 + 2 * j + 2, n * 128:(n + 1) * 128], po4[rows, half, :, :],
                                 rec4[rows, half, :, :], ALU.mult)
                if tb < S // TB - 1:
                    P.copy(kT[:, :, 0:128], kT[:, :, 512:640], eng="pool")
                    P.copy(vS[:, 0, :], vS[:, 4, :], eng="pool")

                if SUB < 4:
                    continue
                P.phase = "mix%d.mrg" % tb
                for g in range(2):
                    sg = [T() for _ in range(4)]
                    sa = [T() for _ in range(4)]
                    W, key = next_slot((8, 512))
                    wload(W.all(), kp(w_in[:, 3584 + g * 512:3584 + (g + 1) * 512]), key)
                    for d4 in range(4):
                        p_ = nb()
                        for dc in range(8):
                            P.mm(p_.all(), W[:, dc, d4 * 128:(d4 + 1) * 128], uT[:, dc, :], start=(dc == 0), stop=(dc == 7))
                        P.act(sg[d4][:, F], p_.all(), AF.Sigmoid)
                    W, key = next_slot((8, 512))
                    wload(W.all(), kp(w_pl[:, g * 512:(g + 1) * 512]), key)
                    for d4 in range(4):
                        p_ = nb()
                        for dc in range(8):
                            P.mm(p_.all(), W[:, dc, d4 * 128:(d4 + 1) * 128], y_lru[:, dc, :], start=(dc == 0), stop=(dc == 7))
                        P.tt(sg[d4][:, F], sg[d4][:, F], p_.all(), ALU.mult)
                    W, key = next_slot((8, 512))
                    wload(W.all(), kp(w_in[:, 4608 + g * 512:4608 + (g + 1) * 512]), key)
                    for d4 in range(4):
                        p_ = nb()
                        for dc in range(8):
                            P.mm(p_.all(), W[:, dc, d4 * 128:(d4 + 1) * 128], uT[:, dc, :], start=(dc == 0), stop=(dc == 7))
                        P.act(sa[d4][:, F], p_.all(), AF.Sigmoid)
                    W, key = next_slot((8, 512))
                    wload(W.all(), kp(w_pa[:, g * 512:(g + 1) * 512]), key)
                    for d4 in range(4):
                        p_ = nb()
                        for dc in range(8):
                            P.mm(p_.all(), W[:, dc, d4 * 128:(d4 + 1) * 128], qy[:, 8 + dc, :], start=(dc == 0), stop=(dc == 7))
                        P.tt(sa[d4][:, F], sa[d4][:, F], p_.all(), ALU.mult)
                        P.tt(merged[:, g * 4 + d4, :], sa[d4][:, F], sg[d4][:, F], ALU.add, eng=PEX)

                P.phase = "mix%d.out" % tb
                k_ = 0
                for g in range(2):
                    W, key = next_slot((8, 512))
                    wload(W.all(), kp(w_o[:, g * 512:(g + 1) * 512]), key)
                    for d4 in range(4):
                        dcp = g * 4 + d4
                        p_ = nb()
                        sq = sqb[k_ % 2]
                        k_ += 1
                        for dc in range(8):
                            P.mm(p_.all(), W[:, dc, d4 * 128:(d4 + 1) * 128], merged[:, dc, :], start=(dc == 0), stop=(dc == 7))
                        P.copy(msb[:, dcp, :], p_.all(), eng="act")
                        P.act(sq.all(), p_.all(), AF.Square)
                        P.mm(banks[6].all(), ones.all(), sq.all(), start=(dcp == 0), stop=(dcp == 7))
                P.act(rstd.all(), banks[6].all(), AF.Ln, bias=epsv, scale=1.0 / D)
                P.act(rstd.all(), rstd.all(), AF.Exp, scale=-0.5)
                for c in range(8):
                    tm = T()
                    P.stt(tm[:, F], msb[:, c, :], gain(3, c), rstd.all(), ALU.mult, ALU.mult)
                    P.tt(hT[:, c, t0:t0 + TB], hT[:, c, t0:t0 + TB], tm[:, F], ALU.add, eng=HUP)
            A.release(m)

        if stage >= 1:
            ffn(0, 1, w_gu1, w_dn1)
        if stage >= 2:
            mixer()
        if stage >= 3:
            ffn(4, 5, w_gu2, w_dn2)

        for sub in range(4):
            for c in range(8):
                P.dma("sp", outT[c * 128:(c + 1) * 128, sub * 512:(sub + 1) * 512], hT[:, c, sub * 512:(sub + 1) * 512],
                      "o%d" % sub)

        P.emit(nc)
        print("arena peak bytes", A.peak, "ops", {e: len(P.ops[e]) for e in ENGS})
    return nc


_CACHE = {}


def _rope_tables():
    half = 32
    inv_freq = 10000.0 ** (-np.arange(half, dtype=np.float64) / half)
    ang = np.arange(S, dtype=np.float64)[:, None] * inv_freq[None, :]
    cos = np.cos(ang).T
    sin = np.sin(ang).T
    C = np.zeros((128, S), np.float32)
    Sg = np.zeros((128, S), np.float32)
    for p in range(128):
        i = p % 32
        C[p] = cos[i]
        Sg[p] = -sin[i] if (p % 64) < 32 else sin[i]
    return np.stack([C, Sg], 0)


def _consts():
    c = np.zeros((128, 128 + 1024), np.float32)
    for m in range(128):
        k = m + 32 if (m % 64) < 32 else m - 32
        c[k, m] = 1.0
    s = np.arange(128)[:, None]
    q = np.arange(128)[None, :]
    prev = (q < s).astype(np.float32)
    cur = (q >= s).astype(np.float32)
    mk = np.concatenate([prev, prev, cur, cur, prev, prev, cur, cur], axis=1)
    c[:, 128:] = mk
    return c


def kernel(**inp):
    stage = int(os.environ.get("MK_STAGE", "3"))
    if stage not in _CACHE:
        _CACHE[stage] = build_program(stage)
    nc = _CACHE[stage]
    f = lambda a: np.ascontiguousarray(np.asarray(a, dtype=np.float32))
    x = f(inp["x"])

    def col(v):
        return f(v).reshape(8, 128).T

    pp = np.zeros((128, NPP), np.float32)
    for i, nm in enumerate(["ffn1_pre_g", "ffn1_post_g", "mix_pre_g", "mix_post_g", "ffn2_pre_g", "ffn2_post_g"]):
        pp[:, PP_GAIN + i * 8:PP_GAIN + (i + 1) * 8] = col(inp[nm][0])
    cw = f(inp["conv_w"][0])
    for k in range(4):
        pp[:, PP_CONVW + k * 8:PP_CONVW + (k + 1) * 8] = col(cw[k])
    pp[:, PP_CONVB:PP_CONVB + 8] = col(inp["conv_b"][0])
    pp[:, PP_BA:PP_BA + 8] = col(inp["lru_b_a"][0])
    pp[:, PP_BX:PP_BX + 8] = col(inp["lru_b_x"][0])
    pp[:, PP_LAM:PP_LAM + 8] = col(inp["lru_lambda"][0])
    pp[:, PP_SINK:PP_SINK + 16] = np.broadcast_to(f(inp["attn_sinks"][0])[None, :], (128, 16))

    w_in = f(inp["w_in"][0])
    kcols = w_in[:, 3072:3328].reshape(D, 4, 64)
    vcols = w_in[:, 3328:3584].reshape(D, 4, 64)
    w_kv = np.concatenate([np.repeat(kcols[:, :, None, :], 2, axis=2).reshape(D, 512),
                           np.repeat(vcols[:, :, None, :], 2, axis=2).reshape(D, 512)], axis=1)
    shared = {
        "pp": pp,
        "w_gu1": f(inp["ffn1_w_gu"][0]), "w_dn1": f(inp["ffn1_w_down"][0]),
        "w_gu2": f(inp["ffn2_w_gu"][0]), "w_dn2": f(inp["ffn2_w_down"][0]),
        "w_in": w_in, "w_kv": f(w_kv),
        "w_pl": f(inp["w_proj_lru"][0]), "w_pa": f(inp["w_proj_attn"][0]), "w_o": f(inp["w_out"][0]),
        "lw_a": f(inp["lru_w_a"][0]), "lw_x": f(inp["lru_w_x"][0]),
        "rope": _rope_tables(), "cst": _consts(),
    }
    in_maps = []
    for b in range(NCORES):
        m = dict(shared)
        m["xT"] = np.ascontiguousarray(x[b].T)
        in_maps.append(m)
    res = run_bass_kernel_spmd(nc, in_maps, core_ids=list(range(NCORES)))
    out = np.stack([np.asarray(res.results[b]["outT"]).T for b in range(NCORES)], axis=0)
    return np.ascontiguousarray(out.astype(np.float32))
```

```python
import os
from contextlib import ExitStack

import numpy as np
import concourse.bass as bass
import concourse.mybir as mybir
from concourse.bass_utils import run_bass_kernel_spmd

F32 = mybir.dt.float32
BF16 = mybir.dt.bfloat16
AF = mybir.ActivationFunctionType
ALU = mybir.AluOpType

D = 1024
S = 2048
B = 8
DFF = 2816
NFF = DFF // 128
EPS = 1e-6
NCORES = 8

ENGS = ("pe", "act", "dve", "pool", "sp")
PE_US = float(os.environ.get("MK_PEUS", "0.24"))


class V:
    __slots__ = ("ap", "regs")

    def __init__(self, ap, regs):
        self.ap = ap
        self.regs = regs


class Buf:
    def __init__(self, ap, space, byte_off, esize, free_shape):
        self.ap = ap
        self.space = space
        self.off = byte_off
        self.esize = esize
        self.fs = tuple(free_shape)
        st = []
        acc = 1
        for d in reversed(self.fs):
            st.append(acc)
            acc *= d
        self.strides = tuple(reversed(st))
        self.nbytes = acc * esize

    def __getitem__(self, key):
        if not isinstance(key, tuple):
            key = (key,)
        ap = self.ap[key]
        fk = key[1:]
        rng = []
        for i, d in enumerate(self.fs):
            if i < len(fk):
                k = fk[i]
                if isinstance(k, slice):
                    a = 0 if k.start is None else k.start
                    b = d if k.stop is None else k.stop
                else:
                    a, b = k, k + 1
            else:
                a, b = 0, d
            assert 0 <= a < b <= d, (key, self.fs)
            rng.append((a, b))
        j = -1
        for i, (a, b) in enumerate(rng):
            if (a, b) != (0, self.fs[i]):
                j = i
        if j < 0:
            return self.all()
        combos = [0]
        for i in range(j):
            a, b = rng[i]
            combos = [c + x * self.strides[i] for c in combos for x in range(a, b)]
            if len(combos) > 64:
                combos = None
                break
        if combos is None:
            lo = sum(a * s for (a, b), s in zip(rng, self.strides))
            hi = sum((b - 1) * s for (a, b), s in zip(rng, self.strides))
            return V(ap, [(self.space, self.off + lo * self.esize, self.off + (hi + 1) * self.esize)])
        a, b = rng[j]
        st = self.strides[j]
        regs = [(self.space, self.off + (c + a * st) * self.esize, self.off + (c + b * st) * self.esize)
                for c in combos]
        regs.sort()
        out = [regs[0]]
        for r in regs[1:]:
            if r[1] == out[-1][2]:
                out[-1] = (out[-1][0], out[-1][1], r[2])
            else:
                out.append(r)
        return V(ap, out)

    def all(self):
        return V(self.ap, [(self.space, self.off, self.off + self.nbytes)])


class Op:
    __slots__ = ("eng", "fn", "idx", "deps", "ddeps", "is_dma", "dma_key", "dma_cnt", "signal", "cnt", "waits",
                 "gid", "odeps", "dmadeps", "cost", "nbytes", "fin", "succ", "npred", "deps_all", "phase", "st", "why", "tag", "tset", "bl")


class Rec:
    __slots__ = ("lo", "hi", "op", "w")

    def __init__(self, lo, hi, op, w):
        self.lo = lo
        self.hi = hi
        self.op = op
        self.w = w


class Prog:
    def __init__(self):
        self.ops = {e: [] for e in ENGS}
        self.recs = {"S": [], "P": []}
        self.dma_cnt = {}
        self.out_dmas = []
        self.ngid = 0
        self.phase = "setup"
        self.last_dma = {}

    def add(self, eng, fn, reads=(), writes=(), dma_key=None, cost=0.3, nbytes=0):
        op = Op()
        op.eng = eng
        op.fn = fn
        op.gid = self.ngid
        op.phase = self.phase
        self.ngid += 1
        op.cost = cost
        op.nbytes = nbytes
        op.odeps = []
        op.dmadeps = []
        op.tset = None
        op.idx = len(self.ops[eng])
        op.is_dma = dma_key is not None
        op.dma_key = dma_key
        op.signal = False
        op.cnt = 0
        op.waits = None
        if op.is_dma:
            self.dma_cnt[dma_key] = self.dma_cnt.get(dma_key, 0) + 16
            op.dma_cnt = self.dma_cnt[dma_key]
        deps = {}
        ddeps = {}

        def add_dep(d, raw):
            if d is op:
                return
            if d.is_dma:
                k = ("d", d.dma_key)
                v = self.dma_cnt[d.dma_key] - (16 if (op.is_dma and op.dma_key == d.dma_key) else 0)
                if v > ddeps.get(k, 0):
                    ddeps[k] = v
                op.dmadeps.append(d)
                return
            if d.eng == eng and not op.is_dma:
                if eng == "pe":
                    op.odeps.append(d)
                    return
            deps[id(d)] = d

        for v in reads:
            for (sp, lo, hi) in v.regs:
                for r in self.recs[sp]:
                    if r.w and r.lo < hi and lo < r.hi:
                        add_dep(r.op, True)
                    elif (sp == "P" and not r.w and r.op.eng != eng
                          and r.lo // 2048 <= (hi - 1) // 2048 and lo // 2048 <= (r.hi - 1) // 2048):
                        add_dep(r.op, True)
        for v in writes:
            for (sp, lo, hi) in v.regs:
                for r in self.recs[sp]:
                    if r.lo < hi and lo < r.hi:
                        add_dep(r.op, False)
        op.deps = list(deps.values())
        op.ddeps = ddeps
        inorder = not op.is_dma
        for v in reads:
            for (sp, lo, hi) in v.regs:
                lst = self.recs[sp]
                if inorder:
                    keep = []
                    for r in lst:
                        if (not r.w) and (not r.op.is_dma) and r.op.eng == eng and lo <= r.lo and r.hi <= hi:
                            if r.op is not op:
                                op.odeps.append(r.op)
                        else:
                            keep.append(r)
                    lst[:] = keep
                lst.append(Rec(lo, hi, op, False))
        for v in writes:
            for (sp, lo, hi) in v.regs:
                lst = self.recs[sp]
                lst[:] = [r for r in lst if not (lo <= r.lo and r.hi <= hi)]
                lst.append(Rec(lo, hi, op, True))
        if op.is_dma:
            prev = self.last_dma.get(eng)
            if prev is not None:
                op.odeps.append(prev)
            self.last_dma[eng] = op
        self.ops[eng].append(op)
        return op

    def schedule(self, window=int(os.environ.get("MK_WIN", "600"))):
        import heapq
        allops = [op for e in ENGS for op in self.ops[e]]
        for op in allops:
            op.succ = []
            op.fin = None
        for op in allops:
            preds = {}
            for d in op.deps:
                preds[id(d)] = d
            for d in op.odeps:
                preds[id(d)] = d
            for d in op.dmadeps:
                preds[id(d)] = d
            op.npred = len(preds)
            for d in preds.values():
                d.succ.append(op)
            op.deps_all = None
        PRIO = os.environ.get("MK_PRIO", "1") == "1"
        SLACK = float(os.environ.get("MK_SLACK", "0.1"))
        for op in sorted(allops, key=lambda o: -o.gid):
            b = 0.0
            for sc in op.succ:
                if sc.bl > b:
                    b = sc.bl
            op.bl = b + (op.cost if not op.is_dma else 2.0 + op.nbytes / 300e3)
        ready = {e: [] for e in ENGS}
        for op in allops:
            if op.npred == 0:
                heapq.heappush(ready[op.eng], (op.gid, id(op), op))
        efree = {e: 0.0 for e in ENGS}
        dma_free = [0.0]
        order = {e: [] for e in ENGS}
        LAT = 0.25
        nleft = len(allops)
        mingid = {e: 0 for e in ENGS}

        elast = {e: None for e in ENGS}
        cur_set = [None]
        def set_pen(op):
            if op.eng != "act" or op.tset is None or cur_set[0] is None:
                return 0.0
            a, b = cur_set[0], op.tset
            if a == b or (a in "ET" and b in "ET") or (a in "GT" and b in "GT"):
                return 0.0
            return 1.3

        def est_start(op, why=False):
            t = efree[op.eng]
            w = ("eng", elast[op.eng])
            for d in op.deps:
                x = d.fin + (LAT if d.eng != op.eng else 0.05)
                if x > t:
                    t = x
                    w = ("dep", d)
            for d in op.dmadeps:
                x = d.fin + LAT
                if x > t:
                    t = x
                    w = ("dma", d)
            for d in op.odeps:
                x = d.fin if not d.is_dma else d.cnt
                if x > t:
                    t = x
                    w = ("ord", d)
            if why:
                op.why = w
            return t

        while nleft:
            best = None
            for e in ENGS:
                h = ready[e]
                if not h:
                    continue
                g0 = h[0][0]
                cand = heapq.nsmallest(int(os.environ.get("MK_CAND", "24")), h)
                if not PRIO:
                    for (g, _, op) in cand:
                        if g - g0 > window:
                            break
                        s = est_start(op) + set_pen(op)
                        if best is None or (s, g) < (best[0], best[1]):
                            best = (s, g, op)
                else:
                    lst = []
                    for (g, _, op) in cand:
                        if g - g0 > window:
                            break
                        lst.append((est_start(op) + set_pen(op), g, op))
                    smin = min(x[0] for x in lst)
                    pick = max((x for x in lst if x[0] <= smin + SLACK), key=lambda x: (x[2].bl, -x[1]))
                    if best is None or (pick[0], pick[1]) < (best[0], best[1]):
                        best = pick
            s, g, op = best
            h = ready[op.eng]
            h.remove((op.gid, id(op), op))
            heapq.heapify(h)
            op.st = s
            if op.eng == "act" and op.tset is not None:
                cur_set[0] = op.tset
            est_start(op, True)
            elast[op.eng] = op
            if op.is_dma:
                issue = 1.15 if op.eng == "pool" else 0.15
                efree[op.eng] = s + issue
                op.cnt = s + issue
                t0 = max(s + issue, dma_free[0])
                dur = op.nbytes / 300e3
                dma_free[0] = t0 + dur
                op.fin = t0 + dur + 2.0
            else:
                op.fin = s + op.cost
                efree[op.eng] = op.fin
            order[op.eng].append(op)
            nleft -= 1
            for sc in op.succ:
                sc.npred -= 1
                if sc.npred == 0:
                    heapq.heappush(ready[sc.eng], (sc.gid, id(sc), sc))
        for e in ENGS:
            assert len(order[e]) == len(self.ops[e])
            self.ops[e] = order[e]
            for i, op in enumerate(order[e]):
                op.idx = i
                op.cnt = 0
        self.est_time = max(op.fin for op in allops)
        if os.environ.get("MK_CRIT") == "1":
            last = max(allops, key=lambda o: o.fin)
            agg = {}
            o = last
            n = 0
            while o is not None and n < 200000:
                kind, p = o.why
                k = (o.phase, o.eng, kind)
                agg[k] = agg.get(k, 0.0) + (o.fin - (p.fin if p is not None else 0.0))
                o = p
                n += 1
            for k in sorted(agg, key=lambda k: -agg[k])[:40]:
                print("CRIT %-10s %-5s %-4s %7.1f" % (k[0], k[1], k[2], agg[k]))
        if os.environ.get("MK_REPORT") == "1":
            ph = {}
            for op in allops:
                d = ph.setdefault(op.phase, {"t0": 1e18, "t1": 0.0, "pe": 0.0, "act": 0.0, "dve": 0.0, "pool": 0.0, "sp": 0.0})
                d["t0"] = min(d["t0"], op.st)
                if not op.is_dma:
                    d["t1"] = max(d["t1"], op.fin)
                    d[op.eng] += op.cost
            for k, d in ph.items():
                print("%-10s t0 %7.1f t1 %7.1f span %6.1f | pe %6.1f act %6.1f dve %6.1f pool %6.1f" % (
                    k, d["t0"], d["t1"], d["t1"] - d["t0"], d["pe"], d["act"], d["dve"], d["pool"]))

    def mm(self, out, lhsT, rhs, start=True, stop=True, **kw):
        n = rhs.ap.free_size()
        return self.add("pe", lambda e: e.matmul(out.ap, lhsT.ap, rhs.ap, start=start, stop=stop, **kw),
                        reads=[lhsT, rhs], writes=[out], cost=(PE_US * n / 512.0 if n >= 256 else 0.09))

    def act(self, out, in_, func, bias=None, scale=None, eng="act"):
        reads = [in_]
        kw = {}
        if bias is not None:
            if isinstance(bias, V):
                reads.append(bias)
                kw["bias"] = bias.ap
            else:
                kw["bias"] = float(bias)
        if scale is not None:
            if isinstance(scale, V):
                reads.append(scale)
                kw["scale"] = scale.ap
            else:
                kw["scale"] = float(scale)
        op = self.add(eng, lambda e: e.activation(out.ap, in_.ap, func, **kw), reads=reads, writes=[out],
                      cost=(in_.ap.free_size() + 260) / 1400.0)
        op.tset = {AF.Exp: "E", AF.Ln: "E", AF.Tanh: "T", AF.Sqrt: "Q", AF.Silu: "U", AF.Sigmoid: "G", AF.Gelu_apprx_tanh: "L"}.get(func)
        return op

    def tt(self, out, in0, in1, op, eng="dve"):
        return self.add(eng, lambda e: e.tensor_tensor(out.ap, in0.ap, in1.ap, op), reads=[in0, in1], writes=[out],
                        cost=self.vcost(eng, in0.ap.free_size()))

    def ts(self, out, in0, s1, s2, op0, op1=None, eng="dve"):
        reads = [in0]
        a1 = s1
        a2 = s2
        if isinstance(s1, V):
            reads.append(s1)
            a1 = s1.ap
        if isinstance(s2, V):
            reads.append(s2)
            a2 = s2.ap
        c = self.vcost(eng, in0.ap.free_size())
        if op1 is None:
            return self.add(eng, lambda e: e.tensor_scalar(out.ap, in0.ap, a1, None, op0), reads=reads, writes=[out], cost=c)
        return self.add(eng, lambda e: e.tensor_scalar(out.ap, in0.ap, a1, a2, op0, op1), reads=reads, writes=[out], cost=c)

    def stt(self, out, in0, scalar, in1, op0, op1):
        reads = [in0, in1]
        a = scalar
        if isinstance(scalar, V):
            reads.append(scalar)
            a = scalar.ap
        return self.add("dve", lambda e: e.scalar_tensor_tensor(out.ap, in0.ap, a, in1.ap, op0, op1),
                        reads=reads, writes=[out], cost=in0.ap.free_size() / 640.0 + 0.1)

    def recip(self, out, in_):
        return self.add("dve", lambda e: e.reciprocal(out.ap, in_.ap), reads=[in_], writes=[out],
                        cost=in_.ap.free_size() / 156.0 + 0.05)

    def copy(self, out, in_, eng="dve"):
        if eng == "act":
            return self.add("act", lambda e: e.copy(out.ap, in_.ap), reads=[in_], writes=[out],
                            cost=(in_.ap.free_size() + 260) / 1400.0)
        return self.add(eng, lambda e: e.tensor_copy(out.ap, in_.ap), reads=[in_], writes=[out],
                        cost=self.vcost(eng, in_.ap.free_size()))

    def scan(self, out, d0, d1, init, op0, op1):
        return self.add("dve", lambda e: e.tensor_tensor_scan(out.ap, d0.ap, d1.ap, init.ap, op0, op1),
                        reads=[d0, d1, init], writes=[out], cost=2 * d0.ap.free_size() / 960.0 + 0.15)

    def memset(self, out, val, eng="pool"):
        return self.add(eng, lambda e: e.memset(out.ap, val), writes=[out], cost=self.vcost(eng, out.ap.free_size()))

    @staticmethod
    def vcost(eng, n):
        if eng == "pool":
            return n / 450.0 + 0.25
        return n / 1000.0 + 0.09

    def dma(self, eng, out, in_, key, **kw):
        reads = [in_] if isinstance(in_, V) else []
        writes = [out] if isinstance(out, V) else []
        oa = out.ap if isinstance(out, V) else out
        ia = in_.ap if isinstance(in_, V) else in_
        sb = out if isinstance(out, V) else in_
        nb_ = sb.ap.partition_size() * sb.ap.free_size() * 4
        op = self.add(eng, lambda e: e.dma_start(out=oa, in_=ia, **kw), reads=reads, writes=writes, dma_key=key,
                      nbytes=nb_)
        if not isinstance(out, V):
            self.out_dmas.append(op)
        return op

    def emit(self, nc, final_eng="sp"):
        if os.environ.get("MK_SCHED", "1") == "1":
            self.schedule()
            print("scheduler estimate us", round(self.est_time, 1))
        fin = Op()
        fin.eng = final_eng
        fin.fn = None
        fin.idx = len(self.ops[final_eng])
        fin.is_dma = False
        fin.dma_key = None
        fin.signal = False
        fin.cnt = 0
        fin.deps = []
        fin.ddeps = {("d", o.dma_key): self.dma_cnt[o.dma_key] for o in self.out_dmas}
        self.ops[final_eng].append(fin)

        for eng in ENGS:
            seen = {}
            for op in self.ops[eng]:
                w = {}
                for k, val in op.ddeps.items():
                    if seen.get(k, 0) >= val:
                        continue
                    w[k] = (val, None)
                for d in op.deps:
                    k = ("e", d.eng)
                    if seen.get(k, -1) >= d.idx:
                        continue
                    if k not in w or w[k][0] < d.idx:
                        w[k] = (d.idx, d)
                for k, (val, d) in w.items():
                    seen[k] = val
                    if d is not None:
                        d.signal = True
                op.waits = list(w.items())
        for eng in ENGS:
            c = 0
            for op in self.ops[eng]:
                if op.signal:
                    c += 1
                    op.cnt = c

        with ExitStack() as st:
            esem = {e: st.enter_context(nc.semaphore("s_" + e)) for e in ENGS}
            dsem = {k: st.enter_context(nc.semaphore("d_" + str(k))) for k in self.dma_cnt}
            block = st.enter_context(nc.Block())

            def run(eng, e):
                for op in self.ops[eng]:
                    for k, (val, d) in op.waits:
                        if k[0] == "d":
                            e.wait_ge(dsem[k[1]], val)
                        else:
                            e.wait_ge(esem[k[1]], d.cnt)
                    if op.fn is None:
                        continue
                    ins = op.fn(e)
                    if op.is_dma:
                        ins.then_inc(dsem[op.dma_key], 16)
                    elif op.signal:
                        ins.then_inc(esem[eng], 1)

            @block.tensor
            def _(e):
                run("pe", e)

            @block.scalar
            def _(e):
                run("act", e)

            @block.vector
            def _(e):
                run("dve", e)

            @block.gpsimd
            def _(e):
                run("pool", e)

            @block.sync
            def _(e):
                run("sp", e)


class Arena:
    def __init__(self, arena_ap, nbytes):
        self.ap = arena_ap
        self.n = nbytes
        self.top = 0
        self.peak = 0

    def alloc(self, free_shape, dtype):
        es = 2 if dtype == BF16 else 4
        n = es
        for d in free_shape:
            n *= d
        off = (self.top + 63) // 64 * 64
        assert off + n <= self.n, ("arena overflow", off, n, self.n)
        self.top = off + n
        self.peak = max(self.peak, self.top)
        ap = self.ap[:, off // 4:(off + n + 3) // 4]
        if dtype == BF16:
            ap = ap.bitcast(BF16)
        if len(free_shape) == 2:
            ap = ap.rearrange("p (a b) -> p a b", a=free_shape[0])
        elif len(free_shape) == 3:
            ap = ap.rearrange("p (a b c) -> p a b c", a=free_shape[0], b=free_shape[1])
        elif len(free_shape) == 4:
            ap = ap.rearrange("p (a b c d) -> p a b c d", a=free_shape[0], b=free_shape[1], c=free_shape[2])
        return Buf(ap, "S", off, es, free_shape)

    def view_at(self, off, free_shape, dtype):
        es = 2 if dtype == BF16 else 4
        n = es
        for d in free_shape:
            n *= d
        ap = self.ap[:, off // 4:(off + n + 3) // 4]
        if dtype == BF16:
            ap = ap.bitcast(BF16)
        if len(free_shape) == 2:
            ap = ap.rearrange("p (a b) -> p a b", a=free_shape[0])
        return Buf(ap, "S", off, es, free_shape)

    def mark(self):
        return self.top

    def release(self, m):
        self.top = m


ARENA_BYTES = int(os.environ.get("MK_ARENA", "212480"))
SLOT_BYTES = 8192
NSLOTS = 4

PP_GAIN = 0
PP_CONVW = 48
PP_CONVB = 80
PP_BA = 88
PP_BX = 96
PP_LAM = 104
PP_SINK = 112
NPP = 128


def build_program(stage):
    HUP = os.environ.get("MK_HUP", "dve")
    SUB = int(os.environ.get("MK_SUB", "9"))
    M2L = int(os.environ.get("MK_M2", "9"))
    M3L = int(os.environ.get("MK_M3", "9"))
    nc = bass.Bass("TRN2", target_bir_lowering=False)
    dram = {}

    def din(name, shape, dt=F32):
        dram[name] = nc.dram_tensor(name, list(shape), dt, kind="ExternalInput").ap()
        return dram[name]

    xT = din("xT", (D, S))
    pp_d = din("pp", (128, NPP))
    w_gu1 = din("w_gu1", (D, 2 * DFF))
    w_dn1 = din("w_dn1", (DFF, D))
    w_gu2 = din("w_gu2", (D, 2 * DFF))
    w_dn2 = din("w_dn2", (DFF, D))
    w_in = din("w_in", (D, 5632))
    w_kv = din("w_kv", (D, 1024))
    w_pl = din("w_pl", (D, D))
    w_pa = din("w_pa", (D, D))
    w_o = din("w_o", (D, D))
    lw_a = din("lw_a", (16, 64, 64))
    lw_x = din("lw_x", (16, 64, 64))
    rope_d = din("rope", (2, 128, S))
    cst_d = din("cst", (128, 128 + 1024))
    outT = nc.dram_tensor("outT", [D, S], F32, kind="ExternalOutput").ap()

    P = Prog()
    with ExitStack() as st:
        arena_t = st.enter_context(nc.sbuf_tensor("arena", [128, ARENA_BYTES // 4], F32))
        psum_t = st.enter_context(nc.psum_tensor("psum", [128, 4096], F32))
        A = Arena(arena_t, ARENA_BYTES)

        def bank(i, n=1):
            return Buf(psum_t[:, i * 512:(i + n) * 512], "P", i * 2048, 4, (n * 512,))

        banks = [bank(i) for i in range(8)]

        hT = A.alloc((8, S), F32)
        pp = A.alloc((NPP,), F32)
        hp = A.alloc((48,), F32)
        cf = A.alloc((8,), F32)
        lc = A.alloc((64,), F32)
        ones = A.alloc((128,), BF16)
        BDa = A.alloc((8, 128), BF16)
        BDx = A.alloc((8, 128), BF16)
        slots = [A.alloc((SLOT_BYTES // 2,), BF16) for _ in range(NSLOTS)]
        slot_i = [0]

        def next_slot(shape):
            i = slot_i[0] % NSLOTS
            slot_i[0] += 1
            sb = slots[i]
            n = 1
            for d in shape:
                n *= d
            assert n * 2 <= SLOT_BYTES
            ap = sb.ap[:, 0:n]
            if len(shape) == 2:
                ap = ap.rearrange("p (a b) -> p a b", a=shape[0])
            return Buf(ap, "S", sb.off, 2, shape), "w%d" % i

        def wload(dst, src, key):
            P.dma("pool", dst, src, key)

        def kp(ap):
            return ap.rearrange("(k p) n -> p k n", p=128)

        P.dma("sp", pp.all(), pp_d, "pp")
        for sub in range(4):
            for c in range(8):
                P.dma("sp", hT[:, c, sub * 512:(sub + 1) * 512], xT[c * 128:(c + 1) * 128, sub * 512:(sub + 1) * 512],
                      "x%d" % sub)
        P.memset(ones.all(), 1.0)
        P.memset(cf[:, 0:1], EPS)
        P.memset(cf[:, 1:2], 1.0)
        P.ts(hp.all(), pp[:, 0:48], 0.5, None, ALU.mult)
        epsv = cf[:, 0:1]
        onev = cf[:, 1:2]

        def gain(i, c):
            return pp[:, PP_GAIN + i * 8 + c:PP_GAIN + i * 8 + c + 1]

        def hgain(i, c):
            return hp[:, i * 8 + c:i * 8 + c + 1]

        if stage >= 2:
            P.memset(BDa.all(), 0.0)
            P.memset(BDx.all(), 0.0)
            for (bd, lw, key) in ((BDa, lw_a, "bda"), (BDx, lw_x, "bdx")):
                src = lw.rearrange("(c two) i o -> two i c o", two=2)
                for two in range(2):
                    P.dma("pool", bd[two * 64:(two + 1) * 64, :, two * 64:(two + 1) * 64], src[two], key)
            X = lc[:, 0:8]
            T_ = lc[:, 8:16]
            CL = lc[:, 16:24]
            HCL = lc[:, 24:32]
            P.act(X, pp[:, PP_LAM:PP_LAM + 8], AF.Exp, scale=-1.0)
            P.ts(T_, X, 1.0 / 3.0, -0.5, ALU.mult, ALU.add)
            P.tt(T_, T_, X, ALU.mult)
            P.ts(T_, T_, 1.0, None, ALU.add)
            P.tt(T_, T_, X, ALU.mult)
            P.ts(CL, T_, -8.0, None, ALU.mult)
            P.ts(HCL, T_, -4.0, None, ALU.mult)
            P.ts(lc[:, 32:48], pp[:, PP_BA:PP_BA + 16], 0.5, None, ALU.mult)
            P.act(lc[:, 48:64], pp[:, PP_SINK:PP_SINK + 16], AF.Exp)

        def prenorm(gi, uT, t0, ntok, sq_bufs, rstd, ssb):
            k = 0
            for sub in range(ntok // 512):
                ts_ = t0 + sub * 512
                ss = banks[ssb[sub % len(ssb)]]
                rs = rstd[:, sub * 512:(sub + 1) * 512]
                for c in range(8):
                    sq = sq_bufs[k % len(sq_bufs)]
                    k += 1
                    P.act(sq.all(), hT[:, c, ts_:ts_ + 512], AF.Square)
                    P.mm(ss.all(), ones.all(), sq.all(), start=(c == 0), stop=(c == 7))
                P.act(rs, ss.all(), AF.Ln, bias=epsv, scale=1.0 / D)
                P.act(rs, rs, AF.Exp, scale=-0.5)
                for c in range(8):
                    P.stt(uT[:, c, sub * 512:(sub + 1) * 512], hT[:, c, ts_:ts_ + 512], gain(gi, c),
                          rs, ALU.mult, ALU.mult)

        def ffn(gi_pre, gi_post, w_gu, w_dn):
            m = A.mark()
            TB = 1024
            uT = A.alloc((8, TB), BF16)
            actT = A.alloc((NFF, TB), BF16)
            fsb = A.alloc((8, TB), F32)
            sqb = [A.alloc((512,), BF16) for _ in range(2)]
            rstd = A.alloc((TB,), F32)
            sgb = [A.alloc((512,), F32) for _ in range(2)]
            tmpb = [A.alloc((512,), F32) for _ in range(2)]
            for tb in range(S // TB):
                T0 = tb * TB
                P.phase = "ffn%d.%d" % (gi_pre // 4 + 1, tb)
                prenorm(gi_pre, uT, T0, TB, sqb, rstd, (6, 7))
                k = 0
                for grp in range(NFF // 2):
                    W, key = next_slot((8, 512))
                    wload(W[:, :, 0:256], kp(w_gu[:, grp * 256:(grp + 1) * 256]), key)
                    wload(W[:, :, 256:512], kp(w_gu[:, DFF + grp * 256:DFF + (grp + 1) * 256]), key)
                    for f2 in range(2):
                        ffc = grp * 2 + f2
                        for sub in range(2):
                            pg = banks[(k % 2) * 2]
                            pu = banks[(k % 2) * 2 + 1]
                            sg = sgb[k % 2]
                            k += 1
                            for dc in range(8):
                                P.mm(pg.all(), W[:, dc, f2 * 128:(f2 + 1) * 128], uT[:, dc, sub * 512:(sub + 1) * 512],
                                     start=(dc == 0), stop=(dc == 7))
                            for dc in range(8):
                                P.mm(pu.all(), W[:, dc, 256 + f2 * 128:256 + (f2 + 1) * 128],
                                     uT[:, dc, sub * 512:(sub + 1) * 512], start=(dc == 0), stop=(dc == 7))
                            P.act(sg.all(), pg.all(), AF.Silu)
                            P.tt(actT[:, ffc, sub * 512:(sub + 1) * 512], sg.all(), pu.all(), ALU.mult)
                k = 0
                for dc in range(8):
                    W, key = next_slot((NFF, 128))
                    wload(W.all(), kp(w_dn[:, dc * 128:(dc + 1) * 128]), key)
                    for sub in range(2):
                        pf = banks[4 + (k % 2)]
                        sq = sqb[k % 2]
                        k += 1
                        for ffc in range(NFF):
                            P.mm(pf.all(), W[:, ffc, :], actT[:, ffc, sub * 512:(sub + 1) * 512],
                                 start=(ffc == 0), stop=(ffc == NFF - 1))
                        P.copy(fsb[:, dc, sub * 512:(sub + 1) * 512], pf.all(), eng="act")
                        P.act(sq.all(), pf.all(), AF.Square)
                        P.mm(banks[6 + sub].all(), ones.all(), sq.all(), start=(dc == 0), stop=(dc == 7))
                for sub in range(2):
                    rs = rstd[:, sub * 512:(sub + 1) * 512]
                    P.act(rs, banks[6 + sub].all(), AF.Ln, bias=epsv, scale=1.0 / D)
                    P.act(rs, rs, AF.Exp, scale=-0.5)
                k = 0
                for sub in range(2):
                    ts_ = T0 + sub * 512
                    for c in range(8):
                        tmp = tmpb[k % 2]
                        k += 1
                        P.stt(tmp.all(), fsb[:, c, sub * 512:(sub + 1) * 512], hgain(gi_post, c),
                              rstd[:, sub * 512:(sub + 1) * 512], ALU.mult, ALU.mult)
                        P.tt(hT[:, c, ts_:ts_ + 512], hT[:, c, ts_:ts_ + 512], tmp.all(), ALU.add, eng=HUP)
            A.release(m)

        def mixer():
            m = A.mark()
            TB = 512
            DB = int(os.environ.get("MK_DB", "1"))
            uTs = [A.alloc((8, TB), BF16) for _ in range(DB)]
            y_lrus = [A.alloc((8, TB), BF16) for _ in range(int(os.environ.get("MK_DBY", "1")))]
            qy = A.alloc((16, TB), BF16)
            msb = A.view_at(qy.off, (8, TB), F32)
            kT = A.alloc((4, 640), BF16)
            vS = A.alloc((5, 512), BF16)
            merged = A.alloc((8, TB), BF16)
            ropeC = A.alloc((TB,), F32)
            ropeS = A.alloc((TB,), F32)
            NT = int(os.environ.get("MK_NT", "19"))
            tmp = [A.alloc((516,), F32) for _ in range(NT)]
            Eb = [A.alloc((1024,), BF16) for _ in range(int(os.environ.get("MK_EB", "2")))]
            qbb = [A.alloc((512,), BF16) for _ in range(2)]
            sqb = [A.alloc((512,), BF16) for _ in range(2)]
            rstd = A.alloc((TB,), F32)
            Psw = A.alloc((128,), BF16)
            mask = A.alloc((1024,), BF16)
            xcar = A.alloc((8, 4), F32)
            hcar = A.alloc((8,), F32)
            ti = [0]
            bi = [0]
            qi = [0]

            def T():
                t = tmp[ti[0] % NT]
                ti[0] += 1
                return t

            def nb():
                b = banks[bi[0] % 6]
                bi[0] += 1
                return b

            def QB():
                b = qbb[qi[0] % 2]
                qi[0] += 1
                return b

            print("mixer arena top", A.top, "of", ARENA_BYTES)
            OFF = os.environ.get("MK_OFF", "0") == "1"
            PEX = "pool" if OFF else "dve"
            P.dma("pool", Psw.all(), cst_d[:, 0:128], "psw")
            P.dma("pool", mask.all(), cst_d[:, 128:1152], "msk")
            P.memset(xcar.all(), 0.0)
            P.memset(hcar.all(), 0.0)
            C1 = 0.7978845608028654
            C2 = C1 * 0.044715
            cvw = lambda k, c: pp[:, PP_CONVW + k * 8 + c:PP_CONVW + k * 8 + c + 1]
            cvb = lambda c: pp[:, PP_CONVB + c:PP_CONVB + c + 1]
            lcv = lambda base, c: lc[:, base + c:base + c + 1]
            F = slice(0, 512)
            pstride = ARENA_BYTES // 4

            for tb in range(S // TB):
                t0 = tb * TB
                uT = uTs[tb % DB]
                y_lru = y_lrus[tb % len(y_lrus)]
                P.phase = "mix%d.lru" % tb
                prenorm(2, uT, t0, TB, sqb, rstd, (6, 7))
                P.dma("sp", ropeC.all(), rope_d[0][:, t0:t0 + TB], "rc")
                P.dma("sp", ropeS.all(), rope_d[1][:, t0:t0 + TB], "rs")

                for cp in range(4 if SUB >= 1 else 0):
                    W, key = next_slot((8, 512))
                    wload(W[:, :, 0:256], kp(w_in[:, cp * 256:(cp + 1) * 256]), key)
                    wload(W[:, :, 256:512], kp(w_in[:, 1024 + cp * 256:1024 + (cp + 1) * 256]), key)
                    for c2 in range(2):
                        c = cp * 2 + c2
                        pg = nb()
                        px = nb()
                        for dc in range(8):
                            P.mm(px.all(), W[:, dc, 256 + c2 * 128:256 + (c2 + 1) * 128], uT[:, dc, :],
                                 start=(dc == 0), stop=(dc == 7))
                        for dc in range(8):
                            P.mm(pg.all(), W[:, dc, c2 * 128:(c2 + 1) * 128], uT[:, dc, :],
                                 start=(dc == 0), stop=(dc == 7))
                        xf = T()
                        P.copy(xf[:, 0:3], xcar[:, c, 0:3], eng="pool")
                        P.copy(xf[:, 3:515], px.all(), eng="act")
                        P.copy(xcar[:, c, 0:3], xf[:, 512:515], eng="pool")
                        if M2L < 2:
                            continue
                        xc = T()
                        P.ts(xc[:, F], xf[:, 0:512], cvw(0, c), cvb(c), ALU.mult, ALU.add, eng=PEX)
                        for k in range(1, 4):
                            P.stt(xc[:, F], xf[:, k:k + 512], cvw(k, c), xc[:, F], ALU.mult, ALU.add)
                        xcb = QB()
                        P.copy(xcb.all(), xc[:, F], eng="act")
                        pr = nb()
                        pi = nb()
                        P.mm(pr.all(), BDa[:, c, :], xcb.all())
                        P.mm(pi.all(), BDx[:, c, :], xcb.all())
                        if M2L < 3:
                            continue
                        thr = T()
                        thi = T()
                        P.act(thr[:, F], pr.all(), AF.Tanh, bias=lcv(32, c), scale=0.5)
                        P.act(thi[:, F], pi.all(), AF.Tanh, bias=lcv(40, c), scale=0.5)
                        mu = T()
                        P.act(mu[:, F], thr[:, F], AF.Exp, bias=lcv(16, c), scale=lcv(16, c))
                        a = thr
                        P.act(a[:, F], thr[:, F], AF.Exp, bias=lcv(24, c), scale=lcv(24, c))
                        sq = T()
                        NG = os.environ.get("MK_NGELU", "1") == "1"
                        if NG:
                            P.act(sq[:, F], pg.all(), AF.Gelu_apprx_tanh)
                        else:
                            P.act(sq[:, F], pg.all(), AF.Square)
                            P.ts(sq[:, F], sq[:, F], C2, C1, ALU.mult, ALU.add, eng=PEX)
                            P.tt(sq[:, F], sq[:, F], pg.all(), ALU.mult)
                            P.act(sq[:, F], sq[:, F], AF.Tanh)
                        P.act(mu[:, F], mu[:, F], AF.Sqrt, bias=onev, scale=-1.0)
                        if M2L < 4:
                            continue
                        t1 = thi
                        P.stt(t1[:, F], thi[:, F], 1.0, xc[:, F], ALU.add, ALU.mult)
                        P.stt(t1[:, F], t1[:, F], 0.5, mu[:, F], ALU.mult, ALU.mult)
                        if M2L < 5:
                            continue
                        hs = xc
                        P.scan(hs[:, F], a[:, F], t1[:, F], hcar[:, c:c + 1], ALU.mult, ALU.add)
                        if M2L < 6:
                            continue
                        P.copy(hcar[:, c:c + 1], hs[:, 511:512], eng="dve")
                        if NG:
                            P.tt(y_lru[:, c, :], sq[:, F], hs[:, F], ALU.mult)
                        else:
                            P.stt(sq[:, F], sq[:, F], 1.0, pg.all(), ALU.add, ALU.mult)
                            P.stt(y_lru[:, c, :], sq[:, F], 0.5, hs[:, F], ALU.mult, ALU.mult)

                if SUB < 2:
                    continue
                P.phase = "mix%d.qkv" % tb
                RCL = int(os.environ.get("MK_RC", "9"))

                def rope_chunk(pq, outv):
                    if RCL < 1:
                        return
                    qb = QB()
                    P.copy(qb.all(), pq.all(), eng="act")
                    ps = nb()
                    P.mm(ps.all(), (ones if os.environ.get("MK_X") == "1" else Psw).all(), qb.all())
                    if RCL < 2:
                        return
                    r1 = T()
                    r2 = T()
                    P.tt(r1[:, F], ropeC.all(), pq.all(), ALU.mult)
                    P.tt(r2[:, F], ropeS.all(), ps.all(), ALU.mult)
                    if RCL >= 3:
                        P.tt(outv, r1[:, F], r2[:, F], ALU.add, eng=PEX)

                for qp in range(2):
                    W, key = next_slot((8, 512))
                    wload(W.all(), kp(w_in[:, 2048 + qp * 512:2048 + (qp + 1) * 512]), key)
                    for c4 in range(4):
                        pq = nb()
                        for dc in range(8):
                            P.mm(pq.all(), W[:, dc, c4 * 128:(c4 + 1) * 128], uT[:, dc, :], start=(dc == 0), stop=(dc == 7))
                        rope_chunk(pq, qy[:, qp * 4 + c4, :])
                W, key = next_slot((8, 512))
                wload(W.all(), kp(w_kv[:, 0:512]), key)
                for j in range(4 if M3L >= 2 else 0):
                    pk = nb()
                    for dc in range(8):
                        P.mm(pk.all(), W[:, dc, j * 128:(j + 1) * 128], uT[:, dc, :], start=(dc == 0), stop=(dc == 7))
                    rope_chunk(pk, kT[:, j, 128:640])
                W, key = next_slot((8, 512))
                wload(W.all(), kp(w_kv[:, 512:1024]), key)
                for i in range(4 if M3L >= 3 else 0):
                    pv = nb()
                    for dc in range(8):
                        P.mm(pv.all(), uT[:, dc, i * 128:(i + 1) * 128], W[:, dc, :], start=(dc == 0), stop=(dc == 7))
                    P.copy(vS[:, 1 + i, :], pv.all(), eng="act")

                P.phase = "mix%d.att" % tb
                k_ = 0
                for n in range(4 if SUB >= 3 else 0):
                    nglob = tb * 4 + n
                    kbs = (0, 1) if nglob > 0 else (1,)
                    for j in range(4):
                        pb = (k_ % 2) * 2
                        S2 = Buf(psum_t[:, pb * 512:(pb + 2) * 512].rearrange("p (h k g q) -> p h k g q", h=2, k=2, g=2),
                                 "P", pb * 2048, 4, (2, 2, 2, 128))
                        po = banks[4 + (k_ % 2)]
                        den = banks[6 + (k_ % 2)]
                        E = Eb[k_ % len(Eb)]
                        k_ += 1
                        for kb in kbs:
                            kc = slice((n + kb) * 128, (n + kb + 1) * 128)
                            for hh2 in range(2):
                                for half in range(2):
                                    rows = slice(half * 64, half * 64 + 64)
                                    P.mm(S2[:, half, kb, hh2, :], kT[rows, j, kc],
                                         qy[rows, 2 * j + hh2, n * 128:(n + 1) * 128])
                        P.act(E.all(), S2.all(), AF.Exp, scale=0.125)
                        P.tt(E.all(), E.all(), mask.all(), ALU.mult, eng=("pool" if os.environ.get("MK_MASKPOOL", "0") == "1" else "dve"))
                        E4 = Buf(E.ap.rearrange("p (h k g q) -> p h k g q", h=2, k=2, g=2), "S", E.off, 2, (2, 2, 2, 128))
                        for idx, kb in enumerate(kbs):
                            P.mm(po.all(), vS[:, n + kb, j * 128:(j + 1) * 128], E4[:, :, kb, :, :],
                                 start=(idx == 0), stop=(idx == len(kbs) - 1))
                        for idx, kb in enumerate(kbs):
                            P.mm(den.all(), ones.all(), E4[:, :, kb, :, :],
                                 start=(idx == 0), stop=(idx == len(kbs) - 1))
                        rec = T()
                        sink_ap = bass.AP(arena_t, lc.off // 4 + 48 + 4 * j, [[pstride, 128], [1, 2], [2, 2], [0, 128]])
                        sinkv = V(sink_ap, lc[:, 48:64].regs)
                        rec4 = Buf(rec.ap[:, 0:512].rearrange("p (h g q) -> p h g q", h=2, g=2), "S", rec.off, 4, (2, 2, 128))
                        den4 = Buf(den.ap.rearrange("p (h g q) -> p h g q", h=2, g=2), "P", den.off, 4, (2, 2, 128))
                        po4 = Buf(po.ap.rearrange("p (h g q) -> p h g q", h=2, g=2), "P", po.off, 4, (2, 2, 128))
                        P.tt(rec4.all(), den4.all(), sinkv, ALU.add)
                        P.act(rec[:, F], rec[:, F], AF.Ln)
                        P.act(rec[:, F], rec[:, F], AF.Exp, scale=-1.0)
                        for half in range(2):
                            rows = slice(half * 64, half * 64 + 64)
                            P.tt(qy[rows, 8 + 2 * j:8 + 2 * j + 2, n * 128:(n + 1) * 128], po4[rows, half, :, :],
                                 rec4[rows, half, :, :], ALU.mult)
                if tb < S // TB - 1:
                    P.copy(kT[:, :, 0:128], kT[:, :, 512:640], eng="pool")
                    P.copy(vS[:, 0, :], vS[:, 4, :], eng="pool")

                if SUB < 4:
                    continue
                P.phase = "mix%d.mrg" % tb
                for g in range(2):
                    sg = [T() for _ in range(4)]
                    sa = [T() for _ in range(4)]
                    W, key = next_slot((8, 512))
                    wload(W.all(), kp(w_in[:, 3584 + g * 512:3584 + (g + 1) * 512]), key)
                    for d4 in range(4):
                        p_ = nb()
                        for dc in range(8):
                            P.mm(p_.all(), W[:, dc, d4 * 128:(d4 + 1) * 128], uT[:, dc, :], start=(dc == 0), stop=(dc == 7))
                        P.act(sg[d4][:, F], p_.all(), AF.Sigmoid)
                    W, key = next_slot((8, 512))
                    wload(W.all(), kp(w_pl[:, g * 512:(g + 1) * 512]), key)
                    for d4 in range(4):
                        p_ = nb()
                        for dc in range(8):
                            P.mm(p_.all(), W[:, dc, d4 * 128:(d4 + 1) * 128], y_lru[:, dc, :], start=(dc == 0), stop=(dc == 7))
                        P.tt(sg[d4][:, F], sg[d4][:, F], p_.all(), ALU.mult)
                    W, key = next_slot((8, 512))
                    wload(W.all(), kp(w_in[:, 4608 + g * 512:4608 + (g + 1) * 512]), key)
                    for d4 in range(4):
                        p_ = nb()
                        for dc in range(8):
                            P.mm(p_.all(), W[:, dc, d4 * 128:(d4 + 1) * 128], uT[:, dc, :], start=(dc == 0), stop=(dc == 7))
                        P.act(sa[d4][:, F], p_.all(), AF.Sigmoid)
                    W, key = next_slot((8, 512))
                    wload(W.all(), kp(w_pa[:, g * 512:(g + 1) * 512]), key)
                    for d4 in range(4):
                        p_ = nb()
                        for dc in range(8):
                            P.mm(p_.all(), W[:, dc, d4 * 128:(d4 + 1) * 128], qy[:, 8 + dc, :], start=(dc == 0), stop=(dc == 7))
                        P.tt(sa[d4][:, F], sa[d4][:, F], p_.all(), ALU.mult)
                        P.tt(merged[:, g * 4 + d4, :], sa[d4][:, F], sg[d4][:, F], ALU.add, eng=PEX)

                P.phase = "mix%d.out" % tb
                k_ = 0
                for g in range(2):
                    W, key = next_slot((8, 512))
                    wload(W.all(), kp(w_o[:, g * 512:(g + 1) * 512]), key)
                    for d4 in range(4):
                        dcp = g * 4 + d4
                        p_ = nb()
                        sq = sqb[k_ % 2]
                        k_ += 1
                        for dc in range(8):
                            P.mm(p_.all(), W[:, dc, d4 * 128:(d4 + 1) * 128], merged[:, dc, :], start=(dc == 0), stop=(dc == 7))
                        P.copy(msb[:, dcp, :], p_.all(), eng="act")
                        P.act(sq.all(), p_.all(), AF.Square)
                        P.mm(banks[6].all(), ones.all(), sq.all(), start=(dcp == 0), stop=(dcp == 7))
                P.act(rstd.all(), banks[6].all(), AF.Ln, bias=epsv, scale=1.0 / D)
                P.act(rstd.all(), rstd.all(), AF.Exp, scale=-0.5)
                for c in range(8):
                    tm = T()
                    P.stt(tm[:, F], msb[:, c, :], gain(3, c), rstd.all(), ALU.mult, ALU.mult)
                    P.tt(hT[:, c, t0:t0 + TB], hT[:, c, t0:t0 + TB], tm[:, F], ALU.add, eng=HUP)
            A.release(m)

        if stage >= 1:
            ffn(0, 1, w_gu1, w_dn1)
        if stage >= 2:
            mixer()
        if stage >= 3:
            ffn(4, 5, w_gu2, w_dn2)

        for sub in range(4):
            for c in range(8):
                P.dma("sp", outT[c * 128:(c + 1) * 128, sub * 512:(sub + 1) * 512], hT[:, c, sub * 512:(sub + 1) * 512],
                      "o%d" % sub)

        P.emit(nc)
        print("arena peak bytes", A.peak, "ops", {e: len(P.ops[e]) for e in ENGS})
    return nc


_CACHE = {}


def _rope_tables():
    half = 32
    inv_freq = 10000.0 ** (-np.arange(half, dtype=np.float64) / half)
    ang = np.arange(S, dtype=np.float64)[:, None] * inv_freq[None, :]
    cos = np.cos(ang).T
    sin = np.sin(ang).T
    C = np.zeros((128, S), np.float32)
    Sg = np.zeros((128, S), np.float32)
    for p in range(128):
        i = p % 32
        C[p] = cos[i]
        Sg[p] = -sin[i] if (p % 64) < 32 else sin[i]
    return np.stack([C, Sg], 0)


def _consts():
    c = np.zeros((128, 128 + 1024), np.float32)
    for m in range(128):
        k = m + 32 if (m % 64) < 32 else m - 32
        c[k, m] = 1.0
    s = np.arange(128)[:, None]
    q = np.arange(128)[None, :]
    prev = (q < s).astype(np.float32)
    cur = (q >= s).astype(np.float32)
    mk = np.concatenate([prev, prev, cur, cur, prev, prev, cur, cur], axis=1)
    c[:, 128:] = mk
    return c


def kernel(**inp):
    stage = int(os.environ.get("MK_STAGE", "3"))
    if stage not in _CACHE:
        _CACHE[stage] = build_program(stage)
    nc = _CACHE[stage]
    f = lambda a: np.ascontiguousarray(np.asarray(a, dtype=np.float32))
    x = f(inp["x"])

    def col(v):
        return f(v).reshape(8, 128).T

    pp = np.zeros((128, NPP), np.float32)
    for i, nm in enumerate(["ffn1_pre_g", "ffn1_post_g", "mix_pre_g", "mix_post_g", "ffn2_pre_g", "ffn2_post_g"]):
        pp[:, PP_GAIN + i * 8:PP_GAIN + (i + 1) * 8] = col(inp[nm][0])
    cw = f(inp["conv_w"][0])
    for k in range(4):
        pp[:, PP_CONVW + k * 8:PP_CONVW + (k + 1) * 8] = col(cw[k])
    pp[:, PP_CONVB:PP_CONVB + 8] = col(inp["conv_b"][0])
    pp[:, PP_BA:PP_BA + 8] = col(inp["lru_b_a"][0])
    pp[:, PP_BX:PP_BX + 8] = col(inp["lru_b_x"][0])
    pp[:, PP_LAM:PP_LAM + 8] = col(inp["lru_lambda"][0])
    pp[:, PP_SINK:PP_SINK + 16] = np.broadcast_to(f(inp["attn_sinks"][0])[None, :], (128, 16))

    w_in = f(inp["w_in"][0])
    kcols = w_in[:, 3072:3328].reshape(D, 4, 64)
    vcols = w_in[:, 3328:3584].reshape(D, 4, 64)
    w_kv = np.concatenate([np.repeat(kcols[:, :, None, :], 2, axis=2).reshape(D, 512),
                           np.repeat(vcols[:, :, None, :], 2, axis=2).reshape(D, 512)], axis=1)
    shared = {
        "pp": pp,
        "w_gu1": f(inp["ffn1_w_gu"][0]), "w_dn1": f(inp["ffn1_w_down"][0]),
        "w_gu2": f(inp["ffn2_w_gu"][0]), "w_dn2": f(inp["ffn2_w_down"][0]),
        "w_in": w_in, "w_kv": f(w_kv),
        "w_pl": f(inp["w_proj_lru"][0]), "w_pa": f(inp["w_proj_attn"][0]), "w_o": f(inp["w_out"][0]),
        "lw_a": f(inp["lru_w_a"][0]), "lw_x": f(inp["lru_w_x"][0]),
        "rope": _rope_tables(), "cst": _consts(),
    }
    in_maps = []
    for b in range(NCORES):
        m = dict(shared)
        m["xT"] = np.ascontiguousarray(x[b].T)
        in_maps.append(m)
    res = run_bass_kernel_spmd(nc, in_maps, core_ids=list(range(NCORES)))
    out = np.stack([np.asarray(res.results[b]["outT"]).T for b in range(NCORES)], axis=0)
    return np.ascontiguousarray(out.astype(np.float32))
```
